# Optimizing a Trainium2 kernel written in Bass

```python
import math
import jax, jax.numpy as jnp
from jax import lax
import numpy as np

D_MODEL = 1024
BATCH = 2
SEQ = 8192
DEPTH = 2

HEAD_DIM = 64
D_MIX = D_MODEL
GROUP_W = D_MIX // 4
N_HEADS_A = GROUP_W // HEAD_DIM
N_HEADS_B = GROUP_W // HEAD_DIM
N_GROUPS_C = 4
N_GROUPS_D = 4
MOBA_BLOCK = 256
MOBA_TOPK = 3
QBLOCK = 128
ROPE_THETA = 500000.0
ROPE_DIM = HEAD_DIM // 4
SGU_CHUNK = 128
POOL_WINDOWS = (2, 4, 8, 16)
D_FF = 2816
CONV_WIDTH = 3
DN_ALPHA = (2 * DEPTH) ** 0.25
DN_BETA = (8 * DEPTH) ** -0.25
LN_EPS = 1e-5
IN_SIZES = (GROUP_W, GROUP_W, GROUP_W, GROUP_W, GROUP_W, GROUP_W, N_HEADS_B, GROUP_W, GROUP_W, GROUP_W)
IN_SPLITS = tuple(int(s) for s in np.cumsum(IN_SIZES)[:-1])
N_IN = int(sum(IN_SIZES))

kernel_name = "hybrid_moba_fox_sgu_pool_block"


def layer_norm(x, g, b):
    xf = x.astype(jnp.float32)
    mu = jnp.mean(xf, axis=-1, keepdims=True)
    var = jnp.mean(jnp.square(xf - mu), axis=-1, keepdims=True)
    return ((xf - mu) * lax.rsqrt(var + LN_EPS) * g.astype(jnp.float32) + b.astype(jnp.float32)).astype(x.dtype)


def to_heads(t, n_heads):
    b, s, _ = t.shape
    return t.reshape(b, s, n_heads, -1).transpose(0, 2, 1, 3)


def from_heads(t):
    b, h, s, d = t.shape
    return t.transpose(0, 2, 1, 3).reshape(b, s, h * d)


def rotary_tables(seq_len):
    pos = jnp.arange(seq_len, dtype=jnp.float32)
    inv_freq = ROPE_THETA ** (-jnp.arange(0, ROPE_DIM, 2, dtype=jnp.float32) / ROPE_DIM)
    ang = pos[:, None] * inv_freq[None, :]
    return jnp.cos(ang), jnp.sin(ang)


def partial_rotary(x, cos, sin):
    half = ROPE_DIM // 2
    cos = cos.astype(x.dtype)
    sin = sin.astype(x.dtype)
    x1 = x[..., :half]
    x2 = x[..., half:ROPE_DIM]
    return jnp.concatenate([x1 * cos - x2 * sin, x1 * sin + x2 * cos, x[..., ROPE_DIM:]], axis=-1)


def moba_attention(q, k, v):
    bsz, h, s, hd = q.shape
    s_pad = -(-s // MOBA_BLOCK) * MOBA_BLOCK
    pad = ((0, 0), (0, 0), (0, s_pad - s), (0, 0))
    q, k, v = jnp.pad(q, pad), jnp.pad(k, pad), jnp.pad(v, pad)
    nb = s_pad // MOBA_BLOCK
    n_sel = min(MOBA_TOPK, nb)
    scale = hd ** -0.5
    kb = k.reshape(bsz, h, nb, MOBA_BLOCK, hd)
    vb = v.reshape(bsz, h, nb, MOBA_BLOCK, hd)
    kbar = jnp.mean(kb.astype(jnp.float32), axis=3)
    gate = jnp.einsum('bhsd,bhnd->bhsn', q.astype(jnp.float32), kbar)
    q_blk = jnp.arange(s_pad) // MOBA_BLOCK
    fully_past = jnp.arange(nb)[None, :] < q_blk[:, None]
    gate = jnp.where(fully_past, gate, -jnp.inf)
    _, sel = lax.top_k(gate, n_sel)
    bi = jnp.arange(bsz)[:, None, None, None]
    hi = jnp.arange(h)[None, :, None, None]

    def one_chunk(n):
        start = n * QBLOCK
        blk = start // MOBA_BLOCK
        qc = lax.dynamic_slice_in_dim(q, start, QBLOCK, axis=2)
        selc = lax.dynamic_slice_in_dim(sel, start, QBLOCK, axis=2)
        valid = selc < blk
        ks = kb[bi, hi, selc]
        vs = vb[bi, hi, selc]
        s_sel = jnp.einsum('bhqd,bhqnjd->bhqnj', qc, ks).astype(jnp.float32) * scale
        s_sel = jnp.where(valid[..., None], s_sel, -jnp.inf).reshape(bsz, h, QBLOCK, n_sel * MOBA_BLOCK)
        k_own = lax.dynamic_slice_in_dim(k, blk * MOBA_BLOCK, MOBA_BLOCK, axis=2)
        v_own = lax.dynamic_slice_in_dim(v, blk * MOBA_BLOCK, MOBA_BLOCK, axis=2)
        s_own = jnp.einsum('bhqd,bhjd->bhqj', qc, k_own).astype(jnp.float32) * scale
        qpos = start + jnp.arange(QBLOCK)
        kpos = blk * MOBA_BLOCK + jnp.arange(MOBA_BLOCK)
        s_own = jnp.where(kpos[None, :] <= qpos[:, None], s_own, -jnp.inf)
        p = jax.nn.softmax(jnp.concatenate([s_sel, s_own], axis=-1), axis=-1)
        p_sel = p[..., :n_sel * MOBA_BLOCK].reshape(bsz, h, QBLOCK, n_sel, MOBA_BLOCK).astype(v.dtype)
        p_own = p[..., n_sel * MOBA_BLOCK:].astype(v.dtype)
        return (jnp.einsum('bhqnj,bhqnjd->bhqd', p_sel, vs)
                + jnp.einsum('bhqj,bhjd->bhqd', p_own, v_own))

    outs = lax.map(one_chunk, jnp.arange(s_pad // QBLOCK))
    out = outs.transpose(1, 2, 0, 3, 4).reshape(bsz, h, s_pad, hd)
    return out[:, :, :s]


def forgetting_attention(q, k, v, log_f):
    bsz, h, s, hd = q.shape
    scale = hd ** -0.5
    c = jnp.cumsum(log_f, axis=-1)
    kpos = jnp.arange(s)

    def one_block(n):
        start = n * QBLOCK
        qc = lax.dynamic_slice_in_dim(q, start, QBLOCK, axis=2)
        cq = lax.dynamic_slice_in_dim(c, start, QBLOCK, axis=2)
        logits = (jnp.einsum('bhqd,bhkd->bhqk', qc, k).astype(jnp.float32) * scale
                  + cq[..., :, None] - c[..., None, :])
        qpos = start + jnp.arange(QBLOCK)
        logits = jnp.where(kpos[None, :] <= qpos[:, None], logits, -jnp.inf)
        p = jax.nn.softmax(logits, axis=-1).astype(v.dtype)
        return jnp.einsum('bhqk,bhkd->bhqd', p, v)

    outs = lax.map(one_block, jnp.arange(s // QBLOCK))
    return outs.transpose(1, 2, 0, 3, 4).reshape(bsz, h, s, hd)


def causal_spatial_gating(u, v, ln_g, ln_b, w_s, b_s):
    bsz, s, width = v.shape
    g = N_GROUPS_C
    c = width // g
    vn = layer_norm(v.reshape(bsz, s, g, c), ln_g.reshape(g, c), ln_b.reshape(g, c))
    vc = vn.reshape(bsz, s // SGU_CHUNK, SGU_CHUNK, g, c)
    causal = jnp.tril(jnp.ones((SGU_CHUNK, SGU_CHUNK), dtype=bool))
    w = jnp.where(causal[None], w_s, jnp.zeros_like(w_s))
    mixed = jnp.einsum('gts,bnsgc->bntgc', w, vc) + b_s.T[None, None, :, :, None]
    return u * mixed.reshape(bsz, s, width)


def multiscale_pool(p, w_g, scale):
    bsz, s, width = p.shape
    g = len(POOL_WINDOWS)
    c = width // g
    pg = p.reshape(bsz, s, g, c).astype(jnp.float32)
    cs = jnp.concatenate([jnp.zeros((bsz, 1, g, c), jnp.float32), jnp.cumsum(pg, axis=1)], axis=1)
    t = jnp.arange(s)
    outs = []
    for gi, win in enumerate(POOL_WINDOWS):
        cs_g = cs[:, :, gi]
        lo = jnp.maximum(t + 1 - win, 0)
        window_sum = cs_g[:, 1:] - cs_g[:, lo]
        count = jnp.minimum(t + 1, win).astype(jnp.float32)
        outs.append(window_sum / count[None, :, None] - pg[:, :, gi])
    pooled = jnp.stack(outs, axis=2).astype(p.dtype)
    mixed = jnp.einsum('bsgc,gcd->bsgd', pooled, w_g)
    return mixed.reshape(bsz, s, width) * scale


def hybrid_mixer(x, w_in, b_forget, sgu_ln_g, sgu_ln_b, sgu_w, sgu_b, pool_w, pool_scale, w_o, cos, sin):
    z = x @ w_in
    aq, ak, av, bq, bk, bv, bf, cu, cv, dp = jnp.split(z, IN_SPLITS, axis=-1)
    ya = moba_attention(partial_rotary(to_heads(aq, N_HEADS_A), cos, sin),
                        partial_rotary(to_heads(ak, N_HEADS_A), cos, sin),
                        to_heads(av, N_HEADS_A))
    log_f = jax.nn.log_sigmoid((bf + b_forget).astype(jnp.float32)).transpose(0, 2, 1)
    yb = forgetting_attention(to_heads(bq, N_HEADS_B), to_heads(bk, N_HEADS_B),
                              to_heads(bv, N_HEADS_B), log_f)
    yc = causal_spatial_gating(jax.nn.gelu(cu, approximate=False), jax.nn.gelu(cv, approximate=False),
                               sgu_ln_g, sgu_ln_b, sgu_w, sgu_b)
    yd = multiscale_pool(dp, pool_w, pool_scale)
    y = jnp.concatenate([from_heads(ya), from_heads(yb), yc, yd], axis=-1)
    return y @ w_o


def conv_gated_ffn(x, w_up, conv_w, conv_b, w_down):
    h = x @ w_up
    h = lax.conv_general_dilated(h, conv_w[:, None, :], window_strides=(1,),
                                 padding=[(CONV_WIDTH - 1, 0)],
                                 dimension_numbers=('NWC', 'WIO', 'NWC'),
                                 feature_group_count=h.shape[-1]) + conv_b
    gate, val = jnp.split(h, 2, axis=-1)
    return (jax.nn.silu(gate) * val) @ w_down


def setup_inputs(seed: int = 0) -> dict:
    key = jax.random.key(seed)
    ks = jax.random.split(key, 20)
    L = DEPTH
    nrm = jax.random.normal
    f32 = jnp.float32
    ch = GROUP_W // N_GROUPS_D
    return {
        'x': nrm(ks[0], (BATCH, SEQ, D_MODEL), f32),
        'w_in': nrm(ks[1], (L, D_MODEL, N_IN), f32) * D_MODEL ** -0.5,
        'b_forget': 3.0 + 0.1 * nrm(ks[2], (L, N_HEADS_B), f32),
        'sgu_ln_g': 1.0 + 0.05 * nrm(ks[3], (L, GROUP_W), f32),
        'sgu_ln_b': 0.02 * nrm(ks[4], (L, GROUP_W), f32),
        'sgu_w': nrm(ks[5], (L, N_GROUPS_C, SGU_CHUNK, SGU_CHUNK), f32) * SGU_CHUNK ** -0.5,
        'sgu_b': 1.0 + 0.1 * nrm(ks[6], (L, N_GROUPS_C, SGU_CHUNK), f32),
        'pool_w': nrm(ks[7], (L, N_GROUPS_D, ch, ch), f32) * ch ** -0.5,
        'pool_scale': 1.0 + 0.1 * nrm(ks[8], (L, GROUP_W), f32),
        'w_o': nrm(ks[9], (L, D_MIX, D_MODEL), f32) * D_MIX ** -0.5 * DN_BETA,
        'ln1_g': 1.0 + 0.05 * nrm(ks[10], (L, D_MODEL), f32),
        'ln1_b': 0.02 * nrm(ks[11], (L, D_MODEL), f32),
        'w_up': nrm(ks[12], (L, D_MODEL, 2 * D_FF), f32) * D_MODEL ** -0.5,
        'conv_w': nrm(ks[13], (L, CONV_WIDTH, 2 * D_FF), f32) * CONV_WIDTH ** -0.5,
        'conv_b': 0.02 * nrm(ks[14], (L, 2 * D_FF), f32),
        'w_down': nrm(ks[15], (L, D_FF, D_MODEL), f32) * D_FF ** -0.5 * DN_BETA,
        'ln2_g': 1.0 + 0.05 * nrm(ks[16], (L, D_MODEL), f32),
        'ln2_b': 0.02 * nrm(ks[17], (L, D_MODEL), f32),
    }


def reference(x, w_in, b_forget, sgu_ln_g, sgu_ln_b, sgu_w, sgu_b, pool_w, pool_scale, w_o,
              ln1_g, ln1_b, w_up, conv_w, conv_b, w_down, ln2_g, ln2_b):
    cos, sin = rotary_tables(x.shape[1])
    for l in range(DEPTH):
        y = hybrid_mixer(x, w_in[l], b_forget[l], sgu_ln_g[l], sgu_ln_b[l], sgu_w[l], sgu_b[l],
                         pool_w[l], pool_scale[l], w_o[l], cos, sin)
        x = layer_norm(DN_ALPHA * x + y, ln1_g[l], ln1_b[l])
        y = conv_gated_ffn(x, w_up[l], conv_w[l], conv_b[l], w_down[l])
        x = layer_norm(DN_ALPHA * x + y, ln2_g[l], ln2_b[l])
    return x
```

```python
import contextlib
import os
DBG = int(os.environ.get('MIX_DBG', '99'))
import numpy as np
import ml_dtypes
import concourse.bass as bass
import concourse.mybir as mybir
from concourse.bass_utils import run_bass_kernel_spmd

F32 = mybir.dt.float32
BF16 = mybir.dt.bfloat16
AF = mybir.ActivationFunctionType
ALU = mybir.AluOpType
AX = mybir.AxisListType

S = 8192
D = 1024
NB = 2
DFF = 2816
ALPHA = 4.0 ** 0.25
EPS = 1e-5
BIGM = 30000.0
NEGF = -1.0e30


class Res:
    __slots__ = ("name", "w", "r", "excl")

    def __init__(self, name, excl=False):
        self.name = name
        self.w = None
        self.r = []
        self.excl = excl


class Sched:
    STREAMS = ("pe", "act", "dve", "pool", "sp")

    def __init__(self, nc):
        self.nc = nc
        self.ops = {s: [] for s in self.STREAMS}
        self.ccount = {s: 0 for s in self.STREAMS}
        self.dcount = {}
        self.known = {s: {} for s in self.STREAMS}
        self.final_events = []

    def _need(self, stream, ev, waits):
        if ev is None:
            return
        sem, val, src = ev
        if src == stream and src == "pe":
            return
        if self.known[stream].get(sem, 0) >= val:
            return
        self.known[stream][sem] = val
        waits.append((sem, val))

    def op(self, stream, fn, reads=(), writes=(), dma_key=None):
        ex = [r for r in reads if r.excl]
        if ex:
            reads = [r for r in reads if not r.excl]
            writes = list(writes) + [r for r in ex if r not in writes]
        waits = []
        for r in reads:
            self._need(stream, r.w, waits)
        for w in writes:
            self._need(stream, w.w, waits)
            for e in w.r:
                self._need(stream, e, waits)
        if dma_key is not None:
            k = "d_" + dma_key
            self.dcount[k] = self.dcount.get(k, 0) + 1
            ev = (k, 16 * self.dcount[k], "dma")
            sig = (k, 16)
        else:
            self.ccount[stream] += 1
            ev = ("c_" + stream, self.ccount[stream], stream)
            sig = ("c_" + stream, 1)
        for r in reads:
            r.r.append(ev)
        for w in writes:
            w.w = ev
            w.r = []
        self.ops[stream].append((waits, fn, sig))
        return ev

    def emit(self, final_events):
        nc = self.nc
        names = set()
        for s in self.STREAMS:
            for waits, fn, sig in self.ops[s]:
                names.add(sig[0])
                for (sem, val) in waits:
                    names.add(sem)
        with contextlib.ExitStack() as st:
            sems = {n: st.enter_context(nc.semaphore(n)) for n in sorted(names)}
            block = st.enter_context(nc.Block())

            def make(stream):
                def body(eng):
                    for waits, fn, sig in self.ops[stream]:
                        for (sem, val) in waits:
                            eng.wait_ge(sems[sem], val)
                        ins = fn(eng)
                        ins.then_inc(sems[sig[0]], sig[1])
                    if stream == "sp":
                        for (sem, val, src) in final_events:
                            eng.wait_ge(sems[sem], val)
                return body

            block.tensor(make("pe"))
            block.scalar(make("act"))
            block.vector(make("dve"))
            block.gpsimd(make("pool"))
            block.sync(make("sp"))


class Ctx:
    def __init__(self, nc):
        self.nc = nc
        self.st = contextlib.ExitStack()
        self.S = Sched(nc)
        self.nres = 0

    def sb(self, name, shape, dt):
        return self.st.enter_context(self.nc.sbuf_tensor(name, shape, dt))

    def ps(self, name, shape, dt):
        return self.st.enter_context(self.nc.psum_tensor(name, shape, dt))

    def res(self, name="r", excl=False):
        self.nres += 1
        return Res("%s%d" % (name, self.nres), excl)

    def pe(self, fn, reads=(), writes=()):
        return self.S.op("pe", fn, reads, writes)

    def act(self, fn, reads=(), writes=()):
        return self.S.op("act", fn, reads, writes)

    def dve(self, fn, reads=(), writes=()):
        return self.S.op("dve", fn, reads, writes)

    def pool(self, fn, reads=(), writes=()):
        return self.S.op("pool", fn, reads, writes)

    def dma(self, queue, out, in_, reads=(), writes=(), key=None):
        if key is None:
            key = (writes[0].name if writes else reads[0].name)
        return self.S.op(queue, lambda e: e.dma_start(out=out, in_=in_), reads, writes, dma_key=key)


def _mm_group(out, pairs):
    def fn(e):
        n = len(pairs)
        ins = None
        for i, (l, r) in enumerate(pairs):
            ins = e.matmul(out, lhsT=l, rhs=r, start=(i == 0), stop=(i == n - 1))
        return ins
    return fn


NFM = 352
NTM = 257


def build_mixer(SEQ=S, PH=9):
    NT, NTT = SEQ // 512, SEQ // 128
    nc = bass.Bass("TRN2", target_bir_lowering=False)
    C = Ctx(nc)
    dt = nc.dram_tensor
    xT = dt("xT", [D, SEQ], F32, kind="ExternalInput").ap()
    wfm = dt("wfm", [D, NFM], F32, kind="ExternalInput").ap()
    wtm = dt("wtm", [D, NTM], F32, kind="ExternalInput").ap()
    cs = dt("cs", [2, 16, SEQ], F32, kind="ExternalInput").ap()
    onehot = dt("onehot", [32, SEQ], BF16, kind="ExternalInput").ap()
    sguT = dt("sguT", [128, 128], F32, kind="ExternalInput").ap()
    tri = dt("tri", [128, 128], F32, kind="ExternalInput").ap()
    ident = dt("ident", [128, 128], F32, kind="ExternalInput").ap()
    sgub = dt("sgub", [128, 1], F32, kind="ExternalInput").ap()
    lng = dt("lng", [128, 64], F32, kind="ExternalInput").ap()
    lnb = dt("lnb", [128, 64], F32, kind="ExternalInput").ap()
    poolw = dt("poolw", [64, 64], F32, kind="ExternalInput").ap()
    pscale = dt("pscale", [128, 64], F32, kind="ExternalInput").ap()
    band = dt("band", [3, 128, 128], F32, kind="ExternalInput").ap()
    bfor = dt("bfor", [128, 1], F32, kind="ExternalInput").ap()
    y = dt("y", [SEQ, 256], BF16, kind="ExternalOutput").ap()

    sb, ps, res = C.sb, C.ps, C.res
    QA = sb("QA", [128, SEQ], BF16)
    KA = sb("KA", [128, SEQ], BF16)
    QB = sb("QB", [128, SEQ], BF16)
    KB = sb("KB", [128, SEQ], BF16)
    VA = sb("VA", [128, NTT, 65], BF16)
    VB = sb("VB", [128, NTT, 65], BF16)
    xb = [sb("xb%d" % i, [128, 8, 512], BF16) for i in range(2)]
    wfm_b = sb("wfm_b", [128, 8, NFM], BF16)
    wtm_b = sb("wtm_b", [128, 8, NTM], BF16)
    cst = [sb("cst%d" % i, [16, 2, 512], F32) for i in range(2)]
    t1 = [sb("t1_%d" % i, [16, 512], F32) for i in range(2)]
    t2 = [sb("t2_%d" % i, [16, 512], F32) for i in range(2)]
    pTb = [sb("pTb%d" % i, [64, 512], BF16) for i in range(2)]
    ug = [sb("ug%d" % i, [128, 128], F32) for i in range(2)]
    stats = [sb("stats%d" % i, [128, 6], F32) for i in range(2)]
    mv = [sb("mv%d" % i, [128, 2], F32) for i in range(2)]
    rstd = [sb("rstd%d" % i, [128, 1], F32) for i in range(2)]
    vn = [sb("vn%d" % i, [128, 64], F32) for i in range(2)]
    vnb = [sb("vnb%d" % i, [128, 64], BF16) for i in range(2)]
    pwb = [sb("pwb%d" % i, [128, 64], BF16) for i in range(2)]
    YC = [sb("YC%d" % i, [128, 4, 64], BF16) for i in range(2)]
    YD = [sb("YD%d" % i, [128, 4, 64], BF16) for i in range(2)]
    UG = sb("UG", [128, NTT, 128], BF16)
    MV = sb("MV", [128, NTT, 2], F32)
    VE = sb("VE", [128, NTT], F32)
    RSTD = sb("RSTD", [128, NTT], F32)
    YAB = [sb("YAB%d" % i, [128, 4, 128], BF16) for i in range(2)]
    MBZ = sb("MBZ", [128, 64 + NTT * 32], F32)
    GM = sb("GM", [128, 32], F32)
    top8 = [sb("top8_%d" % i, [128, 8], F32) for i in range(2)]
    FRAW = sb("FRAW", [128, NTT], F32)
    LF = sb("LF", [128, NTT], F32)
    TOT = sb("TOT", [128, NTT], F32)
    PREF = sb("PREF", [128, NTT], F32)
    CP = sb("CP", [128, NTT], F32)
    Z = sb("Z", [128, 128], F32)
    kbar = sb("kbar", [64, 32], F32)
    kbar_b = sb("kbar_b", [64, 32], BF16)
    PT = [sb("PT%d" % i, [128, 512], BF16) for i in range(3)]
    rl = [sb("rl%d" % i, [128, 1], F32) for i in range(4)]
    ident_f = sb("ident_f", [128, 128], F32)
    tri_f = sb("tri_f", [128, 128], F32)
    tri_b = sb("tri_b", [128, 128], BF16)
    ones_f = sb("ones_f", [128, 128], F32)
    sgu_f = sb("sgu_f", [128, 128], F32)
    wmT_b = sb("wmT_b", [128, 128], BF16)
    band_b = sb("band_b", [128, 3, 128], BF16)
    sgub_s = sb("sgub_s", [128, 1], F32)
    lng_s = sb("lng_s", [128, 64], F32)
    lnb_s = sb("lnb_s", [128, 64], F32)
    poolw_b = sb("poolw_b", [64, 64], BF16)
    pscale_s = sb("pscale_s", [128, 64], F32)
    bfor_s = sb("bfor_s", [128, 1], F32)
    negb = sb("negb", [128, 1], F32)
    ef = sb("ef", [128, NTT], F32)
    PS = [ps("ps%d" % i, [128, 512], F32) for i in range(8)]
    RPS = [res("ps", True) for _ in range(8)]

    R = lambda n: res(n)
    RQA = [R("qa") for _ in range(NT)]
    RQAm = [R("qam") for _ in range(NT)]
    RKA = [R("ka") for _ in range(NT)]
    RQB = [R("qb") for _ in range(NT)]
    RQBc = [R("qbc") for _ in range(NT)]
    RKB = [R("kb") for _ in range(NT)]
    RVA = [R("va") for _ in range(NT)]
    RVB = [R("vb") for _ in range(NT)]
    Rxb = [R("xb") for _ in range(2)]
    Rcs = [R("cs") for _ in range(2)]
    Rt1 = [R("t1") for _ in range(2)]
    Rt2 = [R("t2") for _ in range(2)]
    RpT = [R("pT") for _ in range(2)]
    Rug = [R("ug") for _ in range(2)]
    Rst = [R("st") for _ in range(2)]
    Rmv = [R("mv") for _ in range(2)]
    Rrs = [R("rs") for _ in range(2)]
    Rvn = [R("vn") for _ in range(2)]
    Rvnb = [R("vnb") for _ in range(2)]
    Rpwb = [R("pwb") for _ in range(2)]
    RYC = [R("yc") for _ in range(2)]
    RYD = [R("yd") for _ in range(2)]
    RUG = [R("ugall") for _ in range(NT)]
    RMV, RVE, RRSTD = R("mvall"), R("ve"), R("rstdall")
    RYAB = [R("yab") for _ in range(2)]
    RMBZ = [R("mbz") for _ in range(NT)]
    RGM = R("gm")
    Rtop = [R("top") for _ in range(2)]
    RFRAW, RLF, RTOT, RPREF, RCP, RZ, Rkbar, Rkbarb, Ref = (R("fraw"), R("lf"), R("tot"), R("pref"),
                                                            R("cp"), R("z"), R("kbar"), R("kbarb"), R("ef"))
    RPT = [R("pt") for _ in range(3)]
    Rrl = [R("rl") for _ in range(4)]
    Rc = {k: R(k) for k in ["wfm", "wtm", "ident", "tri_f", "tri_b", "ones", "sgu_f", "wmT", "band", "sgub",
                            "lng", "lnb", "poolw", "pscale", "bfor", "negb", "kaoh", "misc"]}

    C.dma("pool", wfm_b[:], wfm.rearrange("(kc f) n -> f kc n", f=128), writes=[Rc["wfm"]])
    C.dma("pool", wtm_b[:], wtm.rearrange("(kc f) n -> f kc n", f=128), writes=[Rc["wtm"]])
    C.dma("sp", ident_f[:], ident, writes=[Rc["ident"]])
    C.dma("sp", tri_f[:], tri, writes=[Rc["tri_f"]])
    C.dma("pool", tri_b[:], tri, writes=[Rc["tri_b"]])
    C.dma("sp", sgu_f[:], sguT, writes=[Rc["sgu_f"]])
    C.dma("pool", band_b[:], band.rearrange("k s t -> s k t"), writes=[Rc["band"]])
    C.dma("sp", sgub_s[:], sgub, writes=[Rc["sgub"]])
    C.dma("sp", lng_s[:], lng, writes=[Rc["lng"]])
    C.dma("sp", lnb_s[:], lnb, writes=[Rc["lnb"]])
    C.dma("pool", poolw_b[:], poolw, writes=[Rc["poolw"]])
    C.dma("sp", pscale_s[:], pscale, writes=[Rc["pscale"]])
    C.dma("sp", bfor_s[:], bfor, writes=[Rc["bfor"]])
    C.dma("sp", KA[64:96, :], onehot, writes=[Rc["kaoh"]])
    C.dve(lambda e: e.tensor_tensor(out=wmT_b[:], in0=sgu_f[:], in1=tri_f[:], op=ALU.mult),
          reads=[Rc["sgu_f"], Rc["tri_f"]], writes=[Rc["wmT"]])
    C.dve(lambda e: e.tensor_scalar(out=negb[:], in0=bfor_s[:], scalar1=-1.0, scalar2=None, op0=ALU.mult),
          reads=[Rc["bfor"]], writes=[Rc["negb"]])
    C.dve(lambda e: e.memset(ones_f[:], 1.0), writes=[Rc["ones"]])
    C.dve(lambda e: e.memset(VA[:, :, 64:65], 1.0), writes=RVA)
    C.dve(lambda e: e.memset(VB[:, :, 64:65], 1.0), writes=RVB)
    C.dve(lambda e: e.memset(KB[64:65, :], 1.0), writes=RKB)
    C.dve(lambda e: e.memset(MBZ[:], 0.0), writes=RMBZ)
    C.dve(lambda e: e.memset(GM[:], NEGF), writes=[RGM])
    C.dve(lambda e: e.memset(Z[:], 0.0), writes=[RZ])
    C.dve(lambda e: e.memset(kbar[:], 0.0), writes=[Rkbar])
    C.dve(lambda e: e.memset(PREF[:, 0:1], 0.0), writes=[RPREF])

    if PH == 0:
        return _finish_mixer(C, nc)
    xT_v = xT.rearrange("(kc f) t -> f kc t", f=128)

    def load_tile(T):
        sl = T % 2
        C.dma("pool", xb[sl][:], xT_v[:, :, T * 512:(T + 1) * 512], writes=[Rxb[sl]])
        C.dma("sp", cst[sl][:], cs[:, :, T * 512:(T + 1) * 512].rearrange("k p t -> p k t"), writes=[Rcs[sl]])

    load_tile(0)
    fm_cols = {"qA": (0, 64), "kA": (64, 64), "qB": (128, 64), "kB": (192, 64), "pD": (256, 64),
               "qAp": (320, 16), "kAp": (336, 16)}
    def p1_tile(T):
        sl = T % 2
        if T + 1 < NT:
            load_tile(T + 1)
        cols = slice(T * 512, (T + 1) * 512)

        def fm(name, bank):
            c0, n = fm_cols[name]
            C.pe(_mm_group(PS[bank][0:n, :], [(wfm_b[:, kc, c0:c0 + n], xb[sl][:, kc, :]) for kc in range(8)]),
                 reads=[Rc["wfm"], Rxb[sl]], writes=[RPS[bank]])

        def rot(nm, nmp, dst, Rdst):
            fm(nm, 0)
            fm(nmp, 1)
            if DBG == 10:
                return
            C.act(lambda e, dst=dst: e.copy(out=dst[0:64, cols], in_=PS[0][0:64, :]), reads=[RPS[0]], writes=[Rdst[T]])
            if DBG == 11:
                return
            C.dve(lambda e: e.tensor_tensor(out=t1[sl][:], in0=PS[0][0:16, :], in1=cst[sl][:, 0, :], op=ALU.mult),
                  reads=[RPS[0], Rcs[sl]], writes=[Rt1[sl]])
            C.dve(lambda e: e.tensor_tensor(out=t2[sl][:], in0=PS[1][0:16, :], in1=cst[sl][:, 1, :], op=ALU.mult),
                  reads=[RPS[1], Rcs[sl]], writes=[Rt2[sl]])
            if DBG == 12:
                return
            C.dve(lambda e, dst=dst: e.tensor_tensor(out=dst[0:16, cols], in0=t1[sl][:], in1=t2[sl][:], op=ALU.add),
                  reads=[Rt1[sl], Rt2[sl]], writes=[Rdst[T]])
        if DBG < 1:
            return
        rot("qA", "qAp", QA, RQA)
        rot("kA", "kAp", KA, RKA)
        if DBG < 2 or (10 <= DBG < 20):
            return
        C.dve(lambda e: e.tensor_reduce(out=kbar[:, 2 * T:2 * T + 2],
                                        in_=KA[0:64, cols].rearrange("p (b j) -> p b j", j=256),
                                        axis=AX.X, op=ALU.add), reads=[RKA[T]], writes=[Rkbar])
        if DBG < 3:
            return
        fm("qB", 0)
        C.act(lambda e: e.copy(out=QB[0:64, cols], in_=PS[0][0:64, :]), reads=[RPS[0]], writes=[RQB[T]])
        fm("kB", 1)
        C.act(lambda e: e.copy(out=KB[0:64, cols], in_=PS[1][0:64, :]), reads=[RPS[1]], writes=[RKB[T]])
        fm("pD", 0)
        C.act(lambda e: e.copy(out=pTb[sl][:], in_=PS[0][0:64, :]), reads=[RPS[0]], writes=[RpT[sl]])
        if DBG < 4:
            return
        def sub(c):
            tt = 4 * T + c
            s2 = tt % 2
            bk = 2 + s2
            C.pe(_mm_group(PS[bk][:, 0:NTM], [(xb[sl][:, kc, c * 128:(c + 1) * 128], wtm_b[:, kc, :]) for kc in range(8)]),
                 reads=[Rc["wtm"], Rxb[sl]], writes=[RPS[bk]])
            C.act(lambda e, bk=bk, tt=tt: e.copy(out=VA[:, tt, 0:64], in_=PS[bk][:, 0:64]), reads=[RPS[bk]], writes=[RVA[T]])
            C.act(lambda e, bk=bk, tt=tt: e.copy(out=VB[:, tt, 0:64], in_=PS[bk][:, 64:128]), reads=[RPS[bk]], writes=[RVB[T]])
            C.dve(lambda e, bk=bk, tt=tt: e.tensor_copy(out=FRAW[:, tt:tt + 1], in_=PS[bk][:, 256:257]),
                  reads=[RPS[bk]], writes=[RFRAW])
            C.act(lambda e, bk=bk, tt=tt: e.activation(out=UG[:, tt, :], in_=PS[bk][:, 128:256], func=AF.Gelu),
                  reads=[RPS[bk]], writes=[RUG[T]])
            C.dve(lambda e, tt=tt, s2=s2: e.bn_stats(out=stats[s2][:], in_=UG[:, tt, 64:128]), reads=[RUG[T]], writes=[Rst[s2]])
            C.dve(lambda e, tt=tt, s2=s2: e.bn_aggr(out=MV[:, tt, :], in_=stats[s2][:]), reads=[Rst[s2]], writes=[RMV])
            C.pe(lambda e, c=c: e.matmul(PS[5][:, 0:64], lhsT=pTb[sl][:, c * 128:(c + 1) * 128], rhs=poolw_b[:],
                                         start=True, stop=True), reads=[RpT[sl], Rc["poolw"]], writes=[RPS[5]])
            C.act(lambda e, s2=s2: e.copy(out=pwb[s2][:], in_=PS[5][:, 0:64]), reads=[RPS[5]], writes=[Rpwb[s2]])
            if tt == 0:
                C.pe(lambda e, s2=s2: e.matmul(PS[6][:, 0:64], lhsT=band_b[:, 2, :], rhs=pwb[s2][:], start=True, stop=True),
                     reads=[Rc["band"], Rpwb[s2]], writes=[RPS[6]])
            else:
                C.pe(_mm_group(PS[6][:, 0:64], [(band_b[:, 0, :], pwb[s2][:]), (band_b[:, 1, :], pwb[1 - s2][:])]),
                     reads=[Rc["band"], Rpwb[0], Rpwb[1]], writes=[RPS[6]])
            C.dve(lambda e, c=c: e.tensor_tensor(out=YD[sl][:, c, :], in0=PS[6][:, 0:64], in1=pscale_s[:], op=ALU.mult),
                  reads=[RPS[6], Rc["pscale"]], writes=[RYD[sl]])
        for c in range(4):
            sub(c)
        C.dma("sp", y[T * 512:(T + 1) * 512, 192:256].rearrange("(c p) n -> p c n", p=128), YD[sl][:],
              reads=[RYD[sl]], key="yd%d" % sl)

    for T in range(NT):
        p1_tile(T)

    if PH == 1:
        return _finish_mixer(C, nc)
    C.dve(lambda e: e.tensor_scalar(out=VE[:], in0=MV[:, :, 1], scalar1=EPS, scalar2=None, op0=ALU.add),
          reads=[RMV], writes=[RVE])
    C.act(lambda e: e.activation(out=VE[:], in_=VE[:], func=AF.Sqrt), reads=[RVE], writes=[RVE])
    C.dve(lambda e: e.reciprocal(out=RSTD[:], in_=VE[:]), reads=[RVE], writes=[RRSTD])

    def c_sub(T, c):
        tt = 4 * T + c
        s2 = tt % 2
        sl = T % 2
        C.dve(lambda e: e.tensor_scalar(out=vn[s2][:], in0=UG[:, tt, 64:128], scalar1=MV[:, tt, 0:1],
                                        scalar2=RSTD[:, tt:tt + 1], op0=ALU.subtract, op1=ALU.mult),
              reads=[RUG[T], RMV, RRSTD], writes=[Rvn[s2]])
        C.dve(lambda e: e.tensor_tensor(out=vn[s2][:], in0=vn[s2][:], in1=lng_s[:], op=ALU.mult),
              reads=[Rvn[s2], Rc["lng"]], writes=[Rvn[s2]])
        C.dve(lambda e: e.tensor_tensor(out=vnb[s2][:], in0=vn[s2][:], in1=lnb_s[:], op=ALU.add),
              reads=[Rvn[s2], Rc["lnb"]], writes=[Rvnb[s2]])
        C.pe(lambda e: e.matmul(PS[4 + s2][:, 0:64], lhsT=wmT_b[:], rhs=vnb[s2][:], start=True, stop=True),
             reads=[Rc["wmT"], Rvnb[s2]], writes=[RPS[4 + s2]])
        C.dve(lambda e: e.scalar_tensor_tensor(out=YC[sl][:, c, :], in0=PS[4 + s2][:, 0:64],
                                               scalar=sgub_s[:, 0:1], in1=UG[:, tt, 0:64],
                                               op0=ALU.add, op1=ALU.mult),
              reads=[RPS[4 + s2], Rc["sgub"], RUG[T]], writes=[RYC[sl]])

    for T in range(NT):
        for c in range(4):
            c_sub(T, c)
        C.dma("sp", y[T * 512:(T + 1) * 512, 128:192].rearrange("(c p) n -> p c n", p=128), YC[T % 2][:],
              reads=[RYC[T % 2]], key="yc%d" % (T % 2))

    if PH == 2:
        return _finish_mixer(C, nc)
    C.act(lambda e: e.activation(out=ef[:], in_=FRAW[:], func=AF.Exp, bias=negb[:, 0:1], scale=-1.0),
          reads=[RFRAW, Rc["negb"]], writes=[Ref])
    C.act(lambda e: e.activation(out=LF[:], in_=ef[:], func=AF.Ln, bias=1.0, scale=1.0), reads=[Ref], writes=[RLF])
    C.pe(lambda e: e.matmul(PS[0][:, 0:NTT], lhsT=ones_f[:], rhs=LF[:], start=True, stop=True),
         reads=[Rc["ones"], RLF], writes=[RPS[0]])
    C.dve(lambda e: e.tensor_copy(out=TOT[:], in_=PS[0][:, 0:NTT]), reads=[RPS[0]], writes=[RTOT])
    for j in range(1, NTT):
        C.dve(lambda e, j=j: e.tensor_tensor(out=PREF[:, j:j + 1], in0=PREF[:, j - 1:j], in1=TOT[:, j - 1:j], op=ALU.add),
              reads=[RPREF, RTOT], writes=[RPREF])
    C.pe(lambda e: e.matmul(PS[1][:, 0:NTT], lhsT=tri_f[:], rhs=LF[:], start=True, stop=True),
         reads=[Rc["tri_f"], RLF], writes=[RPS[1]])
    C.dve(lambda e: e.tensor_tensor(out=CP[:], in0=PS[1][:, 0:NTT], in1=PREF[:], op=ALU.add),
          reads=[RPS[1], RPREF], writes=[RCP])
    C.dve(lambda e: e.tensor_scalar(out=Z[:, 64:64 + NTT], in0=CP[:], scalar1=-8.0, scalar2=None, op0=ALU.mult),
          reads=[RCP], writes=[RZ])
    C.dve(lambda e: e.tensor_scalar(out=kbar_b[:], in0=kbar[:], scalar1=1.0 / 256.0, scalar2=None, op0=ALU.mult),
          reads=[Rkbar], writes=[Rkbarb])
    def p2_tile(T):
        bk = 2 + (T % 2)
        def tr4(e, T=T, bk=bk):
            ins = None
            for c in range(4):
                tt = 4 * T + c
                ins = e.matmul(PS[bk][0:65, c * 128:(c + 1) * 128], lhsT=Z[:, tt:tt + 65], rhs=ident_f[:],
                               start=True, stop=True)
            return ins
        C.pe(tr4, reads=[RZ, Rc["ident"]], writes=[RPS[bk]])
        C.act(lambda e, T=T, bk=bk: e.copy(out=QB[64:65, T * 512:(T + 1) * 512], in_=PS[bk][64:65, :]),
              reads=[RPS[bk]], writes=[RQBc[T]])
        def gate(c):
            tt = 4 * T + c
            b = tt // 2
            if b == 0:
                return
            s2 = tt % 2
            C.pe(lambda e, tt=tt: e.matmul(PS[4 + (tt % 2)][:, 0:32], lhsT=QA[0:64, tt * 128:(tt + 1) * 128], rhs=kbar_b[:],
                                           start=True, stop=True), reads=[RQA[T], Rkbarb], writes=[RPS[4 + s2]])
            C.dve(lambda e, tt=tt, b=b: e.tensor_copy(out=GM[:, 0:b], in_=PS[4 + (tt % 2)][:, 0:b]),
                  reads=[RPS[4 + s2]], writes=[RGM])
            C.dve(lambda e, s2=s2: e.max(out=top8[s2][:], in_=GM[:]), reads=[RGM], writes=[Rtop[s2]])
            C.dve(lambda e, tt=tt, b=b, s2=s2: e.tensor_scalar(out=MBZ[:, 64 + 32 * tt:64 + 32 * tt + b], in0=GM[:, 0:b],
                                                               scalar1=top8[s2][:, 2:3], scalar2=1.0,
                                                               op0=ALU.is_ge, op1=ALU.subtract),
                  reads=[RGM, Rtop[s2]], writes=[RMBZ[T]])
        for c in range(4):
            gate(c)
        bk2 = 6 + (T % 2)

        def trm(e, T=T, bk2=bk2):
            ins = None
            for c in range(4):
                tt = 4 * T + c
                ins = e.matmul(PS[bk2][0:96, c * 128:(c + 1) * 128], lhsT=MBZ[:, 32 * tt:32 * tt + 96], rhs=ident_f[:],
                               start=True, stop=True)
            return ins
        C.pe(trm, reads=[RMBZ[T], Rc["ident"]], writes=[RPS[bk2]])
        C.act(lambda e, T=T, bk2=bk2: e.copy(out=QA[64:96, T * 512:(T + 1) * 512], in_=PS[bk2][64:96, :]),
              reads=[RPS[bk2]], writes=[RQAm[T]])

    for T in range(NT):
        p2_tile(T)

    if PH == 3:
        return _finish_mixer(C, nc)
    ROacc = [[RPS[3]] * 4, [RPS[4]] * 4]
    state = {"pair": 0, "oi": 0}

    def att_pair(qi, att, kj, Oacc, RO, ysl, oi):
        Q, K, V = (QA, KA, VA) if att == 0 else (QB, KB, VB)
        RQ, RQx, RK, RV = (RQA, RQAm, RKA, RVA) if att == 0 else (RQB, RQBc, RKB, RVB)
        rows = 96 if att == 0 else 65
        d = kj - 4 * qi
        off = 128 * max(d, 0)
        n = 512 - off
        sbk = state["pair"] % 3
        pt = state["pair"] % 3
        state["pair"] += 1
        C.pe(lambda e: e.matmul(PS[sbk][:, 0:n], lhsT=K[0:rows, kj * 128:(kj + 1) * 128],
                                rhs=Q[0:rows, qi * 512 + off:(qi + 1) * 512], start=True, stop=True),
             reads=[RQ[qi], RQx[qi], RK[kj // 4]] + ([Rc["kaoh"]] if att == 0 else []), writes=[RPS[sbk]])
        if att == 0:
            C.act(lambda e: e.activation(out=PT[pt][:, 0:n], in_=PS[sbk][:, 0:n], func=AF.Exp, scale=0.125),
                  reads=[RPS[sbk]], writes=[RPT[pt]])
        else:
            C.act(lambda e: e.activation(out=PT[pt][:, 0:n], in_=PS[sbk][:, 0:n], func=AF.Exp,
                                         bias=CP[:, kj:kj + 1], scale=0.125),
                  reads=[RPS[sbk], RCP], writes=[RPT[pt]])
        if d >= 0:
            C.dve(lambda e: e.tensor_tensor(out=PT[pt][:, 0:128], in0=PT[pt][:, 0:128], in1=tri_b[:], op=ALU.mult),
                  reads=[RPT[pt], Rc["tri_b"]], writes=[RPT[pt]])
        c0 = max(d, 0)

        def pv(e):
            ins = None
            for c in range(c0, 4):
                ins = e.matmul(Oacc[:, c, :], lhsT=PT[pt][:, c * 128 - off:(c + 1) * 128 - off], rhs=V[:, kj, :],
                               start=(kj == 0 and c == 0), stop=(kj == 4 * qi + c), skip_group_check=True)
            return ins
        C.pe(pv, reads=[RPT[pt], RV[kj // 4]], writes=[RO[c] for c in range(c0, 4)])
        if d >= 0:
            c = d
            r4 = (2 * (oi % 2) + (c % 2))
            C.dve(lambda e: e.reciprocal(out=rl[r4][:], in_=Oacc[:, c, 64:65]), reads=[RO[c]], writes=[Rrl[r4]])
            C.dve(lambda e: e.tensor_scalar(out=YAB[ysl][:, c, att * 64:(att + 1) * 64], in0=Oacc[:, c, 0:64],
                                            scalar1=rl[r4][:, 0:1], scalar2=None, op0=ALU.mult),
                  reads=[RO[c], Rrl[r4]], writes=[RYAB[ysl]])

    def att_block(qi, att):
        ysl = qi % 2
        oi = state["oi"]
        state["oi"] += 1
        ob = 3 + (oi % 2)
        Oacc = PS[ob][:, 0:260].rearrange("p (c n) -> p c n", n=65)
        RO = ROacc[ob - 3]
        for kj in range(4 * qi + 4):
            att_pair(qi, att, kj, Oacc, RO, ysl, oi)

    def att_out(qi):
        ysl = qi % 2
        C.dma("sp", y[qi * 512:(qi + 1) * 512, 0:128].rearrange("(c p) n -> p c n", p=128), YAB[ysl][:],
              reads=[RYAB[ysl]], key="yab%d" % ysl)

    for qi in range(NT):
        att_block(qi, 0)
        att_block(qi, 1)
        att_out(qi)

    if os.environ.get("MIX_DUMP"):
        allres = RQA + RQAm + RKA + RQB + RQBc + RKB + RVA + RVB + RUG + [RCP, RLF, RFRAW, RMV, RRSTD, Rkbar, Rkbarb, RZ, Rc["kaoh"]] + RMBZ + Rpwb + [Rc["band"], Rc["tri_b"], Rc["poolw"]]
        for nm, t, shp, dty in (("d_cp", CP, [128, NTT], F32), ("d_lf", LF, [128, NTT], F32), ("d_fraw", FRAW, [128, NTT], F32),
                                ("d_rstd", RSTD, [128, NTT], F32),
                                ("d_ug", UG, [128, NTT, 128], BF16), ("d_qa", QA, [128, SEQ], BF16), ("d_ka", KA, [128, SEQ], BF16),
                                ("d_qb", QB, [128, SEQ], BF16), ("d_kb", KB, [128, SEQ], BF16), ("d_va", VA, [128, NTT, 65], BF16),
                                ("d_vb", VB, [128, NTT, 65], BF16), ("d_kbar", kbar, [64, 32], F32), ("d_mbz", MBZ, [128, 64 + NTT * 32], F32),
                                ("d_pwb0", pwb[0], [128, 64], BF16), ("d_band", band_b, [128, 3, 128], BF16), ("d_trib", tri_b, [128, 128], BF16)):
            dd = dt(nm, shp, dty, kind="ExternalOutput").ap()
            C.dma("sp", dd, t[:], reads=allres, key=nm)

    return _finish_mixer(C, nc)


def _finish_mixer(C, nc):
    finals = []
    for k, cnt in C.S.dcount.items():
        finals.append((k, 16 * cnt, "dma"))
    C.S.emit(finals)
    C.st.close()
    return nc


def _rot_tables():
    pos = np.arange(S, dtype=np.float32)
    inv_freq = (np.float32(500000.0) ** (-np.arange(0, 16, 2, dtype=np.float32) / np.float32(16))).astype(np.float32)
    ang = (pos[:, None] * inv_freq[None, :]).astype(np.float32)
    cos = np.cos(ang).astype(np.float32).T
    sin = np.sin(ang).astype(np.float32).T
    cs = np.zeros((2, 16, S), np.float32)
    cs[0, 0:8] = cos
    cs[0, 8:16] = cos
    cs[1, 0:8] = -sin
    cs[1, 8:16] = sin
    return cs


def _band_mats(win):
    t = np.arange(128)
    s = np.arange(128)
    cur = ((t[None, :] - s[:, None] >= 0) & (t[None, :] - s[:, None] < win)).astype(np.float32) / win
    cur -= np.eye(128, dtype=np.float32)
    prev = ((t[None, :] + 128 - s[:, None]) < win).astype(np.float32) / win
    cnt = np.minimum(t + 1, win).astype(np.float32)
    first = ((t[None, :] - s[:, None] >= 0) & (t[None, :] - s[:, None] < win)).astype(np.float32) / cnt[None, :]
    first -= np.eye(128, dtype=np.float32)
    return np.stack([cur, prev, first]).astype(np.float32)


_CACHE = {}


def _mixer_inputs(xT_b, l, h, P):
    w_in = P["w_in"][l]
    a0 = 0
    qa = w_in[:, 0 + 64 * h:0 + 64 * h + 64]
    ka = w_in[:, 256 + 64 * h:256 + 64 * h + 64]
    va = w_in[:, 512 + 64 * h:512 + 64 * h + 64]
    qb = w_in[:, 768 + 64 * h:768 + 64 * h + 64]
    kb = w_in[:, 1024 + 64 * h:1024 + 64 * h + 64]
    vb = w_in[:, 1280 + 64 * h:1280 + 64 * h + 64]
    fb = w_in[:, 1536 + h:1536 + h + 1]
    cu = w_in[:, 1540 + 64 * h:1540 + 64 * h + 64]
    cv = w_in[:, 1796 + 64 * h:1796 + 64 * h + 64]
    dp = w_in[:, 2052 + 64 * h:2052 + 64 * h + 64]
    perm = np.concatenate([np.arange(8, 16), np.arange(0, 8)])
    wfm = np.concatenate([qa, ka, qb, kb, dp, qa[:, perm], ka[:, perm]], axis=1)
    wtm = np.concatenate([va, vb, cu, cv, fb], axis=1)
    rep = lambda v: np.ascontiguousarray(np.broadcast_to(v[None, :], (128, v.shape[0]))).astype(np.float32)
    return {
        "xT": xT_b,
        "wfm": np.ascontiguousarray(wfm), "wtm": np.ascontiguousarray(wtm),
        "cs": _CACHE["cs"], "onehot": _CACHE["onehot"],
        "sguT": np.ascontiguousarray(P["sgu_w"][l, h].T), "tri": _CACHE["tri"], "ident": _CACHE["ident"],
        "sgub": np.ascontiguousarray(P["sgu_b"][l, h].reshape(128, 1)),
        "lng": rep(P["sgu_ln_g"][l, 64 * h:64 * h + 64]), "lnb": rep(P["sgu_ln_b"][l, 64 * h:64 * h + 64]),
        "poolw": np.ascontiguousarray(P["pool_w"][l, h]), "pscale": rep(P["pool_scale"][l, 64 * h:64 * h + 64]),
        "band": _CACHE["band"][h],
        "bfor": np.full((128, 1), P["b_forget"][l, h], np.float32),
    }


def _consts():
    if "cs" in _CACHE:
        return
    _CACHE["cs"] = _rot_tables()
    oh = np.zeros((32, S), np.float32)
    for n in range(32):
        oh[n, n * 256:(n + 1) * 256] = BIGM
    _CACHE["onehot"] = oh.astype(ml_dtypes.bfloat16)
    sidx = np.arange(128)
    _CACHE["tri"] = (sidx[:, None] <= sidx[None, :]).astype(np.float32)
    _CACHE["ident"] = np.eye(128, dtype=np.float32)
    _CACHE["band"] = [_band_mats(w) for w in (2, 4, 8, 16)]


def run_mixer(xT_all, l, P):
    _consts()
    if "nc_m" not in _CACHE:
        _CACHE["nc_m"] = build_mixer()
    in_maps = [_mixer_inputs(xT_all[c // 4], l, c % 4, P) for c in range(8)]
    res = run_bass_kernel_spmd(_CACHE["nc_m"], in_maps, core_ids=list(range(8)))
    y = np.zeros((NB, S, 1024), ml_dtypes.bfloat16)
    for c in range(8):
        b, h = c // 4, c % 4
        yy = np.asarray(res.results[c]["y"]).view(ml_dtypes.bfloat16).reshape(S, 256) if res.results[c]["y"].dtype != ml_dtypes.bfloat16 else res.results[c]["y"]
        for m in range(4):
            y[b, :, 256 * m + 64 * h:256 * m + 64 * h + 64] = yy[:, 64 * m:64 * m + 64]
    return y


NTOK = 2048
NCH = 44


def build_post(NT4=4):
    nc = bass.Bass("TRN2", target_bir_lowering=False)
    C = Ctx(nc)
    dt = nc.dram_tensor
    NTK = NT4 * 512
    yin = dt("yin", [128 + NTK, D], BF16, kind="ExternalInput").ap()
    xin = dt("xin", [128 + NTK, D], F32, kind="ExternalInput").ap()
    flag = dt("flag", [128, 1], F32, kind="ExternalInput").ap()
    wo = dt("wo", [D, D], F32, kind="ExternalInput").ap()
    wup = dt("wup", [NCH, 128, 8, 128], F32, kind="ExternalInput").ap()
    wdn = dt("wdn", [DFF, D], F32, kind="ExternalInput").ap()
    convw = dt("convw", [128, NCH, 3], F32, kind="ExternalInput").ap()
    convb = dt("convb", [128, NCH], F32, kind="ExternalInput").ap()
    lnp = dt("lnp", [4, 128, D], F32, kind="ExternalInput").ap()
    ident = dt("ident", [128, 128], F32, kind="ExternalInput").ap()
    xo = dt("xo", [NTK, D], F32, kind="ExternalOutput").ap()

    sb, ps, res = C.sb, C.ps, C.res
    wd_b = sb("wd_b", [128, 22, D], BF16)
    wo_b = sb("wo_b", [128, 8, D], BF16)
    A_T = sb("A_T", [128, 22, 512], BF16)
    X1T = [sb("X1T%d" % i, [128, 8, 512], BF16) for i in range(2)]
    X1Th = sb("X1Th", [128, 8, 2], BF16)
    X1 = sb("X1", [128, 4, D], F32)
    wu = [[sb("wu%d_%d" % (i, j), [128, 8, 128], BF16) for j in range(2)] for i in range(2)]
    H = [sb("H%d" % i, [128, 514], F32) for i in range(2)]
    tg = [sb("tg%d" % i, [128, 512], F32) for i in range(2)]
    tv = [sb("tv%d" % i, [128, 512], F32) for i in range(2)]
    sg = sb("sg", [128, 512], F32)
    HALO = sb("HALO", [128, NCH, 2], F32)
    lnp_s = sb("lnp_s", [128, 4, D], F32)
    yt = [sb("yt%d" % i, [128, D], BF16) for i in range(2)]
    xt = [sb("xt%d" % i, [128, D], F32) for i in range(2)]
    rr = sb("rr", [128, D], F32)
    x1b = sb("x1b", [128, D], BF16)
    yT = sb("yT", [128, 8, 128], BF16)
    x2 = [sb("x2_%d" % i, [128, D], F32) for i in range(2)]
    stats = sb("stats", [128, 12], F32)
    mv = sb("mv", [128, 2], F32)
    sd = sb("sd", [128, 1], F32)
    rstd = sb("rstd", [128, 1], F32)
    cw_s = sb("cw_s", [128, NCH, 3], F32)
    cb_s = sb("cb_s", [128, NCH], F32)
    flag_s = sb("flag_s", [128, 1], F32)
    ident_b = sb("ident_b", [128, 128], BF16)
    PS = [ps("ps%d" % i, [128, 512], F32) for i in range(7)]
    PSB = ps("psb", [128, 1024], BF16)
    RPS = [res("ps", True) for _ in range(7)]
    RPSB = res("psb", True)
    R = lambda n: res(n)
    Rwd, Rwo, RAT, RX1, RX1Th, RHALO, Rlnp, Rrr, Rx1b, RyT = (R("wd"), R("wo"), R("at"), R("x1"), R("x1th"), R("halo"),
                                                             R("lnp"), R("rr"), R("x1b"), R("yT"))
    RX1T = [R("x1t") for _ in range(2)]
    Rwu = [[R("wu") for _ in range(2)] for _ in range(2)]
    RH = [R("h") for _ in range(2)]
    Rtg = [R("tg") for _ in range(2)]
    Rtv = [R("tv") for _ in range(2)]
    Rsg = R("sg")
    Ryt = [R("yt") for _ in range(2)]
    Rxt = [R("xt") for _ in range(2)]
    Rx2 = [R("x2") for _ in range(2)]
    Rst, Rmv, Rsd, Rrstd, Rcw, Rcb, Rflag, Rid = (R("st"), R("mv"), R("sd"), R("rstd"), R("cw"), R("cb"), R("flag"), R("id"))

    C.dma("pool", ident_b[:], ident, writes=[Rid])
    wo_v = wo.rearrange("(kc f) n -> f kc n", f=128)
    for kc in range(0, 8, 4):
        C.dma("pool", wo_b[:, kc:kc + 4, :], wo_v[:, kc:kc + 4, :], writes=[Rwo], key="wo")
    C.dma("sp", lnp_s[:], lnp.rearrange("k p n -> p k n"), writes=[Rlnp])
    C.dma("sp", cw_s[:], convw, writes=[Rcw])
    C.dma("sp", cb_s[:], convb, writes=[Rcb])
    C.dma("sp", flag_s[:], flag, writes=[Rflag])

    def load_sub(i):
        sl = i % 2
        C.dma("sp", yt[sl][:], yin[i * 128:(i + 1) * 128, :], writes=[Ryt[sl]])
        C.dma("sp", xt[sl][:], xin[i * 128:(i + 1) * 128, :], writes=[Rxt[sl]])

    def layer_norm(src, Rsrc, dst, Rdst, gi):
        def st(e):
            e.bn_stats(out=stats[:, 0:6], in_=src[:, 0:512])
            return e.bn_stats(out=stats[:, 6:12], in_=src[:, 512:1024])
        C.dve(st, reads=[Rsrc], writes=[Rst])
        C.dve(lambda e: e.bn_aggr(out=mv[:], in_=stats[:]), reads=[Rst], writes=[Rmv])
        C.dve(lambda e: e.tensor_scalar(out=sd[:], in0=mv[:, 1:2], scalar1=EPS, scalar2=None, op0=ALU.add),
              reads=[Rmv], writes=[Rsd])
        C.act(lambda e: e.activation(out=sd[:], in_=sd[:], func=AF.Sqrt), reads=[Rsd], writes=[Rsd])
        C.dve(lambda e: e.reciprocal(out=rstd[:], in_=sd[:]), reads=[Rsd], writes=[Rrstd])
        C.dve(lambda e: e.tensor_scalar(out=src[:], in0=src[:], scalar1=mv[:, 0:1], scalar2=rstd[:, 0:1],
                                        op0=ALU.subtract, op1=ALU.mult), reads=[Rsrc, Rmv, Rrstd], writes=[Rsrc])
        C.dve(lambda e: e.tensor_tensor(out=src[:], in0=src[:], in1=lnp_s[:, gi, :], op=ALU.mult),
              reads=[Rsrc, Rlnp], writes=[Rsrc])
        C.dve(lambda e: e.tensor_tensor(out=dst, in0=src[:], in1=lnp_s[:, gi + 1, :], op=ALU.add),
              reads=[Rsrc, Rlnp], writes=[Rdst])

    def stage_a(i):
        sl = i % 2
        if i + 1 <= NT4 * 4:
            load_sub(i + 1)

        def tr_y(e):
            ins = None
            for kc in range(8):
                ins = e.transpose(PSB[:, kc * 128:(kc + 1) * 128], yt[sl][:, kc * 128:(kc + 1) * 128], ident_b[:])
            return ins
        C.pe(tr_y, reads=[Ryt[sl], Rid], writes=[RPSB])
        C.act(lambda e: e.copy(out=yT[:].rearrange("p k n -> p (k n)"), in_=PSB[:]), reads=[RPSB], writes=[RyT])
        for half in range(2):
            C.pe(_mm_group(PS[half][:], [(yT[:, kc, :], wo_b[:, kc, half * 512:(half + 1) * 512]) for kc in range(8)]),
                 reads=[RyT, Rwo], writes=[RPS[half]])
        for half in range(2):
            C.dve(lambda e, half=half: e.scalar_tensor_tensor(out=rr[:, half * 512:(half + 1) * 512], in0=xt[sl][:, half * 512:(half + 1) * 512],
                                                              scalar=ALPHA, in1=PS[half][:], op0=ALU.mult, op1=ALU.add),
                  reads=[Rxt[sl], RPS[half]], writes=[Rrr])
        if i == 0:
            layer_norm(rr, Rrr, rr[:], Rrr, 0)
            x1src, Rx1src = rr[:], Rrr
        else:
            c = (i - 1) % 4
            layer_norm(rr, Rrr, X1[:, c, :], RX1, 0)
            x1src, Rx1src = X1[:, c, :], RX1
        C.act(lambda e: e.copy(out=x1b[:], in_=x1src), reads=[Rx1src], writes=[Rx1b])

        def tr_x(e):
            ins = None
            for kc in range(8):
                ins = e.transpose(PSB[:, kc * 128:(kc + 1) * 128], x1b[:, kc * 128:(kc + 1) * 128], ident_b[:])
            return ins
        C.pe(tr_x, reads=[Rx1b, Rid], writes=[RPSB])
        psv = PSB[:].rearrange("p (k n) -> p k n", n=128)
        if i == 0:
            C.act(lambda e: e.copy(out=X1Th[:], in_=psv[:, :, 126:128]), reads=[RPSB], writes=[RX1Th])
        else:
            T = (i - 1) // 4
            c = (i - 1) % 4
            C.act(lambda e: e.copy(out=X1T[T % 2][:, :, c * 128:(c + 1) * 128], in_=psv), reads=[RPSB], writes=[RX1T[T % 2]])

    wup_loaded = {}

    def load_wup(T, cc):
        sl = (T * 22 + cc) % 2
        C.dma("pool", wu[sl][0][:], wup[cc], writes=[Rwu[sl][0]])
        C.dma("pool", wu[sl][1][:], wup[22 + cc], writes=[Rwu[sl][1]])

    state = {"k": 0}

    def ffn_chunk(T, cc, which):
        ch = cc + 22 * which
        wsl = (T * 22 + cc) % 2
        k = state["k"]
        state["k"] += 1
        hb = k % 2
        bank = 2 + hb
        xs = X1T[T % 2]
        C.pe(_mm_group(PS[bank][:], [(wu[wsl][which][:, kc, :], xs[:, kc, :]) for kc in range(8)]),
             reads=[Rwu[wsl][which], RX1T[T % 2]], writes=[RPS[bank]])
        if T == 0:
            C.pe(_mm_group(PS[4][:, 0:2], [(wu[wsl][which][:, kc, :], X1Th[:, kc, :]) for kc in range(8)]),
                 reads=[Rwu[wsl][which], RX1Th], writes=[RPS[4]])
            C.dve(lambda e: e.tensor_scalar(out=H[hb][:, 0:2], in0=PS[4][:, 0:2], scalar1=flag_s[:, 0:1], scalar2=None, op0=ALU.mult),
                  reads=[RPS[4], Rflag], writes=[RH[hb]])
        else:
            C.dve(lambda e: e.tensor_copy(out=H[hb][:, 0:2], in_=HALO[:, ch, :]), reads=[RHALO], writes=[RH[hb]])
        C.act(lambda e: e.copy(out=H[hb][:, 2:514], in_=PS[bank][:]), reads=[RPS[bank]], writes=[RH[hb]])
        C.dve(lambda e: e.tensor_copy(out=HALO[:, ch, :], in_=H[hb][:, 512:514]), reads=[RH[hb]], writes=[RHALO])
        t, Rt = (tg[cc % 2], Rtg[cc % 2]) if which == 0 else (tv[cc % 2], Rtv[cc % 2])
        C.act(lambda e: e.activation(out=t[:], in_=H[hb][:, 0:512], func=AF.Identity, bias=cb_s[:, ch:ch + 1],
                                     scale=cw_s[:, ch, 0:1]), reads=[RH[hb], Rcw, Rcb], writes=[Rt])
        C.dve(lambda e: e.scalar_tensor_tensor(out=t[:], in0=H[hb][:, 1:513], scalar=cw_s[:, ch, 1:2], in1=t[:],
                                               op0=ALU.mult, op1=ALU.add), reads=[RH[hb], Rcw, Rt], writes=[Rt])
        C.dve(lambda e: e.scalar_tensor_tensor(out=t[:], in0=H[hb][:, 2:514], scalar=cw_s[:, ch, 2:3], in1=t[:],
                                               op0=ALU.mult, op1=ALU.add), reads=[RH[hb], Rcw, Rt], writes=[Rt])

    def ffn_pair(T, cc):
        if T * 22 + cc + 1 < NT4 * 22:
            nT, ncc = divmod(T * 22 + cc + 1, 22)
            load_wup(nT, ncc)
        ffn_chunk(T, cc, 0)
        ffn_chunk(T, cc, 1)
        s2 = cc % 2
        C.act(lambda e: e.activation(out=sg[:], in_=tg[s2][:], func=AF.Silu), reads=[Rtg[s2]], writes=[Rsg])
        C.dve(lambda e: e.tensor_tensor(out=A_T[:, cc, :], in0=sg[:], in1=tv[s2][:], op=ALU.mult),
              reads=[Rsg, Rtv[s2]], writes=[RAT])

    def stage_c(T, c):
        j = T * 4 + c
        banks = (5, 6) if j % 2 == 0 else (0, 1)
        for half in range(2):
            C.pe(_mm_group(PS[banks[half]][:], [(A_T[:, cc, c * 128:(c + 1) * 128], wd_b[:, cc, half * 512:(half + 1) * 512])
                                                 for cc in range(22)]), reads=[RAT, Rwd], writes=[RPS[banks[half]]])
        for half in range(2):
            C.dve(lambda e, half=half: e.scalar_tensor_tensor(out=rr[:, half * 512:(half + 1) * 512], in0=X1[:, c, half * 512:(half + 1) * 512],
                                                              scalar=ALPHA, in1=PS[banks[half]][:], op0=ALU.mult, op1=ALU.add),
                  reads=[RX1, RPS[banks[half]]], writes=[Rrr])
        o = j % 2
        layer_norm(rr, Rrr, x2[o][:], Rx2[o], 2)
        C.dma("sp", xo[j * 128:(j + 1) * 128, :], x2[o][:], reads=[Rx2[o]], key="xo%d" % o)

    load_sub(0)
    load_wup(0, 0)
    wd_v = wdn.rearrange("(cc p) n -> p cc n", p=128)
    stage_a(0)
    for T in range(NT4):
        for c in range(4):
            stage_a(1 + 4 * T + c)
        if T == 0:
            for c0 in range(0, 22, 2):
                C.dma("pool", wd_b[:, c0:c0 + 2, :], wd_v[:, c0:c0 + 2, :], writes=[Rwd], key="wd")
        for cc in range(22):
            ffn_pair(T, cc)
        for c in range(4):
            stage_c(T, c)
    return _finish_mixer(C, nc)


def _post_inputs(y_b, x_b, q, l, P):
    t0 = q * NTOK
    if q == 0:
        yh = np.zeros((128, D), ml_dtypes.bfloat16)
        xh = np.zeros((128, D), np.float32)
    else:
        yh = y_b[t0 - 128:t0]
        xh = x_b[t0 - 128:t0]
    rep = lambda v: np.broadcast_to(v[None, :], (128, v.shape[0]))
    key = ("post_w", l)
    if key not in _CACHE:
        w_up = P["w_up"][l]
        _CACHE[key] = {
            "wo": np.ascontiguousarray(P["w_o"][l]),
            "wup": np.ascontiguousarray(w_up.reshape(8, 128, NCH, 128).transpose(2, 1, 0, 3)),
            "wdn": np.ascontiguousarray(P["w_down"][l]),
            "convw": np.ascontiguousarray(P["conv_w"][l].reshape(3, NCH, 128).transpose(2, 1, 0)),
            "convb": np.ascontiguousarray(P["conv_b"][l].reshape(NCH, 128).T),
            "lnp": np.ascontiguousarray(np.stack([rep(P["ln1_g"][l]), rep(P["ln1_b"][l]), rep(P["ln2_g"][l]), rep(P["ln2_b"][l])]).astype(np.float32)),
            "ident": np.eye(128, dtype=np.float32),
        }
    m = dict(_CACHE[key])
    m["yin"] = np.ascontiguousarray(np.concatenate([yh, y_b[t0:t0 + NTOK]], axis=0))
    m["xin"] = np.ascontiguousarray(np.concatenate([xh, x_b[t0:t0 + NTOK]], axis=0))
    m["flag"] = np.full((128, 1), 0.0 if q == 0 else 1.0, np.float32)
    return m


def run_post(y, x, l, P):
    if "nc_p" not in _CACHE:
        _CACHE["nc_p"] = build_post()
    in_maps = [_post_inputs(y[c // 4], x[c // 4], c % 4, l, P) for c in range(8)]
    res = run_bass_kernel_spmd(_CACHE["nc_p"], in_maps, core_ids=list(range(8)))
    out = np.zeros((NB, S, D), np.float32)
    for c in range(8):
        out[c // 4, (c % 4) * NTOK:(c % 4 + 1) * NTOK] = res.results[c]["xo"]
    return out


def kernel(**inputs):
    P = {k: np.asarray(v) for k, v in inputs.items()}
    x = np.ascontiguousarray(P["x"], dtype=np.float32)
    for l in range(2):
        xT = [np.ascontiguousarray(x[b].T) for b in range(NB)]
        y = run_mixer(xT, l, P)
        x = run_post(y, x, l, P)
    return x
```

```python
import contextlib
import os
DBG = int(os.environ.get('MIX_DBG', '99'))
import numpy as np
import ml_dtypes
import concourse.bass as bass
import concourse.mybir as mybir
from concourse.bass_utils import run_bass_kernel_spmd

F32 = mybir.dt.float32
BF16 = mybir.dt.bfloat16
AF = mybir.ActivationFunctionType
ALU = mybir.AluOpType
AX = mybir.AxisListType

S = 8192
D = 1024
NB = 2
DFF = 2816
ALPHA = 4.0 ** 0.25
EPS = 1e-5
BIGM = 30000.0
NEGF = -1.0e30


class Res:
    __slots__ = ("name", "w", "r", "excl")

    def __init__(self, name, excl=False):
        self.name = name
        self.w = None
        self.r = []
        self.excl = excl


class Sched:
    STREAMS = ("pe", "act", "dve", "pool", "sp")

    def __init__(self, nc):
        self.nc = nc
        self.ops = {s: [] for s in self.STREAMS}
        self.ccount = {s: 0 for s in self.STREAMS}
        self.dcount = {}
        self.known = {s: {} for s in self.STREAMS}
        self.final_events = []

    def _need(self, stream, ev, waits):
        if ev is None:
            return
        sem, val, src = ev
        if src == stream and src == "pe":
            return
        if self.known[stream].get(sem, 0) >= val:
            return
        self.known[stream][sem] = val
        waits.append((sem, val))

    def op(self, stream, fn, reads=(), writes=(), dma_key=None):
        ex = [r for r in reads if r.excl]
        if ex:
            reads = [r for r in reads if not r.excl]
            writes = list(writes) + [r for r in ex if r not in writes]
        waits = []
        for r in reads:
            self._need(stream, r.w, waits)
        for w in writes:
            self._need(stream, w.w, waits)
            for e in w.r:
                self._need(stream, e, waits)
        if dma_key is not None:
            k = "d_" + dma_key
            self.dcount[k] = self.dcount.get(k, 0) + 1
            ev = (k, 16 * self.dcount[k], "dma")
            sig = (k, 16)
        else:
            self.ccount[stream] += 1
            ev = ("c_" + stream, self.ccount[stream], stream)
            sig = ("c_" + stream, 1)
        for r in reads:
            r.r.append(ev)
        for w in writes:
            w.w = ev
            w.r = []
        self.ops[stream].append((waits, fn, sig))
        return ev

    def emit(self, final_events):
        nc = self.nc
        names = set()
        for s in self.STREAMS:
            for waits, fn, sig in self.ops[s]:
                names.add(sig[0])
                for (sem, val) in waits:
                    names.add(sem)
        with contextlib.ExitStack() as st:
            sems = {n: st.enter_context(nc.semaphore(n)) for n in sorted(names)}
            block = st.enter_context(nc.Block())

            def make(stream):
                def body(eng):
                    for waits, fn, sig in self.ops[stream]:
                        for (sem, val) in waits:
                            eng.wait_ge(sems[sem], val)
                        ins = fn(eng)
                        ins.then_inc(sems[sig[0]], sig[1])
                    if stream == "sp":
                        for (sem, val, src) in final_events:
                            eng.wait_ge(sems[sem], val)
                return body

            block.tensor(make("pe"))
            block.scalar(make("act"))
            block.vector(make("dve"))
            block.gpsimd(make("pool"))
            block.sync(make("sp"))


class Ctx:
    def __init__(self, nc):
        self.nc = nc
        self.st = contextlib.ExitStack()
        self.S = Sched(nc)
        self.nres = 0

    def sb(self, name, shape, dt):
        return self.st.enter_context(self.nc.sbuf_tensor(name, shape, dt))

    def ps(self, name, shape, dt):
        return self.st.enter_context(self.nc.psum_tensor(name, shape, dt))

    def res(self, name="r", excl=False):
        self.nres += 1
        return Res("%s%d" % (name, self.nres), excl)

    def pe(self, fn, reads=(), writes=()):
        return self.S.op("pe", fn, reads, writes)

    def act(self, fn, reads=(), writes=()):
        return self.S.op("act", fn, reads, writes)

    def dve(self, fn, reads=(), writes=()):
        return self.S.op("dve", fn, reads, writes)

    def pool(self, fn, reads=(), writes=()):
        return self.S.op("pool", fn, reads, writes)

    def dma(self, queue, out, in_, reads=(), writes=(), key=None):
        if key is None:
            key = (writes[0].name if writes else reads[0].name)
        return self.S.op(queue, lambda e: e.dma_start(out=out, in_=in_), reads, writes, dma_key=key)


def _mm_group(out, pairs):
    def fn(e):
        n = len(pairs)
        ins = None
        for i, (l, r) in enumerate(pairs):
            ins = e.matmul(out, lhsT=l, rhs=r, start=(i == 0), stop=(i == n - 1))
        return ins
    return fn


NFM = 352
NTM = 257


def build_mixer(SEQ=S, PH=9):
    NT, NTT = SEQ // 512, SEQ // 128
    nc = bass.Bass("TRN2", target_bir_lowering=False)
    C = Ctx(nc)
    dt = nc.dram_tensor
    xT = dt("xT", [D, SEQ], F32, kind="ExternalInput").ap()
    wfm = dt("wfm", [D, NFM], F32, kind="ExternalInput").ap()
    wtm = dt("wtm", [D, NTM], F32, kind="ExternalInput").ap()
    cs = dt("cs", [2, 16, SEQ], F32, kind="ExternalInput").ap()
    onehot = dt("onehot", [32, SEQ], BF16, kind="ExternalInput").ap()
    sguT = dt("sguT", [128, 128], F32, kind="ExternalInput").ap()
    tri = dt("tri", [128, 128], F32, kind="ExternalInput").ap()
    ident = dt("ident", [128, 128], F32, kind="ExternalInput").ap()
    sgub = dt("sgub", [128, 1], F32, kind="ExternalInput").ap()
    lng = dt("lng", [128, 64], F32, kind="ExternalInput").ap()
    lnb = dt("lnb", [128, 64], F32, kind="ExternalInput").ap()
    poolw = dt("poolw", [64, 64], F32, kind="ExternalInput").ap()
    pscale = dt("pscale", [128, 64], F32, kind="ExternalInput").ap()
    band = dt("band", [3, 128, 128], F32, kind="ExternalInput").ap()
    bfor = dt("bfor", [128, 1], F32, kind="ExternalInput").ap()
    y = dt("y", [SEQ, 256], BF16, kind="ExternalOutput").ap()

    sb, ps, res = C.sb, C.ps, C.res
    QA = sb("QA", [128, SEQ], BF16)
    KA = sb("KA", [128, SEQ], BF16)
    QB = sb("QB", [128, SEQ], BF16)
    KB = sb("KB", [128, SEQ], BF16)
    VA = sb("VA", [128, NTT, 65], BF16)
    VB = sb("VB", [128, NTT, 65], BF16)
    xb = [sb("xb%d" % i, [128, 8, 512], BF16) for i in range(2)]
    wfm_b = sb("wfm_b", [128, 8, NFM], BF16)
    wtm_b = sb("wtm_b", [128, 8, NTM], BF16)
    cst = [sb("cst%d" % i, [16, 2, 512], F32) for i in range(2)]
    t1 = [sb("t1_%d" % i, [16, 512], F32) for i in range(2)]
    t2 = [sb("t2_%d" % i, [16, 512], F32) for i in range(2)]
    pTb = [sb("pTb%d" % i, [64, 512], BF16) for i in range(2)]
    ug = [sb("ug%d" % i, [128, 128], F32) for i in range(2)]
    stats = [sb("stats%d" % i, [128, 6], F32) for i in range(2)]
    mv = [sb("mv%d" % i, [128, 2], F32) for i in range(2)]
    rstd = [sb("rstd%d" % i, [128, 1], F32) for i in range(2)]
    vn = [sb("vn%d" % i, [128, 64], F32) for i in range(2)]
    vnb = [sb("vnb%d" % i, [128, 64], BF16) for i in range(2)]
    pwb = [sb("pwb%d" % i, [128, 64], BF16) for i in range(2)]
    YC = [sb("YC%d" % i, [128, 4, 64], BF16) for i in range(2)]
    YD = [sb("YD%d" % i, [128, 4, 64], BF16) for i in range(2)]
    UG = sb("UG", [128, NTT, 128], BF16)
    MV = sb("MV", [128, NTT, 2], F32)
    VE = sb("VE", [128, NTT], F32)
    RSTD = sb("RSTD", [128, NTT], F32)
    YAB = [sb("YAB%d" % i, [128, 4, 128], BF16) for i in range(2)]
    MBZ = sb("MBZ", [128, 64 + NTT * 32], F32)
    GM = sb("GM", [128, 32], F32)
    top8 = [sb("top8_%d" % i, [128, 8], F32) for i in range(2)]
    FRAW = sb("FRAW", [128, NTT], F32)
    LF = sb("LF", [128, NTT], F32)
    TOT = sb("TOT", [128, NTT], F32)
    PREF = sb("PREF", [128, NTT], F32)
    CP = sb("CP", [128, NTT], F32)
    Z = sb("Z", [128, 128], F32)
    kbar = sb("kbar", [64, 32], F32)
    kbar_b = sb("kbar_b", [64, 32], BF16)
    PT = [sb("PT%d" % i, [128, 512], BF16) for i in range(4)]
    rl = [sb("rl%d" % i, [128, 1], F32) for i in range(4)]
    ident_f = sb("ident_f", [128, 128], F32)
    tri_f = sb("tri_f", [128, 128], F32)
    tri_b = sb("tri_b", [128, 128], BF16)
    ones_f = sb("ones_f", [128, 128], F32)
    sgu_f = sb("sgu_f", [128, 128], F32)
    wmT_b = sb("wmT_b", [128, 128], BF16)
    band_b = sb("band_b", [128, 3, 128], BF16)
    sgub_s = sb("sgub_s", [128, 1], F32)
    lng_s = sb("lng_s", [128, 64], F32)
    lnb_s = sb("lnb_s", [128, 64], F32)
    poolw_b = sb("poolw_b", [64, 64], BF16)
    pscale_s = sb("pscale_s", [128, 64], F32)
    bfor_s = sb("bfor_s", [128, 1], F32)
    negb = sb("negb", [128, 1], F32)
    ef = sb("ef", [128, NTT], F32)
    PS = [ps("ps%d" % i, [128, 512], F32) for i in range(8)]
    RPS = [res("ps", True) for _ in range(8)]

    R = lambda n: res(n)
    RQA = [R("qa") for _ in range(NT)]
    RQAm = [R("qam") for _ in range(NT)]
    RKA = [R("ka") for _ in range(NT)]
    RQB = [R("qb") for _ in range(NT)]
    RQBc = [R("qbc") for _ in range(NT)]
    RKB = [R("kb") for _ in range(NT)]
    RVA = [R("va") for _ in range(NT)]
    RVB = [R("vb") for _ in range(NT)]
    Rxb = [R("xb") for _ in range(2)]
    Rcs = [R("cs") for _ in range(2)]
    Rt1 = [R("t1") for _ in range(2)]
    Rt2 = [R("t2") for _ in range(2)]
    RpT = [R("pT") for _ in range(2)]
    Rug = [R("ug") for _ in range(2)]
    Rst = [R("st") for _ in range(2)]
    Rmv = [R("mv") for _ in range(2)]
    Rrs = [R("rs") for _ in range(2)]
    Rvn = [R("vn") for _ in range(2)]
    Rvnb = [R("vnb") for _ in range(2)]
    Rpwb = [R("pwb") for _ in range(2)]
    RYC = [R("yc") for _ in range(2)]
    RYD = [R("yd") for _ in range(2)]
    RUG = [R("ugall") for _ in range(NT)]
    RMV, RVE, RRSTD = R("mvall"), R("ve"), R("rstdall")
    RYAB = [R("yab") for _ in range(2)]
    RMBZ = [R("mbz") for _ in range(NT)]
    RGM = R("gm")
    Rtop = [R("top") for _ in range(2)]
    RFRAW, RLF, RTOT, RPREF, RCP, RZ, Rkbar, Rkbarb, Ref = (R("fraw"), R("lf"), R("tot"), R("pref"),
                                                            R("cp"), R("z"), R("kbar"), R("kbarb"), R("ef"))
    RPT = [R("pt") for _ in range(4)]
    Rrl = [R("rl") for _ in range(4)]
    Rc = {k: R(k) for k in ["wfm", "wtm", "ident", "tri_f", "tri_b", "ones", "sgu_f", "wmT", "band", "sgub",
                            "lng", "lnb", "poolw", "pscale", "bfor", "negb", "kaoh", "misc"]}

    C.dma("pool", wfm_b[:], wfm.rearrange("(kc f) n -> f kc n", f=128), writes=[Rc["wfm"]])
    C.dma("pool", wtm_b[:], wtm.rearrange("(kc f) n -> f kc n", f=128), writes=[Rc["wtm"]])
    C.dma("sp", ident_f[:], ident, writes=[Rc["ident"]])
    C.dma("sp", tri_f[:], tri, writes=[Rc["tri_f"]])
    C.dma("pool", tri_b[:], tri, writes=[Rc["tri_b"]])
    C.dma("sp", sgu_f[:], sguT, writes=[Rc["sgu_f"]])
    C.dma("pool", band_b[:], band.rearrange("k s t -> s k t"), writes=[Rc["band"]])
    C.dma("sp", sgub_s[:], sgub, writes=[Rc["sgub"]])
    C.dma("sp", lng_s[:], lng, writes=[Rc["lng"]])
    C.dma("sp", lnb_s[:], lnb, writes=[Rc["lnb"]])
    C.dma("pool", poolw_b[:], poolw, writes=[Rc["poolw"]])
    C.dma("sp", pscale_s[:], pscale, writes=[Rc["pscale"]])
    C.dma("sp", bfor_s[:], bfor, writes=[Rc["bfor"]])
    C.dma("sp", KA[64:96, :], onehot, writes=[Rc["kaoh"]])
    C.dve(lambda e: e.tensor_tensor(out=wmT_b[:], in0=sgu_f[:], in1=tri_f[:], op=ALU.mult),
          reads=[Rc["sgu_f"], Rc["tri_f"]], writes=[Rc["wmT"]])
    C.dve(lambda e: e.tensor_scalar(out=negb[:], in0=bfor_s[:], scalar1=-1.0, scalar2=None, op0=ALU.mult),
          reads=[Rc["bfor"]], writes=[Rc["negb"]])
    C.dve(lambda e: e.memset(ones_f[:], 1.0), writes=[Rc["ones"]])
    C.dve(lambda e: e.memset(VA[:, :, 64:65], 1.0), writes=RVA)
    C.dve(lambda e: e.memset(VB[:, :, 64:65], 1.0), writes=RVB)
    C.dve(lambda e: e.memset(KB[64:65, :], 1.0), writes=RKB)
    C.dve(lambda e: e.memset(MBZ[:], 0.0), writes=RMBZ)
    C.dve(lambda e: e.memset(GM[:], NEGF), writes=[RGM])
    C.dve(lambda e: e.memset(Z[:], 0.0), writes=[RZ])
    C.dve(lambda e: e.memset(kbar[:], 0.0), writes=[Rkbar])
    C.dve(lambda e: e.memset(PREF[:, 0:1], 0.0), writes=[RPREF])

    if PH == 0:
        return _finish_mixer(C, nc)
    xT_v = xT.rearrange("(kc f) t -> f kc t", f=128)

    def load_tile(T):
        sl = T % 2
        C.dma("pool", xb[sl][:], xT_v[:, :, T * 512:(T + 1) * 512], writes=[Rxb[sl]])
        C.dma("sp", cst[sl][:], cs[:, :, T * 512:(T + 1) * 512].rearrange("k p t -> p k t"), writes=[Rcs[sl]])

    load_tile(0)
    fm_cols = {"qA": (0, 64), "kA": (64, 64), "qB": (128, 64), "kB": (192, 64), "pD": (256, 64),
               "qAp": (320, 16), "kAp": (336, 16)}
    def p1_tile(T):
        sl = T % 2
        if T + 1 < NT:
            load_tile(T + 1)
        cols = slice(T * 512, (T + 1) * 512)

        def fm(name, bank):
            c0, n = fm_cols[name]
            C.pe(_mm_group(PS[bank][0:n, :], [(wfm_b[:, kc, c0:c0 + n], xb[sl][:, kc, :]) for kc in range(8)]),
                 reads=[Rc["wfm"], Rxb[sl]], writes=[RPS[bank]])

        def rot(nm, nmp, dst, Rdst):
            fm(nm, 0)
            fm(nmp, 1)
            if DBG == 10:
                return
            C.act(lambda e, dst=dst: e.copy(out=dst[0:64, cols], in_=PS[0][0:64, :]), reads=[RPS[0]], writes=[Rdst[T]])
            if DBG == 11:
                return
            C.dve(lambda e: e.tensor_tensor(out=t1[sl][:], in0=PS[0][0:16, :], in1=cst[sl][:, 0, :], op=ALU.mult),
                  reads=[RPS[0], Rcs[sl]], writes=[Rt1[sl]])
            C.dve(lambda e: e.tensor_tensor(out=t2[sl][:], in0=PS[1][0:16, :], in1=cst[sl][:, 1, :], op=ALU.mult),
                  reads=[RPS[1], Rcs[sl]], writes=[Rt2[sl]])
            if DBG == 12:
                return
            C.dve(lambda e, dst=dst: e.tensor_tensor(out=dst[0:16, cols], in0=t1[sl][:], in1=t2[sl][:], op=ALU.add),
                  reads=[Rt1[sl], Rt2[sl]], writes=[Rdst[T]])
        if DBG < 1:
            return
        rot("qA", "qAp", QA, RQA)
        rot("kA", "kAp", KA, RKA)
        if DBG < 2 or (10 <= DBG < 20):
            return
        C.dve(lambda e: e.tensor_reduce(out=kbar[:, 2 * T:2 * T + 2],
                                        in_=KA[0:64, cols].rearrange("p (b j) -> p b j", j=256),
                                        axis=AX.X, op=ALU.add), reads=[RKA[T]], writes=[Rkbar])
        if DBG < 3:
            return
        fm("qB", 0)
        C.act(lambda e: e.copy(out=QB[0:64, cols], in_=PS[0][0:64, :]), reads=[RPS[0]], writes=[RQB[T]])
        fm("kB", 1)
        C.act(lambda e: e.copy(out=KB[0:64, cols], in_=PS[1][0:64, :]), reads=[RPS[1]], writes=[RKB[T]])
        fm("pD", 0)
        C.act(lambda e: e.copy(out=pTb[sl][:], in_=PS[0][0:64, :]), reads=[RPS[0]], writes=[RpT[sl]])
        if DBG < 4:
            return
        def sub(c):
            tt = 4 * T + c
            s2 = tt % 2
            bk = 2 + s2
            C.pe(_mm_group(PS[bk][:, 0:NTM], [(xb[sl][:, kc, c * 128:(c + 1) * 128], wtm_b[:, kc, :]) for kc in range(8)]),
                 reads=[Rc["wtm"], Rxb[sl]], writes=[RPS[bk]])
            C.act(lambda e, bk=bk, tt=tt: e.copy(out=VA[:, tt, 0:64], in_=PS[bk][:, 0:64]), reads=[RPS[bk]], writes=[RVA[T]])
            C.act(lambda e, bk=bk, tt=tt: e.copy(out=VB[:, tt, 0:64], in_=PS[bk][:, 64:128]), reads=[RPS[bk]], writes=[RVB[T]])
            C.dve(lambda e, bk=bk, tt=tt: e.tensor_copy(out=FRAW[:, tt:tt + 1], in_=PS[bk][:, 256:257]),
                  reads=[RPS[bk]], writes=[RFRAW])
            C.act(lambda e, bk=bk, tt=tt: e.activation(out=UG[:, tt, :], in_=PS[bk][:, 128:256], func=AF.Gelu),
                  reads=[RPS[bk]], writes=[RUG[T]])
            C.dve(lambda e, tt=tt, s2=s2: e.bn_stats(out=stats[s2][:], in_=UG[:, tt, 64:128]), reads=[RUG[T]], writes=[Rst[s2]])
            C.dve(lambda e, tt=tt, s2=s2: e.bn_aggr(out=MV[:, tt, :], in_=stats[s2][:]), reads=[Rst[s2]], writes=[RMV])
            C.pe(lambda e, c=c: e.matmul(PS[5][:, 0:64], lhsT=pTb[sl][:, c * 128:(c + 1) * 128], rhs=poolw_b[:],
                                         start=True, stop=True), reads=[RpT[sl], Rc["poolw"]], writes=[RPS[5]])
            C.act(lambda e, s2=s2: e.copy(out=pwb[s2][:], in_=PS[5][:, 0:64]), reads=[RPS[5]], writes=[Rpwb[s2]])
            if tt == 0:
                C.pe(lambda e, s2=s2: e.matmul(PS[6][:, 0:64], lhsT=band_b[:, 2, :], rhs=pwb[s2][:], start=True, stop=True),
                     reads=[Rc["band"], Rpwb[s2]], writes=[RPS[6]])
            else:
                C.pe(_mm_group(PS[6][:, 0:64], [(band_b[:, 0, :], pwb[s2][:]), (band_b[:, 1, :], pwb[1 - s2][:])]),
                     reads=[Rc["band"], Rpwb[0], Rpwb[1]], writes=[RPS[6]])
            C.dve(lambda e, c=c: e.tensor_tensor(out=YD[sl][:, c, :], in0=PS[6][:, 0:64], in1=pscale_s[:], op=ALU.mult),
                  reads=[RPS[6], Rc["pscale"]], writes=[RYD[sl]])
        for c in range(4):
            sub(c)
        C.dma("sp", y[T * 512:(T + 1) * 512, 192:256].rearrange("(c p) n -> p c n", p=128), YD[sl][:],
              reads=[RYD[sl]], key="yd%d" % sl)

    for T in range(NT):
        p1_tile(T)

    if PH == 1:
        return _finish_mixer(C, nc)
    C.dve(lambda e: e.tensor_scalar(out=VE[:], in0=MV[:, :, 1], scalar1=EPS, scalar2=None, op0=ALU.add),
          reads=[RMV], writes=[RVE])
    C.act(lambda e: e.activation(out=VE[:], in_=VE[:], func=AF.Sqrt), reads=[RVE], writes=[RVE])
    C.dve(lambda e: e.reciprocal(out=RSTD[:], in_=VE[:]), reads=[RVE], writes=[RRSTD])

    def c_sub(T, c):
        tt = 4 * T + c
        s2 = tt % 2
        sl = T % 2
        C.dve(lambda e: e.tensor_scalar(out=vn[s2][:], in0=UG[:, tt, 64:128], scalar1=MV[:, tt, 0:1],
                                        scalar2=RSTD[:, tt:tt + 1], op0=ALU.subtract, op1=ALU.mult),
              reads=[RUG[T], RMV, RRSTD], writes=[Rvn[s2]])
        C.dve(lambda e: e.tensor_tensor(out=vn[s2][:], in0=vn[s2][:], in1=lng_s[:], op=ALU.mult),
              reads=[Rvn[s2], Rc["lng"]], writes=[Rvn[s2]])
        C.dve(lambda e: e.tensor_tensor(out=vnb[s2][:], in0=vn[s2][:], in1=lnb_s[:], op=ALU.add),
              reads=[Rvn[s2], Rc["lnb"]], writes=[Rvnb[s2]])
        C.pe(lambda e: e.matmul(PS[4 + s2][:, 0:64], lhsT=wmT_b[:], rhs=vnb[s2][:], start=True, stop=True),
             reads=[Rc["wmT"], Rvnb[s2]], writes=[RPS[4 + s2]])
        C.dve(lambda e: e.scalar_tensor_tensor(out=YC[sl][:, c, :], in0=PS[4 + s2][:, 0:64],
                                               scalar=sgub_s[:, 0:1], in1=UG[:, tt, 0:64],
                                               op0=ALU.add, op1=ALU.mult),
              reads=[RPS[4 + s2], Rc["sgub"], RUG[T]], writes=[RYC[sl]])

    for T in range(NT):
        for c in range(4):
            c_sub(T, c)
        C.dma("sp", y[T * 512:(T + 1) * 512, 128:192].rearrange("(c p) n -> p c n", p=128), YC[T % 2][:],
              reads=[RYC[T % 2]], key="yc%d" % (T % 2))

    if PH == 2:
        return _finish_mixer(C, nc)
    C.act(lambda e: e.activation(out=ef[:], in_=FRAW[:], func=AF.Exp, bias=negb[:, 0:1], scale=-1.0),
          reads=[RFRAW, Rc["negb"]], writes=[Ref])
    C.act(lambda e: e.activation(out=LF[:], in_=ef[:], func=AF.Ln, bias=1.0, scale=1.0), reads=[Ref], writes=[RLF])
    C.pe(lambda e: e.matmul(PS[0][:, 0:NTT], lhsT=ones_f[:], rhs=LF[:], start=True, stop=True),
         reads=[Rc["ones"], RLF], writes=[RPS[0]])
    C.dve(lambda e: e.tensor_copy(out=TOT[:], in_=PS[0][:, 0:NTT]), reads=[RPS[0]], writes=[RTOT])
    for j in range(1, NTT):
        C.dve(lambda e, j=j: e.tensor_tensor(out=PREF[:, j:j + 1], in0=PREF[:, j - 1:j], in1=TOT[:, j - 1:j], op=ALU.add),
              reads=[RPREF, RTOT], writes=[RPREF])
    C.pe(lambda e: e.matmul(PS[1][:, 0:NTT], lhsT=tri_f[:], rhs=LF[:], start=True, stop=True),
         reads=[Rc["tri_f"], RLF], writes=[RPS[1]])
    C.dve(lambda e: e.tensor_tensor(out=CP[:], in0=PS[1][:, 0:NTT], in1=PREF[:], op=ALU.add),
          reads=[RPS[1], RPREF], writes=[RCP])
    C.dve(lambda e: e.tensor_scalar(out=Z[:, 64:64 + NTT], in0=CP[:], scalar1=-8.0, scalar2=None, op0=ALU.mult),
          reads=[RCP], writes=[RZ])
    C.dve(lambda e: e.tensor_scalar(out=kbar_b[:], in0=kbar[:], scalar1=1.0 / 256.0, scalar2=None, op0=ALU.mult),
          reads=[Rkbar], writes=[Rkbarb])
    def p2_tile(T):
        bk = 2 + (T % 2)
        def tr4(e, T=T, bk=bk):
            ins = None
            for c in range(4):
                tt = 4 * T + c
                ins = e.matmul(PS[bk][0:65, c * 128:(c + 1) * 128], lhsT=Z[:, tt:tt + 65], rhs=ident_f[:],
                               start=True, stop=True)
            return ins
        C.pe(tr4, reads=[RZ, Rc["ident"]], writes=[RPS[bk]])
        C.act(lambda e, T=T, bk=bk: e.copy(out=QB[64:65, T * 512:(T + 1) * 512], in_=PS[bk][64:65, :]),
              reads=[RPS[bk]], writes=[RQBc[T]])
        def gate(c):
            tt = 4 * T + c
            b = tt // 2
            if b == 0:
                return
            s2 = tt % 2
            C.pe(lambda e, tt=tt: e.matmul(PS[4 + (tt % 2)][:, 0:32], lhsT=QA[0:64, tt * 128:(tt + 1) * 128], rhs=kbar_b[:],
                                           start=True, stop=True), reads=[RQA[T], Rkbarb], writes=[RPS[4 + s2]])
            C.dve(lambda e, tt=tt, b=b: e.tensor_copy(out=GM[:, 0:b], in_=PS[4 + (tt % 2)][:, 0:b]),
                  reads=[RPS[4 + s2]], writes=[RGM])
            C.dve(lambda e, s2=s2: e.max(out=top8[s2][:], in_=GM[:]), reads=[RGM], writes=[Rtop[s2]])
            C.dve(lambda e, tt=tt, b=b, s2=s2: e.tensor_scalar(out=MBZ[:, 64 + 32 * tt:64 + 32 * tt + b], in0=GM[:, 0:b],
                                                               scalar1=top8[s2][:, 2:3], scalar2=1.0,
                                                               op0=ALU.is_ge, op1=ALU.subtract),
                  reads=[RGM, Rtop[s2]], writes=[RMBZ[T]])
        for c in range(4):
            gate(c)
        bk2 = 6 + (T % 2)

        def trm(e, T=T, bk2=bk2):
            ins = None
            for c in range(4):
                tt = 4 * T + c
                ins = e.matmul(PS[bk2][0:96, c * 128:(c + 1) * 128], lhsT=MBZ[:, 32 * tt:32 * tt + 96], rhs=ident_f[:],
                               start=True, stop=True)
            return ins
        C.pe(trm, reads=[RMBZ[T], Rc["ident"]], writes=[RPS[bk2]])
        C.act(lambda e, T=T, bk2=bk2: e.copy(out=QA[64:96, T * 512:(T + 1) * 512], in_=PS[bk2][64:96, :]),
              reads=[RPS[bk2]], writes=[RQAm[T]])

    for T in range(NT):
        p2_tile(T)

    if PH == 3:
        return _finish_mixer(C, nc)
    ROacc = [[RPS[3]] * 4, [RPS[4]] * 4]
    LOOK = 2

    def att_score(p):
        qi, att, kj, oi, idx = p
        Q, K = (QA, KA) if att == 0 else (QB, KB)
        RQ, RQx, RK = (RQA, RQAm, RKA) if att == 0 else (RQB, RQBc, RKB)
        rows = 96 if att == 0 else 65
        d = kj - 4 * qi
        off = 128 * max(d, 0)
        n = 512 - off
        sbk = idx % 3
        pt = idx % 4
        C.pe(lambda e: e.matmul(PS[sbk][:, 0:n], lhsT=K[0:rows, kj * 128:(kj + 1) * 128],
                                rhs=Q[0:rows, qi * 512 + off:(qi + 1) * 512], start=True, stop=True),
             reads=[RQ[qi], RQx[qi], RK[kj // 4]] + ([Rc["kaoh"]] if att == 0 else []), writes=[RPS[sbk]])
        if att == 0:
            C.act(lambda e: e.activation(out=PT[pt][:, 0:n], in_=PS[sbk][:, 0:n], func=AF.Exp, scale=0.125),
                  reads=[RPS[sbk]], writes=[RPT[pt]])
        else:
            C.act(lambda e: e.activation(out=PT[pt][:, 0:n], in_=PS[sbk][:, 0:n], func=AF.Exp,
                                         bias=CP[:, kj:kj + 1], scale=0.125),
                  reads=[RPS[sbk], RCP], writes=[RPT[pt]])
        if d >= 0:
            C.dve(lambda e: e.tensor_tensor(out=PT[pt][:, 0:128], in0=PT[pt][:, 0:128], in1=tri_b[:], op=ALU.mult),
                  reads=[RPT[pt], Rc["tri_b"]], writes=[RPT[pt]])

    def att_pv(p):
        qi, att, kj, oi, idx = p
        V = VA if att == 0 else VB
        RV = RVA if att == 0 else RVB
        ysl = qi % 2
        ob = 3 + (oi % 2)
        Oacc = PS[ob][:, 0:260].rearrange("p (c n) -> p c n", n=65)
        RO = ROacc[ob - 3]
        d = kj - 4 * qi
        off = 128 * max(d, 0)
        pt = idx % 4
        c0 = max(d, 0)

        def pv(e):
            ins = None
            for c in range(c0, 4):
                ins = e.matmul(Oacc[:, c, :], lhsT=PT[pt][:, c * 128 - off:(c + 1) * 128 - off], rhs=V[:, kj, :],
                               start=(kj == 0 and c == 0), stop=(kj == 4 * qi + c), skip_group_check=True)
            return ins
        C.pe(pv, reads=[RPT[pt], RV[kj // 4]], writes=[RO[0]])
        if d >= 0:
            c = d
            r4 = (2 * (oi % 2) + (c % 2))
            C.dve(lambda e: e.reciprocal(out=rl[r4][:], in_=Oacc[:, c, 64:65]), reads=[RO[c]], writes=[Rrl[r4]])
            C.dve(lambda e: e.tensor_scalar(out=YAB[ysl][:, c, att * 64:(att + 1) * 64], in0=Oacc[:, c, 0:64],
                                            scalar1=rl[r4][:, 0:1], scalar2=None, op0=ALU.mult),
                  reads=[RO[c], Rrl[r4]], writes=[RYAB[ysl]])
        if att == 1 and d == 3:
            C.dma("sp", y[qi * 512:(qi + 1) * 512, 0:128].rearrange("(c p) n -> p c n", p=128), YAB[ysl][:],
                  reads=[RYAB[ysl]], key="yab%d" % ysl)

    plist = []
    oi = 0
    for qi in range(NT):
        for att in range(2):
            for kj in range(4 * qi + 4):
                plist.append((qi, att, kj, oi, len(plist)))
            oi += 1
    for i in range(len(plist) + LOOK):
        if i < len(plist):
            att_score(plist[i])
        if i - LOOK >= 0:
            att_pv(plist[i - LOOK])

    if os.environ.get("MIX_DUMP"):
        allres = RQA + RQAm + RKA + RQB + RQBc + RKB + RVA + RVB + RUG + [RCP, RLF, RFRAW, RMV, RRSTD, Rkbar, Rkbarb, RZ, Rc["kaoh"]] + RMBZ + Rpwb + [Rc["band"], Rc["tri_b"], Rc["poolw"]]
        for nm, t, shp, dty in (("d_cp", CP, [128, NTT], F32), ("d_lf", LF, [128, NTT], F32), ("d_fraw", FRAW, [128, NTT], F32),
                                ("d_rstd", RSTD, [128, NTT], F32),
                                ("d_ug", UG, [128, NTT, 128], BF16), ("d_qa", QA, [128, SEQ], BF16), ("d_ka", KA, [128, SEQ], BF16),
                                ("d_qb", QB, [128, SEQ], BF16), ("d_kb", KB, [128, SEQ], BF16), ("d_va", VA, [128, NTT, 65], BF16),
                                ("d_vb", VB, [128, NTT, 65], BF16), ("d_kbar", kbar, [64, 32], F32), ("d_mbz", MBZ, [128, 64 + NTT * 32], F32),
                                ("d_pwb0", pwb[0], [128, 64], BF16), ("d_band", band_b, [128, 3, 128], BF16), ("d_trib", tri_b, [128, 128], BF16)):
            dd = dt(nm, shp, dty, kind="ExternalOutput").ap()
            C.dma("sp", dd, t[:], reads=allres, key=nm)

    return _finish_mixer(C, nc)


def _finish_mixer(C, nc):
    finals = []
    for k, cnt in C.S.dcount.items():
        finals.append((k, 16 * cnt, "dma"))
    C.S.emit(finals)
    C.st.close()
    return nc


def _rot_tables():
    pos = np.arange(S, dtype=np.float32)
    inv_freq = (np.float32(500000.0) ** (-np.arange(0, 16, 2, dtype=np.float32) / np.float32(16))).astype(np.float32)
    ang = (pos[:, None] * inv_freq[None, :]).astype(np.float32)
    cos = np.cos(ang).astype(np.float32).T
    sin = np.sin(ang).astype(np.float32).T
    cs = np.zeros((2, 16, S), np.float32)
    cs[0, 0:8] = cos
    cs[0, 8:16] = cos
    cs[1, 0:8] = -sin
    cs[1, 8:16] = sin
    return cs


def _band_mats(win):
    t = np.arange(128)
    s = np.arange(128)
    cur = ((t[None, :] - s[:, None] >= 0) & (t[None, :] - s[:, None] < win)).astype(np.float32) / win
    cur -= np.eye(128, dtype=np.float32)
    prev = ((t[None, :] + 128 - s[:, None]) < win).astype(np.float32) / win
    cnt = np.minimum(t + 1, win).astype(np.float32)
    first = ((t[None, :] - s[:, None] >= 0) & (t[None, :] - s[:, None] < win)).astype(np.float32) / cnt[None, :]
    first -= np.eye(128, dtype=np.float32)
    return np.stack([cur, prev, first]).astype(np.float32)


_CACHE = {}


def _mixer_inputs(xT_b, l, h, P):
    w_in = P["w_in"][l]
    a0 = 0
    qa = w_in[:, 0 + 64 * h:0 + 64 * h + 64]
    ka = w_in[:, 256 + 64 * h:256 + 64 * h + 64]
    va = w_in[:, 512 + 64 * h:512 + 64 * h + 64]
    qb = w_in[:, 768 + 64 * h:768 + 64 * h + 64]
    kb = w_in[:, 1024 + 64 * h:1024 + 64 * h + 64]
    vb = w_in[:, 1280 + 64 * h:1280 + 64 * h + 64]
    fb = w_in[:, 1536 + h:1536 + h + 1]
    cu = w_in[:, 1540 + 64 * h:1540 + 64 * h + 64]
    cv = w_in[:, 1796 + 64 * h:1796 + 64 * h + 64]
    dp = w_in[:, 2052 + 64 * h:2052 + 64 * h + 64]
    perm = np.concatenate([np.arange(8, 16), np.arange(0, 8)])
    wfm = np.concatenate([qa, ka, qb, kb, dp, qa[:, perm], ka[:, perm]], axis=1)
    wtm = np.concatenate([va, vb, cu, cv, fb], axis=1)
    rep = lambda v: np.ascontiguousarray(np.broadcast_to(v[None, :], (128, v.shape[0]))).astype(np.float32)
    return {
        "xT": xT_b,
        "wfm": np.ascontiguousarray(wfm), "wtm": np.ascontiguousarray(wtm),
        "cs": _CACHE["cs"], "onehot": _CACHE["onehot"],
        "sguT": np.ascontiguousarray(P["sgu_w"][l, h].T), "tri": _CACHE["tri"], "ident": _CACHE["ident"],
        "sgub": np.ascontiguousarray(P["sgu_b"][l, h].reshape(128, 1)),
        "lng": rep(P["sgu_ln_g"][l, 64 * h:64 * h + 64]), "lnb": rep(P["sgu_ln_b"][l, 64 * h:64 * h + 64]),
        "poolw": np.ascontiguousarray(P["pool_w"][l, h]), "pscale": rep(P["pool_scale"][l, 64 * h:64 * h + 64]),
        "band": _CACHE["band"][h],
        "bfor": np.full((128, 1), P["b_forget"][l, h], np.float32),
    }


def _consts():
    if "cs" in _CACHE:
        return
    _CACHE["cs"] = _rot_tables()
    oh = np.zeros((32, S), np.float32)
    for n in range(32):
        oh[n, n * 256:(n + 1) * 256] = BIGM
    _CACHE["onehot"] = oh.astype(ml_dtypes.bfloat16)
    sidx = np.arange(128)
    _CACHE["tri"] = (sidx[:, None] <= sidx[None, :]).astype(np.float32)
    _CACHE["ident"] = np.eye(128, dtype=np.float32)
    _CACHE["band"] = [_band_mats(w) for w in (2, 4, 8, 16)]


def run_mixer(xT_all, l, P):
    _consts()
    if "nc_m" not in _CACHE:
        _CACHE["nc_m"] = build_mixer()
    in_maps = [_mixer_inputs(xT_all[c // 4], l, c % 4, P) for c in range(8)]
    res = run_bass_kernel_spmd(_CACHE["nc_m"], in_maps, core_ids=list(range(8)))
    y = np.zeros((NB, S, 1024), ml_dtypes.bfloat16)
    for c in range(8):
        b, h = c // 4, c % 4
        yy = np.asarray(res.results[c]["y"]).view(ml_dtypes.bfloat16).reshape(S, 256) if res.results[c]["y"].dtype != ml_dtypes.bfloat16 else res.results[c]["y"]
        for m in range(4):
            y[b, :, 256 * m + 64 * h:256 * m + 64 * h + 64] = yy[:, 64 * m:64 * m + 64]
    return y


NTOK = 2048
NCH = 44


def build_post(NT4=4):
    nc = bass.Bass("TRN2", target_bir_lowering=False)
    C = Ctx(nc)
    dt = nc.dram_tensor
    NTK = NT4 * 512
    yin = dt("yin", [128 + NTK, D], BF16, kind="ExternalInput").ap()
    xin = dt("xin", [128 + NTK, D], F32, kind="ExternalInput").ap()
    flag = dt("flag", [128, 1], F32, kind="ExternalInput").ap()
    wo = dt("wo", [D, D], F32, kind="ExternalInput").ap()
    wup = dt("wup", [NCH, 128, 8, 128], F32, kind="ExternalInput").ap()
    wdn = dt("wdn", [DFF, D], F32, kind="ExternalInput").ap()
    convw = dt("convw", [128, NCH, 3], F32, kind="ExternalInput").ap()
    convb = dt("convb", [128, NCH], F32, kind="ExternalInput").ap()
    lnp = dt("lnp", [4, 128, D], F32, kind="ExternalInput").ap()
    ident = dt("ident", [128, 128], F32, kind="ExternalInput").ap()
    xo = dt("xo", [NTK, D], F32, kind="ExternalOutput").ap()

    sb, ps, res = C.sb, C.ps, C.res
    wd_b = sb("wd_b", [128, 22, D], BF16)
    wo_b = sb("wo_b", [128, 8, D], BF16)
    A_T = sb("A_T", [128, 22, 512], BF16)
    X1T = [sb("X1T%d" % i, [128, 8, 512], BF16) for i in range(2)]
    X1Th = sb("X1Th", [128, 8, 2], BF16)
    X1 = sb("X1", [128, 4, D], F32)
    wu = [[sb("wu%d_%d" % (i, j), [128, 8, 128], BF16) for j in range(2)] for i in range(2)]
    H = [sb("H%d" % i, [128, 514], F32) for i in range(2)]
    tg = [sb("tg%d" % i, [128, 512], F32) for i in range(2)]
    tv = [sb("tv%d" % i, [128, 512], F32) for i in range(2)]
    sg = sb("sg", [128, 512], F32)
    HALO = sb("HALO", [128, NCH, 2], F32)
    lnp_s = sb("lnp_s", [128, 4, D], F32)
    yt = [sb("yt%d" % i, [128, D], BF16) for i in range(2)]
    xt = [sb("xt%d" % i, [128, D], F32) for i in range(2)]
    rr = sb("rr", [128, D], F32)
    x1b = sb("x1b", [128, D], BF16)
    yT = sb("yT", [128, 8, 128], BF16)
    x2 = [sb("x2_%d" % i, [128, D], F32) for i in range(2)]
    stats = sb("stats", [128, 12], F32)
    mv = sb("mv", [128, 2], F32)
    sd = sb("sd", [128, 1], F32)
    rstd = sb("rstd", [128, 1], F32)
    cw_s = sb("cw_s", [128, NCH, 3], F32)
    cb_s = sb("cb_s", [128, NCH], F32)
    flag_s = sb("flag_s", [128, 1], F32)
    ident_b = sb("ident_b", [128, 128], BF16)
    PS = [ps("ps%d" % i, [128, 512], F32) for i in range(7)]
    PSB = ps("psb", [128, 1024], BF16)
    RPS = [res("ps", True) for _ in range(7)]
    RPSB = res("psb", True)
    R = lambda n: res(n)
    Rwd, Rwo, RAT, RX1, RX1Th, RHALO, Rlnp, Rrr, Rx1b, RyT = (R("wd"), R("wo"), R("at"), R("x1"), R("x1th"), R("halo"),
                                                             R("lnp"), R("rr"), R("x1b"), R("yT"))
    RX1T = [R("x1t") for _ in range(2)]
    Rwu = [[R("wu") for _ in range(2)] for _ in range(2)]
    RH = [R("h") for _ in range(2)]
    Rtg = [R("tg") for _ in range(2)]
    Rtv = [R("tv") for _ in range(2)]
    Rsg = R("sg")
    Ryt = [R("yt") for _ in range(2)]
    Rxt = [R("xt") for _ in range(2)]
    Rx2 = [R("x2") for _ in range(2)]
    Rst, Rmv, Rsd, Rrstd, Rcw, Rcb, Rflag, Rid = (R("st"), R("mv"), R("sd"), R("rstd"), R("cw"), R("cb"), R("flag"), R("id"))

    C.dma("pool", ident_b[:], ident, writes=[Rid])
    wo_v = wo.rearrange("(kc f) n -> f kc n", f=128)
    for kc in range(0, 8, 4):
        C.dma("pool", wo_b[:, kc:kc + 4, :], wo_v[:, kc:kc + 4, :], writes=[Rwo], key="wo")
    C.dma("sp", lnp_s[:], lnp.rearrange("k p n -> p k n"), writes=[Rlnp])
    C.dma("sp", cw_s[:], convw, writes=[Rcw])
    C.dma("sp", cb_s[:], convb, writes=[Rcb])
    C.dma("sp", flag_s[:], flag, writes=[Rflag])

    def load_sub(i):
        sl = i % 2
        C.dma("sp", yt[sl][:], yin[i * 128:(i + 1) * 128, :], writes=[Ryt[sl]])
        C.dma("sp", xt[sl][:], xin[i * 128:(i + 1) * 128, :], writes=[Rxt[sl]])

    def layer_norm(src, Rsrc, dst, Rdst, gi):
        def st(e):
            e.bn_stats(out=stats[:, 0:6], in_=src[:, 0:512])
            return e.bn_stats(out=stats[:, 6:12], in_=src[:, 512:1024])
        C.dve(st, reads=[Rsrc], writes=[Rst])
        C.dve(lambda e: e.bn_aggr(out=mv[:], in_=stats[:]), reads=[Rst], writes=[Rmv])
        C.dve(lambda e: e.tensor_scalar(out=sd[:], in0=mv[:, 1:2], scalar1=EPS, scalar2=None, op0=ALU.add),
              reads=[Rmv], writes=[Rsd])
        C.act(lambda e: e.activation(out=sd[:], in_=sd[:], func=AF.Sqrt), reads=[Rsd], writes=[Rsd])
        C.dve(lambda e: e.reciprocal(out=rstd[:], in_=sd[:]), reads=[Rsd], writes=[Rrstd])
        C.dve(lambda e: e.tensor_scalar(out=src[:], in0=src[:], scalar1=mv[:, 0:1], scalar2=rstd[:, 0:1],
                                        op0=ALU.subtract, op1=ALU.mult), reads=[Rsrc, Rmv, Rrstd], writes=[Rsrc])
        C.dve(lambda e: e.tensor_tensor(out=src[:], in0=src[:], in1=lnp_s[:, gi, :], op=ALU.mult),
              reads=[Rsrc, Rlnp], writes=[Rsrc])
        C.dve(lambda e: e.tensor_tensor(out=dst, in0=src[:], in1=lnp_s[:, gi + 1, :], op=ALU.add),
              reads=[Rsrc, Rlnp], writes=[Rdst])

    def stage_a(i):
        sl = i % 2
        if i + 1 <= NT4 * 4:
            load_sub(i + 1)

        def tr_y(e):
            ins = None
            for kc in range(8):
                ins = e.transpose(PSB[:, kc * 128:(kc + 1) * 128], yt[sl][:, kc * 128:(kc + 1) * 128], ident_b[:])
            return ins
        C.pe(tr_y, reads=[Ryt[sl], Rid], writes=[RPSB])
        C.act(lambda e: e.copy(out=yT[:].rearrange("p k n -> p (k n)"), in_=PSB[:]), reads=[RPSB], writes=[RyT])
        for half in range(2):
            C.pe(_mm_group(PS[half][:], [(yT[:, kc, :], wo_b[:, kc, half * 512:(half + 1) * 512]) for kc in range(8)]),
                 reads=[RyT, Rwo], writes=[RPS[half]])
        for half in range(2):
            C.dve(lambda e, half=half: e.scalar_tensor_tensor(out=rr[:, half * 512:(half + 1) * 512], in0=xt[sl][:, half * 512:(half + 1) * 512],
                                                              scalar=ALPHA, in1=PS[half][:], op0=ALU.mult, op1=ALU.add),
                  reads=[Rxt[sl], RPS[half]], writes=[Rrr])
        if i == 0:
            layer_norm(rr, Rrr, rr[:], Rrr, 0)
            x1src, Rx1src = rr[:], Rrr
        else:
            c = (i - 1) % 4
            layer_norm(rr, Rrr, X1[:, c, :], RX1, 0)
            x1src, Rx1src = X1[:, c, :], RX1
        C.act(lambda e: e.copy(out=x1b[:], in_=x1src), reads=[Rx1src], writes=[Rx1b])

        def tr_x(e):
            ins = None
            for kc in range(8):
                ins = e.transpose(PSB[:, kc * 128:(kc + 1) * 128], x1b[:, kc * 128:(kc + 1) * 128], ident_b[:])
            return ins
        C.pe(tr_x, reads=[Rx1b, Rid], writes=[RPSB])
        psv = PSB[:].rearrange("p (k n) -> p k n", n=128)
        if i == 0:
            C.act(lambda e: e.copy(out=X1Th[:], in_=psv[:, :, 126:128]), reads=[RPSB], writes=[RX1Th])
        else:
            T = (i - 1) // 4
            c = (i - 1) % 4
            C.act(lambda e: e.copy(out=X1T[T % 2][:, :, c * 128:(c + 1) * 128], in_=psv), reads=[RPSB], writes=[RX1T[T % 2]])

    wup_loaded = {}

    def load_wup(T, cc):
        sl = (T * 22 + cc) % 2
        C.dma("pool", wu[sl][0][:], wup[cc], writes=[Rwu[sl][0]])
        C.dma("pool", wu[sl][1][:], wup[22 + cc], writes=[Rwu[sl][1]])

    state = {"k": 0}

    def ffn_chunk(T, cc, which):
        ch = cc + 22 * which
        wsl = (T * 22 + cc) % 2
        k = state["k"]
        state["k"] += 1
        hb = k % 2
        bank = 2 + hb
        xs = X1T[T % 2]
        C.pe(_mm_group(PS[bank][:], [(wu[wsl][which][:, kc, :], xs[:, kc, :]) for kc in range(8)]),
             reads=[Rwu[wsl][which], RX1T[T % 2]], writes=[RPS[bank]])
        if T == 0:
            C.pe(_mm_group(PS[4][:, 0:2], [(wu[wsl][which][:, kc, :], X1Th[:, kc, :]) for kc in range(8)]),
                 reads=[Rwu[wsl][which], RX1Th], writes=[RPS[4]])
            C.dve(lambda e: e.tensor_scalar(out=H[hb][:, 0:2], in0=PS[4][:, 0:2], scalar1=flag_s[:, 0:1], scalar2=None, op0=ALU.mult),
                  reads=[RPS[4], Rflag], writes=[RH[hb]])
        else:
            C.dve(lambda e: e.tensor_copy(out=H[hb][:, 0:2], in_=HALO[:, ch, :]), reads=[RHALO], writes=[RH[hb]])
        C.act(lambda e: e.copy(out=H[hb][:, 2:514], in_=PS[bank][:]), reads=[RPS[bank]], writes=[RH[hb]])
        C.dve(lambda e: e.tensor_copy(out=HALO[:, ch, :], in_=H[hb][:, 512:514]), reads=[RH[hb]], writes=[RHALO])
        t, Rt = (tg[cc % 2], Rtg[cc % 2]) if which == 0 else (tv[cc % 2], Rtv[cc % 2])
        C.act(lambda e: e.activation(out=t[:], in_=H[hb][:, 0:512], func=AF.Identity, bias=cb_s[:, ch:ch + 1],
                                     scale=cw_s[:, ch, 0:1]), reads=[RH[hb], Rcw, Rcb], writes=[Rt])
        C.dve(lambda e: e.scalar_tensor_tensor(out=t[:], in0=H[hb][:, 1:513], scalar=cw_s[:, ch, 1:2], in1=t[:],
                                               op0=ALU.mult, op1=ALU.add), reads=[RH[hb], Rcw, Rt], writes=[Rt])
        C.dve(lambda e: e.scalar_tensor_tensor(out=t[:], in0=H[hb][:, 2:514], scalar=cw_s[:, ch, 2:3], in1=t[:],
                                               op0=ALU.mult, op1=ALU.add), reads=[RH[hb], Rcw, Rt], writes=[Rt])

    def ffn_pair(T, cc):
        if T * 22 + cc + 1 < NT4 * 22:
            nT, ncc = divmod(T * 22 + cc + 1, 22)
            load_wup(nT, ncc)
        ffn_chunk(T, cc, 0)
        ffn_chunk(T, cc, 1)
        s2 = cc % 2
        C.act(lambda e: e.activation(out=sg[:], in_=tg[s2][:], func=AF.Silu), reads=[Rtg[s2]], writes=[Rsg])
        C.dve(lambda e: e.tensor_tensor(out=A_T[:, cc, :], in0=sg[:], in1=tv[s2][:], op=ALU.mult),
              reads=[Rsg, Rtv[s2]], writes=[RAT])

    def stage_c(T, c):
        j = T * 4 + c
        banks = (5, 6) if j % 2 == 0 else (0, 1)
        for half in range(2):
            C.pe(_mm_group(PS[banks[half]][:], [(A_T[:, cc, c * 128:(c + 1) * 128], wd_b[:, cc, half * 512:(half + 1) * 512])
                                                 for cc in range(22)]), reads=[RAT, Rwd], writes=[RPS[banks[half]]])
        for half in range(2):
            C.dve(lambda e, half=half: e.scalar_tensor_tensor(out=rr[:, half * 512:(half + 1) * 512], in0=X1[:, c, half * 512:(half + 1) * 512],
                                                              scalar=ALPHA, in1=PS[banks[half]][:], op0=ALU.mult, op1=ALU.add),
                  reads=[RX1, RPS[banks[half]]], writes=[Rrr])
        o = j % 2
        layer_norm(rr, Rrr, x2[o][:], Rx2[o], 2)
        C.dma("sp", xo[j * 128:(j + 1) * 128, :], x2[o][:], reads=[Rx2[o]], key="xo%d" % o)

    load_sub(0)
    load_wup(0, 0)
    wd_v = wdn.rearrange("(cc p) n -> p cc n", p=128)
    stage_a(0)
    for T in range(NT4):
        for c in range(4):
            stage_a(1 + 4 * T + c)
        if T == 0:
            for c0 in range(0, 22, 2):
                C.dma("pool", wd_b[:, c0:c0 + 2, :], wd_v[:, c0:c0 + 2, :], writes=[Rwd], key="wd")
        for cc in range(22):
            ffn_pair(T, cc)
        for c in range(4):
            stage_c(T, c)
    return _finish_mixer(C, nc)


def _post_inputs(y_b, x_b, q, l, P):
    t0 = q * NTOK
    if q == 0:
        yh = np.zeros((128, D), ml_dtypes.bfloat16)
        xh = np.zeros((128, D), np.float32)
    else:
        yh = y_b[t0 - 128:t0]
        xh = x_b[t0 - 128:t0]
    rep = lambda v: np.broadcast_to(v[None, :], (128, v.shape[0]))
    key = ("post_w", l)
    if key not in _CACHE:
        w_up = P["w_up"][l]
        _CACHE[key] = {
            "wo": np.ascontiguousarray(P["w_o"][l]),
            "wup": np.ascontiguousarray(w_up.reshape(8, 128, NCH, 128).transpose(2, 1, 0, 3)),
            "wdn": np.ascontiguousarray(P["w_down"][l]),
            "convw": np.ascontiguousarray(P["conv_w"][l].reshape(3, NCH, 128).transpose(2, 1, 0)),
            "convb": np.ascontiguousarray(P["conv_b"][l].reshape(NCH, 128).T),
            "lnp": np.ascontiguousarray(np.stack([rep(P["ln1_g"][l]), rep(P["ln1_b"][l]), rep(P["ln2_g"][l]), rep(P["ln2_b"][l])]).astype(np.float32)),
            "ident": np.eye(128, dtype=np.float32),
        }
    m = dict(_CACHE[key])
    m["yin"] = np.ascontiguousarray(np.concatenate([yh, y_b[t0:t0 + NTOK]], axis=0))
    m["xin"] = np.ascontiguousarray(np.concatenate([xh, x_b[t0:t0 + NTOK]], axis=0))
    m["flag"] = np.full((128, 1), 0.0 if q == 0 else 1.0, np.float32)
    return m


def run_post(y, x, l, P):
    if "nc_p" not in _CACHE:
        _CACHE["nc_p"] = build_post()
    in_maps = [_post_inputs(y[c // 4], x[c // 4], c % 4, l, P) for c in range(8)]
    res = run_bass_kernel_spmd(_CACHE["nc_p"], in_maps, core_ids=list(range(8)))
    out = np.zeros((NB, S, D), np.float32)
    for c in range(8):
        out[c // 4, (c % 4) * NTOK:(c % 4 + 1) * NTOK] = res.results[c]["xo"]
    return out


def kernel(**inputs):
    P = {k: np.asarray(v) for k, v in inputs.items()}
    x = np.ascontiguousarray(P["x"], dtype=np.float32)
    for l in range(2):
        xT = [np.ascontiguousarray(x[b].T) for b in range(NB)]
        y = run_mixer(xT, l, P)
        x = run_post(y, x, l, P)
    return x
```

```python
import contextlib
import os
DBG = int(os.environ.get('MIX_DBG', '99'))
import numpy as np
import ml_dtypes
import concourse.bass as bass
import concourse.mybir as mybir
from concourse.bass_utils import run_bass_kernel_spmd

F32 = mybir.dt.float32
BF16 = mybir.dt.bfloat16
AF = mybir.ActivationFunctionType
ALU = mybir.AluOpType
AX = mybir.AxisListType

S = 8192
D = 1024
NB = 2
DFF = 2816
ALPHA = 4.0 ** 0.25
EPS = 1e-5
BIGM = 30000.0
NEGF = -1.0e30


class Res:
    __slots__ = ("name", "w", "r", "excl")

    def __init__(self, name, excl=False):
        self.name = name
        self.w = None
        self.r = []
        self.excl = excl


class Sched:
    STREAMS = ("pe", "act", "dve", "pool", "sp")

    def __init__(self, nc):
        self.nc = nc
        self.ops = {s: [] for s in self.STREAMS}
        self.ccount = {s: 0 for s in self.STREAMS}
        self.dcount = {}
        self.known = {s: {} for s in self.STREAMS}
        self.final_events = []

    def _need(self, stream, ev, waits):
        if ev is None:
            return
        sem, val, src = ev
        if src == stream and src == "pe":
            return
        if self.known[stream].get(sem, 0) >= val:
            return
        self.known[stream][sem] = val
        waits.append((sem, val))

    def op(self, stream, fn, reads=(), writes=(), dma_key=None):
        ex = [r for r in reads if r.excl]
        if ex:
            reads = [r for r in reads if not r.excl]
            writes = list(writes) + [r for r in ex if r not in writes]
        waits = []
        for r in reads:
            self._need(stream, r.w, waits)
        for w in writes:
            self._need(stream, w.w, waits)
            for e in w.r:
                self._need(stream, e, waits)
        if dma_key is not None:
            k = "d_" + dma_key
            self.dcount[k] = self.dcount.get(k, 0) + 1
            ev = (k, 16 * self.dcount[k], "dma")
            sig = (k, 16)
        else:
            self.ccount[stream] += 1
            ev = ("c_" + stream, self.ccount[stream], stream)
            sig = ("c_" + stream, 1)
        for r in reads:
            r.r.append(ev)
        for w in writes:
            w.w = ev
            w.r = []
        self.ops[stream].append((waits, fn, sig))
        return ev

    def emit(self, final_events):
        nc = self.nc
        names = set()
        for s in self.STREAMS:
            for waits, fn, sig in self.ops[s]:
                names.add(sig[0])
                for (sem, val) in waits:
                    names.add(sem)
        with contextlib.ExitStack() as st:
            sems = {n: st.enter_context(nc.semaphore(n)) for n in sorted(names)}
            block = st.enter_context(nc.Block())

            def make(stream):
                def body(eng):
                    for waits, fn, sig in self.ops[stream]:
                        for (sem, val) in waits:
                            eng.wait_ge(sems[sem], val)
                        ins = fn(eng)
                        ins.then_inc(sems[sig[0]], sig[1])
                    if stream == "sp":
                        for (sem, val, src) in final_events:
                            eng.wait_ge(sems[sem], val)
                return body

            block.tensor(make("pe"))
            block.scalar(make("act"))
            block.vector(make("dve"))
            block.gpsimd(make("pool"))
            block.sync(make("sp"))


class Ctx:
    def __init__(self, nc):
        self.nc = nc
        self.st = contextlib.ExitStack()
        self.S = Sched(nc)
        self.nres = 0

    def sb(self, name, shape, dt):
        return self.st.enter_context(self.nc.sbuf_tensor(name, shape, dt))

    def ps(self, name, shape, dt):
        return self.st.enter_context(self.nc.psum_tensor(name, shape, dt))

    def res(self, name="r", excl=False):
        self.nres += 1
        return Res("%s%d" % (name, self.nres), excl)

    def pe(self, fn, reads=(), writes=()):
        return self.S.op("pe", fn, reads, writes)

    def act(self, fn, reads=(), writes=()):
        return self.S.op("act", fn, reads, writes)

    def dve(self, fn, reads=(), writes=()):
        return self.S.op("dve", fn, reads, writes)

    def pool(self, fn, reads=(), writes=()):
        return self.S.op("pool", fn, reads, writes)

    def dma(self, queue, out, in_, reads=(), writes=(), key=None):
        if key is None:
            key = (writes[0].name if writes else reads[0].name)
        return self.S.op(queue, lambda e: e.dma_start(out=out, in_=in_), reads, writes, dma_key=key)


def _mm_group(out, pairs):
    def fn(e):
        n = len(pairs)
        ins = None
        for i, (l, r) in enumerate(pairs):
            ins = e.matmul(out, lhsT=l, rhs=r, start=(i == 0), stop=(i == n - 1))
        return ins
    return fn


NFM = 352
NTM = 257


def build_mixer(SEQ=S, PH=9):
    NT, NTT = SEQ // 512, SEQ // 128
    nc = bass.Bass("TRN2", target_bir_lowering=False)
    C = Ctx(nc)
    dt = nc.dram_tensor
    xT = dt("xT", [D, SEQ], F32, kind="ExternalInput").ap()
    wfm = dt("wfm", [D, NFM], F32, kind="ExternalInput").ap()
    wtm = dt("wtm", [D, NTM], F32, kind="ExternalInput").ap()
    cs = dt("cs", [2, 16, SEQ], F32, kind="ExternalInput").ap()
    onehot = dt("onehot", [32, SEQ], BF16, kind="ExternalInput").ap()
    sguT = dt("sguT", [128, 128], F32, kind="ExternalInput").ap()
    tri = dt("tri", [128, 128], F32, kind="ExternalInput").ap()
    ident = dt("ident", [128, 128], F32, kind="ExternalInput").ap()
    sgub = dt("sgub", [128, 1], F32, kind="ExternalInput").ap()
    lng = dt("lng", [128, 64], F32, kind="ExternalInput").ap()
    lnb = dt("lnb", [128, 64], F32, kind="ExternalInput").ap()
    poolw = dt("poolw", [64, 64], F32, kind="ExternalInput").ap()
    pscale = dt("pscale", [128, 64], F32, kind="ExternalInput").ap()
    band = dt("band", [3, 128, 128], F32, kind="ExternalInput").ap()
    bfor = dt("bfor", [128, 1], F32, kind="ExternalInput").ap()
    y = dt("y", [SEQ, 256], BF16, kind="ExternalOutput").ap()

    sb, ps, res = C.sb, C.ps, C.res
    QA = sb("QA", [128, SEQ], BF16)
    KA = sb("KA", [128, SEQ], BF16)
    QB = sb("QB", [128, SEQ], BF16)
    KB = sb("KB", [128, SEQ], BF16)
    VA = sb("VA", [128, NTT, 65], BF16)
    VB = sb("VB", [128, NTT, 65], BF16)
    xb = [sb("xb%d" % i, [128, 8, 512], BF16) for i in range(2)]
    wfm_b = sb("wfm_b", [128, 8, NFM], BF16)
    wtm_b = sb("wtm_b", [128, 8, NTM], BF16)
    cst = [sb("cst%d" % i, [16, 2, 512], F32) for i in range(2)]
    t1 = [sb("t1_%d" % i, [16, 512], F32) for i in range(2)]
    t2 = [sb("t2_%d" % i, [16, 512], F32) for i in range(2)]
    pTb = [sb("pTb%d" % i, [64, 512], BF16) for i in range(2)]
    ug = [sb("ug%d" % i, [128, 128], F32) for i in range(2)]
    stats = [sb("stats%d" % i, [128, 6], F32) for i in range(2)]
    mv = [sb("mv%d" % i, [128, 2], F32) for i in range(2)]
    rstd = [sb("rstd%d" % i, [128, 1], F32) for i in range(2)]
    vn = [sb("vn%d" % i, [128, 64], F32) for i in range(2)]
    vnb = [sb("vnb%d" % i, [128, 64], BF16) for i in range(2)]
    pwb = [sb("pwb%d" % i, [128, 64], BF16) for i in range(2)]
    YC = [sb("YC%d" % i, [128, 4, 64], BF16) for i in range(2)]
    YD = [sb("YD%d" % i, [128, 4, 64], BF16) for i in range(2)]
    UG = sb("UG", [128, NTT, 128], BF16)
    MV = sb("MV", [128, NTT, 2], F32)
    VE = sb("VE", [128, NTT], F32)
    RSTD = sb("RSTD", [128, NTT], F32)
    YAB = [sb("YAB%d" % i, [128, 4, 128], BF16) for i in range(2)]
    MBZ = sb("MBZ", [128, 64 + NTT * 32], F32)
    GM = sb("GM", [128, 32], F32)
    top8 = [sb("top8_%d" % i, [128, 8], F32) for i in range(2)]
    FRAW = sb("FRAW", [128, NTT], F32)
    LF = sb("LF", [128, NTT], F32)
    TOT = sb("TOT", [128, NTT], F32)
    PREF = sb("PREF", [128, NTT], F32)
    CP = sb("CP", [128, NTT], F32)
    Z = sb("Z", [128, 128], F32)
    kbar = sb("kbar", [64, 32], F32)
    kbar_b = sb("kbar_b", [64, 32], BF16)
    PT = [sb("PT%d" % i, [128, 512], BF16) for i in range(4)]
    rl = [sb("rl%d" % i, [128, 1], F32) for i in range(4)]
    ident_f = sb("ident_f", [128, 128], F32)
    tri_f = sb("tri_f", [128, 128], F32)
    tri_b = sb("tri_b", [128, 128], BF16)
    ones_f = sb("ones_f", [128, 128], F32)
    sgu_f = sb("sgu_f", [128, 128], F32)
    wmT_b = sb("wmT_b", [128, 128], BF16)
    band_b = sb("band_b", [128, 3, 128], BF16)
    sgub_s = sb("sgub_s", [128, 1], F32)
    lng_s = sb("lng_s", [128, 64], F32)
    lnb_s = sb("lnb_s", [128, 64], F32)
    poolw_b = sb("poolw_b", [64, 64], BF16)
    pscale_s = sb("pscale_s", [128, 64], F32)
    bfor_s = sb("bfor_s", [128, 1], F32)
    negb = sb("negb", [128, 1], F32)
    ef = sb("ef", [128, NTT], F32)
    PS = [ps("ps%d" % i, [128, 512], F32) for i in range(8)]
    RPS = [res("ps", True) for _ in range(8)]

    R = lambda n: res(n)
    RQA = [R("qa") for _ in range(NT)]
    RQAm = [R("qam") for _ in range(NT)]
    RKA = [R("ka") for _ in range(NT)]
    RQB = [R("qb") for _ in range(NT)]
    RQBc = [R("qbc") for _ in range(NT)]
    RKB = [R("kb") for _ in range(NT)]
    RVA = [R("va") for _ in range(NT)]
    RVB = [R("vb") for _ in range(NT)]
    Rxb = [R("xb") for _ in range(2)]
    Rcs = [R("cs") for _ in range(2)]
    Rt1 = [R("t1") for _ in range(2)]
    Rt2 = [R("t2") for _ in range(2)]
    RpT = [R("pT") for _ in range(2)]
    Rug = [R("ug") for _ in range(2)]
    Rst = [R("st") for _ in range(2)]
    Rmv = [R("mv") for _ in range(2)]
    Rrs = [R("rs") for _ in range(2)]
    Rvn = [R("vn") for _ in range(2)]
    Rvnb = [R("vnb") for _ in range(2)]
    Rpwb = [R("pwb") for _ in range(2)]
    RYC = [R("yc") for _ in range(2)]
    RYD = [R("yd") for _ in range(2)]
    RUG = [R("ugall") for _ in range(NT)]
    RMV, RVE, RRSTD = R("mvall"), R("ve"), R("rstdall")
    RYAB = [R("yab") for _ in range(2)]
    RMBZ = [R("mbz") for _ in range(NT)]
    RGM = R("gm")
    Rtop = [R("top") for _ in range(2)]
    RFRAW, RLF, RTOT, RPREF, RCP, RZ, Rkbar, Rkbarb, Ref = (R("fraw"), R("lf"), R("tot"), R("pref"),
                                                            R("cp"), R("z"), R("kbar"), R("kbarb"), R("ef"))
    RPT = [R("pt") for _ in range(4)]
    Rrl = [R("rl") for _ in range(4)]
    Rc = {k: R(k) for k in ["wfm", "wtm", "ident", "tri_f", "tri_b", "ones", "sgu_f", "wmT", "band", "sgub",
                            "lng", "lnb", "poolw", "pscale", "bfor", "negb", "kaoh", "misc"]}

    C.dma("pool", wfm_b[:], wfm.rearrange("(kc f) n -> f kc n", f=128), writes=[Rc["wfm"]])
    C.dma("pool", wtm_b[:], wtm.rearrange("(kc f) n -> f kc n", f=128), writes=[Rc["wtm"]])
    C.dma("sp", ident_f[:], ident, writes=[Rc["ident"]])
    C.dma("sp", tri_f[:], tri, writes=[Rc["tri_f"]])
    C.dma("pool", tri_b[:], tri, writes=[Rc["tri_b"]])
    C.dma("sp", sgu_f[:], sguT, writes=[Rc["sgu_f"]])
    C.dma("pool", band_b[:], band.rearrange("k s t -> s k t"), writes=[Rc["band"]])
    C.dma("sp", sgub_s[:], sgub, writes=[Rc["sgub"]])
    C.dma("sp", lng_s[:], lng, writes=[Rc["lng"]])
    C.dma("sp", lnb_s[:], lnb, writes=[Rc["lnb"]])
    C.dma("pool", poolw_b[:], poolw, writes=[Rc["poolw"]])
    C.dma("sp", pscale_s[:], pscale, writes=[Rc["pscale"]])
    C.dma("sp", bfor_s[:], bfor, writes=[Rc["bfor"]])
    C.dma("sp", KA[64:96, :], onehot, writes=[Rc["kaoh"]])
    C.dve(lambda e: e.tensor_tensor(out=wmT_b[:], in0=sgu_f[:], in1=tri_f[:], op=ALU.mult),
          reads=[Rc["sgu_f"], Rc["tri_f"]], writes=[Rc["wmT"]])
    C.dve(lambda e: e.tensor_scalar(out=negb[:], in0=bfor_s[:], scalar1=-1.0, scalar2=None, op0=ALU.mult),
          reads=[Rc["bfor"]], writes=[Rc["negb"]])
    C.dve(lambda e: e.memset(ones_f[:], 1.0), writes=[Rc["ones"]])
    C.dve(lambda e: e.memset(VA[:, :, 64:65], 1.0), writes=RVA)
    C.dve(lambda e: e.memset(VB[:, :, 64:65], 1.0), writes=RVB)
    C.dve(lambda e: e.memset(KB[64:65, :], 1.0), writes=RKB)
    C.dve(lambda e: e.memset(MBZ[:], 0.0), writes=RMBZ)
    C.dve(lambda e: e.memset(GM[:], NEGF), writes=[RGM])
    C.dve(lambda e: e.memset(Z[:], 0.0), writes=[RZ])
    C.dve(lambda e: e.memset(kbar[:], 0.0), writes=[Rkbar])
    C.dve(lambda e: e.memset(PREF[:, 0:1], 0.0), writes=[RPREF])

    if PH == 0:
        return _finish_mixer(C, nc)
    xT_v = xT.rearrange("(kc f) t -> f kc t", f=128)

    def load_tile(T):
        sl = T % 2
        C.dma("pool", xb[sl][:], xT_v[:, :, T * 512:(T + 1) * 512], writes=[Rxb[sl]])
        C.dma("sp", cst[sl][:], cs[:, :, T * 512:(T + 1) * 512].rearrange("k p t -> p k t"), writes=[Rcs[sl]])

    load_tile(0)
    fm_cols = {"qA": (0, 64), "kA": (64, 64), "qB": (128, 64), "kB": (192, 64), "pD": (256, 64),
               "qAp": (320, 16), "kAp": (336, 16)}
    def p1_tile(T):
        sl = T % 2
        if T + 1 < NT:
            load_tile(T + 1)
        cols = slice(T * 512, (T + 1) * 512)

        def fm(name, bank):
            c0, n = fm_cols[name]
            C.pe(_mm_group(PS[bank][0:n, :], [(wfm_b[:, kc, c0:c0 + n], xb[sl][:, kc, :]) for kc in range(8)]),
                 reads=[Rc["wfm"], Rxb[sl]], writes=[RPS[bank]])

        def rot(nm, nmp, dst, Rdst):
            fm(nm, 0)
            fm(nmp, 1)
            if DBG == 10:
                return
            C.act(lambda e, dst=dst: e.copy(out=dst[0:64, cols], in_=PS[0][0:64, :]), reads=[RPS[0]], writes=[Rdst[T]])
            if DBG == 11:
                return
            C.dve(lambda e: e.tensor_tensor(out=t1[sl][:], in0=PS[0][0:16, :], in1=cst[sl][:, 0, :], op=ALU.mult),
                  reads=[RPS[0], Rcs[sl]], writes=[Rt1[sl]])
            C.dve(lambda e: e.tensor_tensor(out=t2[sl][:], in0=PS[1][0:16, :], in1=cst[sl][:, 1, :], op=ALU.mult),
                  reads=[RPS[1], Rcs[sl]], writes=[Rt2[sl]])
            if DBG == 12:
                return
            C.dve(lambda e, dst=dst: e.tensor_tensor(out=dst[0:16, cols], in0=t1[sl][:], in1=t2[sl][:], op=ALU.add),
                  reads=[Rt1[sl], Rt2[sl]], writes=[Rdst[T]])
        if DBG < 1:
            return
        rot("qA", "qAp", QA, RQA)
        rot("kA", "kAp", KA, RKA)
        if DBG < 2 or (10 <= DBG < 20):
            return
        C.dve(lambda e: e.tensor_reduce(out=kbar[:, 2 * T:2 * T + 2],
                                        in_=KA[0:64, cols].rearrange("p (b j) -> p b j", j=256),
                                        axis=AX.X, op=ALU.add), reads=[RKA[T]], writes=[Rkbar])
        if DBG < 3:
            return
        fm("qB", 0)
        C.act(lambda e: e.copy(out=QB[0:64, cols], in_=PS[0][0:64, :]), reads=[RPS[0]], writes=[RQB[T]])
        fm("kB", 1)
        C.act(lambda e: e.copy(out=KB[0:64, cols], in_=PS[1][0:64, :]), reads=[RPS[1]], writes=[RKB[T]])
        fm("pD", 0)
        C.act(lambda e: e.copy(out=pTb[sl][:], in_=PS[0][0:64, :]), reads=[RPS[0]], writes=[RpT[sl]])
        if DBG < 4:
            return
        def sub(c):
            tt = 4 * T + c
            s2 = tt % 2
            bk = 2 + s2
            C.pe(_mm_group(PS[bk][:, 0:NTM], [(xb[sl][:, kc, c * 128:(c + 1) * 128], wtm_b[:, kc, :]) for kc in range(8)]),
                 reads=[Rc["wtm"], Rxb[sl]], writes=[RPS[bk]])
            C.act(lambda e, bk=bk, tt=tt: e.copy(out=VA[:, tt, 0:64], in_=PS[bk][:, 0:64]), reads=[RPS[bk]], writes=[RVA[T]])
            C.act(lambda e, bk=bk, tt=tt: e.copy(out=VB[:, tt, 0:64], in_=PS[bk][:, 64:128]), reads=[RPS[bk]], writes=[RVB[T]])
            C.dve(lambda e, bk=bk, tt=tt: e.tensor_copy(out=FRAW[:, tt:tt + 1], in_=PS[bk][:, 256:257]),
                  reads=[RPS[bk]], writes=[RFRAW])
            C.act(lambda e, bk=bk, tt=tt: e.activation(out=UG[:, tt, :], in_=PS[bk][:, 128:256], func=AF.Gelu),
                  reads=[RPS[bk]], writes=[RUG[T]])
            C.dve(lambda e, tt=tt, s2=s2: e.bn_stats(out=stats[s2][:], in_=UG[:, tt, 64:128]), reads=[RUG[T]], writes=[Rst[s2]])
            C.dve(lambda e, tt=tt, s2=s2: e.bn_aggr(out=MV[:, tt, :], in_=stats[s2][:]), reads=[Rst[s2]], writes=[RMV])
            C.pe(lambda e, c=c: e.matmul(PS[5][:, 0:64], lhsT=pTb[sl][:, c * 128:(c + 1) * 128], rhs=poolw_b[:],
                                         start=True, stop=True), reads=[RpT[sl], Rc["poolw"]], writes=[RPS[5]])
            C.act(lambda e, s2=s2: e.copy(out=pwb[s2][:], in_=PS[5][:, 0:64]), reads=[RPS[5]], writes=[Rpwb[s2]])
            if tt == 0:
                C.pe(lambda e, s2=s2: e.matmul(PS[6][:, 0:64], lhsT=band_b[:, 2, :], rhs=pwb[s2][:], start=True, stop=True),
                     reads=[Rc["band"], Rpwb[s2]], writes=[RPS[6]])
            else:
                C.pe(_mm_group(PS[6][:, 0:64], [(band_b[:, 0, :], pwb[s2][:]), (band_b[:, 1, :], pwb[1 - s2][:])]),
                     reads=[Rc["band"], Rpwb[0], Rpwb[1]], writes=[RPS[6]])
            C.dve(lambda e, c=c: e.tensor_tensor(out=YD[sl][:, c, :], in0=PS[6][:, 0:64], in1=pscale_s[:], op=ALU.mult),
                  reads=[RPS[6], Rc["pscale"]], writes=[RYD[sl]])
        for c in range(4):
            sub(c)
        C.dma("sp", y[T * 512:(T + 1) * 512, 192:256].rearrange("(c p) n -> p c n", p=128), YD[sl][:],
              reads=[RYD[sl]], key="yd%d" % sl)

    for T in range(NT):
        p1_tile(T)

    if PH == 1:
        return _finish_mixer(C, nc)
    C.dve(lambda e: e.tensor_scalar(out=VE[:], in0=MV[:, :, 1], scalar1=EPS, scalar2=None, op0=ALU.add),
          reads=[RMV], writes=[RVE])
    C.act(lambda e: e.activation(out=VE[:], in_=VE[:], func=AF.Sqrt), reads=[RVE], writes=[RVE])
    C.dve(lambda e: e.reciprocal(out=RSTD[:], in_=VE[:]), reads=[RVE], writes=[RRSTD])

    def c_sub(T, c):
        tt = 4 * T + c
        s2 = tt % 2
        sl = T % 2
        C.dve(lambda e: e.tensor_scalar(out=vn[s2][:], in0=UG[:, tt, 64:128], scalar1=MV[:, tt, 0:1],
                                        scalar2=RSTD[:, tt:tt + 1], op0=ALU.subtract, op1=ALU.mult),
              reads=[RUG[T], RMV, RRSTD], writes=[Rvn[s2]])
        C.dve(lambda e: e.tensor_tensor(out=vn[s2][:], in0=vn[s2][:], in1=lng_s[:], op=ALU.mult),
              reads=[Rvn[s2], Rc["lng"]], writes=[Rvn[s2]])
        C.dve(lambda e: e.tensor_tensor(out=vnb[s2][:], in0=vn[s2][:], in1=lnb_s[:], op=ALU.add),
              reads=[Rvn[s2], Rc["lnb"]], writes=[Rvnb[s2]])
        C.pe(lambda e: e.matmul(PS[4 + s2][:, 0:64], lhsT=wmT_b[:], rhs=vnb[s2][:], start=True, stop=True),
             reads=[Rc["wmT"], Rvnb[s2]], writes=[RPS[4 + s2]])
        C.dve(lambda e: e.scalar_tensor_tensor(out=YC[sl][:, c, :], in0=PS[4 + s2][:, 0:64],
                                               scalar=sgub_s[:, 0:1], in1=UG[:, tt, 0:64],
                                               op0=ALU.add, op1=ALU.mult),
              reads=[RPS[4 + s2], Rc["sgub"], RUG[T]], writes=[RYC[sl]])

    for T in range(NT):
        for c in range(4):
            c_sub(T, c)
        C.dma("sp", y[T * 512:(T + 1) * 512, 128:192].rearrange("(c p) n -> p c n", p=128), YC[T % 2][:],
              reads=[RYC[T % 2]], key="yc%d" % (T % 2))

    if PH == 2:
        return _finish_mixer(C, nc)
    C.act(lambda e: e.activation(out=ef[:], in_=FRAW[:], func=AF.Exp, bias=negb[:, 0:1], scale=-1.0),
          reads=[RFRAW, Rc["negb"]], writes=[Ref])
    C.act(lambda e: e.activation(out=LF[:], in_=ef[:], func=AF.Ln, bias=1.0, scale=1.0), reads=[Ref], writes=[RLF])
    C.pe(lambda e: e.matmul(PS[0][:, 0:NTT], lhsT=ones_f[:], rhs=LF[:], start=True, stop=True),
         reads=[Rc["ones"], RLF], writes=[RPS[0]])
    C.dve(lambda e: e.tensor_copy(out=TOT[:], in_=PS[0][:, 0:NTT]), reads=[RPS[0]], writes=[RTOT])
    for j in range(1, NTT):
        C.dve(lambda e, j=j: e.tensor_tensor(out=PREF[:, j:j + 1], in0=PREF[:, j - 1:j], in1=TOT[:, j - 1:j], op=ALU.add),
              reads=[RPREF, RTOT], writes=[RPREF])
    C.pe(lambda e: e.matmul(PS[1][:, 0:NTT], lhsT=tri_f[:], rhs=LF[:], start=True, stop=True),
         reads=[Rc["tri_f"], RLF], writes=[RPS[1]])
    C.dve(lambda e: e.tensor_tensor(out=CP[:], in0=PS[1][:, 0:NTT], in1=PREF[:], op=ALU.add),
          reads=[RPS[1], RPREF], writes=[RCP])
    C.dve(lambda e: e.tensor_scalar(out=Z[:, 64:64 + NTT], in0=CP[:], scalar1=-8.0, scalar2=None, op0=ALU.mult),
          reads=[RCP], writes=[RZ])
    C.dve(lambda e: e.tensor_scalar(out=kbar_b[:], in0=kbar[:], scalar1=1.0 / 256.0, scalar2=None, op0=ALU.mult),
          reads=[Rkbar], writes=[Rkbarb])
    def p2_tile(T):
        bk = 2 + (T % 2)
        def tr4(e, T=T, bk=bk):
            ins = None
            for c in range(4):
                tt = 4 * T + c
                ins = e.matmul(PS[bk][0:65, c * 128:(c + 1) * 128], lhsT=Z[:, tt:tt + 65], rhs=ident_f[:],
                               start=True, stop=True)
            return ins
        C.pe(tr4, reads=[RZ, Rc["ident"]], writes=[RPS[bk]])
        C.act(lambda e, T=T, bk=bk: e.copy(out=QB[64:65, T * 512:(T + 1) * 512], in_=PS[bk][64:65, :]),
              reads=[RPS[bk]], writes=[RQBc[T]])
        def gate(c):
            tt = 4 * T + c
            b = tt // 2
            if b == 0:
                return
            s2 = tt % 2
            C.pe(lambda e, tt=tt: e.matmul(PS[4 + (tt % 2)][:, 0:32], lhsT=QA[0:64, tt * 128:(tt + 1) * 128], rhs=kbar_b[:],
                                           start=True, stop=True), reads=[RQA[T], Rkbarb], writes=[RPS[4 + s2]])
            C.dve(lambda e, tt=tt, b=b: e.tensor_copy(out=GM[:, 0:b], in_=PS[4 + (tt % 2)][:, 0:b]),
                  reads=[RPS[4 + s2]], writes=[RGM])
            C.dve(lambda e, s2=s2: e.max(out=top8[s2][:], in_=GM[:]), reads=[RGM], writes=[Rtop[s2]])
            C.dve(lambda e, tt=tt, b=b, s2=s2: e.tensor_scalar(out=MBZ[:, 64 + 32 * tt:64 + 32 * tt + b], in0=GM[:, 0:b],
                                                               scalar1=top8[s2][:, 2:3], scalar2=1.0,
                                                               op0=ALU.is_ge, op1=ALU.subtract),
                  reads=[RGM, Rtop[s2]], writes=[RMBZ[T]])
        for c in range(4):
            gate(c)
        bk2 = 6 + (T % 2)

        def trm(e, T=T, bk2=bk2):
            ins = None
            for c in range(4):
                tt = 4 * T + c
                ins = e.matmul(PS[bk2][0:96, c * 128:(c + 1) * 128], lhsT=MBZ[:, 32 * tt:32 * tt + 96], rhs=ident_f[:],
                               start=True, stop=True)
            return ins
        C.pe(trm, reads=[RMBZ[T], Rc["ident"]], writes=[RPS[bk2]])
        C.act(lambda e, T=T, bk2=bk2: e.copy(out=QA[64:96, T * 512:(T + 1) * 512], in_=PS[bk2][64:96, :]),
              reads=[RPS[bk2]], writes=[RQAm[T]])

    for T in range(NT):
        p2_tile(T)

    if PH == 3:
        return _finish_mixer(C, nc)
    ROacc = [[RPS[3]] * 4, [RPS[4]] * 4]
    LOOK = 2

    def att_score(p):
        qi, att, kj, oi, idx = p
        Q, K = (QA, KA) if att == 0 else (QB, KB)
        RQ, RQx, RK = (RQA, RQAm, RKA) if att == 0 else (RQB, RQBc, RKB)
        rows = 96 if att == 0 else 65
        d = kj - 4 * qi
        off = 128 * max(d, 0)
        n = 512 - off
        sbk = idx % 3
        pt = idx % 4
        C.pe(lambda e: e.matmul(PS[sbk][:, 0:n], lhsT=K[0:rows, kj * 128:(kj + 1) * 128],
                                rhs=Q[0:rows, qi * 512 + off:(qi + 1) * 512], start=True, stop=True),
             reads=[RQ[qi], RQx[qi], RK[kj // 4]] + ([Rc["kaoh"]] if att == 0 else []), writes=[RPS[sbk]])
        if att == 0:
            C.act(lambda e: e.activation(out=PT[pt][:, 0:n], in_=PS[sbk][:, 0:n], func=AF.Exp, scale=0.125),
                  reads=[RPS[sbk]], writes=[RPT[pt]])
        else:
            C.act(lambda e: e.activation(out=PT[pt][:, 0:n], in_=PS[sbk][:, 0:n], func=AF.Exp,
                                         bias=CP[:, kj:kj + 1], scale=0.125),
                  reads=[RPS[sbk], RCP], writes=[RPT[pt]])
        if d >= 0:
            C.dve(lambda e: e.tensor_tensor(out=PT[pt][:, 0:128], in0=PT[pt][:, 0:128], in1=tri_b[:], op=ALU.mult),
                  reads=[RPT[pt], Rc["tri_b"]], writes=[RPT[pt]])

    def att_pv(p):
        qi, att, kj, oi, idx = p
        V = VA if att == 0 else VB
        RV = RVA if att == 0 else RVB
        ysl = qi % 2
        ob = 3 + (oi % 2)
        Oacc = PS[ob][:, 0:260].rearrange("p (c n) -> p c n", n=65)
        RO = ROacc[ob - 3]
        d = kj - 4 * qi
        off = 128 * max(d, 0)
        pt = idx % 4
        c0 = max(d, 0)

        def pv(e):
            ins = None
            for c in range(c0, 4):
                ins = e.matmul(Oacc[:, c, :], lhsT=PT[pt][:, c * 128 - off:(c + 1) * 128 - off], rhs=V[:, kj, :],
                               start=(kj == 0 and c == 0), stop=(kj == 4 * qi + c), skip_group_check=True)
            return ins
        C.pe(pv, reads=[RPT[pt], RV[kj // 4]], writes=[RO[0]])
        if d >= 0:
            c = d
            r4 = (2 * (oi % 2) + (c % 2))
            C.dve(lambda e: e.reciprocal(out=rl[r4][:], in_=Oacc[:, c, 64:65]), reads=[RO[c]], writes=[Rrl[r4]])
            C.dve(lambda e: e.tensor_scalar(out=YAB[ysl][:, c, att * 64:(att + 1) * 64], in0=Oacc[:, c, 0:64],
                                            scalar1=rl[r4][:, 0:1], scalar2=None, op0=ALU.mult),
                  reads=[RO[c], Rrl[r4]], writes=[RYAB[ysl]])
        if att == 1 and d == 3:
            C.dma("sp", y[qi * 512:(qi + 1) * 512, 0:128].rearrange("(c p) n -> p c n", p=128), YAB[ysl][:],
                  reads=[RYAB[ysl]], key="yab%d" % ysl)

    plist = []
    oi = 0
    for qi in range(NT):
        for att in range(2):
            for kj in range(4 * qi + 4):
                plist.append((qi, att, kj, oi, len(plist)))
            oi += 1
    for i in range(len(plist) + LOOK):
        if i < len(plist):
            att_score(plist[i])
        if i - LOOK >= 0:
            att_pv(plist[i - LOOK])

    if os.environ.get("MIX_DUMP"):
        allres = RQA + RQAm + RKA + RQB + RQBc + RKB + RVA + RVB + RUG + [RCP, RLF, RFRAW, RMV, RRSTD, Rkbar, Rkbarb, RZ, Rc["kaoh"]] + RMBZ + Rpwb + [Rc["band"], Rc["tri_b"], Rc["poolw"]]
        for nm, t, shp, dty in (("d_cp", CP, [128, NTT], F32), ("d_lf", LF, [128, NTT], F32), ("d_fraw", FRAW, [128, NTT], F32),
                                ("d_rstd", RSTD, [128, NTT], F32),
                                ("d_ug", UG, [128, NTT, 128], BF16), ("d_qa", QA, [128, SEQ], BF16), ("d_ka", KA, [128, SEQ], BF16),
                                ("d_qb", QB, [128, SEQ], BF16), ("d_kb", KB, [128, SEQ], BF16), ("d_va", VA, [128, NTT, 65], BF16),
                                ("d_vb", VB, [128, NTT, 65], BF16), ("d_kbar", kbar, [64, 32], F32), ("d_mbz", MBZ, [128, 64 + NTT * 32], F32),
                                ("d_pwb0", pwb[0], [128, 64], BF16), ("d_band", band_b, [128, 3, 128], BF16), ("d_trib", tri_b, [128, 128], BF16)):
            dd = dt(nm, shp, dty, kind="ExternalOutput").ap()
            C.dma("sp", dd, t[:], reads=allres, key=nm)

    return _finish_mixer(C, nc)


def _finish_mixer(C, nc):
    finals = []
    for k, cnt in C.S.dcount.items():
        finals.append((k, 16 * cnt, "dma"))
    C.S.emit(finals)
    C.st.close()
    return nc


def _rot_tables():
    pos = np.arange(S, dtype=np.float32)
    inv_freq = (np.float32(500000.0) ** (-np.arange(0, 16, 2, dtype=np.float32) / np.float32(16))).astype(np.float32)
    ang = (pos[:, None] * inv_freq[None, :]).astype(np.float32)
    cos = np.cos(ang).astype(np.float32).T
    sin = np.sin(ang).astype(np.float32).T
    cs = np.zeros((2, 16, S), np.float32)
    cs[0, 0:8] = cos
    cs[0, 8:16] = cos
    cs[1, 0:8] = -sin
    cs[1, 8:16] = sin
    return cs


def _band_mats(win):
    t = np.arange(128)
    s = np.arange(128)
    cur = ((t[None, :] - s[:, None] >= 0) & (t[None, :] - s[:, None] < win)).astype(np.float32) / win
    cur -= np.eye(128, dtype=np.float32)
    prev = ((t[None, :] + 128 - s[:, None]) < win).astype(np.float32) / win
    cnt = np.minimum(t + 1, win).astype(np.float32)
    first = ((t[None, :] - s[:, None] >= 0) & (t[None, :] - s[:, None] < win)).astype(np.float32) / cnt[None, :]
    first -= np.eye(128, dtype=np.float32)
    return np.stack([cur, prev, first]).astype(np.float32)


_CACHE = {}


def _mixer_inputs(xT_b, l, h, P):
    w_in = P["w_in"][l]
    a0 = 0
    qa = w_in[:, 0 + 64 * h:0 + 64 * h + 64]
    ka = w_in[:, 256 + 64 * h:256 + 64 * h + 64]
    va = w_in[:, 512 + 64 * h:512 + 64 * h + 64]
    qb = w_in[:, 768 + 64 * h:768 + 64 * h + 64]
    kb = w_in[:, 1024 + 64 * h:1024 + 64 * h + 64]
    vb = w_in[:, 1280 + 64 * h:1280 + 64 * h + 64]
    fb = w_in[:, 1536 + h:1536 + h + 1]
    cu = w_in[:, 1540 + 64 * h:1540 + 64 * h + 64]
    cv = w_in[:, 1796 + 64 * h:1796 + 64 * h + 64]
    dp = w_in[:, 2052 + 64 * h:2052 + 64 * h + 64]
    perm = np.concatenate([np.arange(8, 16), np.arange(0, 8)])
    wfm = np.concatenate([qa, ka, qb, kb, dp, qa[:, perm], ka[:, perm]], axis=1)
    wtm = np.concatenate([va, vb, cu, cv, fb], axis=1)
    rep = lambda v: np.ascontiguousarray(np.broadcast_to(v[None, :], (128, v.shape[0]))).astype(np.float32)
    return {
        "xT": xT_b,
        "wfm": np.ascontiguousarray(wfm), "wtm": np.ascontiguousarray(wtm),
        "cs": _CACHE["cs"], "onehot": _CACHE["onehot"],
        "sguT": np.ascontiguousarray(P["sgu_w"][l, h].T), "tri": _CACHE["tri"], "ident": _CACHE["ident"],
        "sgub": np.ascontiguousarray(P["sgu_b"][l, h].reshape(128, 1)),
        "lng": rep(P["sgu_ln_g"][l, 64 * h:64 * h + 64]), "lnb": rep(P["sgu_ln_b"][l, 64 * h:64 * h + 64]),
        "poolw": np.ascontiguousarray(P["pool_w"][l, h]), "pscale": rep(P["pool_scale"][l, 64 * h:64 * h + 64]),
        "band": _CACHE["band"][h],
        "bfor": np.full((128, 1), P["b_forget"][l, h], np.float32),
    }


def _consts():
    if "cs" in _CACHE:
        return
    _CACHE["cs"] = _rot_tables()
    oh = np.zeros((32, S), np.float32)
    for n in range(32):
        oh[n, n * 256:(n + 1) * 256] = BIGM
    _CACHE["onehot"] = oh.astype(ml_dtypes.bfloat16)
    sidx = np.arange(128)
    _CACHE["tri"] = (sidx[:, None] <= sidx[None, :]).astype(np.float32)
    _CACHE["ident"] = np.eye(128, dtype=np.float32)
    _CACHE["band"] = [_band_mats(w) for w in (2, 4, 8, 16)]


def run_mixer(xT_all, l, P):
    _consts()
    if "nc_m" not in _CACHE:
        _CACHE["nc_m"] = build_mixer()
    in_maps = [_mixer_inputs(xT_all[c // 4], l, c % 4, P) for c in range(8)]
    res = run_bass_kernel_spmd(_CACHE["nc_m"], in_maps, core_ids=list(range(8)))
    y = np.zeros((NB, S, 1024), ml_dtypes.bfloat16)
    for c in range(8):
        b, h = c // 4, c % 4
        yy = np.asarray(res.results[c]["y"]).view(ml_dtypes.bfloat16).reshape(S, 256) if res.results[c]["y"].dtype != ml_dtypes.bfloat16 else res.results[c]["y"]
        for m in range(4):
            y[b, :, 256 * m + 64 * h:256 * m + 64 * h + 64] = yy[:, 64 * m:64 * m + 64]
    return y


NTOK = 2048
NCH = 44


def build_post(NT4=4):
    nc = bass.Bass("TRN2", target_bir_lowering=False)
    C = Ctx(nc)
    dt = nc.dram_tensor
    NTK = NT4 * 512
    yin = dt("yin", [128 + NTK, D], BF16, kind="ExternalInput").ap()
    xin = dt("xin", [128 + NTK, D], F32, kind="ExternalInput").ap()
    flag = dt("flag", [128, 1], F32, kind="ExternalInput").ap()
    wo = dt("wo", [D, D], F32, kind="ExternalInput").ap()
    wup = dt("wup", [NCH, 128, 8, 128], F32, kind="ExternalInput").ap()
    wdn = dt("wdn", [DFF, D], F32, kind="ExternalInput").ap()
    convw = dt("convw", [128, NCH, 3], F32, kind="ExternalInput").ap()
    convb = dt("convb", [128, NCH], F32, kind="ExternalInput").ap()
    lnp = dt("lnp", [4, 128, D], F32, kind="ExternalInput").ap()
    ident = dt("ident", [128, 128], F32, kind="ExternalInput").ap()
    xo = dt("xo", [NTK, D], F32, kind="ExternalOutput").ap()

    sb, ps, res = C.sb, C.ps, C.res
    wd_b = sb("wd_b", [128, 22, D], BF16)
    wo_b = sb("wo_b", [128, 8, D], BF16)
    A_T = sb("A_T", [128, 22, 512], BF16)
    X1T = [sb("X1T%d" % i, [128, 8, 512], BF16) for i in range(2)]
    X1Th = sb("X1Th", [128, 8, 2], BF16)
    X1 = sb("X1", [128, 4, D], F32)
    wu = [[sb("wu%d_%d" % (i, j), [128, 8, 128], BF16) for j in range(2)] for i in range(3)]
    H = [sb("H%d" % i, [128, 514], F32) for i in range(2)]
    tg = [sb("tg%d" % i, [128, 512], F32) for i in range(2)]
    tv = [sb("tv%d" % i, [128, 512], F32) for i in range(2)]
    sg = [sb("sg%d" % i, [128, 512], BF16) for i in range(2)]
    HALO = sb("HALO", [128, NCH, 2], F32)
    lnp_s = sb("lnp_s", [128, 4, D], F32)
    yt = [sb("yt%d" % i, [128, D], BF16) for i in range(2)]
    xt = [sb("xt%d" % i, [128, D], F32) for i in range(2)]
    rr = sb("rr", [128, D], F32)
    x1b = sb("x1b", [128, D], BF16)
    yT = sb("yT", [128, 8, 128], BF16)
    x2 = [sb("x2_%d" % i, [128, D], F32) for i in range(2)]
    stats = sb("stats", [128, 12], F32)
    mv = sb("mv", [128, 2], F32)
    sd = sb("sd", [128, 1], F32)
    rstd = sb("rstd", [128, 1], F32)
    cw_s = sb("cw_s", [128, NCH, 3], F32)
    cb_s = sb("cb_s", [128, NCH], F32)
    flag_s = sb("flag_s", [128, 1], F32)
    ident_b = sb("ident_b", [128, 128], BF16)
    PS = [ps("ps%d" % i, [128, 512], F32) for i in range(7)]
    PSB = ps("psb", [128, 1024], BF16)
    RPS = [res("ps", True) for _ in range(7)]
    RPSB = res("psb", True)
    R = lambda n: res(n)
    Rwd, Rwo, RAT, RX1, RX1Th, RHALO, Rlnp, Rrr, Rx1b, RyT = (R("wd"), R("wo"), R("at"), R("x1"), R("x1th"), R("halo"),
                                                             R("lnp"), R("rr"), R("x1b"), R("yT"))
    RX1T = [R("x1t") for _ in range(2)]
    Rwu = [[R("wu") for _ in range(2)] for _ in range(3)]
    RH = [R("h") for _ in range(3)]
    RHh = [R("hh") for _ in range(3)]
    RHALOc = [R("haloc") for _ in range(NCH)]
    Rtg = [R("tg") for _ in range(2)]
    Rtv = [R("tv") for _ in range(2)]
    Rsg = [R("sg") for _ in range(2)]
    Ryt = [R("yt") for _ in range(2)]
    Rxt = [R("xt") for _ in range(2)]
    Rx2 = [R("x2") for _ in range(2)]
    Rst, Rmv, Rsd, Rrstd, Rcw, Rcb, Rflag, Rid = (R("st"), R("mv"), R("sd"), R("rstd"), R("cw"), R("cb"), R("flag"), R("id"))

    C.dma("pool", ident_b[:], ident, writes=[Rid])
    wo_v = wo.rearrange("(kc f) n -> f kc n", f=128)
    for kc in range(0, 8, 4):
        C.dma("pool", wo_b[:, kc:kc + 4, :], wo_v[:, kc:kc + 4, :], writes=[Rwo], key="wo")
    C.dma("sp", lnp_s[:], lnp.rearrange("k p n -> p k n"), writes=[Rlnp])
    C.dma("sp", cw_s[:], convw, writes=[Rcw])
    C.dma("sp", cb_s[:], convb, writes=[Rcb])
    C.dma("sp", flag_s[:], flag, writes=[Rflag])

    def load_sub(i):
        sl = i % 2
        C.dma("sp", yt[sl][:], yin[i * 128:(i + 1) * 128, :], writes=[Ryt[sl]])
        C.dma("sp", xt[sl][:], xin[i * 128:(i + 1) * 128, :], writes=[Rxt[sl]])

    def layer_norm(src, Rsrc, dst, Rdst, gi):
        def st(e):
            e.bn_stats(out=stats[:, 0:6], in_=src[:, 0:512])
            return e.bn_stats(out=stats[:, 6:12], in_=src[:, 512:1024])
        C.dve(st, reads=[Rsrc], writes=[Rst])
        C.dve(lambda e: e.bn_aggr(out=mv[:], in_=stats[:]), reads=[Rst], writes=[Rmv])
        C.dve(lambda e: e.tensor_scalar(out=sd[:], in0=mv[:, 1:2], scalar1=EPS, scalar2=None, op0=ALU.add),
              reads=[Rmv], writes=[Rsd])
        C.act(lambda e: e.activation(out=sd[:], in_=sd[:], func=AF.Sqrt), reads=[Rsd], writes=[Rsd])
        C.dve(lambda e: e.reciprocal(out=rstd[:], in_=sd[:]), reads=[Rsd], writes=[Rrstd])
        C.dve(lambda e: e.tensor_scalar(out=src[:], in0=src[:], scalar1=mv[:, 0:1], scalar2=rstd[:, 0:1],
                                        op0=ALU.subtract, op1=ALU.mult), reads=[Rsrc, Rmv, Rrstd], writes=[Rsrc])
        C.dve(lambda e: e.tensor_tensor(out=src[:], in0=src[:], in1=lnp_s[:, gi, :], op=ALU.mult),
              reads=[Rsrc, Rlnp], writes=[Rsrc])
        C.dve(lambda e: e.tensor_tensor(out=dst, in0=src[:], in1=lnp_s[:, gi + 1, :], op=ALU.add),
              reads=[Rsrc, Rlnp], writes=[Rdst])

    def stage_a1(i):
        sl = i % 2
        if i + 1 <= NT4 * 4:
            load_sub(i + 1)

        def tr_y(e):
            ins = None
            for kc in range(8):
                ins = e.transpose(PSB[:, kc * 128:(kc + 1) * 128], yt[sl][:, kc * 128:(kc + 1) * 128], ident_b[:])
            return ins
        C.pe(tr_y, reads=[Ryt[sl], Rid], writes=[RPSB])
        C.act(lambda e: e.copy(out=yT[:].rearrange("p k n -> p (k n)"), in_=PSB[:]), reads=[RPSB], writes=[RyT])
        for half in range(2):
            C.pe(_mm_group(PS[half][:], [(yT[:, kc, :], wo_b[:, kc, half * 512:(half + 1) * 512]) for kc in range(8)]),
                 reads=[RyT, Rwo], writes=[RPS[half]])
        for half in range(2):
            C.dve(lambda e, half=half: e.scalar_tensor_tensor(out=rr[:, half * 512:(half + 1) * 512], in0=xt[sl][:, half * 512:(half + 1) * 512],
                                                              scalar=ALPHA, in1=PS[half][:], op0=ALU.mult, op1=ALU.add),
                  reads=[Rxt[sl], RPS[half]], writes=[Rrr])
        if i == 0:
            layer_norm(rr, Rrr, x2[0][:], Rx2[0], 0)
        else:
            c = (i - 1) % 4
            layer_norm(rr, Rrr, X1[:, c, :], RX1, 0)

    def stage_a2(i):
        if i == 0:
            x1src, Rx1src = x2[0][:], Rx2[0]
        else:
            c = (i - 1) % 4
            x1src, Rx1src = X1[:, c, :], RX1
        C.act(lambda e: e.copy(out=x1b[:], in_=x1src), reads=[Rx1src], writes=[Rx1b])

        def tr_x(e):
            ins = None
            for kc in range(8):
                ins = e.transpose(PSB[:, kc * 128:(kc + 1) * 128], x1b[:, kc * 128:(kc + 1) * 128], ident_b[:])
            return ins
        C.pe(tr_x, reads=[Rx1b, Rid], writes=[RPSB])
        psv = PSB[:].rearrange("p (k n) -> p k n", n=128)
        if i == 0:
            C.act(lambda e: e.copy(out=X1Th[:], in_=psv[:, :, 126:128]), reads=[RPSB], writes=[RX1Th])
        else:
            T = (i - 1) // 4
            c = (i - 1) % 4
            C.act(lambda e: e.copy(out=X1T[T % 2][:, :, c * 128:(c + 1) * 128], in_=psv), reads=[RPSB], writes=[RX1T[T % 2]])

    wup_loaded = {}

    def load_wup(T, cc):
        sl = (T * 22 + cc) % 3
        if os.environ.get("P_NOLOAD") and T > 0:
            return
        C.dma("pool", wu[sl][0][:], wup[cc], writes=[Rwu[sl][0]])
        C.dma("pool", wu[sl][1][:], wup[22 + cc], writes=[Rwu[sl][1]])

    state = {"k": 0}

    def ffn_chunk(T, cc, which):
        ch = cc + 22 * which
        wsl = (T * 22 + cc) % 3
        k = state["k"]
        state["k"] += 1
        hb = k % 2
        bank = (2, 3)[hb]
        xs = X1T[T % 2]
        if T == 0:
            C.pe(_mm_group(PS[4][:, 0:2], [(wu[wsl][which][:, kc, :], X1Th[:, kc, :]) for kc in range(8)]),
                 reads=[Rwu[wsl][which], RX1Th], writes=[RPS[4]])
            C.act(lambda e: e.activation(out=H[hb][:, 0:2], in_=PS[4][:, 0:2], func=AF.Copy, scale=flag_s[:, 0:1]),
                  reads=[RPS[4], Rflag], writes=[RHh[hb]])
        else:
            C.pool(lambda e: e.tensor_copy(out=H[hb][:, 0:2], in_=HALO[:, ch, :]), reads=[RHALOc[ch]], writes=[RHh[hb]])
        C.pe(_mm_group(PS[bank][:], [(wu[wsl][which][:, kc, :], xs[:, kc, :]) for kc in range(8)]),
             reads=[Rwu[wsl][which], RX1T[T % 2]], writes=[RPS[bank]])
        C.act(lambda e: e.copy(out=H[hb][:, 2:514], in_=PS[bank][:]), reads=[RPS[bank]], writes=[RH[hb]])
        C.pool(lambda e: e.tensor_copy(out=HALO[:, ch, :], in_=H[hb][:, 512:514]), reads=[RH[hb]], writes=[RHALOc[ch]])
        t, Rt = (tg[cc % 2], Rtg[cc % 2]) if which == 0 else (tv[cc % 2], Rtv[cc % 2])
        C.act(lambda e: e.activation(out=t[:], in_=H[hb][:, 0:512], func=AF.Identity, bias=cb_s[:, ch:ch + 1],
                                     scale=cw_s[:, ch, 0:1]), reads=[RH[hb], RHh[hb], Rcw, Rcb], writes=[Rt])
        C.dve(lambda e: e.scalar_tensor_tensor(out=t[:], in0=H[hb][:, 1:513], scalar=cw_s[:, ch, 1:2], in1=t[:],
                                               op0=ALU.mult, op1=ALU.add), reads=[RH[hb], RHh[hb], Rcw, Rt], writes=[Rt])
        C.dve(lambda e: e.scalar_tensor_tensor(out=t[:], in0=H[hb][:, 2:514], scalar=cw_s[:, ch, 2:3], in1=t[:],
                                               op0=ALU.mult, op1=ALU.add), reads=[RH[hb], Rcw, Rt], writes=[Rt])

    def ffn_pair(T, cc):
        if T * 22 + cc + 2 < NT4 * 22:
            nT, ncc = divmod(T * 22 + cc + 2, 22)
            load_wup(nT, ncc)
        ffn_chunk(T, cc, 0)
        ffn_chunk(T, cc, 1)
        s2 = cc % 2
        C.act(lambda e: e.activation(out=sg[s2][:], in_=tg[s2][:], func=AF.Silu), reads=[Rtg[s2]], writes=[Rsg[s2]])
        C.pool(lambda e: e.tensor_tensor(out=A_T[:, cc, :], in0=sg[s2][:], in1=tv[s2][:], op=ALU.mult),
               reads=[Rsg[s2], Rtv[s2]], writes=[RAT])

    def stage_c(T, c):
        j = T * 4 + c
        banks = (5, 6) if j % 2 == 0 else (0, 1)
        for half in range(2):
            C.pe(_mm_group(PS[banks[half]][:], [(A_T[:, cc, c * 128:(c + 1) * 128], wd_b[:, cc, half * 512:(half + 1) * 512])
                                                 for cc in range(22)]), reads=[RAT, Rwd], writes=[RPS[banks[half]]])
        for half in range(2):
            C.dve(lambda e, half=half: e.scalar_tensor_tensor(out=rr[:, half * 512:(half + 1) * 512], in0=X1[:, c, half * 512:(half + 1) * 512],
                                                              scalar=ALPHA, in1=PS[banks[half]][:], op0=ALU.mult, op1=ALU.add),
                  reads=[RX1, RPS[banks[half]]], writes=[Rrr])
        o = j % 2
        layer_norm(rr, Rrr, x2[o][:], Rx2[o], 2)
        C.dma("sp", xo[j * 128:(j + 1) * 128, :], x2[o][:], reads=[Rx2[o]], key="xo%d" % o)

    load_sub(0)
    load_wup(0, 0)
    load_wup(0, 1)
    wd_v = wdn.rearrange("(cc p) n -> p cc n", p=128)
    stage_a1(0)
    for T in range(NT4):
        i0_ = 1 + 4 * T
        stage_a1(i0_)
        if T == 0:
            stage_a2(0)
        for c in range(1, 4):
            stage_a1(i0_ + c)
            stage_a2(i0_ + c - 1)
        stage_a2(i0_ + 3)
        if T == 0:
            for c0 in range(0, 22, 2):
                C.dma("pool", wd_b[:, c0:c0 + 2, :], wd_v[:, c0:c0 + 2, :], writes=[Rwd], key="wd")
        for cc in range(22):
            ffn_pair(T, cc)
        for c in range(4):
            stage_c(T, c)
    return _finish_mixer(C, nc)


def _post_inputs(y_b, x_b, q, l, P):
    t0 = q * NTOK
    if q == 0:
        yh = np.zeros((128, D), ml_dtypes.bfloat16)
        xh = np.zeros((128, D), np.float32)
    else:
        yh = y_b[t0 - 128:t0]
        xh = x_b[t0 - 128:t0]
    rep = lambda v: np.broadcast_to(v[None, :], (128, v.shape[0]))
    key = ("post_w", l)
    if key not in _CACHE:
        w_up = P["w_up"][l]
        _CACHE[key] = {
            "wo": np.ascontiguousarray(P["w_o"][l]),
            "wup": np.ascontiguousarray(w_up.reshape(8, 128, NCH, 128).transpose(2, 1, 0, 3)),
            "wdn": np.ascontiguousarray(P["w_down"][l]),
            "convw": np.ascontiguousarray(P["conv_w"][l].reshape(3, NCH, 128).transpose(2, 1, 0)),
            "convb": np.ascontiguousarray(P["conv_b"][l].reshape(NCH, 128).T),
            "lnp": np.ascontiguousarray(np.stack([rep(P["ln1_g"][l]), rep(P["ln1_b"][l]), rep(P["ln2_g"][l]), rep(P["ln2_b"][l])]).astype(np.float32)),
            "ident": np.eye(128, dtype=np.float32),
        }
    m = dict(_CACHE[key])
    m["yin"] = np.ascontiguousarray(np.concatenate([yh, y_b[t0:t0 + NTOK]], axis=0))
    m["xin"] = np.ascontiguousarray(np.concatenate([xh, x_b[t0:t0 + NTOK]], axis=0))
    m["flag"] = np.full((128, 1), 0.0 if q == 0 else 1.0, np.float32)
    return m


def run_post(y, x, l, P):
    if "nc_p" not in _CACHE:
        _CACHE["nc_p"] = build_post()
    in_maps = [_post_inputs(y[c // 4], x[c // 4], c % 4, l, P) for c in range(8)]
    res = run_bass_kernel_spmd(_CACHE["nc_p"], in_maps, core_ids=list(range(8)))
    out = np.zeros((NB, S, D), np.float32)
    for c in range(8):
        out[c // 4, (c % 4) * NTOK:(c % 4 + 1) * NTOK] = res.results[c]["xo"]
    return out


def kernel(**inputs):
    P = {k: np.asarray(v) for k, v in inputs.items()}
    x = np.ascontiguousarray(P["x"], dtype=np.float32)
    for l in range(2):
        xT = [np.ascontiguousarray(x[b].T) for b in range(NB)]
        y = run_mixer(xT, l, P)
        x = run_post(y, x, l, P)
    return x
```

```python
import contextlib
import os
DBG = int(os.environ.get('MIX_DBG', '99'))
import numpy as np
import ml_dtypes
import concourse.bass as bass
import concourse.mybir as mybir
from concourse.bass_utils import run_bass_kernel_spmd

F32 = mybir.dt.float32
BF16 = mybir.dt.bfloat16
AF = mybir.ActivationFunctionType
ALU = mybir.AluOpType
AX = mybir.AxisListType

S = 8192
D = 1024
NB = 2
DFF = 2816
ALPHA = 4.0 ** 0.25
EPS = 1e-5
BIGM = 30000.0
NEGF = -1.0e30


class Res:
    __slots__ = ("name", "w", "r", "excl")

    def __init__(self, name, excl=False):
        self.name = name
        self.w = None
        self.r = []
        self.excl = excl


class Sched:
    STREAMS = ("pe", "act", "dve", "pool", "sp")

    def __init__(self, nc):
        self.nc = nc
        self.ops = {s: [] for s in self.STREAMS}
        self.ccount = {s: 0 for s in self.STREAMS}
        self.dcount = {}
        self.known = {s: {} for s in self.STREAMS}
        self.final_events = []

    def _need(self, stream, ev, waits):
        if ev is None:
            return
        sem, val, src = ev
        if src == stream and src == "pe":
            return
        if self.known[stream].get(sem, 0) >= val:
            return
        self.known[stream][sem] = val
        waits.append((sem, val))

    def op(self, stream, fn, reads=(), writes=(), dma_key=None):
        ex = [r for r in reads if r.excl]
        if ex:
            reads = [r for r in reads if not r.excl]
            writes = list(writes) + [r for r in ex if r not in writes]
        waits = []
        for r in reads:
            self._need(stream, r.w, waits)
        for w in writes:
            self._need(stream, w.w, waits)
            for e in w.r:
                self._need(stream, e, waits)
        if dma_key is not None:
            k = "d_" + dma_key
            self.dcount[k] = self.dcount.get(k, 0) + 1
            ev = (k, 16 * self.dcount[k], "dma")
            sig = (k, 16)
        else:
            self.ccount[stream] += 1
            ev = ("c_" + stream, self.ccount[stream], stream)
            sig = ("c_" + stream, 1)
        for r in reads:
            r.r.append(ev)
        for w in writes:
            w.w = ev
            w.r = []
        self.ops[stream].append((waits, fn, sig))
        return ev

    def emit(self, final_events):
        nc = self.nc
        names = set()
        for s in self.STREAMS:
            for waits, fn, sig in self.ops[s]:
                names.add(sig[0])
                for (sem, val) in waits:
                    names.add(sem)
        with contextlib.ExitStack() as st:
            sems = {n: st.enter_context(nc.semaphore(n)) for n in sorted(names)}
            block = st.enter_context(nc.Block())

            def make(stream):
                def body(eng):
                    for waits, fn, sig in self.ops[stream]:
                        for (sem, val) in waits:
                            eng.wait_ge(sems[sem], val)
                        ins = fn(eng)
                        ins.then_inc(sems[sig[0]], sig[1])
                    if stream == "sp":
                        for (sem, val, src) in final_events:
                            eng.wait_ge(sems[sem], val)
                return body

            block.tensor(make("pe"))
            block.scalar(make("act"))
            block.vector(make("dve"))
            block.gpsimd(make("pool"))
            block.sync(make("sp"))


class Ctx:
    def __init__(self, nc):
        self.nc = nc
        self.st = contextlib.ExitStack()
        self.S = Sched(nc)
        self.nres = 0

    def sb(self, name, shape, dt):
        return self.st.enter_context(self.nc.sbuf_tensor(name, shape, dt))

    def ps(self, name, shape, dt):
        return self.st.enter_context(self.nc.psum_tensor(name, shape, dt))

    def res(self, name="r", excl=False):
        self.nres += 1
        return Res("%s%d" % (name, self.nres), excl)

    def pe(self, fn, reads=(), writes=()):
        return self.S.op("pe", fn, reads, writes)

    def act(self, fn, reads=(), writes=()):
        return self.S.op("act", fn, reads, writes)

    def dve(self, fn, reads=(), writes=()):
        return self.S.op("dve", fn, reads, writes)

    def pool(self, fn, reads=(), writes=()):
        return self.S.op("pool", fn, reads, writes)

    def dma(self, queue, out, in_, reads=(), writes=(), key=None):
        if key is None:
            key = (writes[0].name if writes else reads[0].name)
        return self.S.op(queue, lambda e: e.dma_start(out=out, in_=in_), reads, writes, dma_key=key)


def _mm_group(out, pairs):
    def fn(e):
        n = len(pairs)
        ins = None
        for i, (l, r) in enumerate(pairs):
            ins = e.matmul(out, lhsT=l, rhs=r, start=(i == 0), stop=(i == n - 1))
        return ins
    return fn


NFM = 352
NTM = 257


def build_mixer(SEQ=S, PH=9):
    NT, NTT = SEQ // 512, SEQ // 128
    nc = bass.Bass("TRN2", target_bir_lowering=False)
    C = Ctx(nc)
    dt = nc.dram_tensor
    xT = dt("xT", [D, SEQ], F32, kind="ExternalInput").ap()
    wfm = dt("wfm", [D, NFM], F32, kind="ExternalInput").ap()
    wtm = dt("wtm", [D, NTM], F32, kind="ExternalInput").ap()
    cs = dt("cs", [2, 16, SEQ], F32, kind="ExternalInput").ap()
    onehot = dt("onehot", [32, SEQ], BF16, kind="ExternalInput").ap()
    sguT = dt("sguT", [128, 128], F32, kind="ExternalInput").ap()
    tri = dt("tri", [128, 128], F32, kind="ExternalInput").ap()
    ident = dt("ident", [128, 128], F32, kind="ExternalInput").ap()
    sgub = dt("sgub", [128, 1], F32, kind="ExternalInput").ap()
    lng = dt("lng", [128, 64], F32, kind="ExternalInput").ap()
    lnb = dt("lnb", [128, 64], F32, kind="ExternalInput").ap()
    poolw = dt("poolw", [64, 64], F32, kind="ExternalInput").ap()
    pscale = dt("pscale", [128, 64], F32, kind="ExternalInput").ap()
    band = dt("band", [3, 128, 128], F32, kind="ExternalInput").ap()
    bfor = dt("bfor", [128, 1], F32, kind="ExternalInput").ap()
    y = dt("y", [SEQ, 256], BF16, kind="ExternalOutput").ap()

    sb, ps, res = C.sb, C.ps, C.res
    QA = sb("QA", [128, SEQ], BF16)
    KA = sb("KA", [128, SEQ], BF16)
    QB = sb("QB", [128, SEQ], BF16)
    KB = sb("KB", [128, SEQ], BF16)
    VA = sb("VA", [128, NTT, 65], BF16)
    VB = sb("VB", [128, NTT, 65], BF16)
    xb = [sb("xb%d" % i, [128, 8, 512], BF16) for i in range(2)]
    wfm_b = sb("wfm_b", [128, 8, NFM], BF16)
    wtm_b = sb("wtm_b", [128, 8, NTM], BF16)
    cst = [sb("cst%d" % i, [16, 2, 512], F32) for i in range(2)]
    t1 = [sb("t1_%d" % i, [16, 512], F32) for i in range(2)]
    t2 = [sb("t2_%d" % i, [16, 512], F32) for i in range(2)]
    pTb = [sb("pTb%d" % i, [64, 512], BF16) for i in range(2)]
    ug = [sb("ug%d" % i, [128, 128], F32) for i in range(2)]
    stats = [sb("stats%d" % i, [128, 6], F32) for i in range(2)]
    mv = [sb("mv%d" % i, [128, 2], F32) for i in range(2)]
    rstd = [sb("rstd%d" % i, [128, 1], F32) for i in range(2)]
    vn = [sb("vn%d" % i, [128, 64], F32) for i in range(2)]
    vnb = [sb("vnb%d" % i, [128, 64], BF16) for i in range(2)]
    pwb = [sb("pwb%d" % i, [128, 64], BF16) for i in range(2)]
    YC = [sb("YC%d" % i, [128, 4, 64], BF16) for i in range(2)]
    YD = [sb("YD%d" % i, [128, 4, 64], BF16) for i in range(2)]
    UG = sb("UG", [128, NTT, 128], BF16)
    MV = sb("MV", [128, NTT, 2], F32)
    VE = sb("VE", [128, NTT], F32)
    RSTD = sb("RSTD", [128, NTT], F32)
    YAB = [sb("YAB%d" % i, [128, 4, 128], BF16) for i in range(2)]
    MBZ = sb("MBZ", [128, 64 + NTT * 32], F32)
    GM = sb("GM", [128, 32], F32)
    top8 = [sb("top8_%d" % i, [128, 8], F32) for i in range(2)]
    FRAW = sb("FRAW", [128, NTT], F32)
    LF = sb("LF", [128, NTT], F32)
    TOT = sb("TOT", [128, NTT], F32)
    PREF = sb("PREF", [128, NTT], F32)
    CP = sb("CP", [128, NTT], F32)
    Z = sb("Z", [128, 128], F32)
    kbar = sb("kbar", [64, 32], F32)
    kbar_b = sb("kbar_b", [64, 32], BF16)
    PT = [sb("PT%d" % i, [128, 512], BF16) for i in range(4)]
    rl = [sb("rl%d" % i, [128, 1], F32) for i in range(4)]
    ident_f = sb("ident_f", [128, 128], F32)
    tri_f = sb("tri_f", [128, 128], F32)
    tri_b = sb("tri_b", [128, 128], BF16)
    ones_f = sb("ones_f", [128, 128], F32)
    sgu_f = sb("sgu_f", [128, 128], F32)
    wmT_b = sb("wmT_b", [128, 128], BF16)
    band_b = sb("band_b", [128, 3, 128], BF16)
    sgub_s = sb("sgub_s", [128, 1], F32)
    lng_s = sb("lng_s", [128, 64], F32)
    lnb_s = sb("lnb_s", [128, 64], F32)
    poolw_b = sb("poolw_b", [64, 64], BF16)
    pscale_s = sb("pscale_s", [128, 64], F32)
    bfor_s = sb("bfor_s", [128, 1], F32)
    negb = sb("negb", [128, 1], F32)
    ef = sb("ef", [128, NTT], F32)
    PS = [ps("ps%d" % i, [128, 512], F32) for i in range(8)]
    RPS = [res("ps", True) for _ in range(8)]

    R = lambda n: res(n)
    RQA = [R("qa") for _ in range(NT)]
    RQAm = [R("qam") for _ in range(NT)]
    RKA = [R("ka") for _ in range(NT)]
    RQB = [R("qb") for _ in range(NT)]
    RQBc = [R("qbc") for _ in range(NT)]
    RKB = [R("kb") for _ in range(NT)]
    RVA = [R("va") for _ in range(NT)]
    RVB = [R("vb") for _ in range(NT)]
    Rxb = [R("xb") for _ in range(2)]
    Rcs = [R("cs") for _ in range(2)]
    Rt1 = [R("t1") for _ in range(2)]
    Rt2 = [R("t2") for _ in range(2)]
    RpT = [R("pT") for _ in range(2)]
    Rug = [R("ug") for _ in range(2)]
    Rst = [R("st") for _ in range(2)]
    Rmv = [R("mv") for _ in range(2)]
    Rrs = [R("rs") for _ in range(2)]
    Rvn = [R("vn") for _ in range(2)]
    Rvnb = [R("vnb") for _ in range(2)]
    Rpwb = [R("pwb") for _ in range(2)]
    RYC = [R("yc") for _ in range(2)]
    RYD = [R("yd") for _ in range(2)]
    RUG = [R("ugall") for _ in range(NT)]
    RMV, RVE, RRSTD = R("mvall"), R("ve"), R("rstdall")
    RYAB = [R("yab") for _ in range(2)]
    RMBZ = [R("mbz") for _ in range(NT)]
    RGM = R("gm")
    Rtop = [R("top") for _ in range(2)]
    RFRAW, RLF, RTOT, RPREF, RCP, RZ, Rkbar, Rkbarb, Ref = (R("fraw"), R("lf"), R("tot"), R("pref"),
                                                            R("cp"), R("z"), R("kbar"), R("kbarb"), R("ef"))
    RPT = [R("pt") for _ in range(4)]
    Rrl = [R("rl") for _ in range(4)]
    Rc = {k: R(k) for k in ["wfm", "wtm", "ident", "tri_f", "tri_b", "ones", "sgu_f", "wmT", "band", "sgub",
                            "lng", "lnb", "poolw", "pscale", "bfor", "negb", "kaoh", "misc"]}

    C.dma("pool", wfm_b[:], wfm.rearrange("(kc f) n -> f kc n", f=128), writes=[Rc["wfm"]])
    C.dma("pool", wtm_b[:], wtm.rearrange("(kc f) n -> f kc n", f=128), writes=[Rc["wtm"]])
    C.dma("sp", ident_f[:], ident, writes=[Rc["ident"]])
    C.dma("sp", tri_f[:], tri, writes=[Rc["tri_f"]])
    C.dma("pool", tri_b[:], tri, writes=[Rc["tri_b"]])
    C.dma("sp", sgu_f[:], sguT, writes=[Rc["sgu_f"]])
    C.dma("pool", band_b[:], band.rearrange("k s t -> s k t"), writes=[Rc["band"]])
    C.dma("sp", sgub_s[:], sgub, writes=[Rc["sgub"]])
    C.dma("sp", lng_s[:], lng, writes=[Rc["lng"]])
    C.dma("sp", lnb_s[:], lnb, writes=[Rc["lnb"]])
    C.dma("pool", poolw_b[:], poolw, writes=[Rc["poolw"]])
    C.dma("sp", pscale_s[:], pscale, writes=[Rc["pscale"]])
    C.dma("sp", bfor_s[:], bfor, writes=[Rc["bfor"]])
    C.dma("sp", KA[64:96, :], onehot, writes=[Rc["kaoh"]])
    C.dve(lambda e: e.tensor_tensor(out=wmT_b[:], in0=sgu_f[:], in1=tri_f[:], op=ALU.mult),
          reads=[Rc["sgu_f"], Rc["tri_f"]], writes=[Rc["wmT"]])
    C.dve(lambda e: e.tensor_scalar(out=negb[:], in0=bfor_s[:], scalar1=-1.0, scalar2=None, op0=ALU.mult),
          reads=[Rc["bfor"]], writes=[Rc["negb"]])
    C.dve(lambda e: e.memset(ones_f[:], 1.0), writes=[Rc["ones"]])
    C.dve(lambda e: e.memset(VA[:, :, 64:65], 1.0), writes=RVA)
    C.dve(lambda e: e.memset(VB[:, :, 64:65], 1.0), writes=RVB)
    C.dve(lambda e: e.memset(KB[64:65, :], 1.0), writes=RKB)
    C.dve(lambda e: e.memset(MBZ[:], 0.0), writes=RMBZ)
    C.dve(lambda e: e.memset(GM[:], NEGF), writes=[RGM])
    C.dve(lambda e: e.memset(Z[:], 0.0), writes=[RZ])
    C.dve(lambda e: e.memset(kbar[:], 0.0), writes=[Rkbar])
    C.dve(lambda e: e.memset(PREF[:, 0:1], 0.0), writes=[RPREF])

    if PH == 0:
        return _finish_mixer(C, nc)
    xT_v = xT.rearrange("(kc f) t -> f kc t", f=128)

    def load_tile(T):
        sl = T % 2
        C.dma("pool", xb[sl][:], xT_v[:, :, T * 512:(T + 1) * 512], writes=[Rxb[sl]])
        C.dma("sp", cst[sl][:], cs[:, :, T * 512:(T + 1) * 512].rearrange("k p t -> p k t"), writes=[Rcs[sl]])

    load_tile(0)
    fm_cols = {"qA": (0, 64), "kA": (64, 64), "qB": (128, 64), "kB": (192, 64), "pD": (256, 64),
               "qAp": (320, 16), "kAp": (336, 16)}
    def p1_tile(T):
        sl = T % 2
        if T + 1 < NT:
            load_tile(T + 1)
        cols = slice(T * 512, (T + 1) * 512)

        def fm(name, bank):
            c0, n = fm_cols[name]
            C.pe(_mm_group(PS[bank][0:n, :], [(wfm_b[:, kc, c0:c0 + n], xb[sl][:, kc, :]) for kc in range(8)]),
                 reads=[Rc["wfm"], Rxb[sl]], writes=[RPS[bank]])

        def rot(nm, nmp, dst, Rdst):
            fm(nm, 0)
            fm(nmp, 1)
            if DBG == 10:
                return
            C.act(lambda e, dst=dst: e.copy(out=dst[0:64, cols], in_=PS[0][0:64, :]), reads=[RPS[0]], writes=[Rdst[T]])
            if DBG == 11:
                return
            C.dve(lambda e: e.tensor_tensor(out=t1[sl][:], in0=PS[0][0:16, :], in1=cst[sl][:, 0, :], op=ALU.mult),
                  reads=[RPS[0], Rcs[sl]], writes=[Rt1[sl]])
            C.dve(lambda e: e.tensor_tensor(out=t2[sl][:], in0=PS[1][0:16, :], in1=cst[sl][:, 1, :], op=ALU.mult),
                  reads=[RPS[1], Rcs[sl]], writes=[Rt2[sl]])
            if DBG == 12:
                return
            C.dve(lambda e, dst=dst: e.tensor_tensor(out=dst[0:16, cols], in0=t1[sl][:], in1=t2[sl][:], op=ALU.add),
                  reads=[Rt1[sl], Rt2[sl]], writes=[Rdst[T]])
        if DBG < 1:
            return
        rot("qA", "qAp", QA, RQA)
        rot("kA", "kAp", KA, RKA)
        if DBG < 2 or (10 <= DBG < 20):
            return
        C.dve(lambda e: e.tensor_reduce(out=kbar[:, 2 * T:2 * T + 2],
                                        in_=KA[0:64, cols].rearrange("p (b j) -> p b j", j=256),
                                        axis=AX.X, op=ALU.add), reads=[RKA[T]], writes=[Rkbar])
        if DBG < 3:
            return
        fm("qB", 0)
        C.act(lambda e: e.copy(out=QB[0:64, cols], in_=PS[0][0:64, :]), reads=[RPS[0]], writes=[RQB[T]])
        fm("kB", 1)
        C.act(lambda e: e.copy(out=KB[0:64, cols], in_=PS[1][0:64, :]), reads=[RPS[1]], writes=[RKB[T]])
        fm("pD", 0)
        C.act(lambda e: e.copy(out=pTb[sl][:], in_=PS[0][0:64, :]), reads=[RPS[0]], writes=[RpT[sl]])
        if DBG < 4:
            return
        def sub(c):
            tt = 4 * T + c
            s2 = tt % 2
            bk = 2 + s2
            C.pe(_mm_group(PS[bk][:, 0:NTM], [(xb[sl][:, kc, c * 128:(c + 1) * 128], wtm_b[:, kc, :]) for kc in range(8)]),
                 reads=[Rc["wtm"], Rxb[sl]], writes=[RPS[bk]])
            C.act(lambda e, bk=bk, tt=tt: e.copy(out=VA[:, tt, 0:64], in_=PS[bk][:, 0:64]), reads=[RPS[bk]], writes=[RVA[T]])
            C.act(lambda e, bk=bk, tt=tt: e.copy(out=VB[:, tt, 0:64], in_=PS[bk][:, 64:128]), reads=[RPS[bk]], writes=[RVB[T]])
            C.dve(lambda e, bk=bk, tt=tt: e.tensor_copy(out=FRAW[:, tt:tt + 1], in_=PS[bk][:, 256:257]),
                  reads=[RPS[bk]], writes=[RFRAW])
            C.act(lambda e, bk=bk, tt=tt: e.activation(out=UG[:, tt, :], in_=PS[bk][:, 128:256], func=AF.Gelu),
                  reads=[RPS[bk]], writes=[RUG[T]])
            C.dve(lambda e, tt=tt, s2=s2: e.bn_stats(out=stats[s2][:], in_=UG[:, tt, 64:128]), reads=[RUG[T]], writes=[Rst[s2]])
            C.dve(lambda e, tt=tt, s2=s2: e.bn_aggr(out=MV[:, tt, :], in_=stats[s2][:]), reads=[Rst[s2]], writes=[RMV])
            C.pe(lambda e, c=c: e.matmul(PS[5][:, 0:64], lhsT=pTb[sl][:, c * 128:(c + 1) * 128], rhs=poolw_b[:],
                                         start=True, stop=True), reads=[RpT[sl], Rc["poolw"]], writes=[RPS[5]])
            C.act(lambda e, s2=s2: e.copy(out=pwb[s2][:], in_=PS[5][:, 0:64]), reads=[RPS[5]], writes=[Rpwb[s2]])
            if tt == 0:
                C.pe(lambda e, s2=s2: e.matmul(PS[6][:, 0:64], lhsT=band_b[:, 2, :], rhs=pwb[s2][:], start=True, stop=True),
                     reads=[Rc["band"], Rpwb[s2]], writes=[RPS[6]])
            else:
                C.pe(_mm_group(PS[6][:, 0:64], [(band_b[:, 0, :], pwb[s2][:]), (band_b[:, 1, :], pwb[1 - s2][:])]),
                     reads=[Rc["band"], Rpwb[0], Rpwb[1]], writes=[RPS[6]])
            C.dve(lambda e, c=c: e.tensor_tensor(out=YD[sl][:, c, :], in0=PS[6][:, 0:64], in1=pscale_s[:], op=ALU.mult),
                  reads=[RPS[6], Rc["pscale"]], writes=[RYD[sl]])
        for c in range(4):
            sub(c)
        C.dma("sp", y[T * 512:(T + 1) * 512, 192:256].rearrange("(c p) n -> p c n", p=128), YD[sl][:],
              reads=[RYD[sl]], key="yd%d" % sl)

    for T in range(NT):
        p1_tile(T)

    if PH == 1:
        return _finish_mixer(C, nc)
    C.dve(lambda e: e.tensor_scalar(out=VE[:], in0=MV[:, :, 1], scalar1=EPS, scalar2=None, op0=ALU.add),
          reads=[RMV], writes=[RVE])
    C.act(lambda e: e.activation(out=VE[:], in_=VE[:], func=AF.Sqrt), reads=[RVE], writes=[RVE])
    C.dve(lambda e: e.reciprocal(out=RSTD[:], in_=VE[:]), reads=[RVE], writes=[RRSTD])

    vn4 = [sb("vn4_%d" % i, [128, 64], F32) for i in range(4)]
    vnb4 = [sb("vnb4_%d" % i, [128, 64], BF16) for i in range(4)]
    Rvn4 = [R("vn4") for _ in range(4)]
    Rvnb4 = [R("vnb4") for _ in range(4)]

    def c_stage(T, stage):
        sl = T % 2
        if stage == 0:
            for c in range(4):
                tt = 4 * T + c
                C.dve(lambda e, c=c, tt=tt: e.tensor_scalar(out=vn4[c][:], in0=UG[:, tt, 64:128], scalar1=MV[:, tt, 0:1],
                                                            scalar2=RSTD[:, tt:tt + 1], op0=ALU.subtract, op1=ALU.mult),
                      reads=[RUG[T], RMV, RRSTD], writes=[Rvn4[c]])
                C.dve(lambda e, c=c: e.tensor_tensor(out=vn4[c][:], in0=vn4[c][:], in1=lng_s[:], op=ALU.mult),
                      reads=[Rvn4[c], Rc["lng"]], writes=[Rvn4[c]])
                C.dve(lambda e, c=c: e.tensor_tensor(out=vnb4[c][:], in0=vn4[c][:], in1=lnb_s[:], op=ALU.add),
                      reads=[Rvn4[c], Rc["lnb"]], writes=[Rvnb4[c]])
        elif stage == 1:
            def mix(e):
                ins = None
                for c in range(4):
                    ins = e.matmul(PS[5][:, c * 64:(c + 1) * 64], lhsT=wmT_b[:], rhs=vnb4[c][:], start=True, stop=True)
                return ins
            C.pe(mix, reads=[Rc["wmT"]] + Rvnb4, writes=[RPS[5]])
        else:
            for c in range(4):
                tt = 4 * T + c
                C.dve(lambda e, c=c, tt=tt: e.scalar_tensor_tensor(out=YC[sl][:, c, :], in0=PS[5][:, c * 64:(c + 1) * 64],
                                                                   scalar=sgub_s[:, 0:1], in1=UG[:, tt, 0:64],
                                                                   op0=ALU.add, op1=ALU.mult),
                      reads=[RPS[5], Rc["sgub"], RUG[T]], writes=[RYC[sl]])
            C.dma("sp", y[T * 512:(T + 1) * 512, 128:192].rearrange("(c p) n -> p c n", p=128), YC[sl][:],
                  reads=[RYC[sl]], key="yc%d" % sl)

    def c_tile(T):
        for s in range(3):
            c_stage(T, s)

    if PH < 4:
        for T in range(NT):
            c_tile(T)

    if PH == 2:
        return _finish_mixer(C, nc)
    C.act(lambda e: e.activation(out=ef[:], in_=FRAW[:], func=AF.Exp, bias=negb[:, 0:1], scale=-1.0),
          reads=[RFRAW, Rc["negb"]], writes=[Ref])
    C.act(lambda e: e.activation(out=LF[:], in_=ef[:], func=AF.Ln, bias=1.0, scale=1.0), reads=[Ref], writes=[RLF])
    C.pe(lambda e: e.matmul(PS[0][:, 0:NTT], lhsT=ones_f[:], rhs=LF[:], start=True, stop=True),
         reads=[Rc["ones"], RLF], writes=[RPS[0]])
    C.dve(lambda e: e.tensor_copy(out=TOT[:], in_=PS[0][:, 0:NTT]), reads=[RPS[0]], writes=[RTOT])
    for j in range(1, NTT):
        C.dve(lambda e, j=j: e.tensor_tensor(out=PREF[:, j:j + 1], in0=PREF[:, j - 1:j], in1=TOT[:, j - 1:j], op=ALU.add),
              reads=[RPREF, RTOT], writes=[RPREF])
    C.pe(lambda e: e.matmul(PS[1][:, 0:NTT], lhsT=tri_f[:], rhs=LF[:], start=True, stop=True),
         reads=[Rc["tri_f"], RLF], writes=[RPS[1]])
    C.dve(lambda e: e.tensor_tensor(out=CP[:], in0=PS[1][:, 0:NTT], in1=PREF[:], op=ALU.add),
          reads=[RPS[1], RPREF], writes=[RCP])
    C.dve(lambda e: e.tensor_scalar(out=Z[:, 64:64 + NTT], in0=CP[:], scalar1=-8.0, scalar2=None, op0=ALU.mult),
          reads=[RCP], writes=[RZ])
    C.dve(lambda e: e.tensor_scalar(out=kbar_b[:], in0=kbar[:], scalar1=1.0 / 256.0, scalar2=None, op0=ALU.mult),
          reads=[Rkbar], writes=[Rkbarb])
    GM4 = [sb("GM4_%d" % i, [128, 32], F32) for i in range(4)]
    top4 = [sb("top4_%d" % i, [128, 8], F32) for i in range(4)]
    RGM4 = [R("gm4") for _ in range(4)]
    Rtop4 = [R("top4") for _ in range(4)]
    for i in range(4):
        C.dve(lambda e, i=i: e.memset(GM4[i][:], NEGF), writes=[RGM4[i]])

    def p2_stage(T, stage):
        if stage == 0:
            def tr4(e):
                ins = None
                for c in range(4):
                    tt = 4 * T + c
                    ins = e.matmul(PS[5][0:65, c * 128:(c + 1) * 128], lhsT=Z[:, tt:tt + 65], rhs=ident_f[:],
                                   start=True, stop=True)
                return ins
            C.pe(tr4, reads=[RZ, Rc["ident"]], writes=[RPS[5]])

            def gates(e):
                ins = None
                for c in range(4):
                    tt = 4 * T + c
                    ins = e.matmul(PS[6][:, c * 32:(c + 1) * 32], lhsT=QA[0:64, tt * 128:(tt + 1) * 128], rhs=kbar_b[:],
                                   start=True, stop=True)
                return ins
            C.pe(gates, reads=[RQA[T], Rkbarb], writes=[RPS[6]])
        elif stage == 1:
            C.act(lambda e: e.copy(out=QB[64:65, T * 512:(T + 1) * 512], in_=PS[5][64:65, :]),
                  reads=[RPS[5]], writes=[RQBc[T]])
            for c in range(4):
                tt = 4 * T + c
                b = tt // 2
                if b == 0:
                    continue
                C.dve(lambda e, c=c, b=b: e.tensor_copy(out=GM4[c][:, 0:b], in_=PS[6][:, c * 32:c * 32 + b]),
                      reads=[RPS[6]], writes=[RGM4[c]])
                C.dve(lambda e, c=c: e.max(out=top4[c][:], in_=GM4[c][:]), reads=[RGM4[c]], writes=[Rtop4[c]])
                C.dve(lambda e, c=c, tt=tt, b=b: e.tensor_scalar(out=MBZ[:, 64 + 32 * tt:64 + 32 * tt + b], in0=GM4[c][:, 0:b],
                                                                 scalar1=top4[c][:, 2:3], scalar2=1.0,
                                                                 op0=ALU.is_ge, op1=ALU.subtract),
                      reads=[RGM4[c], Rtop4[c]], writes=[RMBZ[T]])
        elif stage == 2:
            def trm(e):
                ins = None
                for c in range(4):
                    tt = 4 * T + c
                    ins = e.matmul(PS[7][0:96, c * 128:(c + 1) * 128], lhsT=MBZ[:, 32 * tt:32 * tt + 96], rhs=ident_f[:],
                                   start=True, stop=True)
                return ins
            C.pe(trm, reads=[RMBZ[T], Rc["ident"]], writes=[RPS[7]])
        else:
            C.act(lambda e: e.copy(out=QA[64:96, T * 512:(T + 1) * 512], in_=PS[7][64:96, :]),
                  reads=[RPS[7]], writes=[RQAm[T]])

    def p2_tile(T):
        for s in range(4):
            p2_stage(T, s)

    if PH < 4:
        for T in range(NT):
            p2_tile(T)
    else:
        p2_tile(0)

    if PH == 3:
        return _finish_mixer(C, nc)
    ROacc = [[RPS[3]] * 4, [RPS[4]] * 4]
    LOOK = 2

    def att_score(p):
        qi, att, kj, oi, idx = p
        Q, K = (QA, KA) if att == 0 else (QB, KB)
        RQ, RQx, RK = (RQA, RQAm, RKA) if att == 0 else (RQB, RQBc, RKB)
        rows = 96 if att == 0 else 65
        d = kj - 4 * qi
        off = 128 * max(d, 0)
        n = 512 - off
        sbk = idx % 3
        pt = idx % 4
        C.pe(lambda e: e.matmul(PS[sbk][:, 0:n], lhsT=K[0:rows, kj * 128:(kj + 1) * 128],
                                rhs=Q[0:rows, qi * 512 + off:(qi + 1) * 512], start=True, stop=True),
             reads=[RQ[qi], RQx[qi], RK[kj // 4]] + ([Rc["kaoh"]] if att == 0 else []), writes=[RPS[sbk]])
        if att == 0:
            C.act(lambda e: e.activation(out=PT[pt][:, 0:n], in_=PS[sbk][:, 0:n], func=AF.Exp, scale=0.125),
                  reads=[RPS[sbk]], writes=[RPT[pt]])
        else:
            C.act(lambda e: e.activation(out=PT[pt][:, 0:n], in_=PS[sbk][:, 0:n], func=AF.Exp,
                                         bias=CP[:, kj:kj + 1], scale=0.125),
                  reads=[RPS[sbk], RCP], writes=[RPT[pt]])
        if d >= 0:
            C.dve(lambda e: e.tensor_tensor(out=PT[pt][:, 0:128], in0=PT[pt][:, 0:128], in1=tri_b[:], op=ALU.mult),
                  reads=[RPT[pt], Rc["tri_b"]], writes=[RPT[pt]])

    def att_pv(p):
        qi, att, kj, oi, idx = p
        V = VA if att == 0 else VB
        RV = RVA if att == 0 else RVB
        ysl = qi % 2
        ob = 3 + (oi % 2)
        Oacc = PS[ob][:, 0:260].rearrange("p (c n) -> p c n", n=65)
        RO = ROacc[ob - 3]
        d = kj - 4 * qi
        off = 128 * max(d, 0)
        pt = idx % 4
        c0 = max(d, 0)

        def pv(e):
            ins = None
            for c in range(c0, 4):
                ins = e.matmul(Oacc[:, c, :], lhsT=PT[pt][:, c * 128 - off:(c + 1) * 128 - off], rhs=V[:, kj, :],
                               start=(kj == 0 and c == 0), stop=(kj == 4 * qi + c), skip_group_check=True)
            return ins
        C.pe(pv, reads=[RPT[pt], RV[kj // 4]], writes=[RO[0]])
        if d >= 0:
            c = d
            r4 = (2 * (oi % 2) + (c % 2))
            C.dve(lambda e: e.reciprocal(out=rl[r4][:], in_=Oacc[:, c, 64:65]), reads=[RO[c]], writes=[Rrl[r4]])
            C.dve(lambda e: e.tensor_scalar(out=YAB[ysl][:, c, att * 64:(att + 1) * 64], in0=Oacc[:, c, 0:64],
                                            scalar1=rl[r4][:, 0:1], scalar2=None, op0=ALU.mult),
                  reads=[RO[c], Rrl[r4]], writes=[RYAB[ysl]])
        if att == 1 and d == 3:
            C.dma("sp", y[qi * 512:(qi + 1) * 512, 0:128].rearrange("(c p) n -> p c n", p=128), YAB[ysl][:],
                  reads=[RYAB[ysl]], key="yab%d" % ysl)

    plist = []
    oi = 0
    for qi in range(NT):
        for att in range(2):
            for kj in range(4 * qi + 4):
                plist.append((qi, att, kj, oi, len(plist)))
            oi += 1
    for i in range(len(plist) + LOOK):
        if i < len(plist):
            qi_, att_, kj_ = plist[i][0], plist[i][1], plist[i][2]
            n_ = 4 * qi_ + 4
            if att_ == 0 and qi_ + 1 < NT:
                offs = [0, n_ // 4, n_ // 2, (3 * n_) // 4]
                if kj_ in offs:
                    p2_stage(qi_ + 1, offs.index(kj_))
            if att_ == 1:
                offs = [0, n_ // 3, (2 * n_) // 3]
                if kj_ in offs:
                    c_stage(qi_, offs.index(kj_))
            att_score(plist[i])
        if i - LOOK >= 0:
            att_pv(plist[i - LOOK])

    if os.environ.get("MIX_DUMP"):
        allres = RQA + RQAm + RKA + RQB + RQBc + RKB + RVA + RVB + RUG + [RCP, RLF, RFRAW, RMV, RRSTD, Rkbar, Rkbarb, RZ, Rc["kaoh"]] + RMBZ + Rpwb + [Rc["band"], Rc["tri_b"], Rc["poolw"]]
        for nm, t, shp, dty in (("d_cp", CP, [128, NTT], F32), ("d_lf", LF, [128, NTT], F32), ("d_fraw", FRAW, [128, NTT], F32),
                                ("d_rstd", RSTD, [128, NTT], F32),
                                ("d_ug", UG, [128, NTT, 128], BF16), ("d_qa", QA, [128, SEQ], BF16), ("d_ka", KA, [128, SEQ], BF16),
                                ("d_qb", QB, [128, SEQ], BF16), ("d_kb", KB, [128, SEQ], BF16), ("d_va", VA, [128, NTT, 65], BF16),
                                ("d_vb", VB, [128, NTT, 65], BF16), ("d_kbar", kbar, [64, 32], F32), ("d_mbz", MBZ, [128, 64 + NTT * 32], F32),
                                ("d_pwb0", pwb[0], [128, 64], BF16), ("d_band", band_b, [128, 3, 128], BF16), ("d_trib", tri_b, [128, 128], BF16)):
            dd = dt(nm, shp, dty, kind="ExternalOutput").ap()
            C.dma("sp", dd, t[:], reads=allres, key=nm)

    return _finish_mixer(C, nc)


def _finish_mixer(C, nc):
    finals = []
    for k, cnt in C.S.dcount.items():
        finals.append((k, 16 * cnt, "dma"))
    C.S.emit(finals)
    C.st.close()
    return nc


def _rot_tables():
    pos = np.arange(S, dtype=np.float32)
    inv_freq = (np.float32(500000.0) ** (-np.arange(0, 16, 2, dtype=np.float32) / np.float32(16))).astype(np.float32)
    ang = (pos[:, None] * inv_freq[None, :]).astype(np.float32)
    cos = np.cos(ang).astype(np.float32).T
    sin = np.sin(ang).astype(np.float32).T
    cs = np.zeros((2, 16, S), np.float32)
    cs[0, 0:8] = cos
    cs[0, 8:16] = cos
    cs[1, 0:8] = -sin
    cs[1, 8:16] = sin
    return cs


def _band_mats(win):
    t = np.arange(128)
    s = np.arange(128)
    cur = ((t[None, :] - s[:, None] >= 0) & (t[None, :] - s[:, None] < win)).astype(np.float32) / win
    cur -= np.eye(128, dtype=np.float32)
    prev = ((t[None, :] + 128 - s[:, None]) < win).astype(np.float32) / win
    cnt = np.minimum(t + 1, win).astype(np.float32)
    first = ((t[None, :] - s[:, None] >= 0) & (t[None, :] - s[:, None] < win)).astype(np.float32) / cnt[None, :]
    first -= np.eye(128, dtype=np.float32)
    return np.stack([cur, prev, first]).astype(np.float32)


_CACHE = {}


def _mixer_inputs(xT_b, l, h, P):
    w_in = P["w_in"][l]
    a0 = 0
    qa = w_in[:, 0 + 64 * h:0 + 64 * h + 64]
    ka = w_in[:, 256 + 64 * h:256 + 64 * h + 64]
    va = w_in[:, 512 + 64 * h:512 + 64 * h + 64]
    qb = w_in[:, 768 + 64 * h:768 + 64 * h + 64]
    kb = w_in[:, 1024 + 64 * h:1024 + 64 * h + 64]
    vb = w_in[:, 1280 + 64 * h:1280 + 64 * h + 64]
    fb = w_in[:, 1536 + h:1536 + h + 1]
    cu = w_in[:, 1540 + 64 * h:1540 + 64 * h + 64]
    cv = w_in[:, 1796 + 64 * h:1796 + 64 * h + 64]
    dp = w_in[:, 2052 + 64 * h:2052 + 64 * h + 64]
    perm = np.concatenate([np.arange(8, 16), np.arange(0, 8)])
    wfm = np.concatenate([qa, ka, qb, kb, dp, qa[:, perm], ka[:, perm]], axis=1)
    wtm = np.concatenate([va, vb, cu, cv, fb], axis=1)
    rep = lambda v: np.ascontiguousarray(np.broadcast_to(v[None, :], (128, v.shape[0]))).astype(np.float32)
    return {
        "xT": xT_b,
        "wfm": np.ascontiguousarray(wfm), "wtm": np.ascontiguousarray(wtm),
        "cs": _CACHE["cs"], "onehot": _CACHE["onehot"],
        "sguT": np.ascontiguousarray(P["sgu_w"][l, h].T), "tri": _CACHE["tri"], "ident": _CACHE["ident"],
        "sgub": np.ascontiguousarray(P["sgu_b"][l, h].reshape(128, 1)),
        "lng": rep(P["sgu_ln_g"][l, 64 * h:64 * h + 64]), "lnb": rep(P["sgu_ln_b"][l, 64 * h:64 * h + 64]),
        "poolw": np.ascontiguousarray(P["pool_w"][l, h]), "pscale": rep(P["pool_scale"][l, 64 * h:64 * h + 64]),
        "band": _CACHE["band"][h],
        "bfor": np.full((128, 1), P["b_forget"][l, h], np.float32),
    }


def _consts():
    if "cs" in _CACHE:
        return
    _CACHE["cs"] = _rot_tables()
    oh = np.zeros((32, S), np.float32)
    for n in range(32):
        oh[n, n * 256:(n + 1) * 256] = BIGM
    _CACHE["onehot"] = oh.astype(ml_dtypes.bfloat16)
    sidx = np.arange(128)
    _CACHE["tri"] = (sidx[:, None] <= sidx[None, :]).astype(np.float32)
    _CACHE["ident"] = np.eye(128, dtype=np.float32)
    _CACHE["band"] = [_band_mats(w) for w in (2, 4, 8, 16)]


def run_mixer(xT_all, l, P):
    _consts()
    if "nc_m" not in _CACHE:
        _CACHE["nc_m"] = build_mixer()
    in_maps = [_mixer_inputs(xT_all[c // 4], l, c % 4, P) for c in range(8)]
    res = run_bass_kernel_spmd(_CACHE["nc_m"], in_maps, core_ids=list(range(8)))
    y = np.zeros((NB, S, 1024), ml_dtypes.bfloat16)
    for c in range(8):
        b, h = c // 4, c % 4
        yy = np.asarray(res.results[c]["y"]).view(ml_dtypes.bfloat16).reshape(S, 256) if res.results[c]["y"].dtype != ml_dtypes.bfloat16 else res.results[c]["y"]
        for m in range(4):
            y[b, :, 256 * m + 64 * h:256 * m + 64 * h + 64] = yy[:, 64 * m:64 * m + 64]
    return y


NTOK = 2048
NCH = 44


def build_post(NT4=4):
    nc = bass.Bass("TRN2", target_bir_lowering=False)
    C = Ctx(nc)
    dt = nc.dram_tensor
    NTK = NT4 * 512
    yin = dt("yin", [128 + NTK, D], BF16, kind="ExternalInput").ap()
    xin = dt("xin", [128 + NTK, D], F32, kind="ExternalInput").ap()
    flag = dt("flag", [128, 1], F32, kind="ExternalInput").ap()
    wo = dt("wo", [D, D], F32, kind="ExternalInput").ap()
    wup = dt("wup", [NCH, 128, 8, 128], F32, kind="ExternalInput").ap()
    wdn = dt("wdn", [DFF, D], F32, kind="ExternalInput").ap()
    convw = dt("convw", [128, NCH, 3], F32, kind="ExternalInput").ap()
    convb = dt("convb", [128, NCH], F32, kind="ExternalInput").ap()
    lnp = dt("lnp", [4, 128, D], F32, kind="ExternalInput").ap()
    ident = dt("ident", [128, 128], F32, kind="ExternalInput").ap()
    xo = dt("xo", [NTK, D], F32, kind="ExternalOutput").ap()

    sb, ps, res = C.sb, C.ps, C.res
    wd_b = sb("wd_b", [128, 22, D], BF16)
    wo_b = sb("wo_b", [128, 8, D], BF16)
    A_T = sb("A_T", [128, 22, 512], BF16)
    X1T = [sb("X1T%d" % i, [128, 8, 512], BF16) for i in range(2)]
    X1Th = sb("X1Th", [128, 8, 2], BF16)
    X1 = sb("X1", [128, 4, D], F32)
    wu = [[sb("wu%d_%d" % (i, j), [128, 8, 128], BF16) for j in range(2)] for i in range(3)]
    H = [sb("H%d" % i, [128, 514], F32) for i in range(2)]
    tg = [sb("tg%d" % i, [128, 512], F32) for i in range(2)]
    tv = [sb("tv%d" % i, [128, 512], F32) for i in range(2)]
    sg = [sb("sg%d" % i, [128, 512], BF16) for i in range(2)]
    HALO = sb("HALO", [128, NCH, 2], F32)
    lnp_s = sb("lnp_s", [128, 4, D], F32)
    yt = [sb("yt%d" % i, [128, D], BF16) for i in range(2)]
    xt = [sb("xt%d" % i, [128, D], F32) for i in range(2)]
    rr = sb("rr", [128, D], F32)
    x1b = sb("x1b", [128, D], BF16)
    yT = sb("yT", [128, 8, 128], BF16)
    x2 = [sb("x2_%d" % i, [128, D], F32) for i in range(2)]
    stats = sb("stats", [128, 12], F32)
    mv = sb("mv", [128, 2], F32)
    sd = sb("sd", [128, 1], F32)
    rstd = sb("rstd", [128, 1], F32)
    cw_s = sb("cw_s", [128, NCH, 3], F32)
    cb_s = sb("cb_s", [128, NCH], F32)
    flag_s = sb("flag_s", [128, 1], F32)
    ident_b = sb("ident_b", [128, 128], BF16)
    PS = [ps("ps%d" % i, [128, 512], F32) for i in range(7)]
    PSB = ps("psb", [128, 1024], BF16)
    RPS = [res("ps", True) for _ in range(7)]
    RPSB = res("psb", True)
    R = lambda n: res(n)
    Rwd, Rwo, RAT, RX1, RX1Th, RHALO, Rlnp, Rrr, Rx1b, RyT = (R("wd"), R("wo"), R("at"), R("x1"), R("x1th"), R("halo"),
                                                             R("lnp"), R("rr"), R("x1b"), R("yT"))
    RX1T = [R("x1t") for _ in range(2)]
    Rwu = [[R("wu") for _ in range(2)] for _ in range(3)]
    RH = [R("h") for _ in range(3)]
    RHh = [R("hh") for _ in range(3)]
    RHALOc = [R("haloc") for _ in range(NCH)]
    Rtg = [R("tg") for _ in range(2)]
    Rtv = [R("tv") for _ in range(2)]
    Rsg = [R("sg") for _ in range(2)]
    Ryt = [R("yt") for _ in range(2)]
    Rxt = [R("xt") for _ in range(2)]
    Rx2 = [R("x2") for _ in range(2)]
    Rst, Rmv, Rsd, Rrstd, Rcw, Rcb, Rflag, Rid = (R("st"), R("mv"), R("sd"), R("rstd"), R("cw"), R("cb"), R("flag"), R("id"))

    C.dma("pool", ident_b[:], ident, writes=[Rid])
    wo_v = wo.rearrange("(kc f) n -> f kc n", f=128)
    for kc in range(0, 8, 4):
        C.dma("pool", wo_b[:, kc:kc + 4, :], wo_v[:, kc:kc + 4, :], writes=[Rwo], key="wo")
    C.dma("sp", lnp_s[:], lnp.rearrange("k p n -> p k n"), writes=[Rlnp])
    C.dma("sp", cw_s[:], convw, writes=[Rcw])
    C.dma("sp", cb_s[:], convb, writes=[Rcb])
    C.dma("sp", flag_s[:], flag, writes=[Rflag])

    def load_sub(i):
        sl = i % 2
        C.dma("sp", yt[sl][:], yin[i * 128:(i + 1) * 128, :], writes=[Ryt[sl]])
        C.dma("sp", xt[sl][:], xin[i * 128:(i + 1) * 128, :], writes=[Rxt[sl]])

    def layer_norm(src, Rsrc, dst, Rdst, gi):
        def st(e):
            e.bn_stats(out=stats[:, 0:6], in_=src[:, 0:512])
            return e.bn_stats(out=stats[:, 6:12], in_=src[:, 512:1024])
        C.dve(st, reads=[Rsrc], writes=[Rst])
        C.dve(lambda e: e.bn_aggr(out=mv[:], in_=stats[:]), reads=[Rst], writes=[Rmv])
        C.dve(lambda e: e.tensor_scalar(out=sd[:], in0=mv[:, 1:2], scalar1=EPS, scalar2=None, op0=ALU.add),
              reads=[Rmv], writes=[Rsd])
        C.act(lambda e: e.activation(out=sd[:], in_=sd[:], func=AF.Sqrt), reads=[Rsd], writes=[Rsd])
        C.dve(lambda e: e.reciprocal(out=rstd[:], in_=sd[:]), reads=[Rsd], writes=[Rrstd])
        C.dve(lambda e: e.tensor_scalar(out=src[:], in0=src[:], scalar1=mv[:, 0:1], scalar2=rstd[:, 0:1],
                                        op0=ALU.subtract, op1=ALU.mult), reads=[Rsrc, Rmv, Rrstd], writes=[Rsrc])
        C.dve(lambda e: e.tensor_tensor(out=src[:], in0=src[:], in1=lnp_s[:, gi, :], op=ALU.mult),
              reads=[Rsrc, Rlnp], writes=[Rsrc])
        C.dve(lambda e: e.tensor_tensor(out=dst, in0=src[:], in1=lnp_s[:, gi + 1, :], op=ALU.add),
              reads=[Rsrc, Rlnp], writes=[Rdst])

    def stage_a1(i):
        sl = i % 2
        if i + 1 <= NT4 * 4:
            load_sub(i + 1)

        def tr_y(e):
            ins = None
            for kc in range(8):
                ins = e.transpose(PSB[:, kc * 128:(kc + 1) * 128], yt[sl][:, kc * 128:(kc + 1) * 128], ident_b[:])
            return ins
        C.pe(tr_y, reads=[Ryt[sl], Rid], writes=[RPSB])
        C.act(lambda e: e.copy(out=yT[:].rearrange("p k n -> p (k n)"), in_=PSB[:]), reads=[RPSB], writes=[RyT])
        for half in range(2):
            C.pe(_mm_group(PS[half][:], [(yT[:, kc, :], wo_b[:, kc, half * 512:(half + 1) * 512]) for kc in range(8)]),
                 reads=[RyT, Rwo], writes=[RPS[half]])
        for half in range(2):
            C.dve(lambda e, half=half: e.scalar_tensor_tensor(out=rr[:, half * 512:(half + 1) * 512], in0=xt[sl][:, half * 512:(half + 1) * 512],
                                                              scalar=ALPHA, in1=PS[half][:], op0=ALU.mult, op1=ALU.add),
                  reads=[Rxt[sl], RPS[half]], writes=[Rrr])
        if i == 0:
            layer_norm(rr, Rrr, x2[0][:], Rx2[0], 0)
        else:
            c = (i - 1) % 4
            layer_norm(rr, Rrr, X1[:, c, :], RX1, 0)

    def stage_a2(i):
        if i == 0:
            x1src, Rx1src = x2[0][:], Rx2[0]
        else:
            c = (i - 1) % 4
            x1src, Rx1src = X1[:, c, :], RX1
        C.act(lambda e: e.copy(out=x1b[:], in_=x1src), reads=[Rx1src], writes=[Rx1b])

        def tr_x(e):
            ins = None
            for kc in range(8):
                ins = e.transpose(PSB[:, kc * 128:(kc + 1) * 128], x1b[:, kc * 128:(kc + 1) * 128], ident_b[:])
            return ins
        C.pe(tr_x, reads=[Rx1b, Rid], writes=[RPSB])
        psv = PSB[:].rearrange("p (k n) -> p k n", n=128)
        if i == 0:
            C.act(lambda e: e.copy(out=X1Th[:], in_=psv[:, :, 126:128]), reads=[RPSB], writes=[RX1Th])
        else:
            T = (i - 1) // 4
            c = (i - 1) % 4
            C.act(lambda e: e.copy(out=X1T[T % 2][:, :, c * 128:(c + 1) * 128], in_=psv), reads=[RPSB], writes=[RX1T[T % 2]])

    wup_loaded = {}

    def load_wup(T, cc):
        sl = (T * 22 + cc) % 3
        if os.environ.get("P_NOLOAD") and T > 0:
            return
        C.dma("pool", wu[sl][0][:], wup[cc], writes=[Rwu[sl][0]])
        C.dma("pool", wu[sl][1][:], wup[22 + cc], writes=[Rwu[sl][1]])

    state = {"k": 0}

    def ffn_chunk(T, cc, which):
        ch = cc + 22 * which
        wsl = (T * 22 + cc) % 3
        k = state["k"]
        state["k"] += 1
        hb = k % 2
        bank = (2, 3)[hb]
        xs = X1T[T % 2]
        if T == 0:
            C.pe(_mm_group(PS[4][:, 0:2], [(wu[wsl][which][:, kc, :], X1Th[:, kc, :]) for kc in range(8)]),
                 reads=[Rwu[wsl][which], RX1Th], writes=[RPS[4]])
            C.act(lambda e: e.activation(out=H[hb][:, 0:2], in_=PS[4][:, 0:2], func=AF.Copy, scale=flag_s[:, 0:1]),
                  reads=[RPS[4], Rflag], writes=[RHh[hb]])
        else:
            C.pool(lambda e: e.tensor_copy(out=H[hb][:, 0:2], in_=HALO[:, ch, :]), reads=[RHALOc[ch]], writes=[RHh[hb]])
        C.pe(_mm_group(PS[bank][:], [(wu[wsl][which][:, kc, :], xs[:, kc, :]) for kc in range(8)]),
             reads=[Rwu[wsl][which], RX1T[T % 2]], writes=[RPS[bank]])
        C.act(lambda e: e.copy(out=H[hb][:, 2:514], in_=PS[bank][:]), reads=[RPS[bank]], writes=[RH[hb]])
        C.pool(lambda e: e.tensor_copy(out=HALO[:, ch, :], in_=H[hb][:, 512:514]), reads=[RH[hb]], writes=[RHALOc[ch]])
        t, Rt = (tg[cc % 2], Rtg[cc % 2]) if which == 0 else (tv[cc % 2], Rtv[cc % 2])
        C.act(lambda e: e.activation(out=t[:], in_=H[hb][:, 0:512], func=AF.Identity, bias=cb_s[:, ch:ch + 1],
                                     scale=cw_s[:, ch, 0:1]), reads=[RH[hb], RHh[hb], Rcw, Rcb], writes=[Rt])
        C.dve(lambda e: e.scalar_tensor_tensor(out=t[:], in0=H[hb][:, 1:513], scalar=cw_s[:, ch, 1:2], in1=t[:],
                                               op0=ALU.mult, op1=ALU.add), reads=[RH[hb], RHh[hb], Rcw, Rt], writes=[Rt])
        C.dve(lambda e: e.scalar_tensor_tensor(out=t[:], in0=H[hb][:, 2:514], scalar=cw_s[:, ch, 2:3], in1=t[:],
                                               op0=ALU.mult, op1=ALU.add), reads=[RH[hb], Rcw, Rt], writes=[Rt])

    def ffn_pair(T, cc):
        if T * 22 + cc + 2 < NT4 * 22:
            nT, ncc = divmod(T * 22 + cc + 2, 22)
            load_wup(nT, ncc)
        ffn_chunk(T, cc, 0)
        ffn_chunk(T, cc, 1)
        s2 = cc % 2
        C.act(lambda e: e.activation(out=sg[s2][:], in_=tg[s2][:], func=AF.Silu), reads=[Rtg[s2]], writes=[Rsg[s2]])
        C.pool(lambda e: e.tensor_tensor(out=A_T[:, cc, :], in0=sg[s2][:], in1=tv[s2][:], op=ALU.mult),
               reads=[Rsg[s2], Rtv[s2]], writes=[RAT])

    def stage_c(T, c):
        j = T * 4 + c
        banks = (5, 6) if j % 2 == 0 else (0, 1)
        for half in range(2):
            C.pe(_mm_group(PS[banks[half]][:], [(A_T[:, cc, c * 128:(c + 1) * 128], wd_b[:, cc, half * 512:(half + 1) * 512])
                                                 for cc in range(22)]), reads=[RAT, Rwd], writes=[RPS[banks[half]]])
        for half in range(2):
            C.dve(lambda e, half=half: e.scalar_tensor_tensor(out=rr[:, half * 512:(half + 1) * 512], in0=X1[:, c, half * 512:(half + 1) * 512],
                                                              scalar=ALPHA, in1=PS[banks[half]][:], op0=ALU.mult, op1=ALU.add),
                  reads=[RX1, RPS[banks[half]]], writes=[Rrr])
        o = j % 2
        layer_norm(rr, Rrr, x2[o][:], Rx2[o], 2)
        C.dma("sp", xo[j * 128:(j + 1) * 128, :], x2[o][:], reads=[Rx2[o]], key="xo%d" % o)

    load_sub(0)
    load_wup(0, 0)
    load_wup(0, 1)
    wd_v = wdn.rearrange("(cc p) n -> p cc n", p=128)
    stage_a1(0)
    for T in range(NT4):
        i0_ = 1 + 4 * T
        stage_a1(i0_)
        if T == 0:
            stage_a2(0)
        for c in range(1, 4):
            stage_a1(i0_ + c)
            stage_a2(i0_ + c - 1)
        stage_a2(i0_ + 3)
        if T == 0:
            for c0 in range(0, 22, 2):
                C.dma("pool", wd_b[:, c0:c0 + 2, :], wd_v[:, c0:c0 + 2, :], writes=[Rwd], key="wd")
        for cc in range(22):
            ffn_pair(T, cc)
        for c in range(4):
            stage_c(T, c)
    return _finish_mixer(C, nc)


def _post_inputs(y_b, x_b, q, l, P):
    t0 = q * NTOK
    if q == 0:
        yh = np.zeros((128, D), ml_dtypes.bfloat16)
        xh = np.zeros((128, D), np.float32)
    else:
        yh = y_b[t0 - 128:t0]
        xh = x_b[t0 - 128:t0]
    rep = lambda v: np.broadcast_to(v[None, :], (128, v.shape[0]))
    key = ("post_w", l)
    if key not in _CACHE:
        w_up = P["w_up"][l]
        _CACHE[key] = {
            "wo": np.ascontiguousarray(P["w_o"][l]),
            "wup": np.ascontiguousarray(w_up.reshape(8, 128, NCH, 128).transpose(2, 1, 0, 3)),
            "wdn": np.ascontiguousarray(P["w_down"][l]),
            "convw": np.ascontiguousarray(P["conv_w"][l].reshape(3, NCH, 128).transpose(2, 1, 0)),
            "convb": np.ascontiguousarray(P["conv_b"][l].reshape(NCH, 128).T),
            "lnp": np.ascontiguousarray(np.stack([rep(P["ln1_g"][l]), rep(P["ln1_b"][l]), rep(P["ln2_g"][l]), rep(P["ln2_b"][l])]).astype(np.float32)),
            "ident": np.eye(128, dtype=np.float32),
        }
    m = dict(_CACHE[key])
    m["yin"] = np.ascontiguousarray(np.concatenate([yh, y_b[t0:t0 + NTOK]], axis=0))
    m["xin"] = np.ascontiguousarray(np.concatenate([xh, x_b[t0:t0 + NTOK]], axis=0))
    m["flag"] = np.full((128, 1), 0.0 if q == 0 else 1.0, np.float32)
    return m


def run_post(y, x, l, P):
    if "nc_p" not in _CACHE:
        _CACHE["nc_p"] = build_post()
    in_maps = [_post_inputs(y[c // 4], x[c // 4], c % 4, l, P) for c in range(8)]
    res = run_bass_kernel_spmd(_CACHE["nc_p"], in_maps, core_ids=list(range(8)))
    out = np.zeros((NB, S, D), np.float32)
    for c in range(8):
        out[c // 4, (c % 4) * NTOK:(c % 4 + 1) * NTOK] = res.results[c]["xo"]
    return out


def kernel(**inputs):
    P = {k: np.asarray(v) for k, v in inputs.items()}
    x = np.ascontiguousarray(P["x"], dtype=np.float32)
    for l in range(2):
        xT = [np.ascontiguousarray(x[b].T) for b in range(NB)]
        y = run_mixer(xT, l, P)
        x = run_post(y, x, l, P)
    return x
```

```python
import contextlib
import os
DBG = int(os.environ.get('MIX_DBG', '99'))
import numpy as np
import ml_dtypes
import concourse.bass as bass
import concourse.mybir as mybir
from concourse.bass_utils import run_bass_kernel_spmd

F32 = mybir.dt.float32
BF16 = mybir.dt.bfloat16
AF = mybir.ActivationFunctionType
ALU = mybir.AluOpType
AX = mybir.AxisListType

S = 8192
D = 1024
NB = 2
DFF = 2816
ALPHA = 4.0 ** 0.25
EPS = 1e-5
BIGM = 30000.0
NEGF = -1.0e30


class Res:
    __slots__ = ("name", "w", "r", "excl")

    def __init__(self, name, excl=False):
        self.name = name
        self.w = None
        self.r = []
        self.excl = excl


class Sched:
    STREAMS = ("pe", "act", "dve", "pool", "sp")

    def __init__(self, nc):
        self.nc = nc
        self.ops = {s: [] for s in self.STREAMS}
        self.ccount = {s: 0 for s in self.STREAMS}
        self.dcount = {}
        self.known = {s: {} for s in self.STREAMS}
        self.final_events = []

    def _need(self, stream, ev, waits):
        if ev is None:
            return
        sem, val, src = ev
        if src == stream and src == "pe":
            return
        if self.known[stream].get(sem, 0) >= val:
            return
        self.known[stream][sem] = val
        waits.append((sem, val))

    def op(self, stream, fn, reads=(), writes=(), dma_key=None):
        ex = [r for r in reads if r.excl]
        if ex:
            reads = [r for r in reads if not r.excl]
            writes = list(writes) + [r for r in ex if r not in writes]
        waits = []
        for r in reads:
            self._need(stream, r.w, waits)
        for w in writes:
            self._need(stream, w.w, waits)
            for e in w.r:
                self._need(stream, e, waits)
        if dma_key is not None:
            k = "d_" + dma_key
            self.dcount[k] = self.dcount.get(k, 0) + 1
            ev = (k, 16 * self.dcount[k], "dma")
            sig = (k, 16)
        else:
            self.ccount[stream] += 1
            ev = ("c_" + stream, self.ccount[stream], stream)
            sig = ("c_" + stream, 1)
        for r in reads:
            r.r.append(ev)
        for w in writes:
            w.w = ev
            w.r = []
        self.ops[stream].append((waits, fn, sig))
        return ev

    def emit(self, final_events):
        nc = self.nc
        names = set()
        for s in self.STREAMS:
            for waits, fn, sig in self.ops[s]:
                names.add(sig[0])
                for (sem, val) in waits:
                    names.add(sem)
        with contextlib.ExitStack() as st:
            sems = {n: st.enter_context(nc.semaphore(n)) for n in sorted(names)}
            block = st.enter_context(nc.Block())

            def make(stream):
                def body(eng):
                    for waits, fn, sig in self.ops[stream]:
                        for (sem, val) in waits:
                            eng.wait_ge(sems[sem], val)
                        ins = fn(eng)
                        ins.then_inc(sems[sig[0]], sig[1])
                    if stream == "sp":
                        for (sem, val, src) in final_events:
                            eng.wait_ge(sems[sem], val)
                return body

            block.tensor(make("pe"))
            block.scalar(make("act"))
            block.vector(make("dve"))
            block.gpsimd(make("pool"))
            block.sync(make("sp"))


class Ctx:
    def __init__(self, nc):
        self.nc = nc
        self.st = contextlib.ExitStack()
        self.S = Sched(nc)
        self.nres = 0

    def sb(self, name, shape, dt):
        return self.st.enter_context(self.nc.sbuf_tensor(name, shape, dt))

    def ps(self, name, shape, dt):
        return self.st.enter_context(self.nc.psum_tensor(name, shape, dt))

    def res(self, name="r", excl=False):
        self.nres += 1
        return Res("%s%d" % (name, self.nres), excl)

    def pe(self, fn, reads=(), writes=()):
        return self.S.op("pe", fn, reads, writes)

    def act(self, fn, reads=(), writes=()):
        return self.S.op("act", fn, reads, writes)

    def dve(self, fn, reads=(), writes=()):
        return self.S.op("dve", fn, reads, writes)

    def pool(self, fn, reads=(), writes=()):
        return self.S.op("pool", fn, reads, writes)

    def dma(self, queue, out, in_, reads=(), writes=(), key=None):
        if key is None:
            key = (writes[0].name if writes else reads[0].name)
        return self.S.op(queue, lambda e: e.dma_start(out=out, in_=in_), reads, writes, dma_key=key)


def _mm_group(out, pairs):
    def fn(e):
        n = len(pairs)
        ins = None
        for i, (l, r) in enumerate(pairs):
            ins = e.matmul(out, lhsT=l, rhs=r, start=(i == 0), stop=(i == n - 1))
        return ins
    return fn


NFM = 352
NTM = 257


def build_mixer(SEQ=S, PH=9):
    NT, NTT = SEQ // 512, SEQ // 128
    nc = bass.Bass("TRN2", target_bir_lowering=False)
    C = Ctx(nc)
    dt = nc.dram_tensor
    xT = dt("xT", [D, SEQ], F32, kind="ExternalInput").ap()
    wfm = dt("wfm", [D, NFM], F32, kind="ExternalInput").ap()
    wtm = dt("wtm", [D, NTM], F32, kind="ExternalInput").ap()
    cs = dt("cs", [2, 16, SEQ], F32, kind="ExternalInput").ap()
    onehot = dt("onehot", [32, SEQ], BF16, kind="ExternalInput").ap()
    sguT = dt("sguT", [128, 128], F32, kind="ExternalInput").ap()
    tri = dt("tri", [128, 128], F32, kind="ExternalInput").ap()
    ident = dt("ident", [128, 128], F32, kind="ExternalInput").ap()
    sgub = dt("sgub", [128, 1], F32, kind="ExternalInput").ap()
    lng = dt("lng", [128, 64], F32, kind="ExternalInput").ap()
    lnb = dt("lnb", [128, 64], F32, kind="ExternalInput").ap()
    poolw = dt("poolw", [64, 64], F32, kind="ExternalInput").ap()
    pscale = dt("pscale", [128, 64], F32, kind="ExternalInput").ap()
    band = dt("band", [3, 128, 128], F32, kind="ExternalInput").ap()
    bfor = dt("bfor", [128, 1], F32, kind="ExternalInput").ap()
    y = dt("y", [SEQ, 256], BF16, kind="ExternalOutput").ap()

    sb, ps, res = C.sb, C.ps, C.res
    QA = sb("QA", [128, SEQ], BF16)
    KA = sb("KA", [128, SEQ], BF16)
    QB = sb("QB", [128, SEQ], BF16)
    KB = sb("KB", [128, SEQ], BF16)
    VA = sb("VA", [128, NTT, 65], BF16)
    VB = sb("VB", [128, NTT, 65], BF16)
    xb = [sb("xb%d" % i, [128, 8, 512], BF16) for i in range(2)]
    wfm_b = sb("wfm_b", [128, 8, NFM], BF16)
    wtm_b = sb("wtm_b", [128, 8, NTM], BF16)
    cst = [sb("cst%d" % i, [16, 2, 512], F32) for i in range(2)]
    t1 = [sb("t1_%d" % i, [16, 512], F32) for i in range(2)]
    t2 = [sb("t2_%d" % i, [16, 512], F32) for i in range(2)]
    pTb = [sb("pTb%d" % i, [64, 512], BF16) for i in range(2)]
    ug = [sb("ug%d" % i, [128, 128], F32) for i in range(2)]
    stats = [sb("stats%d" % i, [128, 6], F32) for i in range(2)]
    mv = [sb("mv%d" % i, [128, 2], F32) for i in range(2)]
    rstd = [sb("rstd%d" % i, [128, 1], F32) for i in range(2)]
    vn = [sb("vn%d" % i, [128, 64], F32) for i in range(2)]
    vnb = [sb("vnb%d" % i, [128, 64], BF16) for i in range(2)]
    pwb = [sb("pwb%d" % i, [128, 64], BF16) for i in range(2)]
    YC = [sb("YC%d" % i, [128, 4, 64], BF16) for i in range(2)]
    YD = [sb("YD%d" % i, [128, 4, 64], BF16) for i in range(2)]
    UG = sb("UG", [128, NTT, 128], BF16)
    MV = sb("MV", [128, NTT, 2], F32)
    VE = sb("VE", [128, NTT], F32)
    RSTD = sb("RSTD", [128, NTT], F32)
    YAB = [sb("YAB%d" % i, [128, 4, 128], BF16) for i in range(2)]
    MBZ = sb("MBZ", [128, 64 + NTT * 32], F32)
    GM = sb("GM", [128, 32], F32)
    top8 = [sb("top8_%d" % i, [128, 8], F32) for i in range(2)]
    FRAW = sb("FRAW", [128, NTT], F32)
    LF = sb("LF", [128, NTT], F32)
    TOT = sb("TOT", [128, NTT], F32)
    PREF = sb("PREF", [128, NTT], F32)
    CP = sb("CP", [128, NTT], F32)
    Z = sb("Z", [128, 128], F32)
    kbar = sb("kbar", [64, 32], F32)
    kbar_b = sb("kbar_b", [64, 32], BF16)
    PT = [sb("PT%d" % i, [128, 512], BF16) for i in range(4)]
    rl = [sb("rl%d" % i, [128, 1], F32) for i in range(4)]
    ident_f = sb("ident_f", [128, 128], F32)
    tri_f = sb("tri_f", [128, 128], F32)
    tri_b = sb("tri_b", [128, 128], BF16)
    ones_f = sb("ones_f", [128, 128], F32)
    sgu_f = sb("sgu_f", [128, 128], F32)
    wmT_b = sb("wmT_b", [128, 128], BF16)
    band_b = sb("band_b", [128, 3, 128], BF16)
    sgub_s = sb("sgub_s", [128, 1], F32)
    lng_s = sb("lng_s", [128, 64], F32)
    lnb_s = sb("lnb_s", [128, 64], F32)
    poolw_b = sb("poolw_b", [64, 64], BF16)
    pscale_s = sb("pscale_s", [128, 64], F32)
    bfor_s = sb("bfor_s", [128, 1], F32)
    negb = sb("negb", [128, 1], F32)
    ef = sb("ef", [128, NTT], F32)
    PS = [ps("ps%d" % i, [128, 512], F32) for i in range(8)]
    RPS = [res("ps", True) for _ in range(8)]

    R = lambda n: res(n)
    RQA = [R("qa") for _ in range(NT)]
    RQAm = [R("qam") for _ in range(NT)]
    RKA = [R("ka") for _ in range(NT)]
    RQB = [R("qb") for _ in range(NT)]
    RQBc = [R("qbc") for _ in range(NT)]
    RKB = [R("kb") for _ in range(NT)]
    RVA = [R("va") for _ in range(NT)]
    RVB = [R("vb") for _ in range(NT)]
    Rxb = [R("xb") for _ in range(2)]
    Rcs = [R("cs") for _ in range(2)]
    Rt1 = [R("t1") for _ in range(2)]
    Rt2 = [R("t2") for _ in range(2)]
    RpT = [R("pT") for _ in range(2)]
    Rug = [R("ug") for _ in range(2)]
    Rst = [R("st") for _ in range(2)]
    Rmv = [R("mv") for _ in range(2)]
    Rrs = [R("rs") for _ in range(2)]
    Rvn = [R("vn") for _ in range(2)]
    Rvnb = [R("vnb") for _ in range(2)]
    Rpwb = [R("pwb") for _ in range(2)]
    RYC = [R("yc") for _ in range(2)]
    RYD = [R("yd") for _ in range(2)]
    RUG = [R("ugall") for _ in range(NT)]
    RMV, RVE, RRSTD = R("mvall"), R("ve"), R("rstdall")
    RYAB = [R("yab") for _ in range(2)]
    RMBZ = [R("mbz") for _ in range(NT)]
    RGM = R("gm")
    Rtop = [R("top") for _ in range(2)]
    RFRAW, RLF, RTOT, RPREF, RCP, RZ, Rkbar, Rkbarb, Ref = (R("fraw"), R("lf"), R("tot"), R("pref"),
                                                            R("cp"), R("z"), R("kbar"), R("kbarb"), R("ef"))
    RPT = [R("pt") for _ in range(4)]
    Rrl = [R("rl") for _ in range(4)]
    Rc = {k: R(k) for k in ["wfm", "wtm", "ident", "tri_f", "tri_b", "ones", "sgu_f", "wmT", "band", "sgub",
                            "lng", "lnb", "poolw", "pscale", "bfor", "negb", "kaoh", "misc"]}

    C.dma("pool", wfm_b[:], wfm.rearrange("(kc f) n -> f kc n", f=128), writes=[Rc["wfm"]])
    C.dma("pool", wtm_b[:], wtm.rearrange("(kc f) n -> f kc n", f=128), writes=[Rc["wtm"]])
    C.dma("sp", ident_f[:], ident, writes=[Rc["ident"]])
    C.dma("sp", tri_f[:], tri, writes=[Rc["tri_f"]])
    C.dma("pool", tri_b[:], tri, writes=[Rc["tri_b"]])
    C.dma("sp", sgu_f[:], sguT, writes=[Rc["sgu_f"]])
    C.dma("pool", band_b[:], band.rearrange("k s t -> s k t"), writes=[Rc["band"]])
    C.dma("sp", sgub_s[:], sgub, writes=[Rc["sgub"]])
    C.dma("sp", lng_s[:], lng, writes=[Rc["lng"]])
    C.dma("sp", lnb_s[:], lnb, writes=[Rc["lnb"]])
    C.dma("pool", poolw_b[:], poolw, writes=[Rc["poolw"]])
    C.dma("sp", pscale_s[:], pscale, writes=[Rc["pscale"]])
    C.dma("sp", bfor_s[:], bfor, writes=[Rc["bfor"]])
    C.dma("sp", KA[64:96, :], onehot, writes=[Rc["kaoh"]])
    C.dve(lambda e: e.tensor_tensor(out=wmT_b[:], in0=sgu_f[:], in1=tri_f[:], op=ALU.mult),
          reads=[Rc["sgu_f"], Rc["tri_f"]], writes=[Rc["wmT"]])
    C.dve(lambda e: e.tensor_scalar(out=negb[:], in0=bfor_s[:], scalar1=-1.0, scalar2=None, op0=ALU.mult),
          reads=[Rc["bfor"]], writes=[Rc["negb"]])
    C.dve(lambda e: e.memset(ones_f[:], 1.0), writes=[Rc["ones"]])
    C.dve(lambda e: e.memset(VA[:, :, 64:65], 1.0), writes=RVA)
    C.dve(lambda e: e.memset(VB[:, :, 64:65], 1.0), writes=RVB)
    C.dve(lambda e: e.memset(KB[64:65, :], 1.0), writes=RKB)
    C.dve(lambda e: e.memset(MBZ[:], 0.0), writes=RMBZ)
    C.dve(lambda e: e.memset(GM[:], NEGF), writes=[RGM])
    C.dve(lambda e: e.memset(Z[:], 0.0), writes=[RZ])
    C.dve(lambda e: e.memset(kbar[:], 0.0), writes=[Rkbar])
    C.dve(lambda e: e.memset(PREF[:, 0:1], 0.0), writes=[RPREF])

    if PH == 0:
        return _finish_mixer(C, nc)
    xT_v = xT.rearrange("(kc f) t -> f kc t", f=128)

    def load_tile(T):
        sl = T % 2
        C.dma("pool", xb[sl][:], xT_v[:, :, T * 512:(T + 1) * 512], writes=[Rxb[sl]])
        C.dma("sp", cst[sl][:], cs[:, :, T * 512:(T + 1) * 512].rearrange("k p t -> p k t"), writes=[Rcs[sl]])

    load_tile(0)
    fm_cols = {"qA": (0, 64), "kA": (64, 64), "qB": (128, 64), "kB": (192, 64), "pD": (256, 64),
               "qAp": (320, 16), "kAp": (336, 16)}
    def p1_tile(T):
        sl = T % 2
        if T + 1 < NT:
            load_tile(T + 1)
        cols = slice(T * 512, (T + 1) * 512)

        def fm(name, bank):
            c0, n = fm_cols[name]
            C.pe(_mm_group(PS[bank][0:n, :], [(wfm_b[:, kc, c0:c0 + n], xb[sl][:, kc, :]) for kc in range(8)]),
                 reads=[Rc["wfm"], Rxb[sl]], writes=[RPS[bank]])

        def rot(nm, nmp, dst, Rdst):
            fm(nm, 0)
            fm(nmp, 1)
            if DBG == 10:
                return
            C.act(lambda e, dst=dst: e.copy(out=dst[0:64, cols], in_=PS[0][0:64, :]), reads=[RPS[0]], writes=[Rdst[T]])
            if DBG == 11:
                return
            C.dve(lambda e: e.tensor_tensor(out=t1[sl][:], in0=PS[0][0:16, :], in1=cst[sl][:, 0, :], op=ALU.mult),
                  reads=[RPS[0], Rcs[sl]], writes=[Rt1[sl]])
            C.dve(lambda e: e.tensor_tensor(out=t2[sl][:], in0=PS[1][0:16, :], in1=cst[sl][:, 1, :], op=ALU.mult),
                  reads=[RPS[1], Rcs[sl]], writes=[Rt2[sl]])
            if DBG == 12:
                return
            C.dve(lambda e, dst=dst: e.tensor_tensor(out=dst[0:16, cols], in0=t1[sl][:], in1=t2[sl][:], op=ALU.add),
                  reads=[Rt1[sl], Rt2[sl]], writes=[Rdst[T]])
        if DBG < 1:
            return
        rot("qA", "qAp", QA, RQA)
        rot("kA", "kAp", KA, RKA)
        if DBG < 2 or (10 <= DBG < 20):
            return
        C.dve(lambda e: e.tensor_reduce(out=kbar[:, 2 * T:2 * T + 2],
                                        in_=KA[0:64, cols].rearrange("p (b j) -> p b j", j=256),
                                        axis=AX.X, op=ALU.add), reads=[RKA[T]], writes=[Rkbar])
        if DBG < 3:
            return
        fm("qB", 0)
        C.act(lambda e: e.copy(out=QB[0:64, cols], in_=PS[0][0:64, :]), reads=[RPS[0]], writes=[RQB[T]])
        fm("kB", 1)
        C.act(lambda e: e.copy(out=KB[0:64, cols], in_=PS[1][0:64, :]), reads=[RPS[1]], writes=[RKB[T]])
        fm("pD", 0)
        C.act(lambda e: e.copy(out=pTb[sl][:], in_=PS[0][0:64, :]), reads=[RPS[0]], writes=[RpT[sl]])
        if DBG < 4:
            return
        def sub(c):
            tt = 4 * T + c
            s2 = tt % 2
            bk = 2 + s2
            C.pe(_mm_group(PS[bk][:, 0:NTM], [(xb[sl][:, kc, c * 128:(c + 1) * 128], wtm_b[:, kc, :]) for kc in range(8)]),
                 reads=[Rc["wtm"], Rxb[sl]], writes=[RPS[bk]])
            C.act(lambda e, bk=bk, tt=tt: e.copy(out=VA[:, tt, 0:64], in_=PS[bk][:, 0:64]), reads=[RPS[bk]], writes=[RVA[T]])
            C.act(lambda e, bk=bk, tt=tt: e.copy(out=VB[:, tt, 0:64], in_=PS[bk][:, 64:128]), reads=[RPS[bk]], writes=[RVB[T]])
            C.dve(lambda e, bk=bk, tt=tt: e.tensor_copy(out=FRAW[:, tt:tt + 1], in_=PS[bk][:, 256:257]),
                  reads=[RPS[bk]], writes=[RFRAW])
            C.act(lambda e, bk=bk, tt=tt: e.activation(out=UG[:, tt, :], in_=PS[bk][:, 128:256], func=AF.Gelu),
                  reads=[RPS[bk]], writes=[RUG[T]])
            C.dve(lambda e, tt=tt, s2=s2: e.bn_stats(out=stats[s2][:], in_=UG[:, tt, 64:128]), reads=[RUG[T]], writes=[Rst[s2]])
            C.dve(lambda e, tt=tt, s2=s2: e.bn_aggr(out=MV[:, tt, :], in_=stats[s2][:]), reads=[Rst[s2]], writes=[RMV])
            C.pe(lambda e, c=c: e.matmul(PS[5][:, 0:64], lhsT=pTb[sl][:, c * 128:(c + 1) * 128], rhs=poolw_b[:],
                                         start=True, stop=True), reads=[RpT[sl], Rc["poolw"]], writes=[RPS[5]])
            C.act(lambda e, s2=s2: e.copy(out=pwb[s2][:], in_=PS[5][:, 0:64]), reads=[RPS[5]], writes=[Rpwb[s2]])
            if tt == 0:
                C.pe(lambda e, s2=s2: e.matmul(PS[6][:, 0:64], lhsT=band_b[:, 2, :], rhs=pwb[s2][:], start=True, stop=True),
                     reads=[Rc["band"], Rpwb[s2]], writes=[RPS[6]])
            else:
                C.pe(_mm_group(PS[6][:, 0:64], [(band_b[:, 0, :], pwb[s2][:]), (band_b[:, 1, :], pwb[1 - s2][:])]),
                     reads=[Rc["band"], Rpwb[0], Rpwb[1]], writes=[RPS[6]])
            C.dve(lambda e, c=c: e.tensor_tensor(out=YD[sl][:, c, :], in0=PS[6][:, 0:64], in1=pscale_s[:], op=ALU.mult),
                  reads=[RPS[6], Rc["pscale"]], writes=[RYD[sl]])
        for c in range(4):
            sub(c)
        C.dma("sp", y[T * 512:(T + 1) * 512, 192:256].rearrange("(c p) n -> p c n", p=128), YD[sl][:],
              reads=[RYD[sl]], key="yd%d" % sl)

    for T in range(NT):
        p1_tile(T)

    if PH == 1:
        return _finish_mixer(C, nc)
    C.dve(lambda e: e.tensor_scalar(out=VE[:], in0=MV[:, :, 1], scalar1=EPS, scalar2=None, op0=ALU.add),
          reads=[RMV], writes=[RVE])
    C.act(lambda e: e.activation(out=VE[:], in_=VE[:], func=AF.Sqrt), reads=[RVE], writes=[RVE])
    C.dve(lambda e: e.reciprocal(out=RSTD[:], in_=VE[:]), reads=[RVE], writes=[RRSTD])

    vn4 = [sb("vn4_%d" % i, [128, 64], F32) for i in range(4)]
    vnb4 = [sb("vnb4_%d" % i, [128, 64], BF16) for i in range(4)]
    Rvn4 = [R("vn4") for _ in range(4)]
    Rvnb4 = [R("vnb4") for _ in range(4)]

    def c_stage(T, stage):
        sl = T % 2
        if stage == 0:
            for c in range(4):
                tt = 4 * T + c
                C.dve(lambda e, c=c, tt=tt: e.tensor_scalar(out=vn4[c][:], in0=UG[:, tt, 64:128], scalar1=MV[:, tt, 0:1],
                                                            scalar2=RSTD[:, tt:tt + 1], op0=ALU.subtract, op1=ALU.mult),
                      reads=[RUG[T], RMV, RRSTD], writes=[Rvn4[c]])
                C.dve(lambda e, c=c: e.tensor_tensor(out=vn4[c][:], in0=vn4[c][:], in1=lng_s[:], op=ALU.mult),
                      reads=[Rvn4[c], Rc["lng"]], writes=[Rvn4[c]])
                C.dve(lambda e, c=c: e.tensor_tensor(out=vnb4[c][:], in0=vn4[c][:], in1=lnb_s[:], op=ALU.add),
                      reads=[Rvn4[c], Rc["lnb"]], writes=[Rvnb4[c]])
        elif stage == 1:
            def mix(e):
                ins = None
                for c in range(4):
                    ins = e.matmul(PS[5][:, c * 64:(c + 1) * 64], lhsT=wmT_b[:], rhs=vnb4[c][:], start=True, stop=True)
                return ins
            C.pe(mix, reads=[Rc["wmT"]] + Rvnb4, writes=[RPS[5]])
        else:
            for c in range(4):
                tt = 4 * T + c
                C.dve(lambda e, c=c, tt=tt: e.scalar_tensor_tensor(out=YC[sl][:, c, :], in0=PS[5][:, c * 64:(c + 1) * 64],
                                                                   scalar=sgub_s[:, 0:1], in1=UG[:, tt, 0:64],
                                                                   op0=ALU.add, op1=ALU.mult),
                      reads=[RPS[5], Rc["sgub"], RUG[T]], writes=[RYC[sl]])
            C.dma("sp", y[T * 512:(T + 1) * 512, 128:192].rearrange("(c p) n -> p c n", p=128), YC[sl][:],
                  reads=[RYC[sl]], key="yc%d" % sl)

    def c_tile(T):
        for s in range(3):
            c_stage(T, s)

    if PH < 4:
        for T in range(NT):
            c_tile(T)

    if PH == 2:
        return _finish_mixer(C, nc)
    C.act(lambda e: e.activation(out=ef[:], in_=FRAW[:], func=AF.Exp, bias=negb[:, 0:1], scale=-1.0),
          reads=[RFRAW, Rc["negb"]], writes=[Ref])
    C.act(lambda e: e.activation(out=LF[:], in_=ef[:], func=AF.Ln, bias=1.0, scale=1.0), reads=[Ref], writes=[RLF])
    C.pe(lambda e: e.matmul(PS[0][:, 0:NTT], lhsT=ones_f[:], rhs=LF[:], start=True, stop=True),
         reads=[Rc["ones"], RLF], writes=[RPS[0]])
    C.dve(lambda e: e.tensor_copy(out=TOT[:], in_=PS[0][:, 0:NTT]), reads=[RPS[0]], writes=[RTOT])
    for j in range(1, NTT):
        C.dve(lambda e, j=j: e.tensor_tensor(out=PREF[:, j:j + 1], in0=PREF[:, j - 1:j], in1=TOT[:, j - 1:j], op=ALU.add),
              reads=[RPREF, RTOT], writes=[RPREF])
    C.pe(lambda e: e.matmul(PS[1][:, 0:NTT], lhsT=tri_f[:], rhs=LF[:], start=True, stop=True),
         reads=[Rc["tri_f"], RLF], writes=[RPS[1]])
    C.dve(lambda e: e.tensor_tensor(out=CP[:], in0=PS[1][:, 0:NTT], in1=PREF[:], op=ALU.add),
          reads=[RPS[1], RPREF], writes=[RCP])
    C.dve(lambda e: e.tensor_scalar(out=Z[:, 64:64 + NTT], in0=CP[:], scalar1=-8.0, scalar2=None, op0=ALU.mult),
          reads=[RCP], writes=[RZ])
    C.dve(lambda e: e.tensor_scalar(out=kbar_b[:], in0=kbar[:], scalar1=1.0 / 256.0, scalar2=None, op0=ALU.mult),
          reads=[Rkbar], writes=[Rkbarb])
    GM4 = [sb("GM4_%d" % i, [128, 32], F32) for i in range(4)]
    top4 = [sb("top4_%d" % i, [128, 8], F32) for i in range(4)]
    RGM4 = [R("gm4") for _ in range(4)]
    Rtop4 = [R("top4") for _ in range(4)]
    for i in range(4):
        C.dve(lambda e, i=i: e.memset(GM4[i][:], NEGF), writes=[RGM4[i]])

    def p2_stage(T, stage):
        if stage == 0:
            def tr4(e):
                ins = None
                for c in range(4):
                    tt = 4 * T + c
                    ins = e.matmul(PS[5][0:65, c * 128:(c + 1) * 128], lhsT=Z[:, tt:tt + 65], rhs=ident_f[:],
                                   start=True, stop=True)
                return ins
            C.pe(tr4, reads=[RZ, Rc["ident"]], writes=[RPS[5]])

            def gates(e):
                ins = None
                for c in range(4):
                    tt = 4 * T + c
                    ins = e.matmul(PS[6][:, c * 32:(c + 1) * 32], lhsT=QA[0:64, tt * 128:(tt + 1) * 128], rhs=kbar_b[:],
                                   start=True, stop=True)
                return ins
            C.pe(gates, reads=[RQA[T], Rkbarb], writes=[RPS[6]])
        elif stage == 1:
            C.act(lambda e: e.copy(out=QB[64:65, T * 512:(T + 1) * 512], in_=PS[5][64:65, :]),
                  reads=[RPS[5]], writes=[RQBc[T]])
            for c in range(4):
                tt = 4 * T + c
                b = tt // 2
                if b == 0:
                    continue
                C.dve(lambda e, c=c, b=b: e.tensor_copy(out=GM4[c][:, 0:b], in_=PS[6][:, c * 32:c * 32 + b]),
                      reads=[RPS[6]], writes=[RGM4[c]])
                C.dve(lambda e, c=c: e.max(out=top4[c][:], in_=GM4[c][:]), reads=[RGM4[c]], writes=[Rtop4[c]])
                C.dve(lambda e, c=c, tt=tt, b=b: e.tensor_scalar(out=MBZ[:, 64 + 32 * tt:64 + 32 * tt + b], in0=GM4[c][:, 0:b],
                                                                 scalar1=top4[c][:, 2:3], scalar2=1.0,
                                                                 op0=ALU.is_ge, op1=ALU.subtract),
                      reads=[RGM4[c], Rtop4[c]], writes=[RMBZ[T]])
        elif stage == 2:
            def trm(e):
                ins = None
                for c in range(4):
                    tt = 4 * T + c
                    ins = e.matmul(PS[7][0:96, c * 128:(c + 1) * 128], lhsT=MBZ[:, 32 * tt:32 * tt + 96], rhs=ident_f[:],
                                   start=True, stop=True)
                return ins
            C.pe(trm, reads=[RMBZ[T], Rc["ident"]], writes=[RPS[7]])
        else:
            C.act(lambda e: e.copy(out=QA[64:96, T * 512:(T + 1) * 512], in_=PS[7][64:96, :]),
                  reads=[RPS[7]], writes=[RQAm[T]])

    def p2_tile(T):
        for s in range(4):
            p2_stage(T, s)

    if PH < 4:
        for T in range(NT):
            p2_tile(T)
    else:
        p2_tile(0)

    if PH == 3:
        return _finish_mixer(C, nc)
    ROacc = [[RPS[3]] * 4, [RPS[4]] * 4]
    LOOK = 2

    def att_score(p):
        qi, att, kj, oi, idx = p
        Q, K = (QA, KA) if att == 0 else (QB, KB)
        RQ, RQx, RK = (RQA, RQAm, RKA) if att == 0 else (RQB, RQBc, RKB)
        rows = 96 if att == 0 else 65
        d = kj - 4 * qi
        off = 128 * max(d, 0)
        n = 512 - off
        sbk = idx % 3
        pt = idx % 4
        C.pe(lambda e: e.matmul(PS[sbk][:, 0:n], lhsT=K[0:rows, kj * 128:(kj + 1) * 128],
                                rhs=Q[0:rows, qi * 512 + off:(qi + 1) * 512], start=True, stop=True),
             reads=[RQ[qi], RQx[qi], RK[kj // 4]] + ([Rc["kaoh"]] if att == 0 else []), writes=[RPS[sbk]])
        if att == 0:
            C.act(lambda e: e.activation(out=PT[pt][:, 0:n], in_=PS[sbk][:, 0:n], func=AF.Exp, scale=0.125),
                  reads=[RPS[sbk]], writes=[RPT[pt]])
        else:
            C.act(lambda e: e.activation(out=PT[pt][:, 0:n], in_=PS[sbk][:, 0:n], func=AF.Exp,
                                         bias=CP[:, kj:kj + 1], scale=0.125),
                  reads=[RPS[sbk], RCP], writes=[RPT[pt]])
        if d >= 0:
            C.dve(lambda e: e.tensor_tensor(out=PT[pt][:, 0:128], in0=PT[pt][:, 0:128], in1=tri_b[:], op=ALU.mult),
                  reads=[RPT[pt], Rc["tri_b"]], writes=[RPT[pt]])

    def att_pv(p):
        qi, att, kj, oi, idx = p
        V = VA if att == 0 else VB
        RV = RVA if att == 0 else RVB
        ysl = qi % 2
        ob = 3 + (oi % 2)
        Oacc = PS[ob][:, 0:260].rearrange("p (c n) -> p c n", n=65)
        RO = ROacc[ob - 3]
        d = kj - 4 * qi
        off = 128 * max(d, 0)
        pt = idx % 4
        c0 = max(d, 0)

        def pv(e):
            ins = None
            for c in range(c0, 4):
                ins = e.matmul(Oacc[:, c, :], lhsT=PT[pt][:, c * 128 - off:(c + 1) * 128 - off], rhs=V[:, kj, :],
                               start=(kj == 0 and c == 0), stop=(kj == 4 * qi + c), skip_group_check=True)
            return ins
        C.pe(pv, reads=[RPT[pt], RV[kj // 4]], writes=[RO[0]])
        if d >= 0:
            c = d
            r4 = (2 * (oi % 2) + (c % 2))
            C.dve(lambda e: e.reciprocal(out=rl[r4][:], in_=Oacc[:, c, 64:65]), reads=[RO[c]], writes=[Rrl[r4]])
            C.dve(lambda e: e.tensor_scalar(out=YAB[ysl][:, c, att * 64:(att + 1) * 64], in0=Oacc[:, c, 0:64],
                                            scalar1=rl[r4][:, 0:1], scalar2=None, op0=ALU.mult),
                  reads=[RO[c], Rrl[r4]], writes=[RYAB[ysl]])
        if att == 1 and d == 3:
            C.dma("sp", y[qi * 512:(qi + 1) * 512, 0:128].rearrange("(c p) n -> p c n", p=128), YAB[ysl][:],
                  reads=[RYAB[ysl]], key="yab%d" % ysl)

    plist = []
    oi = 0
    for qi in range(NT):
        for att in range(2):
            for kj in range(4 * qi + 4):
                plist.append((qi, att, kj, oi, len(plist)))
            oi += 1
    for i in range(len(plist) + LOOK):
        if i < len(plist):
            qi_, att_, kj_ = plist[i][0], plist[i][1], plist[i][2]
            n_ = 4 * qi_ + 4
            if att_ == 0 and qi_ + 1 < NT:
                offs = [0, n_ // 4, n_ // 2, (3 * n_) // 4]
                if kj_ in offs:
                    p2_stage(qi_ + 1, offs.index(kj_))
            if att_ == 1:
                offs = [0, n_ // 3, (2 * n_) // 3]
                if kj_ in offs:
                    c_stage(qi_, offs.index(kj_))
            att_score(plist[i])
        if i - LOOK >= 0:
            att_pv(plist[i - LOOK])

    if os.environ.get("MIX_DUMP"):
        allres = RQA + RQAm + RKA + RQB + RQBc + RKB + RVA + RVB + RUG + [RCP, RLF, RFRAW, RMV, RRSTD, Rkbar, Rkbarb, RZ, Rc["kaoh"]] + RMBZ + Rpwb + [Rc["band"], Rc["tri_b"], Rc["poolw"]]
        for nm, t, shp, dty in (("d_cp", CP, [128, NTT], F32), ("d_lf", LF, [128, NTT], F32), ("d_fraw", FRAW, [128, NTT], F32),
                                ("d_rstd", RSTD, [128, NTT], F32),
                                ("d_ug", UG, [128, NTT, 128], BF16), ("d_qa", QA, [128, SEQ], BF16), ("d_ka", KA, [128, SEQ], BF16),
                                ("d_qb", QB, [128, SEQ], BF16), ("d_kb", KB, [128, SEQ], BF16), ("d_va", VA, [128, NTT, 65], BF16),
                                ("d_vb", VB, [128, NTT, 65], BF16), ("d_kbar", kbar, [64, 32], F32), ("d_mbz", MBZ, [128, 64 + NTT * 32], F32),
                                ("d_pwb0", pwb[0], [128, 64], BF16), ("d_band", band_b, [128, 3, 128], BF16), ("d_trib", tri_b, [128, 128], BF16)):
            dd = dt(nm, shp, dty, kind="ExternalOutput").ap()
            C.dma("sp", dd, t[:], reads=allres, key=nm)

    return _finish_mixer(C, nc)


def _finish_mixer(C, nc):
    finals = []
    for k, cnt in C.S.dcount.items():
        finals.append((k, 16 * cnt, "dma"))
    C.S.emit(finals)
    C.st.close()
    return nc


def _rot_tables():
    pos = np.arange(S, dtype=np.float32)
    inv_freq = (np.float32(500000.0) ** (-np.arange(0, 16, 2, dtype=np.float32) / np.float32(16))).astype(np.float32)
    ang = (pos[:, None] * inv_freq[None, :]).astype(np.float32)
    cos = np.cos(ang).astype(np.float32).T
    sin = np.sin(ang).astype(np.float32).T
    cs = np.zeros((2, 16, S), np.float32)
    cs[0, 0:8] = cos
    cs[0, 8:16] = cos
    cs[1, 0:8] = -sin
    cs[1, 8:16] = sin
    return cs


def _band_mats(win):
    t = np.arange(128)
    s = np.arange(128)
    cur = ((t[None, :] - s[:, None] >= 0) & (t[None, :] - s[:, None] < win)).astype(np.float32) / win
    cur -= np.eye(128, dtype=np.float32)
    prev = ((t[None, :] + 128 - s[:, None]) < win).astype(np.float32) / win
    cnt = np.minimum(t + 1, win).astype(np.float32)
    first = ((t[None, :] - s[:, None] >= 0) & (t[None, :] - s[:, None] < win)).astype(np.float32) / cnt[None, :]
    first -= np.eye(128, dtype=np.float32)
    return np.stack([cur, prev, first]).astype(np.float32)


_CACHE = {}


def _mixer_inputs(xT_b, l, h, P):
    w_in = P["w_in"][l]
    a0 = 0
    qa = w_in[:, 0 + 64 * h:0 + 64 * h + 64]
    ka = w_in[:, 256 + 64 * h:256 + 64 * h + 64]
    va = w_in[:, 512 + 64 * h:512 + 64 * h + 64]
    qb = w_in[:, 768 + 64 * h:768 + 64 * h + 64]
    kb = w_in[:, 1024 + 64 * h:1024 + 64 * h + 64]
    vb = w_in[:, 1280 + 64 * h:1280 + 64 * h + 64]
    fb = w_in[:, 1536 + h:1536 + h + 1]
    cu = w_in[:, 1540 + 64 * h:1540 + 64 * h + 64]
    cv = w_in[:, 1796 + 64 * h:1796 + 64 * h + 64]
    dp = w_in[:, 2052 + 64 * h:2052 + 64 * h + 64]
    perm = np.concatenate([np.arange(8, 16), np.arange(0, 8)])
    wfm = np.concatenate([qa, ka, qb, kb, dp, qa[:, perm], ka[:, perm]], axis=1)
    wtm = np.concatenate([va, vb, cu, cv, fb], axis=1)
    rep = lambda v: np.ascontiguousarray(np.broadcast_to(v[None, :], (128, v.shape[0]))).astype(np.float32)
    return {
        "xT": xT_b,
        "wfm": np.ascontiguousarray(wfm), "wtm": np.ascontiguousarray(wtm),
        "cs": _CACHE["cs"], "onehot": _CACHE["onehot"],
        "sguT": np.ascontiguousarray(P["sgu_w"][l, h].T), "tri": _CACHE["tri"], "ident": _CACHE["ident"],
        "sgub": np.ascontiguousarray(P["sgu_b"][l, h].reshape(128, 1)),
        "lng": rep(P["sgu_ln_g"][l, 64 * h:64 * h + 64]), "lnb": rep(P["sgu_ln_b"][l, 64 * h:64 * h + 64]),
        "poolw": np.ascontiguousarray(P["pool_w"][l, h]), "pscale": rep(P["pool_scale"][l, 64 * h:64 * h + 64]),
        "band": _CACHE["band"][h],
        "bfor": np.full((128, 1), P["b_forget"][l, h], np.float32),
    }


def _consts():
    if "cs" in _CACHE:
        return
    _CACHE["cs"] = _rot_tables()
    oh = np.zeros((32, S), np.float32)
    for n in range(32):
        oh[n, n * 256:(n + 1) * 256] = BIGM
    _CACHE["onehot"] = oh.astype(ml_dtypes.bfloat16)
    sidx = np.arange(128)
    _CACHE["tri"] = (sidx[:, None] <= sidx[None, :]).astype(np.float32)
    _CACHE["ident"] = np.eye(128, dtype=np.float32)
    _CACHE["band"] = [_band_mats(w) for w in (2, 4, 8, 16)]


def run_mixer(xT_all, l, P):
    _consts()
    if "nc_m" not in _CACHE:
        _CACHE["nc_m"] = build_mixer()
    in_maps = [_mixer_inputs(xT_all[c // 4], l, c % 4, P) for c in range(8)]
    res = run_bass_kernel_spmd(_CACHE["nc_m"], in_maps, core_ids=list(range(8)))
    y = np.zeros((NB, S, 1024), ml_dtypes.bfloat16)
    for c in range(8):
        b, h = c // 4, c % 4
        yy = np.asarray(res.results[c]["y"]).view(ml_dtypes.bfloat16).reshape(S, 256) if res.results[c]["y"].dtype != ml_dtypes.bfloat16 else res.results[c]["y"]
        for m in range(4):
            y[b, :, 256 * m + 64 * h:256 * m + 64 * h + 64] = yy[:, 64 * m:64 * m + 64]
    return y


NTOK = 2048
NCH = 44


def build_post(NT4=4):
    nc = bass.Bass("TRN2", target_bir_lowering=False)
    C = Ctx(nc)
    dt = nc.dram_tensor
    NTK = NT4 * 512
    yin = dt("yin", [128 + NTK, D], BF16, kind="ExternalInput").ap()
    xin = dt("xin", [128 + NTK, D], F32, kind="ExternalInput").ap()
    flag = dt("flag", [128, 1], F32, kind="ExternalInput").ap()
    wo = dt("wo", [D, D], F32, kind="ExternalInput").ap()
    wup = dt("wup", [NCH, 128, 8, 128], F32, kind="ExternalInput").ap()
    wdn = dt("wdn", [DFF, D], F32, kind="ExternalInput").ap()
    convw = dt("convw", [128, NCH, 3], F32, kind="ExternalInput").ap()
    convb = dt("convb", [128, NCH], F32, kind="ExternalInput").ap()
    lnp = dt("lnp", [4, 128, D], F32, kind="ExternalInput").ap()
    ident = dt("ident", [128, 128], F32, kind="ExternalInput").ap()
    xo = dt("xo", [NTK, D], F32, kind="ExternalOutput").ap()

    sb, ps, res = C.sb, C.ps, C.res
    wd_b = sb("wd_b", [128, 22, D], BF16)
    wo_b = sb("wo_b", [128, 8, D], BF16)
    A_T = sb("A_T", [128, 22, 512], BF16)
    X1T = [sb("X1T%d" % i, [128, 8, 512], BF16) for i in range(2)]
    X1Th = sb("X1Th", [128, 8, 2], BF16)
    X1 = sb("X1", [128, 4, D], F32)
    wu = [[sb("wu%d_%d" % (i, j), [128, 8, 128], BF16) for j in range(2)] for i in range(3)]
    H = [sb("H%d" % i, [128, 514], F32) for i in range(4)]
    tg = [sb("tg%d" % i, [128, 512], F32) for i in range(2)]
    tv = [sb("tv%d" % i, [128, 512], F32) for i in range(2)]
    sg = [sb("sg%d" % i, [128, 512], BF16) for i in range(2)]
    HALO = sb("HALO", [128, NCH, 2], F32)
    lnp_s = sb("lnp_s", [128, 4, D], F32)
    yt = [sb("yt%d" % i, [128, D], BF16) for i in range(2)]
    xt = [sb("xt%d" % i, [128, D], F32) for i in range(2)]
    rr = sb("rr", [128, D], F32)
    x1b = sb("x1b", [128, D], BF16)
    yT = sb("yT", [128, 8, 128], BF16)
    x2 = [sb("x2_%d" % i, [128, D], F32) for i in range(2)]
    stats = sb("stats", [128, 12], F32)
    mv = sb("mv", [128, 2], F32)
    sd = sb("sd", [128, 1], F32)
    rstd = sb("rstd", [128, 1], F32)
    cw_s = sb("cw_s", [128, NCH, 3], F32)
    cb_s = sb("cb_s", [128, NCH], F32)
    flag_s = sb("flag_s", [128, 1], F32)
    ident_b = sb("ident_b", [128, 128], BF16)
    PS = [ps("ps%d" % i, [128, 512], F32) for i in range(7)]
    PSB = ps("psb", [128, 1024], BF16)
    RPS = [res("ps", True) for _ in range(7)]
    RPSB = res("psb", True)
    R = lambda n: res(n)
    Rwd, Rwo, RAT, RX1, RX1Th, RHALO, Rlnp, Rrr, Rx1b, RyT = (R("wd"), R("wo"), R("at"), R("x1"), R("x1th"), R("halo"),
                                                             R("lnp"), R("rr"), R("x1b"), R("yT"))
    RX1T = [R("x1t") for _ in range(2)]
    Rwu = [[R("wu") for _ in range(2)] for _ in range(3)]
    RH = [R("h") for _ in range(4)]
    RHh = [R("hh") for _ in range(4)]
    RHALOc = [R("haloc") for _ in range(NCH)]
    Rtg = [R("tg") for _ in range(2)]
    Rtv = [R("tv") for _ in range(2)]
    Rsg = [R("sg") for _ in range(2)]
    Ryt = [R("yt") for _ in range(2)]
    Rxt = [R("xt") for _ in range(2)]
    Rx2 = [R("x2") for _ in range(2)]
    Rst, Rmv, Rsd, Rrstd, Rcw, Rcb, Rflag, Rid = (R("st"), R("mv"), R("sd"), R("rstd"), R("cw"), R("cb"), R("flag"), R("id"))

    C.dma("pool", ident_b[:], ident, writes=[Rid])
    wo_v = wo.rearrange("(kc f) n -> f kc n", f=128)
    for kc in range(0, 8, 4):
        C.dma("pool", wo_b[:, kc:kc + 4, :], wo_v[:, kc:kc + 4, :], writes=[Rwo], key="wo")
    C.dma("sp", lnp_s[:], lnp.rearrange("k p n -> p k n"), writes=[Rlnp])
    C.dma("sp", cw_s[:], convw, writes=[Rcw])
    C.dma("sp", cb_s[:], convb, writes=[Rcb])
    C.dma("sp", flag_s[:], flag, writes=[Rflag])

    def load_sub(i):
        sl = i % 2
        C.dma("sp", yt[sl][:], yin[i * 128:(i + 1) * 128, :], writes=[Ryt[sl]])
        C.dma("sp", xt[sl][:], xin[i * 128:(i + 1) * 128, :], writes=[Rxt[sl]])

    def layer_norm(src, Rsrc, dst, Rdst, gi):
        def st(e):
            e.bn_stats(out=stats[:, 0:6], in_=src[:, 0:512])
            return e.bn_stats(out=stats[:, 6:12], in_=src[:, 512:1024])
        C.dve(st, reads=[Rsrc], writes=[Rst])
        C.dve(lambda e: e.bn_aggr(out=mv[:], in_=stats[:]), reads=[Rst], writes=[Rmv])
        C.dve(lambda e: e.tensor_scalar(out=sd[:], in0=mv[:, 1:2], scalar1=EPS, scalar2=None, op0=ALU.add),
              reads=[Rmv], writes=[Rsd])
        C.act(lambda e: e.activation(out=sd[:], in_=sd[:], func=AF.Sqrt), reads=[Rsd], writes=[Rsd])
        C.dve(lambda e: e.reciprocal(out=rstd[:], in_=sd[:]), reads=[Rsd], writes=[Rrstd])
        C.dve(lambda e: e.tensor_scalar(out=src[:], in0=src[:], scalar1=mv[:, 0:1], scalar2=rstd[:, 0:1],
                                        op0=ALU.subtract, op1=ALU.mult), reads=[Rsrc, Rmv, Rrstd], writes=[Rsrc])
        C.dve(lambda e: e.tensor_tensor(out=src[:], in0=src[:], in1=lnp_s[:, gi, :], op=ALU.mult),
              reads=[Rsrc, Rlnp], writes=[Rsrc])
        C.dve(lambda e: e.tensor_tensor(out=dst, in0=src[:], in1=lnp_s[:, gi + 1, :], op=ALU.add),
              reads=[Rsrc, Rlnp], writes=[Rdst])

    def stage_a1(i):
        sl = i % 2
        if i + 1 <= NT4 * 4:
            load_sub(i + 1)

        def tr_y(e):
            ins = None
            for kc in range(8):
                ins = e.transpose(PSB[:, kc * 128:(kc + 1) * 128], yt[sl][:, kc * 128:(kc + 1) * 128], ident_b[:])
            return ins
        C.pe(tr_y, reads=[Ryt[sl], Rid], writes=[RPSB])
        C.act(lambda e: e.copy(out=yT[:].rearrange("p k n -> p (k n)"), in_=PSB[:]), reads=[RPSB], writes=[RyT])
        for half in range(2):
            C.pe(_mm_group(PS[half][:], [(yT[:, kc, :], wo_b[:, kc, half * 512:(half + 1) * 512]) for kc in range(8)]),
                 reads=[RyT, Rwo], writes=[RPS[half]])
        for half in range(2):
            C.dve(lambda e, half=half: e.scalar_tensor_tensor(out=rr[:, half * 512:(half + 1) * 512], in0=xt[sl][:, half * 512:(half + 1) * 512],
                                                              scalar=ALPHA, in1=PS[half][:], op0=ALU.mult, op1=ALU.add),
                  reads=[Rxt[sl], RPS[half]], writes=[Rrr])
        if i == 0:
            layer_norm(rr, Rrr, x2[0][:], Rx2[0], 0)
        else:
            c = (i - 1) % 4
            layer_norm(rr, Rrr, X1[:, c, :], RX1, 0)

    def stage_a2(i):
        if i == 0:
            x1src, Rx1src = x2[0][:], Rx2[0]
        else:
            c = (i - 1) % 4
            x1src, Rx1src = X1[:, c, :], RX1
        C.act(lambda e: e.copy(out=x1b[:], in_=x1src), reads=[Rx1src], writes=[Rx1b])

        def tr_x(e):
            ins = None
            for kc in range(8):
                ins = e.transpose(PSB[:, kc * 128:(kc + 1) * 128], x1b[:, kc * 128:(kc + 1) * 128], ident_b[:])
            return ins
        C.pe(tr_x, reads=[Rx1b, Rid], writes=[RPSB])
        psv = PSB[:].rearrange("p (k n) -> p k n", n=128)
        if i == 0:
            C.act(lambda e: e.copy(out=X1Th[:], in_=psv[:, :, 126:128]), reads=[RPSB], writes=[RX1Th])
        else:
            T = (i - 1) // 4
            c = (i - 1) % 4
            C.act(lambda e: e.copy(out=X1T[T % 2][:, :, c * 128:(c + 1) * 128], in_=psv), reads=[RPSB], writes=[RX1T[T % 2]])

    wup_loaded = {}

    def load_wup(T, cc):
        sl = (T * 22 + cc) % 3
        if os.environ.get("P_NOLOAD") and T > 0:
            return
        C.dma("pool", wu[sl][0][:], wup[cc], writes=[Rwu[sl][0]])
        C.dma("pool", wu[sl][1][:], wup[22 + cc], writes=[Rwu[sl][1]])

    state = {"k": 0}

    def ffn_chunk(T, cc, which):
        ch = cc + 22 * which
        wsl = (T * 22 + cc) % 3
        k = 2 * (T * 22 + cc) + which
        hb = k % 4
        bank = (2, 3, 5, 6)[hb]
        xs = X1T[T % 2]
        if T == 0:
            C.pe(_mm_group(PS[4][:, 0:2], [(wu[wsl][which][:, kc, :], X1Th[:, kc, :]) for kc in range(8)]),
                 reads=[Rwu[wsl][which], RX1Th], writes=[RPS[4]])
            C.act(lambda e: e.activation(out=H[hb][:, 0:2], in_=PS[4][:, 0:2], func=AF.Copy, scale=flag_s[:, 0:1]),
                  reads=[RPS[4], Rflag], writes=[RHh[hb]])
        C.pe(_mm_group(PS[bank][:], [(wu[wsl][which][:, kc, :], xs[:, kc, :]) for kc in range(8)]),
             reads=[Rwu[wsl][which], RX1T[T % 2]], writes=[RPS[bank]])
        C.act(lambda e: e.copy(out=H[hb][:, 2:514], in_=PS[bank][:]), reads=[RPS[bank]], writes=[RH[hb]])
        C.pool(lambda e: e.tensor_copy(out=HALO[:, ch, :], in_=H[hb][:, 512:514]), reads=[RH[hb]], writes=[RHALOc[ch]])
        t, Rt = (tg[cc % 2], Rtg[cc % 2]) if which == 0 else (tv[cc % 2], Rtv[cc % 2])
        C.act(lambda e: e.activation(out=t[:], in_=H[hb][:, 0:512], func=AF.Identity, bias=cb_s[:, ch:ch + 1],
                                     scale=cw_s[:, ch, 0:1]), reads=[RH[hb], RHh[hb], Rcw, Rcb], writes=[Rt])
        C.dve(lambda e: e.scalar_tensor_tensor(out=t[:], in0=H[hb][:, 1:513], scalar=cw_s[:, ch, 1:2], in1=t[:],
                                               op0=ALU.mult, op1=ALU.add), reads=[RH[hb], RHh[hb], Rcw, Rt], writes=[Rt])
        C.dve(lambda e: e.scalar_tensor_tensor(out=t[:], in0=H[hb][:, 2:514], scalar=cw_s[:, ch, 2:3], in1=t[:],
                                               op0=ALU.mult, op1=ALU.add), reads=[RH[hb], Rcw, Rt], writes=[Rt])

    def halo_read(T, cc):
        if T == 0:
            return
        for which in range(2):
            ch = cc + 22 * which
            hb = (2 * (T * 22 + cc) + which) % 4
            C.pool(lambda e, ch=ch, hb=hb: e.tensor_copy(out=H[hb][:, 0:2], in_=HALO[:, ch, :]),
                   reads=[RHALOc[ch]], writes=[RHh[hb]])

    def ffn_pair(T, cc):
        if T * 22 + cc + 1 < NT4 * 22:
            nT1, ncc1 = divmod(T * 22 + cc + 1, 22)
            halo_read(nT1, ncc1)
        if T * 22 + cc + 2 < NT4 * 22:
            nT, ncc = divmod(T * 22 + cc + 2, 22)
            load_wup(nT, ncc)
        ffn_chunk(T, cc, 0)
        ffn_chunk(T, cc, 1)
        s2 = cc % 2
        C.act(lambda e: e.activation(out=sg[s2][:], in_=tg[s2][:], func=AF.Silu), reads=[Rtg[s2]], writes=[Rsg[s2]])
        C.pool(lambda e: e.tensor_tensor(out=A_T[:, cc, :], in0=sg[s2][:], in1=tv[s2][:], op=ALU.mult),
               reads=[Rsg[s2], Rtv[s2]], writes=[RAT])

    def stage_c(T, c):
        j = T * 4 + c
        banks = (5, 6) if j % 2 == 0 else (0, 1)
        for half in range(2):
            C.pe(_mm_group(PS[banks[half]][:], [(A_T[:, cc, c * 128:(c + 1) * 128], wd_b[:, cc, half * 512:(half + 1) * 512])
                                                 for cc in range(22)]), reads=[RAT, Rwd], writes=[RPS[banks[half]]])
        for half in range(2):
            C.dve(lambda e, half=half: e.scalar_tensor_tensor(out=rr[:, half * 512:(half + 1) * 512], in0=X1[:, c, half * 512:(half + 1) * 512],
                                                              scalar=ALPHA, in1=PS[banks[half]][:], op0=ALU.mult, op1=ALU.add),
                  reads=[RX1, RPS[banks[half]]], writes=[Rrr])
        o = j % 2
        layer_norm(rr, Rrr, x2[o][:], Rx2[o], 2)
        C.dma("sp", xo[j * 128:(j + 1) * 128, :], x2[o][:], reads=[Rx2[o]], key="xo%d" % o)

    load_sub(0)
    load_wup(0, 0)
    load_wup(0, 1)
    wd_v = wdn.rearrange("(cc p) n -> p cc n", p=128)
    stage_a1(0)
    for T in range(NT4):
        i0_ = 1 + 4 * T
        stage_a1(i0_)
        if T == 0:
            stage_a2(0)
        for c in range(1, 4):
            stage_a1(i0_ + c)
            stage_a2(i0_ + c - 1)
        stage_a2(i0_ + 3)
        if T == 0:
            for c0 in range(0, 22, 2):
                C.dma("pool", wd_b[:, c0:c0 + 2, :], wd_v[:, c0:c0 + 2, :], writes=[Rwd], key="wd")
        for cc in range(22):
            ffn_pair(T, cc)
        for c in range(4):
            stage_c(T, c)
    return _finish_mixer(C, nc)


def _post_inputs(y_b, x_b, q, l, P):
    t0 = q * NTOK
    if q == 0:
        yh = np.zeros((128, D), ml_dtypes.bfloat16)
        xh = np.zeros((128, D), np.float32)
    else:
        yh = y_b[t0 - 128:t0]
        xh = x_b[t0 - 128:t0]
    rep = lambda v: np.broadcast_to(v[None, :], (128, v.shape[0]))
    key = ("post_w", l)
    if key not in _CACHE:
        w_up = P["w_up"][l]
        _CACHE[key] = {
            "wo": np.ascontiguousarray(P["w_o"][l]),
            "wup": np.ascontiguousarray(w_up.reshape(8, 128, NCH, 128).transpose(2, 1, 0, 3)),
            "wdn": np.ascontiguousarray(P["w_down"][l]),
            "convw": np.ascontiguousarray(P["conv_w"][l].reshape(3, NCH, 128).transpose(2, 1, 0)),
            "convb": np.ascontiguousarray(P["conv_b"][l].reshape(NCH, 128).T),
            "lnp": np.ascontiguousarray(np.stack([rep(P["ln1_g"][l]), rep(P["ln1_b"][l]), rep(P["ln2_g"][l]), rep(P["ln2_b"][l])]).astype(np.float32)),
            "ident": np.eye(128, dtype=np.float32),
        }
    m = dict(_CACHE[key])
    m["yin"] = np.ascontiguousarray(np.concatenate([yh, y_b[t0:t0 + NTOK]], axis=0))
    m["xin"] = np.ascontiguousarray(np.concatenate([xh, x_b[t0:t0 + NTOK]], axis=0))
    m["flag"] = np.full((128, 1), 0.0 if q == 0 else 1.0, np.float32)
    return m


def run_post(y, x, l, P):
    if "nc_p" not in _CACHE:
        _CACHE["nc_p"] = build_post()
    in_maps = [_post_inputs(y[c // 4], x[c // 4], c % 4, l, P) for c in range(8)]
    res = run_bass_kernel_spmd(_CACHE["nc_p"], in_maps, core_ids=list(range(8)))
    out = np.zeros((NB, S, D), np.float32)
    for c in range(8):
        out[c // 4, (c % 4) * NTOK:(c % 4 + 1) * NTOK] = res.results[c]["xo"]
    return out


def kernel(**inputs):
    P = {k: np.asarray(v) for k, v in inputs.items()}
    x = np.ascontiguousarray(P["x"], dtype=np.float32)
    for l in range(2):
        xT = [np.ascontiguousarray(x[b].T) for b in range(NB)]
        y = run_mixer(xT, l, P)
        x = run_post(y, x, l, P)
    return x
```

```python
import contextlib
import os
DBG = int(os.environ.get('MIX_DBG', '99'))
import numpy as np
import ml_dtypes
import concourse.bass as bass
import concourse.mybir as mybir
from concourse.bass_utils import run_bass_kernel_spmd

F32 = mybir.dt.float32
BF16 = mybir.dt.bfloat16
AF = mybir.ActivationFunctionType
ALU = mybir.AluOpType
AX = mybir.AxisListType

S = 8192
D = 1024
NB = 2
DFF = 2816
ALPHA = 4.0 ** 0.25
EPS = 1e-5
BIGM = 30000.0
NEGF = -1.0e30


class Res:
    __slots__ = ("name", "w", "r", "excl")

    def __init__(self, name, excl=False):
        self.name = name
        self.w = None
        self.r = []
        self.excl = excl


class Sched:
    STREAMS = ("pe", "act", "dve", "pool", "sp")

    def __init__(self, nc):
        self.nc = nc
        self.ops = {s: [] for s in self.STREAMS}
        self.ccount = {s: 0 for s in self.STREAMS}
        self.dcount = {}
        self.known = {s: {} for s in self.STREAMS}
        self.final_events = []

    def _need(self, stream, ev, waits):
        if ev is None:
            return
        sem, val, src = ev
        if src == stream and src == "pe":
            return
        if self.known[stream].get(sem, 0) >= val:
            return
        self.known[stream][sem] = val
        waits.append((sem, val))

    def op(self, stream, fn, reads=(), writes=(), dma_key=None):
        ex = [r for r in reads if r.excl]
        if ex:
            reads = [r for r in reads if not r.excl]
            writes = list(writes) + [r for r in ex if r not in writes]
        waits = []
        for r in reads:
            self._need(stream, r.w, waits)
        for w in writes:
            self._need(stream, w.w, waits)
            for e in w.r:
                self._need(stream, e, waits)
        if dma_key is not None:
            k = "d_" + dma_key
            self.dcount[k] = self.dcount.get(k, 0) + 1
            ev = (k, 16 * self.dcount[k], "dma")
            sig = (k, 16)
        else:
            self.ccount[stream] += 1
            ev = ("c_" + stream, self.ccount[stream], stream)
            sig = ("c_" + stream, 1)
        for r in reads:
            r.r.append(ev)
        for w in writes:
            w.w = ev
            w.r = []
        self.ops[stream].append((waits, fn, sig))
        return ev

    def emit(self, final_events):
        nc = self.nc
        names = set()
        for s in self.STREAMS:
            for waits, fn, sig in self.ops[s]:
                names.add(sig[0])
                for (sem, val) in waits:
                    names.add(sem)
        with contextlib.ExitStack() as st:
            sems = {n: st.enter_context(nc.semaphore(n)) for n in sorted(names)}
            block = st.enter_context(nc.Block())

            def make(stream):
                def body(eng):
                    for waits, fn, sig in self.ops[stream]:
                        for (sem, val) in waits:
                            eng.wait_ge(sems[sem], val)
                        ins = fn(eng)
                        ins.then_inc(sems[sig[0]], sig[1])
                    if stream == "sp":
                        for (sem, val, src) in final_events:
                            eng.wait_ge(sems[sem], val)
                return body

            block.tensor(make("pe"))
            block.scalar(make("act"))
            block.vector(make("dve"))
            block.gpsimd(make("pool"))
            block.sync(make("sp"))


class Ctx:
    def __init__(self, nc):
        self.nc = nc
        self.st = contextlib.ExitStack()
        self.S = Sched(nc)
        self.nres = 0

    def sb(self, name, shape, dt):
        return self.st.enter_context(self.nc.sbuf_tensor(name, shape, dt))

    def ps(self, name, shape, dt):
        return self.st.enter_context(self.nc.psum_tensor(name, shape, dt))

    def res(self, name="r", excl=False):
        self.nres += 1
        return Res("%s%d" % (name, self.nres), excl)

    def pe(self, fn, reads=(), writes=()):
        return self.S.op("pe", fn, reads, writes)

    def act(self, fn, reads=(), writes=()):
        return self.S.op("act", fn, reads, writes)

    def dve(self, fn, reads=(), writes=()):
        return self.S.op("dve", fn, reads, writes)

    def pool(self, fn, reads=(), writes=()):
        return self.S.op("pool", fn, reads, writes)

    def dma(self, queue, out, in_, reads=(), writes=(), key=None):
        if key is None:
            key = (writes[0].name if writes else reads[0].name)
        return self.S.op(queue, lambda e: e.dma_start(out=out, in_=in_), reads, writes, dma_key=key)


def _mm_group(out, pairs):
    def fn(e):
        n = len(pairs)
        ins = None
        for i, (l, r) in enumerate(pairs):
            ins = e.matmul(out, lhsT=l, rhs=r, start=(i == 0), stop=(i == n - 1))
        return ins
    return fn


NFM = 352
NTM = 257


def build_mixer(SEQ=S, PH=9):
    NT, NTT = SEQ // 512, SEQ // 128
    nc = bass.Bass("TRN2", target_bir_lowering=False)
    C = Ctx(nc)
    dt = nc.dram_tensor
    xT = dt("xT", [D, SEQ], F32, kind="ExternalInput").ap()
    wfm = dt("wfm", [D, NFM], F32, kind="ExternalInput").ap()
    wtm = dt("wtm", [D, NTM], F32, kind="ExternalInput").ap()
    cs = dt("cs", [2, 16, SEQ], F32, kind="ExternalInput").ap()
    onehot = dt("onehot", [32, SEQ], BF16, kind="ExternalInput").ap()
    sguT = dt("sguT", [128, 128], F32, kind="ExternalInput").ap()
    tri = dt("tri", [128, 128], F32, kind="ExternalInput").ap()
    ident = dt("ident", [128, 128], F32, kind="ExternalInput").ap()
    sgub = dt("sgub", [128, 1], F32, kind="ExternalInput").ap()
    lng = dt("lng", [128, 64], F32, kind="ExternalInput").ap()
    lnb = dt("lnb", [128, 64], F32, kind="ExternalInput").ap()
    poolw = dt("poolw", [64, 64], F32, kind="ExternalInput").ap()
    pscale = dt("pscale", [128, 64], F32, kind="ExternalInput").ap()
    band = dt("band", [3, 128, 128], F32, kind="ExternalInput").ap()
    bfor = dt("bfor", [128, 1], F32, kind="ExternalInput").ap()
    y = dt("y", [SEQ, 256], BF16, kind="ExternalOutput").ap()

    sb, ps, res = C.sb, C.ps, C.res
    QA = sb("QA", [128, SEQ], BF16)
    KA = sb("KA", [128, SEQ], BF16)
    QB = sb("QB", [128, SEQ], BF16)
    KB = sb("KB", [128, SEQ], BF16)
    VA = sb("VA", [128, NTT, 65], BF16)
    VB = sb("VB", [128, NTT, 65], BF16)
    xb = [sb("xb%d" % i, [128, 8, 512], BF16) for i in range(2)]
    wfm_b = sb("wfm_b", [128, 8, NFM], BF16)
    wtm_b = sb("wtm_b", [128, 8, NTM], BF16)
    cst = [sb("cst%d" % i, [16, 2, 512], F32) for i in range(2)]
    t1 = [sb("t1_%d" % i, [16, 512], F32) for i in range(2)]
    t2 = [sb("t2_%d" % i, [16, 512], F32) for i in range(2)]
    pTb = [sb("pTb%d" % i, [64, 512], BF16) for i in range(2)]
    ug = [sb("ug%d" % i, [128, 128], F32) for i in range(2)]
    stats = [sb("stats%d" % i, [128, 6], F32) for i in range(2)]
    mv = [sb("mv%d" % i, [128, 2], F32) for i in range(2)]
    rstd = [sb("rstd%d" % i, [128, 1], F32) for i in range(2)]
    vn = [sb("vn%d" % i, [128, 64], F32) for i in range(2)]
    vnb = [sb("vnb%d" % i, [128, 64], BF16) for i in range(2)]
    pwb = [sb("pwb%d" % i, [128, 64], BF16) for i in range(2)]
    YC = [sb("YC%d" % i, [128, 4, 64], BF16) for i in range(2)]
    YD = [sb("YD%d" % i, [128, 4, 64], BF16) for i in range(2)]
    UG = sb("UG", [128, NTT, 128], BF16)
    MV = sb("MV", [128, NTT, 2], F32)
    VE = sb("VE", [128, NTT], F32)
    RSTD = sb("RSTD", [128, NTT], F32)
    YAB = [sb("YAB%d" % i, [128, 4, 128], BF16) for i in range(2)]
    MBZ = sb("MBZ", [128, 64 + NTT * 32], F32)
    GM = sb("GM", [128, 32], F32)
    top8 = [sb("top8_%d" % i, [128, 8], F32) for i in range(2)]
    FRAW = sb("FRAW", [128, NTT], F32)
    LF = sb("LF", [128, NTT], F32)
    TOT = sb("TOT", [128, NTT], F32)
    PREF = sb("PREF", [128, NTT], F32)
    CP = sb("CP", [128, NTT], F32)
    Z = sb("Z", [128, 128], F32)
    kbar = sb("kbar", [64, 32], F32)
    kbar_b = sb("kbar_b", [64, 32], BF16)
    PT = [sb("PT%d" % i, [128, 512], BF16) for i in range(4)]
    rl = [sb("rl%d" % i, [128, 1], F32) for i in range(4)]
    ident_f = sb("ident_f", [128, 128], F32)
    tri_f = sb("tri_f", [128, 128], F32)
    tri_b = sb("tri_b", [128, 128], BF16)
    ones_f = sb("ones_f", [128, 128], F32)
    sgu_f = sb("sgu_f", [128, 128], F32)
    wmT_b = sb("wmT_b", [128, 128], BF16)
    band_b = sb("band_b", [128, 3, 128], BF16)
    sgub_s = sb("sgub_s", [128, 1], F32)
    lng_s = sb("lng_s", [128, 64], F32)
    lnb_s = sb("lnb_s", [128, 64], F32)
    poolw_b = sb("poolw_b", [64, 64], BF16)
    pscale_s = sb("pscale_s", [128, 64], F32)
    bfor_s = sb("bfor_s", [128, 1], F32)
    negb = sb("negb", [128, 1], F32)
    ef = sb("ef", [128, NTT], F32)
    PS = [ps("ps%d" % i, [128, 512], F32) for i in range(8)]
    RPS = [res("ps", True) for _ in range(8)]

    R = lambda n: res(n)
    RQA = [R("qa") for _ in range(NT)]
    RQAm = [R("qam") for _ in range(NT)]
    RKA = [R("ka") for _ in range(NT)]
    RQB = [R("qb") for _ in range(NT)]
    RQBc = [R("qbc") for _ in range(NT)]
    RKB = [R("kb") for _ in range(NT)]
    RVA = [R("va") for _ in range(NT)]
    RVB = [R("vb") for _ in range(NT)]
    Rxb = [R("xb") for _ in range(2)]
    Rcs = [R("cs") for _ in range(2)]
    Rt1 = [R("t1") for _ in range(2)]
    Rt2 = [R("t2") for _ in range(2)]
    RpT = [R("pT") for _ in range(2)]
    Rug = [R("ug") for _ in range(2)]
    Rst = [R("st") for _ in range(2)]
    Rmv = [R("mv") for _ in range(2)]
    Rrs = [R("rs") for _ in range(2)]
    Rvn = [R("vn") for _ in range(2)]
    Rvnb = [R("vnb") for _ in range(2)]
    Rpwb = [R("pwb") for _ in range(2)]
    RYC = [R("yc") for _ in range(2)]
    RYD = [R("yd") for _ in range(2)]
    RUG = [R("ugall") for _ in range(NT)]
    RMV, RVE, RRSTD = R("mvall"), R("ve"), R("rstdall")
    RYAB = [R("yab") for _ in range(2)]
    RMBZ = [R("mbz") for _ in range(NT)]
    RGM = R("gm")
    Rtop = [R("top") for _ in range(2)]
    RFRAW, RLF, RTOT, RPREF, RCP, RZ, Rkbar, Rkbarb, Ref = (R("fraw"), R("lf"), R("tot"), R("pref"),
                                                            R("cp"), R("z"), R("kbar"), R("kbarb"), R("ef"))
    RPT = [R("pt") for _ in range(4)]
    Rrl = [R("rl") for _ in range(4)]
    Rc = {k: R(k) for k in ["wfm", "wtm", "ident", "tri_f", "tri_b", "ones", "sgu_f", "wmT", "band", "sgub",
                            "lng", "lnb", "poolw", "pscale", "bfor", "negb", "kaoh", "misc"]}

    C.dma("pool", wfm_b[:], wfm.rearrange("(kc f) n -> f kc n", f=128), writes=[Rc["wfm"]])
    C.dma("pool", wtm_b[:], wtm.rearrange("(kc f) n -> f kc n", f=128), writes=[Rc["wtm"]])
    C.dma("sp", ident_f[:], ident, writes=[Rc["ident"]])
    C.dma("sp", tri_f[:], tri, writes=[Rc["tri_f"]])
    C.dma("pool", tri_b[:], tri, writes=[Rc["tri_b"]])
    C.dma("sp", sgu_f[:], sguT, writes=[Rc["sgu_f"]])
    C.dma("pool", band_b[:], band.rearrange("k s t -> s k t"), writes=[Rc["band"]])
    C.dma("sp", sgub_s[:], sgub, writes=[Rc["sgub"]])
    C.dma("sp", lng_s[:], lng, writes=[Rc["lng"]])
    C.dma("sp", lnb_s[:], lnb, writes=[Rc["lnb"]])
    C.dma("pool", poolw_b[:], poolw, writes=[Rc["poolw"]])
    C.dma("sp", pscale_s[:], pscale, writes=[Rc["pscale"]])
    C.dma("sp", bfor_s[:], bfor, writes=[Rc["bfor"]])
    C.dma("sp", KA[64:96, :], onehot, writes=[Rc["kaoh"]])
    C.dve(lambda e: e.tensor_tensor(out=wmT_b[:], in0=sgu_f[:], in1=tri_f[:], op=ALU.mult),
          reads=[Rc["sgu_f"], Rc["tri_f"]], writes=[Rc["wmT"]])
    C.dve(lambda e: e.tensor_scalar(out=negb[:], in0=bfor_s[:], scalar1=-1.0, scalar2=None, op0=ALU.mult),
          reads=[Rc["bfor"]], writes=[Rc["negb"]])
    C.dve(lambda e: e.memset(ones_f[:], 1.0), writes=[Rc["ones"]])
    C.dve(lambda e: e.memset(VA[:, :, 64:65], 1.0), writes=RVA)
    C.dve(lambda e: e.memset(VB[:, :, 64:65], 1.0), writes=RVB)
    C.dve(lambda e: e.memset(KB[64:65, :], 1.0), writes=RKB)
    C.dve(lambda e: e.memset(MBZ[:], 0.0), writes=RMBZ)
    C.dve(lambda e: e.memset(GM[:], NEGF), writes=[RGM])
    C.dve(lambda e: e.memset(Z[:], 0.0), writes=[RZ])
    C.dve(lambda e: e.memset(kbar[:], 0.0), writes=[Rkbar])
    C.dve(lambda e: e.memset(PREF[:, 0:1], 0.0), writes=[RPREF])

    if PH == 0:
        return _finish_mixer(C, nc)
    xT_v = xT.rearrange("(kc f) t -> f kc t", f=128)

    def load_tile(T):
        sl = T % 2
        C.dma("pool", xb[sl][:], xT_v[:, :, T * 512:(T + 1) * 512], writes=[Rxb[sl]])
        C.dma("sp", cst[sl][:], cs[:, :, T * 512:(T + 1) * 512].rearrange("k p t -> p k t"), writes=[Rcs[sl]])

    load_tile(0)
    fm_cols = {"qA": (0, 64), "kA": (64, 64), "qB": (128, 64), "kB": (192, 64), "pD": (256, 64),
               "qAp": (320, 16), "kAp": (336, 16)}
    def p1_tile(T):
        sl = T % 2
        if T + 1 < NT:
            load_tile(T + 1)
        cols = slice(T * 512, (T + 1) * 512)

        def fm(name, bank):
            c0, n = fm_cols[name]
            C.pe(_mm_group(PS[bank][0:n, :], [(wfm_b[:, kc, c0:c0 + n], xb[sl][:, kc, :]) for kc in range(8)]),
                 reads=[Rc["wfm"], Rxb[sl]], writes=[RPS[bank]])

        def rot(nm, nmp, dst, Rdst):
            fm(nm, 0)
            fm(nmp, 1)
            if DBG == 10:
                return
            C.act(lambda e, dst=dst: e.copy(out=dst[0:64, cols], in_=PS[0][0:64, :]), reads=[RPS[0]], writes=[Rdst[T]])
            if DBG == 11:
                return
            C.dve(lambda e: e.tensor_tensor(out=t1[sl][:], in0=PS[0][0:16, :], in1=cst[sl][:, 0, :], op=ALU.mult),
                  reads=[RPS[0], Rcs[sl]], writes=[Rt1[sl]])
            C.dve(lambda e: e.tensor_tensor(out=t2[sl][:], in0=PS[1][0:16, :], in1=cst[sl][:, 1, :], op=ALU.mult),
                  reads=[RPS[1], Rcs[sl]], writes=[Rt2[sl]])
            if DBG == 12:
                return
            C.dve(lambda e, dst=dst: e.tensor_tensor(out=dst[0:16, cols], in0=t1[sl][:], in1=t2[sl][:], op=ALU.add),
                  reads=[Rt1[sl], Rt2[sl]], writes=[Rdst[T]])
        if DBG < 1:
            return
        rot("qA", "qAp", QA, RQA)
        rot("kA", "kAp", KA, RKA)
        if DBG < 2 or (10 <= DBG < 20):
            return
        C.dve(lambda e: e.tensor_reduce(out=kbar[:, 2 * T:2 * T + 2],
                                        in_=KA[0:64, cols].rearrange("p (b j) -> p b j", j=256),
                                        axis=AX.X, op=ALU.add), reads=[RKA[T]], writes=[Rkbar])
        if DBG < 3:
            return
        fm("qB", 0)
        C.act(lambda e: e.copy(out=QB[0:64, cols], in_=PS[0][0:64, :]), reads=[RPS[0]], writes=[RQB[T]])
        fm("kB", 1)
        C.act(lambda e: e.copy(out=KB[0:64, cols], in_=PS[1][0:64, :]), reads=[RPS[1]], writes=[RKB[T]])
        fm("pD", 0)
        C.act(lambda e: e.copy(out=pTb[sl][:], in_=PS[0][0:64, :]), reads=[RPS[0]], writes=[RpT[sl]])
        if DBG < 4:
            return
        def sub(c):
            tt = 4 * T + c
            s2 = tt % 2
            bk = 2 + s2
            C.pe(_mm_group(PS[bk][:, 0:NTM], [(xb[sl][:, kc, c * 128:(c + 1) * 128], wtm_b[:, kc, :]) for kc in range(8)]),
                 reads=[Rc["wtm"], Rxb[sl]], writes=[RPS[bk]])
            C.act(lambda e, bk=bk, tt=tt: e.copy(out=VA[:, tt, 0:64], in_=PS[bk][:, 0:64]), reads=[RPS[bk]], writes=[RVA[T]])
            C.act(lambda e, bk=bk, tt=tt: e.copy(out=VB[:, tt, 0:64], in_=PS[bk][:, 64:128]), reads=[RPS[bk]], writes=[RVB[T]])
            C.dve(lambda e, bk=bk, tt=tt: e.tensor_copy(out=FRAW[:, tt:tt + 1], in_=PS[bk][:, 256:257]),
                  reads=[RPS[bk]], writes=[RFRAW])
            C.act(lambda e, bk=bk, tt=tt: e.activation(out=UG[:, tt, :], in_=PS[bk][:, 128:256], func=AF.Gelu),
                  reads=[RPS[bk]], writes=[RUG[T]])
            C.dve(lambda e, tt=tt, s2=s2: e.bn_stats(out=stats[s2][:], in_=UG[:, tt, 64:128]), reads=[RUG[T]], writes=[Rst[s2]])
            C.dve(lambda e, tt=tt, s2=s2: e.bn_aggr(out=MV[:, tt, :], in_=stats[s2][:]), reads=[Rst[s2]], writes=[RMV])
            C.pe(lambda e, c=c: e.matmul(PS[5][:, 0:64], lhsT=pTb[sl][:, c * 128:(c + 1) * 128], rhs=poolw_b[:],
                                         start=True, stop=True), reads=[RpT[sl], Rc["poolw"]], writes=[RPS[5]])
            C.act(lambda e, s2=s2: e.copy(out=pwb[s2][:], in_=PS[5][:, 0:64]), reads=[RPS[5]], writes=[Rpwb[s2]])
            if tt == 0:
                C.pe(lambda e, s2=s2: e.matmul(PS[6][:, 0:64], lhsT=band_b[:, 2, :], rhs=pwb[s2][:], start=True, stop=True),
                     reads=[Rc["band"], Rpwb[s2]], writes=[RPS[6]])
            else:
                C.pe(_mm_group(PS[6][:, 0:64], [(band_b[:, 0, :], pwb[s2][:]), (band_b[:, 1, :], pwb[1 - s2][:])]),
                     reads=[Rc["band"], Rpwb[0], Rpwb[1]], writes=[RPS[6]])
            C.dve(lambda e, c=c: e.tensor_tensor(out=YD[sl][:, c, :], in0=PS[6][:, 0:64], in1=pscale_s[:], op=ALU.mult),
                  reads=[RPS[6], Rc["pscale"]], writes=[RYD[sl]])
        for c in range(4):
            sub(c)
        C.dma("sp", y[T * 512:(T + 1) * 512, 192:256].rearrange("(c p) n -> p c n", p=128), YD[sl][:],
              reads=[RYD[sl]], key="yd%d" % sl)

    for T in range(NT):
        p1_tile(T)

    if PH == 1:
        return _finish_mixer(C, nc)
    C.dve(lambda e: e.tensor_scalar(out=VE[:], in0=MV[:, :, 1], scalar1=EPS, scalar2=None, op0=ALU.add),
          reads=[RMV], writes=[RVE])
    C.act(lambda e: e.activation(out=VE[:], in_=VE[:], func=AF.Sqrt), reads=[RVE], writes=[RVE])
    C.dve(lambda e: e.reciprocal(out=RSTD[:], in_=VE[:]), reads=[RVE], writes=[RRSTD])

    vn4 = [sb("vn4_%d" % i, [128, 64], F32) for i in range(4)]
    vnb4 = [sb("vnb4_%d" % i, [128, 64], BF16) for i in range(4)]
    Rvn4 = [R("vn4") for _ in range(4)]
    Rvnb4 = [R("vnb4") for _ in range(4)]

    def c_stage(T, stage):
        sl = T % 2
        if stage == 0:
            for c in range(4):
                tt = 4 * T + c
                C.dve(lambda e, c=c, tt=tt: e.tensor_scalar(out=vn4[c][:], in0=UG[:, tt, 64:128], scalar1=MV[:, tt, 0:1],
                                                            scalar2=RSTD[:, tt:tt + 1], op0=ALU.subtract, op1=ALU.mult),
                      reads=[RUG[T], RMV, RRSTD], writes=[Rvn4[c]])
                C.dve(lambda e, c=c: e.tensor_tensor(out=vn4[c][:], in0=vn4[c][:], in1=lng_s[:], op=ALU.mult),
                      reads=[Rvn4[c], Rc["lng"]], writes=[Rvn4[c]])
                C.dve(lambda e, c=c: e.tensor_tensor(out=vnb4[c][:], in0=vn4[c][:], in1=lnb_s[:], op=ALU.add),
                      reads=[Rvn4[c], Rc["lnb"]], writes=[Rvnb4[c]])
        elif stage == 1:
            def mix(e):
                ins = None
                for c in range(4):
                    ins = e.matmul(PS[5][:, c * 64:(c + 1) * 64], lhsT=wmT_b[:], rhs=vnb4[c][:], start=True, stop=True)
                return ins
            C.pe(mix, reads=[Rc["wmT"]] + Rvnb4, writes=[RPS[5]])
        else:
            for c in range(4):
                tt = 4 * T + c
                C.dve(lambda e, c=c, tt=tt: e.scalar_tensor_tensor(out=YC[sl][:, c, :], in0=PS[5][:, c * 64:(c + 1) * 64],
                                                                   scalar=sgub_s[:, 0:1], in1=UG[:, tt, 0:64],
                                                                   op0=ALU.add, op1=ALU.mult),
                      reads=[RPS[5], Rc["sgub"], RUG[T]], writes=[RYC[sl]])
            C.dma("sp", y[T * 512:(T + 1) * 512, 128:192].rearrange("(c p) n -> p c n", p=128), YC[sl][:],
                  reads=[RYC[sl]], key="yc%d" % sl)

    def c_tile(T):
        for s in range(3):
            c_stage(T, s)

    if PH < 4:
        for T in range(NT):
            c_tile(T)

    if PH == 2:
        return _finish_mixer(C, nc)
    C.act(lambda e: e.activation(out=ef[:], in_=FRAW[:], func=AF.Exp, bias=negb[:, 0:1], scale=-1.0),
          reads=[RFRAW, Rc["negb"]], writes=[Ref])
    C.act(lambda e: e.activation(out=LF[:], in_=ef[:], func=AF.Ln, bias=1.0, scale=1.0), reads=[Ref], writes=[RLF])
    C.pe(lambda e: e.matmul(PS[0][:, 0:NTT], lhsT=ones_f[:], rhs=LF[:], start=True, stop=True),
         reads=[Rc["ones"], RLF], writes=[RPS[0]])
    C.dve(lambda e: e.tensor_copy(out=TOT[:], in_=PS[0][:, 0:NTT]), reads=[RPS[0]], writes=[RTOT])
    for j in range(1, NTT):
        C.dve(lambda e, j=j: e.tensor_tensor(out=PREF[:, j:j + 1], in0=PREF[:, j - 1:j], in1=TOT[:, j - 1:j], op=ALU.add),
              reads=[RPREF, RTOT], writes=[RPREF])
    C.pe(lambda e: e.matmul(PS[1][:, 0:NTT], lhsT=tri_f[:], rhs=LF[:], start=True, stop=True),
         reads=[Rc["tri_f"], RLF], writes=[RPS[1]])
    C.dve(lambda e: e.tensor_tensor(out=CP[:], in0=PS[1][:, 0:NTT], in1=PREF[:], op=ALU.add),
          reads=[RPS[1], RPREF], writes=[RCP])
    C.dve(lambda e: e.tensor_scalar(out=Z[:, 64:64 + NTT], in0=CP[:], scalar1=-8.0, scalar2=None, op0=ALU.mult),
          reads=[RCP], writes=[RZ])
    C.dve(lambda e: e.tensor_scalar(out=kbar_b[:], in0=kbar[:], scalar1=1.0 / 256.0, scalar2=None, op0=ALU.mult),
          reads=[Rkbar], writes=[Rkbarb])
    GM4 = [sb("GM4_%d" % i, [128, 32], F32) for i in range(4)]
    top4 = [sb("top4_%d" % i, [128, 8], F32) for i in range(4)]
    RGM4 = [R("gm4") for _ in range(4)]
    Rtop4 = [R("top4") for _ in range(4)]
    for i in range(4):
        C.dve(lambda e, i=i: e.memset(GM4[i][:], NEGF), writes=[RGM4[i]])

    def p2_stage(T, stage):
        if stage == 0:
            def tr4(e):
                ins = None
                for c in range(4):
                    tt = 4 * T + c
                    ins = e.matmul(PS[5][0:65, c * 128:(c + 1) * 128], lhsT=Z[:, tt:tt + 65], rhs=ident_f[:],
                                   start=True, stop=True)
                return ins
            C.pe(tr4, reads=[RZ, Rc["ident"]], writes=[RPS[5]])

            def gates(e):
                ins = None
                for c in range(4):
                    tt = 4 * T + c
                    ins = e.matmul(PS[6][:, c * 32:(c + 1) * 32], lhsT=QA[0:64, tt * 128:(tt + 1) * 128], rhs=kbar_b[:],
                                   start=True, stop=True)
                return ins
            C.pe(gates, reads=[RQA[T], Rkbarb], writes=[RPS[6]])
        elif stage == 1:
            C.act(lambda e: e.copy(out=QB[64:65, T * 512:(T + 1) * 512], in_=PS[5][64:65, :]),
                  reads=[RPS[5]], writes=[RQBc[T]])
            for c in range(4):
                tt = 4 * T + c
                b = tt // 2
                if b == 0:
                    continue
                C.dve(lambda e, c=c, b=b: e.tensor_copy(out=GM4[c][:, 0:b], in_=PS[6][:, c * 32:c * 32 + b]),
                      reads=[RPS[6]], writes=[RGM4[c]])
                C.dve(lambda e, c=c: e.max(out=top4[c][:], in_=GM4[c][:]), reads=[RGM4[c]], writes=[Rtop4[c]])
                C.dve(lambda e, c=c, tt=tt, b=b: e.tensor_scalar(out=MBZ[:, 64 + 32 * tt:64 + 32 * tt + b], in0=GM4[c][:, 0:b],
                                                                 scalar1=top4[c][:, 2:3], scalar2=1.0,
                                                                 op0=ALU.is_ge, op1=ALU.subtract),
                      reads=[RGM4[c], Rtop4[c]], writes=[RMBZ[T]])
        elif stage == 2:
            def trm(e):
                ins = None
                for c in range(4):
                    tt = 4 * T + c
                    ins = e.matmul(PS[7][0:96, c * 128:(c + 1) * 128], lhsT=MBZ[:, 32 * tt:32 * tt + 96], rhs=ident_f[:],
                                   start=True, stop=True)
                return ins
            C.pe(trm, reads=[RMBZ[T], Rc["ident"]], writes=[RPS[7]])
        else:
            C.act(lambda e: e.copy(out=QA[64:96, T * 512:(T + 1) * 512], in_=PS[7][64:96, :]),
                  reads=[RPS[7]], writes=[RQAm[T]])

    def p2_tile(T):
        for s in range(4):
            p2_stage(T, s)

    if PH < 4:
        for T in range(NT):
            p2_tile(T)
    else:
        p2_tile(0)

    if PH == 3:
        return _finish_mixer(C, nc)
    ROacc = [[RPS[3]] * 4, [RPS[4]] * 4]
    LOOK = 2

    def att_score(p):
        qi, att, kj, oi, idx = p
        Q, K = (QA, KA) if att == 0 else (QB, KB)
        RQ, RQx, RK = (RQA, RQAm, RKA) if att == 0 else (RQB, RQBc, RKB)
        rows = 96 if att == 0 else 65
        d = kj - 4 * qi
        off = 128 * max(d, 0)
        n = 512 - off
        sbk = idx % 3
        pt = idx % 4
        C.pe(lambda e: e.matmul(PS[sbk][:, 0:n], lhsT=K[0:rows, kj * 128:(kj + 1) * 128],
                                rhs=Q[0:rows, qi * 512 + off:(qi + 1) * 512], start=True, stop=True),
             reads=[RQ[qi], RQx[qi], RK[kj // 4]] + ([Rc["kaoh"]] if att == 0 else []), writes=[RPS[sbk]])
        if att == 0:
            C.act(lambda e: e.activation(out=PT[pt][:, 0:n], in_=PS[sbk][:, 0:n], func=AF.Exp, scale=0.125),
                  reads=[RPS[sbk]], writes=[RPT[pt]])
        else:
            C.act(lambda e: e.activation(out=PT[pt][:, 0:n], in_=PS[sbk][:, 0:n], func=AF.Exp,
                                         bias=CP[:, kj:kj + 1], scale=0.125),
                  reads=[RPS[sbk], RCP], writes=[RPT[pt]])
        if d >= 0:
            C.dve(lambda e: e.tensor_tensor(out=PT[pt][:, 0:128], in0=PT[pt][:, 0:128], in1=tri_b[:], op=ALU.mult),
                  reads=[RPT[pt], Rc["tri_b"]], writes=[RPT[pt]])

    def att_pv(p):
        qi, att, kj, oi, idx = p
        V = VA if att == 0 else VB
        RV = RVA if att == 0 else RVB
        ysl = qi % 2
        ob = 3 + (oi % 2)
        Oacc = PS[ob][:, 0:260].rearrange("p (c n) -> p c n", n=65)
        RO = ROacc[ob - 3]
        d = kj - 4 * qi
        off = 128 * max(d, 0)
        pt = idx % 4
        c0 = max(d, 0)

        def pv(e):
            ins = None
            for c in range(c0, 4):
                ins = e.matmul(Oacc[:, c, :], lhsT=PT[pt][:, c * 128 - off:(c + 1) * 128 - off], rhs=V[:, kj, :],
                               start=(kj == 0 and c == 0), stop=(kj == 4 * qi + c), skip_group_check=True)
            return ins
        C.pe(pv, reads=[RPT[pt], RV[kj // 4]], writes=[RO[0]])
        if d >= 0:
            c = d
            r4 = (2 * (oi % 2) + (c % 2))
            C.dve(lambda e: e.reciprocal(out=rl[r4][:], in_=Oacc[:, c, 64:65]), reads=[RO[c]], writes=[Rrl[r4]])
            C.dve(lambda e: e.tensor_scalar(out=YAB[ysl][:, c, att * 64:(att + 1) * 64], in0=Oacc[:, c, 0:64],
                                            scalar1=rl[r4][:, 0:1], scalar2=None, op0=ALU.mult),
                  reads=[RO[c], Rrl[r4]], writes=[RYAB[ysl]])
        if att == 1 and d == 3:
            C.dma("sp", y[qi * 512:(qi + 1) * 512, 0:128].rearrange("(c p) n -> p c n", p=128), YAB[ysl][:],
                  reads=[RYAB[ysl]], key="yab%d" % ysl)

    plist = []
    oi = 0
    for qi in range(NT):
        for att in range(2):
            for kj in range(4 * qi + 4):
                plist.append((qi, att, kj, oi, len(plist)))
            oi += 1
    for i in range(len(plist) + LOOK):
        if i < len(plist):
            qi_, att_, kj_ = plist[i][0], plist[i][1], plist[i][2]
            n_ = 4 * qi_ + 4
            if att_ == 0 and qi_ + 1 < NT:
                offs = [0, n_ // 4, n_ // 2, (3 * n_) // 4]
                if kj_ in offs:
                    p2_stage(qi_ + 1, offs.index(kj_))
            if att_ == 1:
                offs = [0, n_ // 3, (2 * n_) // 3]
                if kj_ in offs:
                    c_stage(qi_, offs.index(kj_))
            att_score(plist[i])
        if i - LOOK >= 0:
            att_pv(plist[i - LOOK])

    if os.environ.get("MIX_DUMP"):
        allres = RQA + RQAm + RKA + RQB + RQBc + RKB + RVA + RVB + RUG + [RCP, RLF, RFRAW, RMV, RRSTD, Rkbar, Rkbarb, RZ, Rc["kaoh"]] + RMBZ + Rpwb + [Rc["band"], Rc["tri_b"], Rc["poolw"]]
        for nm, t, shp, dty in (("d_cp", CP, [128, NTT], F32), ("d_lf", LF, [128, NTT], F32), ("d_fraw", FRAW, [128, NTT], F32),
                                ("d_rstd", RSTD, [128, NTT], F32),
                                ("d_ug", UG, [128, NTT, 128], BF16), ("d_qa", QA, [128, SEQ], BF16), ("d_ka", KA, [128, SEQ], BF16),
                                ("d_qb", QB, [128, SEQ], BF16), ("d_kb", KB, [128, SEQ], BF16), ("d_va", VA, [128, NTT, 65], BF16),
                                ("d_vb", VB, [128, NTT, 65], BF16), ("d_kbar", kbar, [64, 32], F32), ("d_mbz", MBZ, [128, 64 + NTT * 32], F32),
                                ("d_pwb0", pwb[0], [128, 64], BF16), ("d_band", band_b, [128, 3, 128], BF16), ("d_trib", tri_b, [128, 128], BF16)):
            dd = dt(nm, shp, dty, kind="ExternalOutput").ap()
            C.dma("sp", dd, t[:], reads=allres, key=nm)

    return _finish_mixer(C, nc)


def _finish_mixer(C, nc):
    finals = []
    for k, cnt in C.S.dcount.items():
        finals.append((k, 16 * cnt, "dma"))
    C.S.emit(finals)
    C.st.close()
    return nc


def _rot_tables():
    pos = np.arange(S, dtype=np.float32)
    inv_freq = (np.float32(500000.0) ** (-np.arange(0, 16, 2, dtype=np.float32) / np.float32(16))).astype(np.float32)
    ang = (pos[:, None] * inv_freq[None, :]).astype(np.float32)
    cos = np.cos(ang).astype(np.float32).T
    sin = np.sin(ang).astype(np.float32).T
    cs = np.zeros((2, 16, S), np.float32)
    cs[0, 0:8] = cos
    cs[0, 8:16] = cos
    cs[1, 0:8] = -sin
    cs[1, 8:16] = sin
    return cs


def _band_mats(win):
    t = np.arange(128)
    s = np.arange(128)
    cur = ((t[None, :] - s[:, None] >= 0) & (t[None, :] - s[:, None] < win)).astype(np.float32) / win
    cur -= np.eye(128, dtype=np.float32)
    prev = ((t[None, :] + 128 - s[:, None]) < win).astype(np.float32) / win
    cnt = np.minimum(t + 1, win).astype(np.float32)
    first = ((t[None, :] - s[:, None] >= 0) & (t[None, :] - s[:, None] < win)).astype(np.float32) / cnt[None, :]
    first -= np.eye(128, dtype=np.float32)
    return np.stack([cur, prev, first]).astype(np.float32)


_CACHE = {}


def _mixer_inputs(xT_b, l, h, P):
    w_in = P["w_in"][l]
    a0 = 0
    qa = w_in[:, 0 + 64 * h:0 + 64 * h + 64]
    ka = w_in[:, 256 + 64 * h:256 + 64 * h + 64]
    va = w_in[:, 512 + 64 * h:512 + 64 * h + 64]
    qb = w_in[:, 768 + 64 * h:768 + 64 * h + 64]
    kb = w_in[:, 1024 + 64 * h:1024 + 64 * h + 64]
    vb = w_in[:, 1280 + 64 * h:1280 + 64 * h + 64]
    fb = w_in[:, 1536 + h:1536 + h + 1]
    cu = w_in[:, 1540 + 64 * h:1540 + 64 * h + 64]
    cv = w_in[:, 1796 + 64 * h:1796 + 64 * h + 64]
    dp = w_in[:, 2052 + 64 * h:2052 + 64 * h + 64]
    perm = np.concatenate([np.arange(8, 16), np.arange(0, 8)])
    wfm = np.concatenate([qa, ka, qb, kb, dp, qa[:, perm], ka[:, perm]], axis=1)
    wtm = np.concatenate([va, vb, cu, cv, fb], axis=1)
    rep = lambda v: np.ascontiguousarray(np.broadcast_to(v[None, :], (128, v.shape[0]))).astype(np.float32)
    return {
        "xT": xT_b,
        "wfm": np.ascontiguousarray(wfm), "wtm": np.ascontiguousarray(wtm),
        "cs": _CACHE["cs"], "onehot": _CACHE["onehot"],
        "sguT": np.ascontiguousarray(P["sgu_w"][l, h].T), "tri": _CACHE["tri"], "ident": _CACHE["ident"],
        "sgub": np.ascontiguousarray(P["sgu_b"][l, h].reshape(128, 1)),
        "lng": rep(P["sgu_ln_g"][l, 64 * h:64 * h + 64]), "lnb": rep(P["sgu_ln_b"][l, 64 * h:64 * h + 64]),
        "poolw": np.ascontiguousarray(P["pool_w"][l, h]), "pscale": rep(P["pool_scale"][l, 64 * h:64 * h + 64]),
        "band": _CACHE["band"][h],
        "bfor": np.full((128, 1), P["b_forget"][l, h], np.float32),
    }


def _consts():
    if "cs" in _CACHE:
        return
    _CACHE["cs"] = _rot_tables()
    oh = np.zeros((32, S), np.float32)
    for n in range(32):
        oh[n, n * 256:(n + 1) * 256] = BIGM
    _CACHE["onehot"] = oh.astype(ml_dtypes.bfloat16)
    sidx = np.arange(128)
    _CACHE["tri"] = (sidx[:, None] <= sidx[None, :]).astype(np.float32)
    _CACHE["ident"] = np.eye(128, dtype=np.float32)
    _CACHE["band"] = [_band_mats(w) for w in (2, 4, 8, 16)]


def run_mixer(xT_all, l, P):
    _consts()
    if "nc_m" not in _CACHE:
        _CACHE["nc_m"] = build_mixer()
    in_maps = [_mixer_inputs(xT_all[c // 4], l, c % 4, P) for c in range(8)]
    res = run_bass_kernel_spmd(_CACHE["nc_m"], in_maps, core_ids=list(range(8)))
    y = np.zeros((NB, S, 1024), ml_dtypes.bfloat16)
    for c in range(8):
        b, h = c // 4, c % 4
        yy = np.asarray(res.results[c]["y"]).view(ml_dtypes.bfloat16).reshape(S, 256) if res.results[c]["y"].dtype != ml_dtypes.bfloat16 else res.results[c]["y"]
        for m in range(4):
            y[b, :, 256 * m + 64 * h:256 * m + 64 * h + 64] = yy[:, 64 * m:64 * m + 64]
    return y


NTOK = 2048
NCH = 44


def build_post(NT4=4):
    nc = bass.Bass("TRN2", target_bir_lowering=False)
    C = Ctx(nc)
    dt = nc.dram_tensor
    NTK = NT4 * 512
    yin = dt("yin", [128 + NTK, D], BF16, kind="ExternalInput").ap()
    xin = dt("xin", [128 + NTK, D], F32, kind="ExternalInput").ap()
    flag = dt("flag", [128, 1], F32, kind="ExternalInput").ap()
    wo = dt("wo", [D, D], F32, kind="ExternalInput").ap()
    wup = dt("wup", [NCH, 128, 8, 128], F32, kind="ExternalInput").ap()
    wdn = dt("wdn", [DFF, D], F32, kind="ExternalInput").ap()
    convw = dt("convw", [128, NCH, 3], F32, kind="ExternalInput").ap()
    convb = dt("convb", [128, NCH], F32, kind="ExternalInput").ap()
    lnp = dt("lnp", [4, 128, D], F32, kind="ExternalInput").ap()
    ident = dt("ident", [128, 128], F32, kind="ExternalInput").ap()
    xo = dt("xo", [NTK, D], F32, kind="ExternalOutput").ap()

    sb, ps, res = C.sb, C.ps, C.res
    wd_b = sb("wd_b", [128, 22, D], BF16)
    wo_b = sb("wo_b", [128, 8, D], BF16)
    A_T = sb("A_T", [128, 22, 512], BF16)
    X1T = [sb("X1T%d" % i, [128, 8, 512], BF16) for i in range(2)]
    X1Th = sb("X1Th", [128, 8, 2], BF16)
    X1 = sb("X1", [128, 4, D], F32)
    wu = [[sb("wu%d_%d" % (i, j), [128, 8, 128], BF16) for j in range(2)] for i in range(3)]
    H = [sb("H%d" % i, [128, 514], F32) for i in range(4)]
    tg = [sb("tg%d" % i, [128, 512], F32) for i in range(2)]
    tv = [sb("tv%d" % i, [128, 512], F32) for i in range(2)]
    sg = [sb("sg%d" % i, [128, 512], BF16) for i in range(2)]
    HALO = sb("HALO", [128, NCH, 2], F32)
    lnp_s = sb("lnp_s", [128, 4, D], F32)
    yt = [sb("yt%d" % i, [128, D], BF16) for i in range(2)]
    xt = [sb("xt%d" % i, [128, D], F32) for i in range(2)]
    rr = sb("rr", [128, D], F32)
    x1b = sb("x1b", [128, D], BF16)
    yT = sb("yT", [128, 8, 128], BF16)
    x2 = [sb("x2_%d" % i, [128, D], F32) for i in range(2)]
    stats = sb("stats", [128, 12], F32)
    mv = sb("mv", [128, 2], F32)
    sd = sb("sd", [128, 1], F32)
    rstd = sb("rstd", [128, 1], F32)
    cw_s = sb("cw_s", [128, NCH, 3], F32)
    cb_s = sb("cb_s", [128, NCH], F32)
    flag_s = sb("flag_s", [128, 1], F32)
    ident_b = sb("ident_b", [128, 128], BF16)
    PS = [ps("ps%d" % i, [128, 512], F32) for i in range(7)]
    PSB = ps("psb", [128, 1024], BF16)
    RPS = [res("ps", True) for _ in range(7)]
    RPSB = res("psb", True)
    R = lambda n: res(n)
    Rwd, Rwo, RAT, RX1, RX1Th, RHALO, Rlnp, Rrr, Rx1b, RyT = (R("wd"), R("wo"), R("at"), R("x1"), R("x1th"), R("halo"),
                                                             R("lnp"), R("rr"), R("x1b"), R("yT"))
    RX1T = [R("x1t") for _ in range(2)]
    Rwu = [[R("wu") for _ in range(2)] for _ in range(3)]
    RH = [R("h") for _ in range(4)]
    RHh = [R("hh") for _ in range(4)]
    RHALOc = [R("haloc") for _ in range(NCH)]
    Rtg = [R("tg") for _ in range(2)]
    Rtv = [R("tv") for _ in range(2)]
    Rsg = [R("sg") for _ in range(2)]
    Ryt = [R("yt") for _ in range(2)]
    Rxt = [R("xt") for _ in range(2)]
    Rx2 = [R("x2") for _ in range(2)]
    Rst, Rmv, Rsd, Rrstd, Rcw, Rcb, Rflag, Rid = (R("st"), R("mv"), R("sd"), R("rstd"), R("cw"), R("cb"), R("flag"), R("id"))

    C.dma("pool", ident_b[:], ident, writes=[Rid])
    wo_v = wo.rearrange("(kc f) n -> f kc n", f=128)
    for kc in range(0, 8, 4):
        C.dma("pool", wo_b[:, kc:kc + 4, :], wo_v[:, kc:kc + 4, :], writes=[Rwo], key="wo")
    C.dma("sp", lnp_s[:], lnp.rearrange("k p n -> p k n"), writes=[Rlnp])
    C.dma("sp", cw_s[:], convw, writes=[Rcw])
    C.dma("sp", cb_s[:], convb, writes=[Rcb])
    C.dma("sp", flag_s[:], flag, writes=[Rflag])

    def load_sub(i):
        sl = i % 2
        C.dma("sp", yt[sl][:], yin[i * 128:(i + 1) * 128, :], writes=[Ryt[sl]])
        C.dma("sp", xt[sl][:], xin[i * 128:(i + 1) * 128, :], writes=[Rxt[sl]])

    def layer_norm(src, Rsrc, dst, Rdst, gi):
        def st(e):
            e.bn_stats(out=stats[:, 0:6], in_=src[:, 0:512])
            return e.bn_stats(out=stats[:, 6:12], in_=src[:, 512:1024])
        C.dve(st, reads=[Rsrc], writes=[Rst])
        C.dve(lambda e: e.bn_aggr(out=mv[:], in_=stats[:]), reads=[Rst], writes=[Rmv])
        C.dve(lambda e: e.tensor_scalar(out=sd[:], in0=mv[:, 1:2], scalar1=EPS, scalar2=None, op0=ALU.add),
              reads=[Rmv], writes=[Rsd])
        C.act(lambda e: e.activation(out=sd[:], in_=sd[:], func=AF.Sqrt), reads=[Rsd], writes=[Rsd])
        C.dve(lambda e: e.reciprocal(out=rstd[:], in_=sd[:]), reads=[Rsd], writes=[Rrstd])
        C.dve(lambda e: e.tensor_scalar(out=src[:], in0=src[:], scalar1=mv[:, 0:1], scalar2=rstd[:, 0:1],
                                        op0=ALU.subtract, op1=ALU.mult), reads=[Rsrc, Rmv, Rrstd], writes=[Rsrc])
        C.dve(lambda e: e.tensor_tensor(out=src[:], in0=src[:], in1=lnp_s[:, gi, :], op=ALU.mult),
              reads=[Rsrc, Rlnp], writes=[Rsrc])
        C.dve(lambda e: e.tensor_tensor(out=dst, in0=src[:], in1=lnp_s[:, gi + 1, :], op=ALU.add),
              reads=[Rsrc, Rlnp], writes=[Rdst])

    def a_front(i):
        sl = i % 2

        def tr_y(e):
            ins = None
            for kc in range(8):
                ins = e.transpose(PSB[:, kc * 128:(kc + 1) * 128], yt[sl][:, kc * 128:(kc + 1) * 128], ident_b[:])
            return ins
        C.pe(tr_y, reads=[Ryt[sl], Rid], writes=[RPSB])
        C.act(lambda e: e.copy(out=yT[:].rearrange("p k n -> p (k n)"), in_=PSB[:]), reads=[RPSB], writes=[RyT])
        for half in range(2):
            C.pe(_mm_group(PS[half][:], [(yT[:, kc, :], wo_b[:, kc, half * 512:(half + 1) * 512]) for kc in range(8)]),
                 reads=[RyT, Rwo], writes=[RPS[half]])

    def a_x1src(i):
        if i == 0:
            return x2[0][:], Rx2[0]
        c = (i - 1) % 4
        return X1[:, c, :], RX1

    def a_cast(i):
        x1src, Rx1src = a_x1src(i)
        C.act(lambda e: e.copy(out=x1b[:], in_=x1src), reads=[Rx1src], writes=[Rx1b])

    def a_back(i):
        def tr_x(e):
            ins = None
            for kc in range(8):
                ins = e.transpose(PSB[:, kc * 128:(kc + 1) * 128], x1b[:, kc * 128:(kc + 1) * 128], ident_b[:])
            return ins
        C.pe(tr_x, reads=[Rx1b, Rid], writes=[RPSB])
        psv = PSB[:].rearrange("p (k n) -> p k n", n=128)
        if i == 0:
            C.act(lambda e: e.copy(out=X1Th[:], in_=psv[:, :, 126:128]), reads=[RPSB], writes=[RX1Th])
        else:
            T = (i - 1) // 4
            c = (i - 1) % 4
            C.act(lambda e: e.copy(out=X1T[T % 2][:, :, c * 128:(c + 1) * 128], in_=psv), reads=[RPSB], writes=[RX1T[T % 2]])

    def a_mid(i):
        sl = i % 2
        for half in range(2):
            C.dve(lambda e, half=half: e.scalar_tensor_tensor(out=rr[:, half * 512:(half + 1) * 512], in0=xt[sl][:, half * 512:(half + 1) * 512],
                                                              scalar=ALPHA, in1=PS[half][:], op0=ALU.mult, op1=ALU.add),
                  reads=[Rxt[sl], RPS[half]], writes=[Rrr])
        dst, Rdst = a_x1src(i)
        layer_norm(rr, Rrr, dst, Rdst, 0)

    def stage_a_tile(T):
        subs = ([0] if T == 0 else []) + [1 + 4 * T + c for c in range(4)]
        a_front(subs[0])
        for n, i in enumerate(subs):
            if i + 1 <= NT4 * 4:
                load_sub(i + 1)
            if n >= 1:
                a_cast(subs[n - 1])
            a_mid(i)
            if n >= 1:
                a_back(subs[n - 1])
            if n + 1 < len(subs):
                a_front(subs[n + 1])
        a_cast(subs[-1])
        a_back(subs[-1])

    wup_loaded = {}

    def load_wup(T, cc):
        sl = (T * 22 + cc) % 3
        if os.environ.get("P_NOLOAD") and T > 0:
            return
        C.dma("pool", wu[sl][0][:], wup[cc], writes=[Rwu[sl][0]])
        C.dma("pool", wu[sl][1][:], wup[22 + cc], writes=[Rwu[sl][1]])

    state = {"k": 0}

    def ffn_chunk(T, cc, which):
        ch = cc + 22 * which
        wsl = (T * 22 + cc) % 3
        k = 2 * (T * 22 + cc) + which
        hb = k % 4
        bank = (2, 3, 5, 6)[hb]
        xs = X1T[T % 2]
        if T == 0:
            C.pe(_mm_group(PS[4][:, 0:2], [(wu[wsl][which][:, kc, :], X1Th[:, kc, :]) for kc in range(8)]),
                 reads=[Rwu[wsl][which], RX1Th], writes=[RPS[4]])
            C.act(lambda e: e.activation(out=H[hb][:, 0:2], in_=PS[4][:, 0:2], func=AF.Copy, scale=flag_s[:, 0:1]),
                  reads=[RPS[4], Rflag], writes=[RHh[hb]])
        C.pe(_mm_group(PS[bank][:], [(wu[wsl][which][:, kc, :], xs[:, kc, :]) for kc in range(8)]),
             reads=[Rwu[wsl][which], RX1T[T % 2]], writes=[RPS[bank]])
        C.act(lambda e: e.copy(out=H[hb][:, 2:514], in_=PS[bank][:]), reads=[RPS[bank]], writes=[RH[hb]])
        C.pool(lambda e: e.tensor_copy(out=HALO[:, ch, :], in_=H[hb][:, 512:514]), reads=[RH[hb]], writes=[RHALOc[ch]])
        t, Rt = (tg[cc % 2], Rtg[cc % 2]) if which == 0 else (tv[cc % 2], Rtv[cc % 2])
        C.act(lambda e: e.activation(out=t[:], in_=H[hb][:, 0:512], func=AF.Identity, bias=cb_s[:, ch:ch + 1],
                                     scale=cw_s[:, ch, 0:1]), reads=[RH[hb], RHh[hb], Rcw, Rcb], writes=[Rt])
        C.dve(lambda e: e.scalar_tensor_tensor(out=t[:], in0=H[hb][:, 1:513], scalar=cw_s[:, ch, 1:2], in1=t[:],
                                               op0=ALU.mult, op1=ALU.add), reads=[RH[hb], RHh[hb], Rcw, Rt], writes=[Rt])
        C.dve(lambda e: e.scalar_tensor_tensor(out=t[:], in0=H[hb][:, 2:514], scalar=cw_s[:, ch, 2:3], in1=t[:],
                                               op0=ALU.mult, op1=ALU.add), reads=[RH[hb], Rcw, Rt], writes=[Rt])

    def halo_read(T, cc):
        if T == 0:
            return
        for which in range(2):
            ch = cc + 22 * which
            hb = (2 * (T * 22 + cc) + which) % 4
            C.pool(lambda e, ch=ch, hb=hb: e.tensor_copy(out=H[hb][:, 0:2], in_=HALO[:, ch, :]),
                   reads=[RHALOc[ch]], writes=[RHh[hb]])

    def ffn_pair(T, cc):
        if T * 22 + cc + 1 < NT4 * 22:
            nT1, ncc1 = divmod(T * 22 + cc + 1, 22)
            halo_read(nT1, ncc1)
        if T * 22 + cc + 2 < NT4 * 22:
            nT, ncc = divmod(T * 22 + cc + 2, 22)
            load_wup(nT, ncc)
        ffn_chunk(T, cc, 0)
        ffn_chunk(T, cc, 1)
        s2 = cc % 2
        C.act(lambda e: e.activation(out=sg[s2][:], in_=tg[s2][:], func=AF.Silu), reads=[Rtg[s2]], writes=[Rsg[s2]])
        C.pool(lambda e: e.tensor_tensor(out=A_T[:, cc, :], in0=sg[s2][:], in1=tv[s2][:], op=ALU.mult),
               reads=[Rsg[s2], Rtv[s2]], writes=[RAT])

    def stage_c(T, c):
        j = T * 4 + c
        banks = (5, 6) if j % 2 == 0 else (0, 1)
        for half in range(2):
            C.pe(_mm_group(PS[banks[half]][:], [(A_T[:, cc, c * 128:(c + 1) * 128], wd_b[:, cc, half * 512:(half + 1) * 512])
                                                 for cc in range(22)]), reads=[RAT, Rwd], writes=[RPS[banks[half]]])
        for half in range(2):
            C.dve(lambda e, half=half: e.scalar_tensor_tensor(out=rr[:, half * 512:(half + 1) * 512], in0=X1[:, c, half * 512:(half + 1) * 512],
                                                              scalar=ALPHA, in1=PS[banks[half]][:], op0=ALU.mult, op1=ALU.add),
                  reads=[RX1, RPS[banks[half]]], writes=[Rrr])
        o = j % 2
        layer_norm(rr, Rrr, x2[o][:], Rx2[o], 2)
        C.dma("sp", xo[j * 128:(j + 1) * 128, :], x2[o][:], reads=[Rx2[o]], key="xo%d" % o)

    load_sub(0)
    load_wup(0, 0)
    load_wup(0, 1)
    wd_v = wdn.rearrange("(cc p) n -> p cc n", p=128)
    for T in range(NT4):
        stage_a_tile(T)
        if T == 0:
            for c0 in range(0, 22, 2):
                C.dma("pool", wd_b[:, c0:c0 + 2, :], wd_v[:, c0:c0 + 2, :], writes=[Rwd], key="wd")
        for cc in range(22):
            ffn_pair(T, cc)
        for c in range(4):
            stage_c(T, c)
    return _finish_mixer(C, nc)


def _post_inputs(y_b, x_b, q, l, P):
    t0 = q * NTOK
    if q == 0:
        yh = np.zeros((128, D), ml_dtypes.bfloat16)
        xh = np.zeros((128, D), np.float32)
    else:
        yh = y_b[t0 - 128:t0]
        xh = x_b[t0 - 128:t0]
    rep = lambda v: np.broadcast_to(v[None, :], (128, v.shape[0]))
    key = ("post_w", l)
    if key not in _CACHE:
        w_up = P["w_up"][l]
        _CACHE[key] = {
            "wo": np.ascontiguousarray(P["w_o"][l]),
            "wup": np.ascontiguousarray(w_up.reshape(8, 128, NCH, 128).transpose(2, 1, 0, 3)),
            "wdn": np.ascontiguousarray(P["w_down"][l]),
            "convw": np.ascontiguousarray(P["conv_w"][l].reshape(3, NCH, 128).transpose(2, 1, 0)),
            "convb": np.ascontiguousarray(P["conv_b"][l].reshape(NCH, 128).T),
            "lnp": np.ascontiguousarray(np.stack([rep(P["ln1_g"][l]), rep(P["ln1_b"][l]), rep(P["ln2_g"][l]), rep(P["ln2_b"][l])]).astype(np.float32)),
            "ident": np.eye(128, dtype=np.float32),
        }
    m = dict(_CACHE[key])
    m["yin"] = np.ascontiguousarray(np.concatenate([yh, y_b[t0:t0 + NTOK]], axis=0))
    m["xin"] = np.ascontiguousarray(np.concatenate([xh, x_b[t0:t0 + NTOK]], axis=0))
    m["flag"] = np.full((128, 1), 0.0 if q == 0 else 1.0, np.float32)
    return m


def run_post(y, x, l, P):
    if "nc_p" not in _CACHE:
        _CACHE["nc_p"] = build_post()
    in_maps = [_post_inputs(y[c // 4], x[c // 4], c % 4, l, P) for c in range(8)]
    res = run_bass_kernel_spmd(_CACHE["nc_p"], in_maps, core_ids=list(range(8)))
    out = np.zeros((NB, S, D), np.float32)
    for c in range(8):
        out[c // 4, (c % 4) * NTOK:(c % 4 + 1) * NTOK] = res.results[c]["xo"]
    return out


def kernel(**inputs):
    P = {k: np.asarray(v) for k, v in inputs.items()}
    x = np.ascontiguousarray(P["x"], dtype=np.float32)
    for l in range(2):
        xT = [np.ascontiguousarray(x[b].T) for b in range(NB)]
        y = run_mixer(xT, l, P)
        x = run_post(y, x, l, P)
    return x
```

```python
import contextlib
import os
DBG = int(os.environ.get('MIX_DBG', '99'))
import numpy as np
import ml_dtypes
import concourse.bass as bass
import concourse.mybir as mybir
from concourse.bass_utils import run_bass_kernel_spmd

F32 = mybir.dt.float32
BF16 = mybir.dt.bfloat16
AF = mybir.ActivationFunctionType
ALU = mybir.AluOpType
AX = mybir.AxisListType

S = 8192
D = 1024
NB = 2
DFF = 2816
ALPHA = 4.0 ** 0.25
EPS = 1e-5
BIGM = 30000.0
NEGF = -1.0e30


class Res:
    __slots__ = ("name", "w", "r", "excl")

    def __init__(self, name, excl=False):
        self.name = name
        self.w = None
        self.r = []
        self.excl = excl


class Sched:
    STREAMS = ("pe", "act", "dve", "pool", "sp")

    def __init__(self, nc):
        self.nc = nc
        self.ops = {s: [] for s in self.STREAMS}
        self.ccount = {s: 0 for s in self.STREAMS}
        self.dcount = {}
        self.known = {s: {} for s in self.STREAMS}
        self.final_events = []

    def _need(self, stream, ev, waits):
        if ev is None:
            return
        sem, val, src = ev
        if src == stream and src == "pe":
            return
        if self.known[stream].get(sem, 0) >= val:
            return
        self.known[stream][sem] = val
        waits.append((sem, val))

    def op(self, stream, fn, reads=(), writes=(), dma_key=None):
        ex = [r for r in reads if r.excl]
        if ex:
            reads = [r for r in reads if not r.excl]
            writes = list(writes) + [r for r in ex if r not in writes]
        waits = []
        for r in reads:
            self._need(stream, r.w, waits)
        for w in writes:
            self._need(stream, w.w, waits)
            for e in w.r:
                self._need(stream, e, waits)
        if dma_key is not None:
            k = "d_" + dma_key
            self.dcount[k] = self.dcount.get(k, 0) + 1
            ev = (k, 16 * self.dcount[k], "dma")
            sig = (k, 16)
        else:
            self.ccount[stream] += 1
            ev = ("c_" + stream, self.ccount[stream], stream)
            sig = ("c_" + stream, 1)
        for r in reads:
            r.r.append(ev)
        for w in writes:
            w.w = ev
            w.r = []
        self.ops[stream].append((waits, fn, sig))
        return ev

    def emit(self, final_events):
        nc = self.nc
        names = set()
        for s in self.STREAMS:
            for waits, fn, sig in self.ops[s]:
                names.add(sig[0])
                for (sem, val) in waits:
                    names.add(sem)
        with contextlib.ExitStack() as st:
            sems = {n: st.enter_context(nc.semaphore(n)) for n in sorted(names)}
            block = st.enter_context(nc.Block())

            def make(stream):
                def body(eng):
                    for waits, fn, sig in self.ops[stream]:
                        for (sem, val) in waits:
                            eng.wait_ge(sems[sem], val)
                        ins = fn(eng)
                        ins.then_inc(sems[sig[0]], sig[1])
                    if stream == "sp":
                        for (sem, val, src) in final_events:
                            eng.wait_ge(sems[sem], val)
                return body

            block.tensor(make("pe"))
            block.scalar(make("act"))
            block.vector(make("dve"))
            block.gpsimd(make("pool"))
            block.sync(make("sp"))


class Ctx:
    def __init__(self, nc):
        self.nc = nc
        self.st = contextlib.ExitStack()
        self.S = Sched(nc)
        self.nres = 0

    def sb(self, name, shape, dt):
        return self.st.enter_context(self.nc.sbuf_tensor(name, shape, dt))

    def ps(self, name, shape, dt):
        return self.st.enter_context(self.nc.psum_tensor(name, shape, dt))

    def res(self, name="r", excl=False):
        self.nres += 1
        return Res("%s%d" % (name, self.nres), excl)

    def pe(self, fn, reads=(), writes=()):
        return self.S.op("pe", fn, reads, writes)

    def act(self, fn, reads=(), writes=()):
        return self.S.op("act", fn, reads, writes)

    def dve(self, fn, reads=(), writes=()):
        return self.S.op("dve", fn, reads, writes)

    def pool(self, fn, reads=(), writes=()):
        return self.S.op("pool", fn, reads, writes)

    def dma(self, queue, out, in_, reads=(), writes=(), key=None):
        if key is None:
            key = (writes[0].name if writes else reads[0].name)
        return self.S.op(queue, lambda e: e.dma_start(out=out, in_=in_), reads, writes, dma_key=key)


def _mm_group(out, pairs):
    def fn(e):
        n = len(pairs)
        ins = None
        for i, (l, r) in enumerate(pairs):
            ins = e.matmul(out, lhsT=l, rhs=r, start=(i == 0), stop=(i == n - 1))
        return ins
    return fn


NFM = 352
NTM = 257


def build_mixer(SEQ=S, PH=9):
    NT, NTT = SEQ // 512, SEQ // 128
    nc = bass.Bass("TRN2", target_bir_lowering=False)
    C = Ctx(nc)
    dt = nc.dram_tensor
    xT = dt("xT", [D, SEQ], F32, kind="ExternalInput").ap()
    wfm = dt("wfm", [D, NFM], F32, kind="ExternalInput").ap()
    wtm = dt("wtm", [D, NTM], F32, kind="ExternalInput").ap()
    cs = dt("cs", [2, 16, SEQ], F32, kind="ExternalInput").ap()
    onehot = dt("onehot", [32, SEQ], BF16, kind="ExternalInput").ap()
    sguT = dt("sguT", [128, 128], F32, kind="ExternalInput").ap()
    tri = dt("tri", [128, 128], F32, kind="ExternalInput").ap()
    ident = dt("ident", [128, 128], F32, kind="ExternalInput").ap()
    sgub = dt("sgub", [128, 1], F32, kind="ExternalInput").ap()
    lng = dt("lng", [128, 64], F32, kind="ExternalInput").ap()
    lnb = dt("lnb", [128, 64], F32, kind="ExternalInput").ap()
    poolw = dt("poolw", [64, 64], F32, kind="ExternalInput").ap()
    pscale = dt("pscale", [128, 64], F32, kind="ExternalInput").ap()
    band = dt("band", [3, 128, 128], F32, kind="ExternalInput").ap()
    bfor = dt("bfor", [128, 1], F32, kind="ExternalInput").ap()
    y = dt("y", [SEQ, 256], BF16, kind="ExternalOutput").ap()

    sb, ps, res = C.sb, C.ps, C.res
    QA = sb("QA", [128, SEQ], BF16)
    KA = sb("KA", [128, SEQ], BF16)
    QB = sb("QB", [128, SEQ], BF16)
    KB = sb("KB", [128, SEQ], BF16)
    VA = sb("VA", [128, NTT, 65], BF16)
    VB = sb("VB", [128, NTT, 65], BF16)
    xb = [sb("xb%d" % i, [128, 8, 512], BF16) for i in range(2)]
    wfm_b = sb("wfm_b", [128, 8, NFM], BF16)
    wtm_b = sb("wtm_b", [128, 8, NTM], BF16)
    cst = [sb("cst%d" % i, [16, 2, 512], F32) for i in range(2)]
    t1 = [sb("t1_%d" % i, [16, 512], F32) for i in range(2)]
    t2 = [sb("t2_%d" % i, [16, 512], F32) for i in range(2)]
    pTb = [sb("pTb%d" % i, [64, 512], BF16) for i in range(2)]
    ug = [sb("ug%d" % i, [128, 128], F32) for i in range(2)]
    stats = [sb("stats%d" % i, [128, 6], F32) for i in range(2)]
    mv = [sb("mv%d" % i, [128, 2], F32) for i in range(2)]
    rstd = [sb("rstd%d" % i, [128, 1], F32) for i in range(2)]
    vn = [sb("vn%d" % i, [128, 64], F32) for i in range(2)]
    vnb = [sb("vnb%d" % i, [128, 64], BF16) for i in range(2)]
    pwb = [sb("pwb%d" % i, [128, 64], BF16) for i in range(3)]
    YC = [sb("YC%d" % i, [128, 4, 64], BF16) for i in range(2)]
    YD = [sb("YD%d" % i, [128, 4, 64], BF16) for i in range(2)]
    UG = sb("UG", [128, NTT, 128], BF16)
    MV = sb("MV", [128, NTT, 2], F32)
    VE = sb("VE", [128, NTT], F32)
    RSTD = sb("RSTD", [128, NTT], F32)
    YAB = [sb("YAB%d" % i, [128, 4, 128], BF16) for i in range(2)]
    MBZ = sb("MBZ", [128, 64 + NTT * 32], F32)
    GM = sb("GM", [128, 32], F32)
    top8 = [sb("top8_%d" % i, [128, 8], F32) for i in range(2)]
    FRAW = sb("FRAW", [128, NTT], F32)
    LF = sb("LF", [128, NTT], F32)
    TOT = sb("TOT", [128, NTT], F32)
    PREF = sb("PREF", [128, NTT], F32)
    CP = sb("CP", [128, NTT], F32)
    Z = sb("Z", [128, 128], F32)
    kbar = sb("kbar", [64, 32], F32)
    kbar_b = sb("kbar_b", [64, 32], BF16)
    PT = [sb("PT%d" % i, [128, 512], BF16) for i in range(4)]
    rl = [sb("rl%d" % i, [128, 1], F32) for i in range(4)]
    ident_f = sb("ident_f", [128, 128], F32)
    tri_f = sb("tri_f", [128, 128], F32)
    tri_b = sb("tri_b", [128, 128], BF16)
    ones_f = sb("ones_f", [128, 128], F32)
    sgu_f = sb("sgu_f", [128, 128], F32)
    wmT_b = sb("wmT_b", [128, 128], BF16)
    band_b = sb("band_b", [128, 3, 128], BF16)
    sgub_s = sb("sgub_s", [128, 1], F32)
    lng_s = sb("lng_s", [128, 64], F32)
    lnb_s = sb("lnb_s", [128, 64], F32)
    poolw_b = sb("poolw_b", [64, 64], BF16)
    pscale_s = sb("pscale_s", [128, 64], F32)
    bfor_s = sb("bfor_s", [128, 1], F32)
    negb = sb("negb", [128, 1], F32)
    ef = sb("ef", [128, NTT], F32)
    PS = [ps("ps%d" % i, [128, 512], F32) for i in range(8)]
    RPS = [res("ps", True) for _ in range(8)]

    R = lambda n: res(n)
    RQA = [R("qa") for _ in range(NT)]
    RQAm = [R("qam") for _ in range(NT)]
    RKA = [R("ka") for _ in range(NT)]
    RQB = [R("qb") for _ in range(NT)]
    RQBc = [R("qbc") for _ in range(NT)]
    RKB = [R("kb") for _ in range(NT)]
    RVA = [R("va") for _ in range(NT)]
    RVB = [R("vb") for _ in range(NT)]
    Rxb = [R("xb") for _ in range(2)]
    Rcs = [R("cs") for _ in range(2)]
    Rt1 = [R("t1") for _ in range(2)]
    Rt2 = [R("t2") for _ in range(2)]
    RpT = [R("pT") for _ in range(2)]
    Rug = [R("ug") for _ in range(2)]
    Rst = [R("st") for _ in range(2)]
    Rmv = [R("mv") for _ in range(2)]
    Rrs = [R("rs") for _ in range(2)]
    Rvn = [R("vn") for _ in range(2)]
    Rvnb = [R("vnb") for _ in range(2)]
    Rpwb = [R("pwb") for _ in range(3)]
    RYC = [R("yc") for _ in range(2)]
    RYD = [R("yd") for _ in range(2)]
    RUG = [R("ugall") for _ in range(NT)]
    RMV, RVE, RRSTD = R("mvall"), R("ve"), R("rstdall")
    RYAB = [R("yab") for _ in range(2)]
    RMBZ = [R("mbz") for _ in range(NT)]
    RGM = R("gm")
    Rtop = [R("top") for _ in range(2)]
    RFRAW, RLF, RTOT, RPREF, RCP, RZ, Rkbar, Rkbarb, Ref = (R("fraw"), R("lf"), R("tot"), R("pref"),
                                                            R("cp"), R("z"), R("kbar"), R("kbarb"), R("ef"))
    RPT = [R("pt") for _ in range(4)]
    Rrl = [R("rl") for _ in range(4)]
    Rc = {k: R(k) for k in ["wfm", "wtm", "ident", "tri_f", "tri_b", "ones", "sgu_f", "wmT", "band", "sgub",
                            "lng", "lnb", "poolw", "pscale", "bfor", "negb", "kaoh", "misc"]}

    C.dma("pool", wfm_b[:], wfm.rearrange("(kc f) n -> f kc n", f=128), writes=[Rc["wfm"]])
    C.dma("pool", wtm_b[:], wtm.rearrange("(kc f) n -> f kc n", f=128), writes=[Rc["wtm"]])
    C.dma("sp", ident_f[:], ident, writes=[Rc["ident"]])
    C.dma("sp", tri_f[:], tri, writes=[Rc["tri_f"]])
    C.dma("pool", tri_b[:], tri, writes=[Rc["tri_b"]])
    C.dma("sp", sgu_f[:], sguT, writes=[Rc["sgu_f"]])
    C.dma("pool", band_b[:], band.rearrange("k s t -> s k t"), writes=[Rc["band"]])
    C.dma("sp", sgub_s[:], sgub, writes=[Rc["sgub"]])
    C.dma("sp", lng_s[:], lng, writes=[Rc["lng"]])
    C.dma("sp", lnb_s[:], lnb, writes=[Rc["lnb"]])
    C.dma("pool", poolw_b[:], poolw, writes=[Rc["poolw"]])
    C.dma("sp", pscale_s[:], pscale, writes=[Rc["pscale"]])
    C.dma("sp", bfor_s[:], bfor, writes=[Rc["bfor"]])
    C.dma("sp", KA[64:96, :], onehot, writes=[Rc["kaoh"]])
    C.dve(lambda e: e.tensor_tensor(out=wmT_b[:], in0=sgu_f[:], in1=tri_f[:], op=ALU.mult),
          reads=[Rc["sgu_f"], Rc["tri_f"]], writes=[Rc["wmT"]])
    C.dve(lambda e: e.tensor_scalar(out=negb[:], in0=bfor_s[:], scalar1=-1.0, scalar2=None, op0=ALU.mult),
          reads=[Rc["bfor"]], writes=[Rc["negb"]])
    C.dve(lambda e: e.memset(ones_f[:], 1.0), writes=[Rc["ones"]])
    C.dve(lambda e: e.memset(VA[:, :, 64:65], 1.0), writes=RVA)
    C.dve(lambda e: e.memset(VB[:, :, 64:65], 1.0), writes=RVB)
    C.dve(lambda e: e.memset(KB[64:65, :], 1.0), writes=RKB)
    C.dve(lambda e: e.memset(MBZ[:], 0.0), writes=RMBZ)
    C.dve(lambda e: e.memset(GM[:], NEGF), writes=[RGM])
    C.dve(lambda e: e.memset(Z[:], 0.0), writes=[RZ])
    C.dve(lambda e: e.memset(kbar[:], 0.0), writes=[Rkbar])
    C.dve(lambda e: e.memset(PREF[:, 0:1], 0.0), writes=[RPREF])

    if PH == 0:
        return _finish_mixer(C, nc)
    xT_v = xT.rearrange("(kc f) t -> f kc t", f=128)

    def load_tile(T):
        sl = T % 2
        C.dma("pool", xb[sl][:], xT_v[:, :, T * 512:(T + 1) * 512], writes=[Rxb[sl]])
        C.dma("sp", cst[sl][:], cs[:, :, T * 512:(T + 1) * 512].rearrange("k p t -> p k t"), writes=[Rcs[sl]])

    load_tile(0)
    fm_cols = {"qA": (0, 64), "kA": (64, 64), "qB": (128, 64), "kB": (192, 64), "pD": (256, 64),
               "qAp": (320, 16), "kAp": (336, 16)}
    pend = []

    def p1_tile(T):
        sl = T % 2
        if T + 1 < NT:
            load_tile(T + 1)
        cols = slice(T * 512, (T + 1) * 512)

        def fm(name, bank):
            c0, n = fm_cols[name]
            C.pe(_mm_group(PS[bank][0:n, :], [(wfm_b[:, kc, c0:c0 + n], xb[sl][:, kc, :]) for kc in range(8)]),
                 reads=[Rc["wfm"], Rxb[sl]], writes=[RPS[bank]])

        def rot(nm, nmp, dst, Rdst, b0, b1):
            fm(nm, b0)
            fm(nmp, b1)
            if DBG == 10:
                return
            C.act(lambda e, dst=dst: e.copy(out=dst[0:64, cols], in_=PS[b0][0:64, :]), reads=[RPS[b0]], writes=[Rdst[T]])
            if DBG == 11:
                return
            C.dve(lambda e: e.tensor_tensor(out=t1[sl][:], in0=PS[b0][0:16, :], in1=cst[sl][:, 0, :], op=ALU.mult),
                  reads=[RPS[b0], Rcs[sl]], writes=[Rt1[sl]])
            C.dve(lambda e: e.tensor_tensor(out=t2[sl][:], in0=PS[b1][0:16, :], in1=cst[sl][:, 1, :], op=ALU.mult),
                  reads=[RPS[b1], Rcs[sl]], writes=[Rt2[sl]])
            if DBG == 12:
                return
            C.dve(lambda e, dst=dst: e.tensor_tensor(out=dst[0:16, cols], in0=t1[sl][:], in1=t2[sl][:], op=ALU.add),
                  reads=[Rt1[sl], Rt2[sl]], writes=[Rdst[T]])
        if DBG < 1:
            return
        rot("qA", "qAp", QA, RQA, 0, 1)
        rot("kA", "kAp", KA, RKA, 4, 7)
        if DBG < 2 or (10 <= DBG < 20):
            return
        C.dve(lambda e: e.tensor_reduce(out=kbar[:, 2 * T:2 * T + 2],
                                        in_=KA[0:64, cols].rearrange("p (b j) -> p b j", j=256),
                                        axis=AX.X, op=ALU.add), reads=[RKA[T]], writes=[Rkbar])
        if DBG < 3:
            return
        fm("qB", 0)
        C.act(lambda e: e.copy(out=QB[0:64, cols], in_=PS[0][0:64, :]), reads=[RPS[0]], writes=[RQB[T]])
        fm("kB", 1)
        C.act(lambda e: e.copy(out=KB[0:64, cols], in_=PS[1][0:64, :]), reads=[RPS[1]], writes=[RKB[T]])
        fm("pD", 4)
        C.act(lambda e: e.copy(out=pTb[sl][:], in_=PS[4][0:64, :]), reads=[RPS[4]], writes=[RpT[sl]])
        if DBG < 4:
            return
        def sub(c):
            tt = 4 * T + c
            s2 = tt % 2
            bk = 2 + s2
            C.pe(_mm_group(PS[bk][:, 0:NTM], [(xb[sl][:, kc, c * 128:(c + 1) * 128], wtm_b[:, kc, :]) for kc in range(8)]),
                 reads=[Rc["wtm"], Rxb[sl]], writes=[RPS[bk]])
            C.act(lambda e, bk=bk, tt=tt: e.copy(out=VA[:, tt, 0:64], in_=PS[bk][:, 0:64]), reads=[RPS[bk]], writes=[RVA[T]])
            C.act(lambda e, bk=bk, tt=tt: e.copy(out=VB[:, tt, 0:64], in_=PS[bk][:, 64:128]), reads=[RPS[bk]], writes=[RVB[T]])
            C.dve(lambda e, bk=bk, tt=tt: e.tensor_copy(out=FRAW[:, tt:tt + 1], in_=PS[bk][:, 256:257]),
                  reads=[RPS[bk]], writes=[RFRAW])
            C.act(lambda e, bk=bk, tt=tt: e.activation(out=UG[:, tt, :], in_=PS[bk][:, 128:256], func=AF.Gelu),
                  reads=[RPS[bk]], writes=[RUG[T]])
            C.dve(lambda e, tt=tt, s2=s2: e.bn_stats(out=stats[s2][:], in_=UG[:, tt, 64:128]), reads=[RUG[T]], writes=[Rst[s2]])
            C.dve(lambda e, tt=tt, s2=s2: e.bn_aggr(out=MV[:, tt, :], in_=stats[s2][:]), reads=[Rst[s2]], writes=[RMV])
            s3 = tt % 3
            C.pe(lambda e, c=c: e.matmul(PS[5][:, 0:64], lhsT=pTb[sl][:, c * 128:(c + 1) * 128], rhs=poolw_b[:],
                                         start=True, stop=True), reads=[RpT[sl], Rc["poolw"]], writes=[RPS[5]])
            C.act(lambda e, s3=s3: e.copy(out=pwb[s3][:], in_=PS[5][:, 0:64]), reads=[RPS[5]], writes=[Rpwb[s3]])
            flush_band()
            pend.append((T, c))

        def flush_band():
            while pend:
                bT, bc = pend.pop(0)
                btt = 4 * bT + bc
                bs3, bp3, bsl = btt % 3, (btt - 1) % 3, bT % 2
                if btt == 0:
                    C.pe(lambda e, bs3=bs3: e.matmul(PS[6][:, 0:64], lhsT=band_b[:, 2, :], rhs=pwb[bs3][:], start=True, stop=True),
                         reads=[Rc["band"], Rpwb[bs3]], writes=[RPS[6]])
                else:
                    C.pe(_mm_group(PS[6][:, 0:64], [(band_b[:, 0, :], pwb[bs3][:]), (band_b[:, 1, :], pwb[bp3][:])]),
                         reads=[Rc["band"], Rpwb[bs3], Rpwb[bp3]], writes=[RPS[6]])
                C.dve(lambda e, bc=bc, bsl=bsl: e.tensor_tensor(out=YD[bsl][:, bc, :], in0=PS[6][:, 0:64], in1=pscale_s[:], op=ALU.mult),
                      reads=[RPS[6], Rc["pscale"]], writes=[RYD[bsl]])
        for c in range(4):
            sub(c)
        flush_band()
        C.dma("sp", y[T * 512:(T + 1) * 512, 192:256].rearrange("(c p) n -> p c n", p=128), YD[sl][:],
              reads=[RYD[sl]], key="yd%d" % sl)

    for T in range(NT):
        p1_tile(T)

    if PH == 1:
        return _finish_mixer(C, nc)
    C.dve(lambda e: e.tensor_scalar(out=VE[:], in0=MV[:, :, 1], scalar1=EPS, scalar2=None, op0=ALU.add),
          reads=[RMV], writes=[RVE])
    C.act(lambda e: e.activation(out=VE[:], in_=VE[:], func=AF.Sqrt), reads=[RVE], writes=[RVE])
    C.dve(lambda e: e.reciprocal(out=RSTD[:], in_=VE[:]), reads=[RVE], writes=[RRSTD])

    vn4 = [sb("vn4_%d" % i, [128, 64], F32) for i in range(4)]
    vnb4 = [sb("vnb4_%d" % i, [128, 64], BF16) for i in range(4)]
    Rvn4 = [R("vn4") for _ in range(4)]
    Rvnb4 = [R("vnb4") for _ in range(4)]

    def c_stage(T, stage):
        sl = T % 2
        if stage == 0:
            for c in range(4):
                tt = 4 * T + c
                C.dve(lambda e, c=c, tt=tt: e.tensor_scalar(out=vn4[c][:], in0=UG[:, tt, 64:128], scalar1=MV[:, tt, 0:1],
                                                            scalar2=RSTD[:, tt:tt + 1], op0=ALU.subtract, op1=ALU.mult),
                      reads=[RUG[T], RMV, RRSTD], writes=[Rvn4[c]])
                C.dve(lambda e, c=c: e.tensor_tensor(out=vn4[c][:], in0=vn4[c][:], in1=lng_s[:], op=ALU.mult),
                      reads=[Rvn4[c], Rc["lng"]], writes=[Rvn4[c]])
                C.dve(lambda e, c=c: e.tensor_tensor(out=vnb4[c][:], in0=vn4[c][:], in1=lnb_s[:], op=ALU.add),
                      reads=[Rvn4[c], Rc["lnb"]], writes=[Rvnb4[c]])
        elif stage == 1:
            def mix(e):
                ins = None
                for c in range(4):
                    ins = e.matmul(PS[5][:, c * 64:(c + 1) * 64], lhsT=wmT_b[:], rhs=vnb4[c][:], start=True, stop=True)
                return ins
            C.pe(mix, reads=[Rc["wmT"]] + Rvnb4, writes=[RPS[5]])
        else:
            for c in range(4):
                tt = 4 * T + c
                C.dve(lambda e, c=c, tt=tt: e.scalar_tensor_tensor(out=YC[sl][:, c, :], in0=PS[5][:, c * 64:(c + 1) * 64],
                                                                   scalar=sgub_s[:, 0:1], in1=UG[:, tt, 0:64],
                                                                   op0=ALU.add, op1=ALU.mult),
                      reads=[RPS[5], Rc["sgub"], RUG[T]], writes=[RYC[sl]])
            C.dma("sp", y[T * 512:(T + 1) * 512, 128:192].rearrange("(c p) n -> p c n", p=128), YC[sl][:],
                  reads=[RYC[sl]], key="yc%d" % sl)

    def c_tile(T):
        for s in range(3):
            c_stage(T, s)

    if PH < 4:
        for T in range(NT):
            c_tile(T)

    if PH == 2:
        return _finish_mixer(C, nc)
    C.act(lambda e: e.activation(out=ef[:], in_=FRAW[:], func=AF.Exp, bias=negb[:, 0:1], scale=-1.0),
          reads=[RFRAW, Rc["negb"]], writes=[Ref])
    C.act(lambda e: e.activation(out=LF[:], in_=ef[:], func=AF.Ln, bias=1.0, scale=1.0), reads=[Ref], writes=[RLF])
    C.pe(lambda e: e.matmul(PS[0][:, 0:NTT], lhsT=ones_f[:], rhs=LF[:], start=True, stop=True),
         reads=[Rc["ones"], RLF], writes=[RPS[0]])
    C.dve(lambda e: e.tensor_copy(out=TOT[:], in_=PS[0][:, 0:NTT]), reads=[RPS[0]], writes=[RTOT])
    for j in range(1, NTT):
        C.dve(lambda e, j=j: e.tensor_tensor(out=PREF[:, j:j + 1], in0=PREF[:, j - 1:j], in1=TOT[:, j - 1:j], op=ALU.add),
              reads=[RPREF, RTOT], writes=[RPREF])
    C.pe(lambda e: e.matmul(PS[1][:, 0:NTT], lhsT=tri_f[:], rhs=LF[:], start=True, stop=True),
         reads=[Rc["tri_f"], RLF], writes=[RPS[1]])
    C.dve(lambda e: e.tensor_tensor(out=CP[:], in0=PS[1][:, 0:NTT], in1=PREF[:], op=ALU.add),
          reads=[RPS[1], RPREF], writes=[RCP])
    C.dve(lambda e: e.tensor_scalar(out=Z[:, 64:64 + NTT], in0=CP[:], scalar1=-8.0, scalar2=None, op0=ALU.mult),
          reads=[RCP], writes=[RZ])
    C.dve(lambda e: e.tensor_scalar(out=kbar_b[:], in0=kbar[:], scalar1=1.0 / 256.0, scalar2=None, op0=ALU.mult),
          reads=[Rkbar], writes=[Rkbarb])
    GM4 = [sb("GM4_%d" % i, [128, 32], F32) for i in range(4)]
    top4 = [sb("top4_%d" % i, [128, 8], F32) for i in range(4)]
    RGM4 = [R("gm4") for _ in range(4)]
    Rtop4 = [R("top4") for _ in range(4)]
    for i in range(4):
        C.dve(lambda e, i=i: e.memset(GM4[i][:], NEGF), writes=[RGM4[i]])

    def p2_stage(T, stage):
        if stage == 0:
            def tr4(e):
                ins = None
                for c in range(4):
                    tt = 4 * T + c
                    ins = e.matmul(PS[5][0:65, c * 128:(c + 1) * 128], lhsT=Z[:, tt:tt + 65], rhs=ident_f[:],
                                   start=True, stop=True)
                return ins
            C.pe(tr4, reads=[RZ, Rc["ident"]], writes=[RPS[5]])

            def gates(e):
                ins = None
                for c in range(4):
                    tt = 4 * T + c
                    ins = e.matmul(PS[6][:, c * 32:(c + 1) * 32], lhsT=QA[0:64, tt * 128:(tt + 1) * 128], rhs=kbar_b[:],
                                   start=True, stop=True)
                return ins
            C.pe(gates, reads=[RQA[T], Rkbarb], writes=[RPS[6]])
        elif stage == 1:
            C.act(lambda e: e.copy(out=QB[64:65, T * 512:(T + 1) * 512], in_=PS[5][64:65, :]),
                  reads=[RPS[5]], writes=[RQBc[T]])
            for c in range(4):
                tt = 4 * T + c
                b = tt // 2
                if b == 0:
                    continue
                C.dve(lambda e, c=c, b=b: e.tensor_copy(out=GM4[c][:, 0:b], in_=PS[6][:, c * 32:c * 32 + b]),
                      reads=[RPS[6]], writes=[RGM4[c]])
                C.dve(lambda e, c=c: e.max(out=top4[c][:], in_=GM4[c][:]), reads=[RGM4[c]], writes=[Rtop4[c]])
                C.dve(lambda e, c=c, tt=tt, b=b: e.tensor_scalar(out=MBZ[:, 64 + 32 * tt:64 + 32 * tt + b], in0=GM4[c][:, 0:b],
                                                                 scalar1=top4[c][:, 2:3], scalar2=1.0,
                                                                 op0=ALU.is_ge, op1=ALU.subtract),
                      reads=[RGM4[c], Rtop4[c]], writes=[RMBZ[T]])
        elif stage == 2:
            def trm(e):
                ins = None
                for c in range(4):
                    tt = 4 * T + c
                    ins = e.matmul(PS[7][0:96, c * 128:(c + 1) * 128], lhsT=MBZ[:, 32 * tt:32 * tt + 96], rhs=ident_f[:],
                                   start=True, stop=True)
                return ins
            C.pe(trm, reads=[RMBZ[T], Rc["ident"]], writes=[RPS[7]])
        else:
            C.act(lambda e: e.copy(out=QA[64:96, T * 512:(T + 1) * 512], in_=PS[7][64:96, :]),
                  reads=[RPS[7]], writes=[RQAm[T]])

    def p2_tile(T):
        for s in range(4):
            p2_stage(T, s)

    if PH < 4:
        for T in range(NT):
            p2_tile(T)
    else:
        p2_tile(0)

    if PH == 3:
        return _finish_mixer(C, nc)
    ROacc = [[RPS[3]] * 4, [RPS[4]] * 4]
    LOOK = 2

    def att_score(p):
        qi, att, kj, oi, idx = p
        Q, K = (QA, KA) if att == 0 else (QB, KB)
        RQ, RQx, RK = (RQA, RQAm, RKA) if att == 0 else (RQB, RQBc, RKB)
        rows = 96 if att == 0 else 65
        d = kj - 4 * qi
        off = 128 * max(d, 0)
        n = 512 - off
        sbk = idx % 3
        pt = idx % 4
        C.pe(lambda e: e.matmul(PS[sbk][:, 0:n], lhsT=K[0:rows, kj * 128:(kj + 1) * 128],
                                rhs=Q[0:rows, qi * 512 + off:(qi + 1) * 512], start=True, stop=True),
             reads=[RQ[qi], RQx[qi], RK[kj // 4]] + ([Rc["kaoh"]] if att == 0 else []), writes=[RPS[sbk]])
        if att == 0:
            C.act(lambda e: e.activation(out=PT[pt][:, 0:n], in_=PS[sbk][:, 0:n], func=AF.Exp, scale=0.125),
                  reads=[RPS[sbk]], writes=[RPT[pt]])
        else:
            C.act(lambda e: e.activation(out=PT[pt][:, 0:n], in_=PS[sbk][:, 0:n], func=AF.Exp,
                                         bias=CP[:, kj:kj + 1], scale=0.125),
                  reads=[RPS[sbk], RCP], writes=[RPT[pt]])
        if d >= 0:
            C.dve(lambda e: e.tensor_tensor(out=PT[pt][:, 0:128], in0=PT[pt][:, 0:128], in1=tri_b[:], op=ALU.mult),
                  reads=[RPT[pt], Rc["tri_b"]], writes=[RPT[pt]])

    def att_pv(p):
        qi, att, kj, oi, idx = p
        V = VA if att == 0 else VB
        RV = RVA if att == 0 else RVB
        ysl = qi % 2
        ob = 3 + (oi % 2)
        Oacc = PS[ob][:, 0:260].rearrange("p (c n) -> p c n", n=65)
        RO = ROacc[ob - 3]
        d = kj - 4 * qi
        off = 128 * max(d, 0)
        pt = idx % 4
        c0 = max(d, 0)

        def pv(e):
            ins = None
            for c in range(c0, 4):
                ins = e.matmul(Oacc[:, c, :], lhsT=PT[pt][:, c * 128 - off:(c + 1) * 128 - off], rhs=V[:, kj, :],
                               start=(kj == 0 and c == 0), stop=(kj == 4 * qi + c), skip_group_check=True)
            return ins
        C.pe(pv, reads=[RPT[pt], RV[kj // 4]], writes=[RO[0]])
        if d >= 0:
            c = d
            r4 = (2 * (oi % 2) + (c % 2))
            C.dve(lambda e: e.reciprocal(out=rl[r4][:], in_=Oacc[:, c, 64:65]), reads=[RO[c]], writes=[Rrl[r4]])
            C.dve(lambda e: e.tensor_scalar(out=YAB[ysl][:, c, att * 64:(att + 1) * 64], in0=Oacc[:, c, 0:64],
                                            scalar1=rl[r4][:, 0:1], scalar2=None, op0=ALU.mult),
                  reads=[RO[c], Rrl[r4]], writes=[RYAB[ysl]])
        if att == 1 and d == 3:
            C.dma("sp", y[qi * 512:(qi + 1) * 512, 0:128].rearrange("(c p) n -> p c n", p=128), YAB[ysl][:],
                  reads=[RYAB[ysl]], key="yab%d" % ysl)

    plist = []
    oi = 0
    for qi in range(NT):
        for att in range(2):
            for kj in range(4 * qi + 4):
                plist.append((qi, att, kj, oi, len(plist)))
            oi += 1
    for i in range(len(plist) + LOOK):
        if i < len(plist):
            qi_, att_, kj_ = plist[i][0], plist[i][1], plist[i][2]
            n_ = 4 * qi_ + 4
            if att_ == 0 and qi_ + 1 < NT:
                offs = [0, n_ // 4, n_ // 2, (3 * n_) // 4]
                if kj_ in offs:
                    p2_stage(qi_ + 1, offs.index(kj_))
            if att_ == 1:
                offs = [0, n_ // 3, (2 * n_) // 3]
                if kj_ in offs:
                    c_stage(qi_, offs.index(kj_))
            att_score(plist[i])
        if i - LOOK >= 0:
            att_pv(plist[i - LOOK])

    if os.environ.get("MIX_DUMP"):
        allres = RQA + RQAm + RKA + RQB + RQBc + RKB + RVA + RVB + RUG + [RCP, RLF, RFRAW, RMV, RRSTD, Rkbar, Rkbarb, RZ, Rc["kaoh"]] + RMBZ + Rpwb + [Rc["band"], Rc["tri_b"], Rc["poolw"]]
        for nm, t, shp, dty in (("d_cp", CP, [128, NTT], F32), ("d_lf", LF, [128, NTT], F32), ("d_fraw", FRAW, [128, NTT], F32),
                                ("d_rstd", RSTD, [128, NTT], F32),
                                ("d_ug", UG, [128, NTT, 128], BF16), ("d_qa", QA, [128, SEQ], BF16), ("d_ka", KA, [128, SEQ], BF16),
                                ("d_qb", QB, [128, SEQ], BF16), ("d_kb", KB, [128, SEQ], BF16), ("d_va", VA, [128, NTT, 65], BF16),
                                ("d_vb", VB, [128, NTT, 65], BF16), ("d_kbar", kbar, [64, 32], F32), ("d_mbz", MBZ, [128, 64 + NTT * 32], F32),
                                ("d_pwb0", pwb[0], [128, 64], BF16), ("d_band", band_b, [128, 3, 128], BF16), ("d_trib", tri_b, [128, 128], BF16)):
            dd = dt(nm, shp, dty, kind="ExternalOutput").ap()
            C.dma("sp", dd, t[:], reads=allres, key=nm)

    return _finish_mixer(C, nc)


def _finish_mixer(C, nc):
    finals = []
    for k, cnt in C.S.dcount.items():
        finals.append((k, 16 * cnt, "dma"))
    C.S.emit(finals)
    C.st.close()
    return nc


def _rot_tables():
    pos = np.arange(S, dtype=np.float32)
    inv_freq = (np.float32(500000.0) ** (-np.arange(0, 16, 2, dtype=np.float32) / np.float32(16))).astype(np.float32)
    ang = (pos[:, None] * inv_freq[None, :]).astype(np.float32)
    cos = np.cos(ang).astype(np.float32).T
    sin = np.sin(ang).astype(np.float32).T
    cs = np.zeros((2, 16, S), np.float32)
    cs[0, 0:8] = cos
    cs[0, 8:16] = cos
    cs[1, 0:8] = -sin
    cs[1, 8:16] = sin
    return cs


def _band_mats(win):
    t = np.arange(128)
    s = np.arange(128)
    cur = ((t[None, :] - s[:, None] >= 0) & (t[None, :] - s[:, None] < win)).astype(np.float32) / win
    cur -= np.eye(128, dtype=np.float32)
    prev = ((t[None, :] + 128 - s[:, None]) < win).astype(np.float32) / win
    cnt = np.minimum(t + 1, win).astype(np.float32)
    first = ((t[None, :] - s[:, None] >= 0) & (t[None, :] - s[:, None] < win)).astype(np.float32) / cnt[None, :]
    first -= np.eye(128, dtype=np.float32)
    return np.stack([cur, prev, first]).astype(np.float32)


_CACHE = {}


def _mixer_inputs(xT_b, l, h, P):
    w_in = P["w_in"][l]
    a0 = 0
    qa = w_in[:, 0 + 64 * h:0 + 64 * h + 64]
    ka = w_in[:, 256 + 64 * h:256 + 64 * h + 64]
    va = w_in[:, 512 + 64 * h:512 + 64 * h + 64]
    qb = w_in[:, 768 + 64 * h:768 + 64 * h + 64]
    kb = w_in[:, 1024 + 64 * h:1024 + 64 * h + 64]
    vb = w_in[:, 1280 + 64 * h:1280 + 64 * h + 64]
    fb = w_in[:, 1536 + h:1536 + h + 1]
    cu = w_in[:, 1540 + 64 * h:1540 + 64 * h + 64]
    cv = w_in[:, 1796 + 64 * h:1796 + 64 * h + 64]
    dp = w_in[:, 2052 + 64 * h:2052 + 64 * h + 64]
    perm = np.concatenate([np.arange(8, 16), np.arange(0, 8)])
    wfm = np.concatenate([qa, ka, qb, kb, dp, qa[:, perm], ka[:, perm]], axis=1)
    wtm = np.concatenate([va, vb, cu, cv, fb], axis=1)
    rep = lambda v: np.ascontiguousarray(np.broadcast_to(v[None, :], (128, v.shape[0]))).astype(np.float32)
    return {
        "xT": xT_b,
        "wfm": np.ascontiguousarray(wfm), "wtm": np.ascontiguousarray(wtm),
        "cs": _CACHE["cs"], "onehot": _CACHE["onehot"],
        "sguT": np.ascontiguousarray(P["sgu_w"][l, h].T), "tri": _CACHE["tri"], "ident": _CACHE["ident"],
        "sgub": np.ascontiguousarray(P["sgu_b"][l, h].reshape(128, 1)),
        "lng": rep(P["sgu_ln_g"][l, 64 * h:64 * h + 64]), "lnb": rep(P["sgu_ln_b"][l, 64 * h:64 * h + 64]),
        "poolw": np.ascontiguousarray(P["pool_w"][l, h]), "pscale": rep(P["pool_scale"][l, 64 * h:64 * h + 64]),
        "band": _CACHE["band"][h],
        "bfor": np.full((128, 1), P["b_forget"][l, h], np.float32),
    }


def _consts():
    if "cs" in _CACHE:
        return
    _CACHE["cs"] = _rot_tables()
    oh = np.zeros((32, S), np.float32)
    for n in range(32):
        oh[n, n * 256:(n + 1) * 256] = BIGM
    _CACHE["onehot"] = oh.astype(ml_dtypes.bfloat16)
    sidx = np.arange(128)
    _CACHE["tri"] = (sidx[:, None] <= sidx[None, :]).astype(np.float32)
    _CACHE["ident"] = np.eye(128, dtype=np.float32)
    _CACHE["band"] = [_band_mats(w) for w in (2, 4, 8, 16)]


def run_mixer(xT_all, l, P):
    _consts()
    if "nc_m" not in _CACHE:
        _CACHE["nc_m"] = build_mixer()
    in_maps = [_mixer_inputs(xT_all[c // 4], l, c % 4, P) for c in range(8)]
    res = run_bass_kernel_spmd(_CACHE["nc_m"], in_maps, core_ids=list(range(8)))
    y = np.zeros((NB, S, 1024), ml_dtypes.bfloat16)
    for c in range(8):
        b, h = c // 4, c % 4
        yy = np.asarray(res.results[c]["y"]).view(ml_dtypes.bfloat16).reshape(S, 256) if res.results[c]["y"].dtype != ml_dtypes.bfloat16 else res.results[c]["y"]
        for m in range(4):
            y[b, :, 256 * m + 64 * h:256 * m + 64 * h + 64] = yy[:, 64 * m:64 * m + 64]
    return y


NTOK = 2048
NCH = 44


def build_post(NT4=4):
    nc = bass.Bass("TRN2", target_bir_lowering=False)
    C = Ctx(nc)
    dt = nc.dram_tensor
    NTK = NT4 * 512
    yin = dt("yin", [128 + NTK, D], BF16, kind="ExternalInput").ap()
    xin = dt("xin", [128 + NTK, D], F32, kind="ExternalInput").ap()
    flag = dt("flag", [128, 1], F32, kind="ExternalInput").ap()
    wo = dt("wo", [D, D], F32, kind="ExternalInput").ap()
    wup = dt("wup", [NCH, 128, 8, 128], F32, kind="ExternalInput").ap()
    wdn = dt("wdn", [DFF, D], F32, kind="ExternalInput").ap()
    convw = dt("convw", [128, NCH, 3], F32, kind="ExternalInput").ap()
    convb = dt("convb", [128, NCH], F32, kind="ExternalInput").ap()
    lnp = dt("lnp", [4, 128, D], F32, kind="ExternalInput").ap()
    ident = dt("ident", [128, 128], F32, kind="ExternalInput").ap()
    xo = dt("xo", [NTK, D], F32, kind="ExternalOutput").ap()

    sb, ps, res = C.sb, C.ps, C.res
    wd_b = sb("wd_b", [128, 22, D], BF16)
    wo_b = sb("wo_b", [128, 8, D], BF16)
    A_T = sb("A_T", [128, 22, 512], BF16)
    X1T = [sb("X1T%d" % i, [128, 8, 512], BF16) for i in range(2)]
    X1Th = sb("X1Th", [128, 8, 2], BF16)
    X1 = sb("X1", [128, 4, D], F32)
    wu = [[sb("wu%d_%d" % (i, j), [128, 8, 128], BF16) for j in range(2)] for i in range(3)]
    H = [sb("H%d" % i, [128, 514], F32) for i in range(4)]
    tg = [sb("tg%d" % i, [128, 512], F32) for i in range(2)]
    tv = [sb("tv%d" % i, [128, 512], F32) for i in range(2)]
    sg = [sb("sg%d" % i, [128, 512], BF16) for i in range(2)]
    HALO = sb("HALO", [128, NCH, 2], F32)
    lnp_s = sb("lnp_s", [128, 4, D], F32)
    yt = [sb("yt%d" % i, [128, D], BF16) for i in range(2)]
    xt = [sb("xt%d" % i, [128, D], F32) for i in range(2)]
    rr = sb("rr", [128, D], F32)
    x1b = sb("x1b", [128, D], BF16)
    yT = sb("yT", [128, 8, 128], BF16)
    x2 = [sb("x2_%d" % i, [128, D], F32) for i in range(2)]
    stats = sb("stats", [128, 12], F32)
    mv = sb("mv", [128, 2], F32)
    sd = sb("sd", [128, 1], F32)
    rstd = sb("rstd", [128, 1], F32)
    cw_s = sb("cw_s", [128, NCH, 3], F32)
    cb_s = sb("cb_s", [128, NCH], F32)
    flag_s = sb("flag_s", [128, 1], F32)
    ident_b = sb("ident_b", [128, 128], BF16)
    PS = [ps("ps%d" % i, [128, 512], F32) for i in range(7)]
    PSB = ps("psb", [128, 1024], BF16)
    RPS = [res("ps", True) for _ in range(7)]
    RPSB = res("psb", True)
    R = lambda n: res(n)
    Rwd, Rwo, RAT, RX1, RX1Th, RHALO, Rlnp, Rrr, Rx1b, RyT = (R("wd"), R("wo"), R("at"), R("x1"), R("x1th"), R("halo"),
                                                             R("lnp"), R("rr"), R("x1b"), R("yT"))
    RX1T = [R("x1t") for _ in range(2)]
    Rwu = [[R("wu") for _ in range(2)] for _ in range(3)]
    RH = [R("h") for _ in range(4)]
    RHh = [R("hh") for _ in range(4)]
    RHALOc = [R("haloc") for _ in range(NCH)]
    Rtg = [R("tg") for _ in range(2)]
    Rtv = [R("tv") for _ in range(2)]
    Rsg = [R("sg") for _ in range(2)]
    Ryt = [R("yt") for _ in range(2)]
    Rxt = [R("xt") for _ in range(2)]
    Rx2 = [R("x2") for _ in range(2)]
    Rst, Rmv, Rsd, Rrstd, Rcw, Rcb, Rflag, Rid = (R("st"), R("mv"), R("sd"), R("rstd"), R("cw"), R("cb"), R("flag"), R("id"))

    C.dma("pool", ident_b[:], ident, writes=[Rid])
    wo_v = wo.rearrange("(kc f) n -> f kc n", f=128)
    for kc in range(0, 8, 4):
        C.dma("pool", wo_b[:, kc:kc + 4, :], wo_v[:, kc:kc + 4, :], writes=[Rwo], key="wo")
    C.dma("sp", lnp_s[:], lnp.rearrange("k p n -> p k n"), writes=[Rlnp])
    C.dma("sp", cw_s[:], convw, writes=[Rcw])
    C.dma("sp", cb_s[:], convb, writes=[Rcb])
    C.dma("sp", flag_s[:], flag, writes=[Rflag])

    def load_sub(i):
        sl = i % 2
        C.dma("sp", yt[sl][:], yin[i * 128:(i + 1) * 128, :], writes=[Ryt[sl]])
        C.dma("sp", xt[sl][:], xin[i * 128:(i + 1) * 128, :], writes=[Rxt[sl]])

    def layer_norm(src, Rsrc, dst, Rdst, gi):
        def st(e):
            e.bn_stats(out=stats[:, 0:6], in_=src[:, 0:512])
            return e.bn_stats(out=stats[:, 6:12], in_=src[:, 512:1024])
        C.dve(st, reads=[Rsrc], writes=[Rst])
        C.dve(lambda e: e.bn_aggr(out=mv[:], in_=stats[:]), reads=[Rst], writes=[Rmv])
        C.dve(lambda e: e.tensor_scalar(out=sd[:], in0=mv[:, 1:2], scalar1=EPS, scalar2=None, op0=ALU.add),
              reads=[Rmv], writes=[Rsd])
        C.act(lambda e: e.activation(out=sd[:], in_=sd[:], func=AF.Sqrt), reads=[Rsd], writes=[Rsd])
        C.dve(lambda e: e.reciprocal(out=rstd[:], in_=sd[:]), reads=[Rsd], writes=[Rrstd])
        C.dve(lambda e: e.tensor_scalar(out=src[:], in0=src[:], scalar1=mv[:, 0:1], scalar2=rstd[:, 0:1],
                                        op0=ALU.subtract, op1=ALU.mult), reads=[Rsrc, Rmv, Rrstd], writes=[Rsrc])
        C.dve(lambda e: e.tensor_tensor(out=src[:], in0=src[:], in1=lnp_s[:, gi, :], op=ALU.mult),
              reads=[Rsrc, Rlnp], writes=[Rsrc])
        C.dve(lambda e: e.tensor_tensor(out=dst, in0=src[:], in1=lnp_s[:, gi + 1, :], op=ALU.add),
              reads=[Rsrc, Rlnp], writes=[Rdst])

    def a_front(i):
        sl = i % 2

        def tr_y(e):
            ins = None
            for kc in range(8):
                ins = e.transpose(PSB[:, kc * 128:(kc + 1) * 128], yt[sl][:, kc * 128:(kc + 1) * 128], ident_b[:])
            return ins
        C.pe(tr_y, reads=[Ryt[sl], Rid], writes=[RPSB])
        C.act(lambda e: e.copy(out=yT[:].rearrange("p k n -> p (k n)"), in_=PSB[:]), reads=[RPSB], writes=[RyT])
        for half in range(2):
            C.pe(_mm_group(PS[half][:], [(yT[:, kc, :], wo_b[:, kc, half * 512:(half + 1) * 512]) for kc in range(8)]),
                 reads=[RyT, Rwo], writes=[RPS[half]])

    def a_x1src(i):
        if i == 0:
            return x2[0][:], Rx2[0]
        c = (i - 1) % 4
        return X1[:, c, :], RX1

    def a_cast(i):
        x1src, Rx1src = a_x1src(i)
        C.act(lambda e: e.copy(out=x1b[:], in_=x1src), reads=[Rx1src], writes=[Rx1b])

    def a_back(i):
        def tr_x(e):
            ins = None
            for kc in range(8):
                ins = e.transpose(PSB[:, kc * 128:(kc + 1) * 128], x1b[:, kc * 128:(kc + 1) * 128], ident_b[:])
            return ins
        C.pe(tr_x, reads=[Rx1b, Rid], writes=[RPSB])
        psv = PSB[:].rearrange("p (k n) -> p k n", n=128)
        if i == 0:
            C.act(lambda e: e.copy(out=X1Th[:], in_=psv[:, :, 126:128]), reads=[RPSB], writes=[RX1Th])
        else:
            T = (i - 1) // 4
            c = (i - 1) % 4
            C.act(lambda e: e.copy(out=X1T[T % 2][:, :, c * 128:(c + 1) * 128], in_=psv), reads=[RPSB], writes=[RX1T[T % 2]])

    def a_mid(i):
        sl = i % 2
        for half in range(2):
            C.dve(lambda e, half=half: e.scalar_tensor_tensor(out=rr[:, half * 512:(half + 1) * 512], in0=xt[sl][:, half * 512:(half + 1) * 512],
                                                              scalar=ALPHA, in1=PS[half][:], op0=ALU.mult, op1=ALU.add),
                  reads=[Rxt[sl], RPS[half]], writes=[Rrr])
        dst, Rdst = a_x1src(i)
        layer_norm(rr, Rrr, dst, Rdst, 0)

    def stage_a_tile(T):
        subs = ([0] if T == 0 else []) + [1 + 4 * T + c for c in range(4)]
        a_front(subs[0])
        for n, i in enumerate(subs):
            if i + 1 <= NT4 * 4:
                load_sub(i + 1)
            if n >= 1:
                a_cast(subs[n - 1])
            a_mid(i)
            if n >= 1:
                a_back(subs[n - 1])
            if n + 1 < len(subs):
                a_front(subs[n + 1])
        a_cast(subs[-1])
        a_back(subs[-1])

    wup_loaded = {}

    def load_wup(T, cc):
        sl = (T * 22 + cc) % 3
        if os.environ.get("P_NOLOAD") and T > 0:
            return
        C.dma("pool", wu[sl][0][:], wup[cc], writes=[Rwu[sl][0]])
        C.dma("pool", wu[sl][1][:], wup[22 + cc], writes=[Rwu[sl][1]])

    state = {"k": 0}

    def ffn_chunk(T, cc, which):
        ch = cc + 22 * which
        wsl = (T * 22 + cc) % 3
        k = 2 * (T * 22 + cc) + which
        hb = k % 4
        bank = (2, 3, 5, 6)[hb]
        xs = X1T[T % 2]
        if T == 0:
            C.pe(_mm_group(PS[4][:, 0:2], [(wu[wsl][which][:, kc, :], X1Th[:, kc, :]) for kc in range(8)]),
                 reads=[Rwu[wsl][which], RX1Th], writes=[RPS[4]])
            C.act(lambda e: e.activation(out=H[hb][:, 0:2], in_=PS[4][:, 0:2], func=AF.Copy, scale=flag_s[:, 0:1]),
                  reads=[RPS[4], Rflag], writes=[RHh[hb]])
        C.pe(_mm_group(PS[bank][:], [(wu[wsl][which][:, kc, :], xs[:, kc, :]) for kc in range(8)]),
             reads=[Rwu[wsl][which], RX1T[T % 2]], writes=[RPS[bank]])
        C.act(lambda e: e.copy(out=H[hb][:, 2:514], in_=PS[bank][:]), reads=[RPS[bank]], writes=[RH[hb]])
        C.pool(lambda e: e.tensor_copy(out=HALO[:, ch, :], in_=H[hb][:, 512:514]), reads=[RH[hb]], writes=[RHALOc[ch]])
        t, Rt = (tg[cc % 2], Rtg[cc % 2]) if which == 0 else (tv[cc % 2], Rtv[cc % 2])
        C.act(lambda e: e.activation(out=t[:], in_=H[hb][:, 0:512], func=AF.Identity, bias=cb_s[:, ch:ch + 1],
                                     scale=cw_s[:, ch, 0:1]), reads=[RH[hb], RHh[hb], Rcw, Rcb], writes=[Rt])
        C.dve(lambda e: e.scalar_tensor_tensor(out=t[:], in0=H[hb][:, 1:513], scalar=cw_s[:, ch, 1:2], in1=t[:],
                                               op0=ALU.mult, op1=ALU.add), reads=[RH[hb], RHh[hb], Rcw, Rt], writes=[Rt])
        C.dve(lambda e: e.scalar_tensor_tensor(out=t[:], in0=H[hb][:, 2:514], scalar=cw_s[:, ch, 2:3], in1=t[:],
                                               op0=ALU.mult, op1=ALU.add), reads=[RH[hb], Rcw, Rt], writes=[Rt])

    def halo_read(T, cc):
        if T == 0:
            return
        for which in range(2):
            ch = cc + 22 * which
            hb = (2 * (T * 22 + cc) + which) % 4
            C.pool(lambda e, ch=ch, hb=hb: e.tensor_copy(out=H[hb][:, 0:2], in_=HALO[:, ch, :]),
                   reads=[RHALOc[ch]], writes=[RHh[hb]])

    def ffn_pair(T, cc):
        if T * 22 + cc + 1 < NT4 * 22:
            nT1, ncc1 = divmod(T * 22 + cc + 1, 22)
            halo_read(nT1, ncc1)
        if T * 22 + cc + 2 < NT4 * 22:
            nT, ncc = divmod(T * 22 + cc + 2, 22)
            load_wup(nT, ncc)
        ffn_chunk(T, cc, 0)
        ffn_chunk(T, cc, 1)
        s2 = cc % 2
        C.act(lambda e: e.activation(out=sg[s2][:], in_=tg[s2][:], func=AF.Silu), reads=[Rtg[s2]], writes=[Rsg[s2]])
        C.pool(lambda e: e.tensor_tensor(out=A_T[:, cc, :], in0=sg[s2][:], in1=tv[s2][:], op=ALU.mult),
               reads=[Rsg[s2], Rtv[s2]], writes=[RAT])

    def stage_c(T, c):
        j = T * 4 + c
        banks = (5, 6) if j % 2 == 0 else (0, 1)
        for half in range(2):
            C.pe(_mm_group(PS[banks[half]][:], [(A_T[:, cc, c * 128:(c + 1) * 128], wd_b[:, cc, half * 512:(half + 1) * 512])
                                                 for cc in range(22)]), reads=[RAT, Rwd], writes=[RPS[banks[half]]])
        for half in range(2):
            C.dve(lambda e, half=half: e.scalar_tensor_tensor(out=rr[:, half * 512:(half + 1) * 512], in0=X1[:, c, half * 512:(half + 1) * 512],
                                                              scalar=ALPHA, in1=PS[banks[half]][:], op0=ALU.mult, op1=ALU.add),
                  reads=[RX1, RPS[banks[half]]], writes=[Rrr])
        o = j % 2
        layer_norm(rr, Rrr, x2[o][:], Rx2[o], 2)
        C.dma("sp", xo[j * 128:(j + 1) * 128, :], x2[o][:], reads=[Rx2[o]], key="xo%d" % o)

    load_sub(0)
    load_wup(0, 0)
    load_wup(0, 1)
    wd_v = wdn.rearrange("(cc p) n -> p cc n", p=128)
    for T in range(NT4):
        stage_a_tile(T)
        if T == 0:
            for c0 in range(0, 22, 2):
                C.dma("pool", wd_b[:, c0:c0 + 2, :], wd_v[:, c0:c0 + 2, :], writes=[Rwd], key="wd")
        for cc in range(22):
            ffn_pair(T, cc)
        for c in range(4):
            stage_c(T, c)
    return _finish_mixer(C, nc)


def _post_inputs(y_b, x_b, q, l, P):
    t0 = q * NTOK
    if q == 0:
        yh = np.zeros((128, D), ml_dtypes.bfloat16)
        xh = np.zeros((128, D), np.float32)
    else:
        yh = y_b[t0 - 128:t0]
        xh = x_b[t0 - 128:t0]
    rep = lambda v: np.broadcast_to(v[None, :], (128, v.shape[0]))
    key = ("post_w", l)
    if key not in _CACHE:
        w_up = P["w_up"][l]
        _CACHE[key] = {
            "wo": np.ascontiguousarray(P["w_o"][l]),
            "wup": np.ascontiguousarray(w_up.reshape(8, 128, NCH, 128).transpose(2, 1, 0, 3)),
            "wdn": np.ascontiguousarray(P["w_down"][l]),
            "convw": np.ascontiguousarray(P["conv_w"][l].reshape(3, NCH, 128).transpose(2, 1, 0)),
            "convb": np.ascontiguousarray(P["conv_b"][l].reshape(NCH, 128).T),
            "lnp": np.ascontiguousarray(np.stack([rep(P["ln1_g"][l]), rep(P["ln1_b"][l]), rep(P["ln2_g"][l]), rep(P["ln2_b"][l])]).astype(np.float32)),
            "ident": np.eye(128, dtype=np.float32),
        }
    m = dict(_CACHE[key])
    m["yin"] = np.ascontiguousarray(np.concatenate([yh, y_b[t0:t0 + NTOK]], axis=0))
    m["xin"] = np.ascontiguousarray(np.concatenate([xh, x_b[t0:t0 + NTOK]], axis=0))
    m["flag"] = np.full((128, 1), 0.0 if q == 0 else 1.0, np.float32)
    return m


def run_post(y, x, l, P):
    if "nc_p" not in _CACHE:
        _CACHE["nc_p"] = build_post()
    in_maps = [_post_inputs(y[c // 4], x[c // 4], c % 4, l, P) for c in range(8)]
    res = run_bass_kernel_spmd(_CACHE["nc_p"], in_maps, core_ids=list(range(8)))
    out = np.zeros((NB, S, D), np.float32)
    for c in range(8):
        out[c // 4, (c % 4) * NTOK:(c % 4 + 1) * NTOK] = res.results[c]["xo"]
    return out


def kernel(**inputs):
    P = {k: np.asarray(v) for k, v in inputs.items()}
    x = np.ascontiguousarray(P["x"], dtype=np.float32)
    for l in range(2):
        xT = [np.ascontiguousarray(x[b].T) for b in range(NB)]
        y = run_mixer(xT, l, P)
        x = run_post(y, x, l, P)
    return x
```

```python
import contextlib
import os
DBG = int(os.environ.get('MIX_DBG', '99'))
import numpy as np
import ml_dtypes
import concourse.bass as bass
import concourse.mybir as mybir
from concourse.bass_utils import run_bass_kernel_spmd

F32 = mybir.dt.float32
BF16 = mybir.dt.bfloat16
AF = mybir.ActivationFunctionType
ALU = mybir.AluOpType
AX = mybir.AxisListType

S = 8192
D = 1024
NB = 2
DFF = 2816
ALPHA = 4.0 ** 0.25
EPS = 1e-5
BIGM = 30000.0
NEGF = -1.0e30


class Res:
    __slots__ = ("name", "w", "r", "excl")

    def __init__(self, name, excl=False):
        self.name = name
        self.w = None
        self.r = []
        self.excl = excl


class Sched:
    STREAMS = ("pe", "act", "dve", "pool", "sp")

    def __init__(self, nc):
        self.nc = nc
        self.ops = {s: [] for s in self.STREAMS}
        self.ccount = {s: 0 for s in self.STREAMS}
        self.dcount = {}
        self.known = {s: {} for s in self.STREAMS}
        self.final_events = []

    def _need(self, stream, ev, waits):
        if ev is None:
            return
        sem, val, src = ev
        if src == stream and src == "pe":
            return
        if self.known[stream].get(sem, 0) >= val:
            return
        self.known[stream][sem] = val
        waits.append((sem, val))

    def op(self, stream, fn, reads=(), writes=(), dma_key=None):
        ex = [r for r in reads if r.excl]
        if ex:
            reads = [r for r in reads if not r.excl]
            writes = list(writes) + [r for r in ex if r not in writes]
        waits = []
        for r in reads:
            self._need(stream, r.w, waits)
        for w in writes:
            self._need(stream, w.w, waits)
            for e in w.r:
                self._need(stream, e, waits)
        if dma_key is not None:
            k = "d_" + dma_key
            self.dcount[k] = self.dcount.get(k, 0) + 1
            ev = (k, 16 * self.dcount[k], "dma")
            sig = (k, 16)
        else:
            self.ccount[stream] += 1
            ev = ("c_" + stream, self.ccount[stream], stream)
            sig = ("c_" + stream, 1)
        for r in reads:
            r.r.append(ev)
        for w in writes:
            w.w = ev
            w.r = []
        self.ops[stream].append((waits, fn, sig))
        return ev

    def emit(self, final_events):
        nc = self.nc
        names = set()
        for s in self.STREAMS:
            for waits, fn, sig in self.ops[s]:
                names.add(sig[0])
                for (sem, val) in waits:
                    names.add(sem)
        with contextlib.ExitStack() as st:
            sems = {n: st.enter_context(nc.semaphore(n)) for n in sorted(names)}
            block = st.enter_context(nc.Block())

            def make(stream):
                def body(eng):
                    for waits, fn, sig in self.ops[stream]:
                        for (sem, val) in waits:
                            eng.wait_ge(sems[sem], val)
                        ins = fn(eng)
                        ins.then_inc(sems[sig[0]], sig[1])
                    if stream == "sp":
                        for (sem, val, src) in final_events:
                            eng.wait_ge(sems[sem], val)
                return body

            block.tensor(make("pe"))
            block.scalar(make("act"))
            block.vector(make("dve"))
            block.gpsimd(make("pool"))
            block.sync(make("sp"))


class Ctx:
    def __init__(self, nc):
        self.nc = nc
        self.st = contextlib.ExitStack()
        self.S = Sched(nc)
        self.nres = 0

    def sb(self, name, shape, dt):
        return self.st.enter_context(self.nc.sbuf_tensor(name, shape, dt))

    def ps(self, name, shape, dt):
        return self.st.enter_context(self.nc.psum_tensor(name, shape, dt))

    def res(self, name="r", excl=False):
        self.nres += 1
        return Res("%s%d" % (name, self.nres), excl)

    def pe(self, fn, reads=(), writes=()):
        return self.S.op("pe", fn, reads, writes)

    def act(self, fn, reads=(), writes=()):
        return self.S.op("act", fn, reads, writes)

    def dve(self, fn, reads=(), writes=()):
        return self.S.op("dve", fn, reads, writes)

    def pool(self, fn, reads=(), writes=()):
        return self.S.op("pool", fn, reads, writes)

    def dma(self, queue, out, in_, reads=(), writes=(), key=None):
        if key is None:
            key = (writes[0].name if writes else reads[0].name)
        return self.S.op(queue, lambda e: e.dma_start(out=out, in_=in_), reads, writes, dma_key=key)


def _mm_group(out, pairs):
    def fn(e):
        n = len(pairs)
        ins = None
        for i, (l, r) in enumerate(pairs):
            ins = e.matmul(out, lhsT=l, rhs=r, start=(i == 0), stop=(i == n - 1))
        return ins
    return fn


NFM = 352
NTM = 257


def build_mixer(SEQ=S, PH=9):
    NT, NTT = SEQ // 512, SEQ // 128
    nc = bass.Bass("TRN2", target_bir_lowering=False)
    C = Ctx(nc)
    dt = nc.dram_tensor
    xT = dt("xT", [D, SEQ], F32, kind="ExternalInput").ap()
    wfm = dt("wfm", [D, NFM], F32, kind="ExternalInput").ap()
    wtm = dt("wtm", [D, NTM], F32, kind="ExternalInput").ap()
    cs = dt("cs", [2, 16, SEQ], F32, kind="ExternalInput").ap()
    onehot = dt("onehot", [32, SEQ], BF16, kind="ExternalInput").ap()
    sguT = dt("sguT", [128, 128], F32, kind="ExternalInput").ap()
    tri = dt("tri", [128, 128], F32, kind="ExternalInput").ap()
    ident = dt("ident", [128, 128], F32, kind="ExternalInput").ap()
    sgub = dt("sgub", [128, 1], F32, kind="ExternalInput").ap()
    lng = dt("lng", [128, 64], F32, kind="ExternalInput").ap()
    lnb = dt("lnb", [128, 64], F32, kind="ExternalInput").ap()
    poolw = dt("poolw", [64, 64], F32, kind="ExternalInput").ap()
    pscale = dt("pscale", [128, 64], F32, kind="ExternalInput").ap()
    band = dt("band", [3, 128, 128], F32, kind="ExternalInput").ap()
    bfor = dt("bfor", [128, 1], F32, kind="ExternalInput").ap()
    y = dt("y", [SEQ, 256], BF16, kind="ExternalOutput").ap()

    sb, ps, res = C.sb, C.ps, C.res
    QA = sb("QA", [128, SEQ], BF16)
    KA = sb("KA", [128, SEQ], BF16)
    QB = sb("QB", [128, SEQ], BF16)
    KB = sb("KB", [128, SEQ], BF16)
    VA = sb("VA", [128, NTT, 65], BF16)
    VB = sb("VB", [128, NTT, 65], BF16)
    xb = [sb("xb%d" % i, [128, 8, 512], BF16) for i in range(2)]
    wfm_b = sb("wfm_b", [128, 8, NFM], BF16)
    wtm_b = sb("wtm_b", [128, 8, NTM], BF16)
    cst = [sb("cst%d" % i, [16, 2, 512], F32) for i in range(2)]
    t1 = [sb("t1_%d" % i, [16, 512], F32) for i in range(2)]
    t2 = [sb("t2_%d" % i, [16, 512], F32) for i in range(2)]
    pTb = [sb("pTb%d" % i, [64, 512], BF16) for i in range(2)]
    ug = [sb("ug%d" % i, [128, 128], F32) for i in range(2)]
    stats = [sb("stats%d" % i, [128, 6], F32) for i in range(2)]
    mv = [sb("mv%d" % i, [128, 2], F32) for i in range(2)]
    rstd = [sb("rstd%d" % i, [128, 1], F32) for i in range(2)]
    vn = [sb("vn%d" % i, [128, 64], F32) for i in range(2)]
    vnb = [sb("vnb%d" % i, [128, 64], BF16) for i in range(2)]
    pwb = [sb("pwb%d" % i, [128, 64], BF16) for i in range(3)]
    YC = [sb("YC%d" % i, [128, 4, 64], BF16) for i in range(2)]
    YD = [sb("YD%d" % i, [128, 4, 64], BF16) for i in range(2)]
    UG = sb("UG", [128, NTT, 128], BF16)
    MV = sb("MV", [128, NTT, 2], F32)
    VE = sb("VE", [128, NTT], F32)
    RSTD = sb("RSTD", [128, NTT], F32)
    YAB = [sb("YAB%d" % i, [128, 4, 128], BF16) for i in range(2)]
    MBZ = sb("MBZ", [128, 64 + NTT * 32], F32)
    GM = sb("GM", [128, 32], F32)
    top8 = [sb("top8_%d" % i, [128, 8], F32) for i in range(2)]
    FRAW = sb("FRAW", [128, NTT], F32)
    LF = sb("LF", [128, NTT], F32)
    TOT = sb("TOT", [128, NTT], F32)
    PREF = sb("PREF", [128, NTT], F32)
    CP = sb("CP", [128, NTT], F32)
    Z = sb("Z", [128, 128], F32)
    kbar = sb("kbar", [64, 32], F32)
    kbar_b = sb("kbar_b", [64, 32], BF16)
    PT = [sb("PT%d" % i, [128, 512], BF16) for i in range(5)]
    rl = [sb("rl%d" % i, [128, 1], F32) for i in range(4)]
    ident_f = sb("ident_f", [128, 128], F32)
    tri_f = sb("tri_f", [128, 128], F32)
    tri_b = sb("tri_b", [128, 128], BF16)
    ones_f = sb("ones_f", [128, 128], F32)
    sgu_f = sb("sgu_f", [128, 128], F32)
    wmT_b = sb("wmT_b", [128, 128], BF16)
    band_b = sb("band_b", [128, 3, 128], BF16)
    sgub_s = sb("sgub_s", [128, 1], F32)
    lng_s = sb("lng_s", [128, 64], F32)
    lnb_s = sb("lnb_s", [128, 64], F32)
    poolw_b = sb("poolw_b", [64, 64], BF16)
    pscale_s = sb("pscale_s", [128, 64], F32)
    bfor_s = sb("bfor_s", [128, 1], F32)
    negb = sb("negb", [128, 1], F32)
    ef = sb("ef", [128, NTT], F32)
    PS = [ps("ps%d" % i, [128, 512], F32) for i in range(8)]
    RPS = [res("ps", True) for _ in range(8)]

    R = lambda n: res(n)
    RQA = [R("qa") for _ in range(NT)]
    RQAm = [R("qam") for _ in range(NT)]
    RKA = [R("ka") for _ in range(NT)]
    RQB = [R("qb") for _ in range(NT)]
    RQBc = [R("qbc") for _ in range(NT)]
    RKB = [R("kb") for _ in range(NT)]
    RVA = [R("va") for _ in range(NT)]
    RVB = [R("vb") for _ in range(NT)]
    Rxb = [R("xb") for _ in range(2)]
    Rcs = [R("cs") for _ in range(2)]
    Rt1 = [R("t1") for _ in range(2)]
    Rt2 = [R("t2") for _ in range(2)]
    RpT = [R("pT") for _ in range(2)]
    Rug = [R("ug") for _ in range(2)]
    Rst = [R("st") for _ in range(2)]
    Rmv = [R("mv") for _ in range(2)]
    Rrs = [R("rs") for _ in range(2)]
    Rvn = [R("vn") for _ in range(2)]
    Rvnb = [R("vnb") for _ in range(2)]
    Rpwb = [R("pwb") for _ in range(3)]
    RYC = [R("yc") for _ in range(2)]
    RYD = [R("yd") for _ in range(2)]
    RUG = [R("ugall") for _ in range(NT)]
    RMV, RVE, RRSTD = R("mvall"), R("ve"), R("rstdall")
    RYAB = [R("yab") for _ in range(2)]
    RMBZ = [R("mbz") for _ in range(NT)]
    RGM = R("gm")
    Rtop = [R("top") for _ in range(2)]
    RFRAW, RLF, RTOT, RPREF, RCP, RZ, Rkbar, Rkbarb, Ref = (R("fraw"), R("lf"), R("tot"), R("pref"),
                                                            R("cp"), R("z"), R("kbar"), R("kbarb"), R("ef"))
    RPT = [R("pt") for _ in range(5)]
    Rrl = [R("rl") for _ in range(4)]
    Rc = {k: R(k) for k in ["wfm", "wtm", "ident", "tri_f", "tri_b", "ones", "sgu_f", "wmT", "band", "sgub",
                            "lng", "lnb", "poolw", "pscale", "bfor", "negb", "kaoh", "misc"]}

    C.dma("pool", wfm_b[:], wfm.rearrange("(kc f) n -> f kc n", f=128), writes=[Rc["wfm"]])
    C.dma("pool", wtm_b[:], wtm.rearrange("(kc f) n -> f kc n", f=128), writes=[Rc["wtm"]])
    C.dma("sp", ident_f[:], ident, writes=[Rc["ident"]])
    C.dma("sp", tri_f[:], tri, writes=[Rc["tri_f"]])
    C.dma("pool", tri_b[:], tri, writes=[Rc["tri_b"]])
    C.dma("sp", sgu_f[:], sguT, writes=[Rc["sgu_f"]])
    C.dma("pool", band_b[:], band.rearrange("k s t -> s k t"), writes=[Rc["band"]])
    C.dma("sp", sgub_s[:], sgub, writes=[Rc["sgub"]])
    C.dma("sp", lng_s[:], lng, writes=[Rc["lng"]])
    C.dma("sp", lnb_s[:], lnb, writes=[Rc["lnb"]])
    C.dma("pool", poolw_b[:], poolw, writes=[Rc["poolw"]])
    C.dma("sp", pscale_s[:], pscale, writes=[Rc["pscale"]])
    C.dma("sp", bfor_s[:], bfor, writes=[Rc["bfor"]])
    C.dma("sp", KA[64:96, :], onehot, writes=[Rc["kaoh"]])
    C.dve(lambda e: e.tensor_tensor(out=wmT_b[:], in0=sgu_f[:], in1=tri_f[:], op=ALU.mult),
          reads=[Rc["sgu_f"], Rc["tri_f"]], writes=[Rc["wmT"]])
    C.dve(lambda e: e.tensor_scalar(out=negb[:], in0=bfor_s[:], scalar1=-1.0, scalar2=None, op0=ALU.mult),
          reads=[Rc["bfor"]], writes=[Rc["negb"]])
    C.dve(lambda e: e.memset(ones_f[:], 1.0), writes=[Rc["ones"]])
    C.dve(lambda e: e.memset(VA[:, :, 64:65], 1.0), writes=RVA)
    C.dve(lambda e: e.memset(VB[:, :, 64:65], 1.0), writes=RVB)
    C.dve(lambda e: e.memset(KB[64:65, :], 1.0), writes=RKB)
    C.dve(lambda e: e.memset(MBZ[:], 0.0), writes=RMBZ)
    C.dve(lambda e: e.memset(GM[:], NEGF), writes=[RGM])
    C.dve(lambda e: e.memset(Z[:], 0.0), writes=[RZ])
    C.dve(lambda e: e.memset(kbar[:], 0.0), writes=[Rkbar])
    C.dve(lambda e: e.memset(PREF[:, 0:1], 0.0), writes=[RPREF])

    if PH == 0:
        return _finish_mixer(C, nc)
    xT_v = xT.rearrange("(kc f) t -> f kc t", f=128)

    def load_tile(T):
        sl = T % 2
        C.dma("pool", xb[sl][:], xT_v[:, :, T * 512:(T + 1) * 512], writes=[Rxb[sl]])
        C.dma("sp", cst[sl][:], cs[:, :, T * 512:(T + 1) * 512].rearrange("k p t -> p k t"), writes=[Rcs[sl]])

    load_tile(0)
    fm_cols = {"qA": (0, 64), "kA": (64, 64), "qB": (128, 64), "kB": (192, 64), "pD": (256, 64),
               "qAp": (320, 16), "kAp": (336, 16)}
    pend = []

    def p1_tile(T):
        sl = T % 2
        if T + 1 < NT:
            load_tile(T + 1)
        cols = slice(T * 512, (T + 1) * 512)

        def fm(name, bank):
            c0, n = fm_cols[name]
            C.pe(_mm_group(PS[bank][0:n, :], [(wfm_b[:, kc, c0:c0 + n], xb[sl][:, kc, :]) for kc in range(8)]),
                 reads=[Rc["wfm"], Rxb[sl]], writes=[RPS[bank]])

        def rot(nm, nmp, dst, Rdst, b0, b1):
            fm(nm, b0)
            fm(nmp, b1)
            if DBG == 10:
                return
            C.act(lambda e, dst=dst: e.copy(out=dst[0:64, cols], in_=PS[b0][0:64, :]), reads=[RPS[b0]], writes=[Rdst[T]])
            if DBG == 11:
                return
            C.dve(lambda e: e.tensor_tensor(out=t1[sl][:], in0=PS[b0][0:16, :], in1=cst[sl][:, 0, :], op=ALU.mult),
                  reads=[RPS[b0], Rcs[sl]], writes=[Rt1[sl]])
            C.dve(lambda e: e.tensor_tensor(out=t2[sl][:], in0=PS[b1][0:16, :], in1=cst[sl][:, 1, :], op=ALU.mult),
                  reads=[RPS[b1], Rcs[sl]], writes=[Rt2[sl]])
            if DBG == 12:
                return
            C.dve(lambda e, dst=dst: e.tensor_tensor(out=dst[0:16, cols], in0=t1[sl][:], in1=t2[sl][:], op=ALU.add),
                  reads=[Rt1[sl], Rt2[sl]], writes=[Rdst[T]])
        if DBG < 1:
            return
        rot("qA", "qAp", QA, RQA, 0, 1)
        rot("kA", "kAp", KA, RKA, 4, 7)
        if DBG < 2 or (10 <= DBG < 20):
            return
        C.dve(lambda e: e.tensor_reduce(out=kbar[:, 2 * T:2 * T + 2],
                                        in_=KA[0:64, cols].rearrange("p (b j) -> p b j", j=256),
                                        axis=AX.X, op=ALU.add), reads=[RKA[T]], writes=[Rkbar])
        if DBG < 3:
            return
        fm("qB", 0)
        C.act(lambda e: e.copy(out=QB[0:64, cols], in_=PS[0][0:64, :]), reads=[RPS[0]], writes=[RQB[T]])
        fm("kB", 1)
        C.act(lambda e: e.copy(out=KB[0:64, cols], in_=PS[1][0:64, :]), reads=[RPS[1]], writes=[RKB[T]])
        fm("pD", 4)
        C.act(lambda e: e.copy(out=pTb[sl][:], in_=PS[4][0:64, :]), reads=[RPS[4]], writes=[RpT[sl]])
        if DBG < 4:
            return
        def sub(c):
            tt = 4 * T + c
            s2 = tt % 2
            bk = 2 + s2
            C.pe(_mm_group(PS[bk][:, 0:NTM], [(xb[sl][:, kc, c * 128:(c + 1) * 128], wtm_b[:, kc, :]) for kc in range(8)]),
                 reads=[Rc["wtm"], Rxb[sl]], writes=[RPS[bk]])
            C.act(lambda e, bk=bk, tt=tt: e.copy(out=VA[:, tt, 0:64], in_=PS[bk][:, 0:64]), reads=[RPS[bk]], writes=[RVA[T]])
            C.act(lambda e, bk=bk, tt=tt: e.copy(out=VB[:, tt, 0:64], in_=PS[bk][:, 64:128]), reads=[RPS[bk]], writes=[RVB[T]])
            C.dve(lambda e, bk=bk, tt=tt: e.tensor_copy(out=FRAW[:, tt:tt + 1], in_=PS[bk][:, 256:257]),
                  reads=[RPS[bk]], writes=[RFRAW])
            C.act(lambda e, bk=bk, tt=tt: e.activation(out=UG[:, tt, :], in_=PS[bk][:, 128:256], func=AF.Gelu),
                  reads=[RPS[bk]], writes=[RUG[T]])
            C.dve(lambda e, tt=tt, s2=s2: e.bn_stats(out=stats[s2][:], in_=UG[:, tt, 64:128]), reads=[RUG[T]], writes=[Rst[s2]])
            C.dve(lambda e, tt=tt, s2=s2: e.bn_aggr(out=MV[:, tt, :], in_=stats[s2][:]), reads=[Rst[s2]], writes=[RMV])
            s3 = tt % 3
            C.pe(lambda e, c=c: e.matmul(PS[5][:, 0:64], lhsT=pTb[sl][:, c * 128:(c + 1) * 128], rhs=poolw_b[:],
                                         start=True, stop=True), reads=[RpT[sl], Rc["poolw"]], writes=[RPS[5]])
            C.act(lambda e, s3=s3: e.copy(out=pwb[s3][:], in_=PS[5][:, 0:64]), reads=[RPS[5]], writes=[Rpwb[s3]])
            flush_band()
            pend.append((T, c))

        def flush_band():
            while pend:
                bT, bc = pend.pop(0)
                btt = 4 * bT + bc
                bs3, bp3, bsl = btt % 3, (btt - 1) % 3, bT % 2
                if btt == 0:
                    C.pe(lambda e, bs3=bs3: e.matmul(PS[6][:, 0:64], lhsT=band_b[:, 2, :], rhs=pwb[bs3][:], start=True, stop=True),
                         reads=[Rc["band"], Rpwb[bs3]], writes=[RPS[6]])
                else:
                    C.pe(_mm_group(PS[6][:, 0:64], [(band_b[:, 0, :], pwb[bs3][:]), (band_b[:, 1, :], pwb[bp3][:])]),
                         reads=[Rc["band"], Rpwb[bs3], Rpwb[bp3]], writes=[RPS[6]])
                C.dve(lambda e, bc=bc, bsl=bsl: e.tensor_tensor(out=YD[bsl][:, bc, :], in0=PS[6][:, 0:64], in1=pscale_s[:], op=ALU.mult),
                      reads=[RPS[6], Rc["pscale"]], writes=[RYD[bsl]])
        for c in range(4):
            sub(c)
        flush_band()
        C.dma("sp", y[T * 512:(T + 1) * 512, 192:256].rearrange("(c p) n -> p c n", p=128), YD[sl][:],
              reads=[RYD[sl]], key="yd%d" % sl)

    for T in range(NT):
        p1_tile(T)

    if PH == 1:
        return _finish_mixer(C, nc)
    C.dve(lambda e: e.tensor_scalar(out=VE[:], in0=MV[:, :, 1], scalar1=EPS, scalar2=None, op0=ALU.add),
          reads=[RMV], writes=[RVE])
    C.act(lambda e: e.activation(out=VE[:], in_=VE[:], func=AF.Sqrt), reads=[RVE], writes=[RVE])
    C.dve(lambda e: e.reciprocal(out=RSTD[:], in_=VE[:]), reads=[RVE], writes=[RRSTD])

    vn4 = [sb("vn4_%d" % i, [128, 64], F32) for i in range(4)]
    vnb4 = [sb("vnb4_%d" % i, [128, 64], BF16) for i in range(4)]
    Rvn4 = [R("vn4") for _ in range(4)]
    Rvnb4 = [R("vnb4") for _ in range(4)]

    def c_stage(T, stage):
        sl = T % 2
        if stage == 0:
            for c in range(4):
                tt = 4 * T + c
                C.dve(lambda e, c=c, tt=tt: e.tensor_scalar(out=vn4[c][:], in0=UG[:, tt, 64:128], scalar1=MV[:, tt, 0:1],
                                                            scalar2=RSTD[:, tt:tt + 1], op0=ALU.subtract, op1=ALU.mult),
                      reads=[RUG[T], RMV, RRSTD], writes=[Rvn4[c]])
                C.dve(lambda e, c=c: e.tensor_tensor(out=vn4[c][:], in0=vn4[c][:], in1=lng_s[:], op=ALU.mult),
                      reads=[Rvn4[c], Rc["lng"]], writes=[Rvn4[c]])
                C.dve(lambda e, c=c: e.tensor_tensor(out=vnb4[c][:], in0=vn4[c][:], in1=lnb_s[:], op=ALU.add),
                      reads=[Rvn4[c], Rc["lnb"]], writes=[Rvnb4[c]])
        elif stage == 1:
            def mix(e):
                ins = None
                for c in range(4):
                    ins = e.matmul(PS[5][:, c * 64:(c + 1) * 64], lhsT=wmT_b[:], rhs=vnb4[c][:], start=True, stop=True)
                return ins
            C.pe(mix, reads=[Rc["wmT"]] + Rvnb4, writes=[RPS[5]])
        else:
            for c in range(4):
                tt = 4 * T + c
                C.dve(lambda e, c=c, tt=tt: e.scalar_tensor_tensor(out=YC[sl][:, c, :], in0=PS[5][:, c * 64:(c + 1) * 64],
                                                                   scalar=sgub_s[:, 0:1], in1=UG[:, tt, 0:64],
                                                                   op0=ALU.add, op1=ALU.mult),
                      reads=[RPS[5], Rc["sgub"], RUG[T]], writes=[RYC[sl]])
            C.dma("sp", y[T * 512:(T + 1) * 512, 128:192].rearrange("(c p) n -> p c n", p=128), YC[sl][:],
                  reads=[RYC[sl]], key="yc%d" % sl)

    def c_tile(T):
        for s in range(3):
            c_stage(T, s)

    if PH < 4:
        for T in range(NT):
            c_tile(T)

    if PH == 2:
        return _finish_mixer(C, nc)
    C.act(lambda e: e.activation(out=ef[:], in_=FRAW[:], func=AF.Exp, bias=negb[:, 0:1], scale=-1.0),
          reads=[RFRAW, Rc["negb"]], writes=[Ref])
    C.act(lambda e: e.activation(out=LF[:], in_=ef[:], func=AF.Ln, bias=1.0, scale=1.0), reads=[Ref], writes=[RLF])
    C.pe(lambda e: e.matmul(PS[0][:, 0:NTT], lhsT=ones_f[:], rhs=LF[:], start=True, stop=True),
         reads=[Rc["ones"], RLF], writes=[RPS[0]])
    C.dve(lambda e: e.tensor_copy(out=TOT[:], in_=PS[0][:, 0:NTT]), reads=[RPS[0]], writes=[RTOT])
    for j in range(1, NTT):
        C.dve(lambda e, j=j: e.tensor_tensor(out=PREF[:, j:j + 1], in0=PREF[:, j - 1:j], in1=TOT[:, j - 1:j], op=ALU.add),
              reads=[RPREF, RTOT], writes=[RPREF])
    C.pe(lambda e: e.matmul(PS[1][:, 0:NTT], lhsT=tri_f[:], rhs=LF[:], start=True, stop=True),
         reads=[Rc["tri_f"], RLF], writes=[RPS[1]])
    C.dve(lambda e: e.tensor_tensor(out=CP[:], in0=PS[1][:, 0:NTT], in1=PREF[:], op=ALU.add),
          reads=[RPS[1], RPREF], writes=[RCP])
    C.dve(lambda e: e.tensor_scalar(out=Z[:, 64:64 + NTT], in0=CP[:], scalar1=-8.0, scalar2=None, op0=ALU.mult),
          reads=[RCP], writes=[RZ])
    C.dve(lambda e: e.tensor_scalar(out=kbar_b[:], in0=kbar[:], scalar1=1.0 / 256.0, scalar2=None, op0=ALU.mult),
          reads=[Rkbar], writes=[Rkbarb])
    GM4 = [sb("GM4_%d" % i, [128, 32], F32) for i in range(4)]
    top4 = [sb("top4_%d" % i, [128, 8], F32) for i in range(4)]
    RGM4 = [R("gm4") for _ in range(4)]
    Rtop4 = [R("top4") for _ in range(4)]
    for i in range(4):
        C.dve(lambda e, i=i: e.memset(GM4[i][:], NEGF), writes=[RGM4[i]])

    def p2_stage(T, stage):
        if stage == 0:
            def tr4(e):
                ins = None
                for c in range(4):
                    tt = 4 * T + c
                    ins = e.matmul(PS[5][0:65, c * 128:(c + 1) * 128], lhsT=Z[:, tt:tt + 65], rhs=ident_f[:],
                                   start=True, stop=True)
                return ins
            C.pe(tr4, reads=[RZ, Rc["ident"]], writes=[RPS[5]])

            def gates(e):
                ins = None
                for c in range(4):
                    tt = 4 * T + c
                    ins = e.matmul(PS[6][:, c * 32:(c + 1) * 32], lhsT=QA[0:64, tt * 128:(tt + 1) * 128], rhs=kbar_b[:],
                                   start=True, stop=True)
                return ins
            C.pe(gates, reads=[RQA[T], Rkbarb], writes=[RPS[6]])
        elif stage == 1:
            C.act(lambda e: e.copy(out=QB[64:65, T * 512:(T + 1) * 512], in_=PS[5][64:65, :]),
                  reads=[RPS[5]], writes=[RQBc[T]])
            for c in range(4):
                tt = 4 * T + c
                b = tt // 2
                if b == 0:
                    continue
                C.dve(lambda e, c=c, b=b: e.tensor_copy(out=GM4[c][:, 0:b], in_=PS[6][:, c * 32:c * 32 + b]),
                      reads=[RPS[6]], writes=[RGM4[c]])
                C.dve(lambda e, c=c: e.max(out=top4[c][:], in_=GM4[c][:]), reads=[RGM4[c]], writes=[Rtop4[c]])
                C.dve(lambda e, c=c, tt=tt, b=b: e.tensor_scalar(out=MBZ[:, 64 + 32 * tt:64 + 32 * tt + b], in0=GM4[c][:, 0:b],
                                                                 scalar1=top4[c][:, 2:3], scalar2=1.0,
                                                                 op0=ALU.is_ge, op1=ALU.subtract),
                      reads=[RGM4[c], Rtop4[c]], writes=[RMBZ[T]])
        elif stage == 2:
            def trm(e):
                ins = None
                for c in range(4):
                    tt = 4 * T + c
                    ins = e.matmul(PS[7][0:96, c * 128:(c + 1) * 128], lhsT=MBZ[:, 32 * tt:32 * tt + 96], rhs=ident_f[:],
                                   start=True, stop=True)
                return ins
            C.pe(trm, reads=[RMBZ[T], Rc["ident"]], writes=[RPS[7]])
        else:
            C.act(lambda e: e.copy(out=QA[64:96, T * 512:(T + 1) * 512], in_=PS[7][64:96, :]),
                  reads=[RPS[7]], writes=[RQAm[T]])

    def p2_tile(T):
        for s in range(4):
            p2_stage(T, s)

    if PH < 4:
        for T in range(NT):
            p2_tile(T)
    else:
        p2_tile(0)

    if PH == 3:
        return _finish_mixer(C, nc)
    ROacc = [[RPS[3]] * 4, [RPS[4]] * 4]
    LOOK = 3

    def att_score(p):
        qi, att, kj, oi, idx = p
        Q, K = (QA, KA) if att == 0 else (QB, KB)
        RQ, RQx, RK = (RQA, RQAm, RKA) if att == 0 else (RQB, RQBc, RKB)
        rows = 96 if att == 0 else 65
        d = kj - 4 * qi
        off = 128 * max(d, 0)
        n = 512 - off
        sbk = idx % 3
        pt = idx % 5
        C.pe(lambda e: e.matmul(PS[sbk][:, 0:n], lhsT=K[0:rows, kj * 128:(kj + 1) * 128],
                                rhs=Q[0:rows, qi * 512 + off:(qi + 1) * 512], start=True, stop=True),
             reads=[RQ[qi], RQx[qi], RK[kj // 4]] + ([Rc["kaoh"]] if att == 0 else []), writes=[RPS[sbk]])
        if att == 0:
            C.act(lambda e: e.activation(out=PT[pt][:, 0:n], in_=PS[sbk][:, 0:n], func=AF.Exp, scale=0.125),
                  reads=[RPS[sbk]], writes=[RPT[pt]])
        else:
            C.act(lambda e: e.activation(out=PT[pt][:, 0:n], in_=PS[sbk][:, 0:n], func=AF.Exp,
                                         bias=CP[:, kj:kj + 1], scale=0.125),
                  reads=[RPS[sbk], RCP], writes=[RPT[pt]])
        if d >= 0:
            C.dve(lambda e: e.tensor_tensor(out=PT[pt][:, 0:128], in0=PT[pt][:, 0:128], in1=tri_b[:], op=ALU.mult),
                  reads=[RPT[pt], Rc["tri_b"]], writes=[RPT[pt]])

    def att_pv(p):
        qi, att, kj, oi, idx = p
        V = VA if att == 0 else VB
        RV = RVA if att == 0 else RVB
        ysl = qi % 2
        ob = 3 + (oi % 2)
        Oacc = PS[ob][:, 0:260].rearrange("p (c n) -> p c n", n=65)
        RO = ROacc[ob - 3]
        d = kj - 4 * qi
        off = 128 * max(d, 0)
        pt = idx % 5
        c0 = max(d, 0)

        def pv(e):
            ins = None
            for c in range(c0, 4):
                ins = e.matmul(Oacc[:, c, :], lhsT=PT[pt][:, c * 128 - off:(c + 1) * 128 - off], rhs=V[:, kj, :],
                               start=(kj == 0 and c == 0), stop=(kj == 4 * qi + c), skip_group_check=True)
            return ins
        C.pe(pv, reads=[RPT[pt], RV[kj // 4]], writes=[RO[0]])
        if d >= 0:
            c = d
            r4 = (2 * (oi % 2) + (c % 2))
            C.dve(lambda e: e.reciprocal(out=rl[r4][:], in_=Oacc[:, c, 64:65]), reads=[RO[c]], writes=[Rrl[r4]])
            C.dve(lambda e: e.tensor_scalar(out=YAB[ysl][:, c, att * 64:(att + 1) * 64], in0=Oacc[:, c, 0:64],
                                            scalar1=rl[r4][:, 0:1], scalar2=None, op0=ALU.mult),
                  reads=[RO[c], Rrl[r4]], writes=[RYAB[ysl]])
        if att == 1 and d == 3:
            C.dma("sp", y[qi * 512:(qi + 1) * 512, 0:128].rearrange("(c p) n -> p c n", p=128), YAB[ysl][:],
                  reads=[RYAB[ysl]], key="yab%d" % ysl)

    plist = []
    oi = 0
    for qi in range(NT):
        for att in range(2):
            for kj in range(4 * qi + 4):
                plist.append((qi, att, kj, oi, len(plist)))
            oi += 1
    for i in range(len(plist) + LOOK):
        if i < len(plist):
            qi_, att_, kj_ = plist[i][0], plist[i][1], plist[i][2]
            n_ = 4 * qi_ + 4
            if att_ == 0 and qi_ + 1 < NT:
                offs = [0, n_ // 4, n_ // 2, (3 * n_) // 4]
                if kj_ in offs:
                    p2_stage(qi_ + 1, offs.index(kj_))
            if att_ == 1:
                offs = [0, n_ // 3, (2 * n_) // 3]
                if kj_ in offs:
                    c_stage(qi_, offs.index(kj_))
            att_score(plist[i])
        if i - LOOK >= 0:
            att_pv(plist[i - LOOK])

    if os.environ.get("MIX_DUMP"):
        allres = RQA + RQAm + RKA + RQB + RQBc + RKB + RVA + RVB + RUG + [RCP, RLF, RFRAW, RMV, RRSTD, Rkbar, Rkbarb, RZ, Rc["kaoh"]] + RMBZ + Rpwb + [Rc["band"], Rc["tri_b"], Rc["poolw"]]
        for nm, t, shp, dty in (("d_cp", CP, [128, NTT], F32), ("d_lf", LF, [128, NTT], F32), ("d_fraw", FRAW, [128, NTT], F32),
                                ("d_rstd", RSTD, [128, NTT], F32),
                                ("d_ug", UG, [128, NTT, 128], BF16), ("d_qa", QA, [128, SEQ], BF16), ("d_ka", KA, [128, SEQ], BF16),
                                ("d_qb", QB, [128, SEQ], BF16), ("d_kb", KB, [128, SEQ], BF16), ("d_va", VA, [128, NTT, 65], BF16),
                                ("d_vb", VB, [128, NTT, 65], BF16), ("d_kbar", kbar, [64, 32], F32), ("d_mbz", MBZ, [128, 64 + NTT * 32], F32),
                                ("d_pwb0", pwb[0], [128, 64], BF16), ("d_band", band_b, [128, 3, 128], BF16), ("d_trib", tri_b, [128, 128], BF16)):
            dd = dt(nm, shp, dty, kind="ExternalOutput").ap()
            C.dma("sp", dd, t[:], reads=allres, key=nm)

    return _finish_mixer(C, nc)


def _finish_mixer(C, nc):
    finals = []
    for k, cnt in C.S.dcount.items():
        finals.append((k, 16 * cnt, "dma"))
    C.S.emit(finals)
    C.st.close()
    return nc


def _rot_tables():
    pos = np.arange(S, dtype=np.float32)
    inv_freq = (np.float32(500000.0) ** (-np.arange(0, 16, 2, dtype=np.float32) / np.float32(16))).astype(np.float32)
    ang = (pos[:, None] * inv_freq[None, :]).astype(np.float32)
    cos = np.cos(ang).astype(np.float32).T
    sin = np.sin(ang).astype(np.float32).T
    cs = np.zeros((2, 16, S), np.float32)
    cs[0, 0:8] = cos
    cs[0, 8:16] = cos
    cs[1, 0:8] = -sin
    cs[1, 8:16] = sin
    return cs


def _band_mats(win):
    t = np.arange(128)
    s = np.arange(128)
    cur = ((t[None, :] - s[:, None] >= 0) & (t[None, :] - s[:, None] < win)).astype(np.float32) / win
    cur -= np.eye(128, dtype=np.float32)
    prev = ((t[None, :] + 128 - s[:, None]) < win).astype(np.float32) / win
    cnt = np.minimum(t + 1, win).astype(np.float32)
    first = ((t[None, :] - s[:, None] >= 0) & (t[None, :] - s[:, None] < win)).astype(np.float32) / cnt[None, :]
    first -= np.eye(128, dtype=np.float32)
    return np.stack([cur, prev, first]).astype(np.float32)


_CACHE = {}


def _mixer_inputs(xT_b, l, h, P):
    w_in = P["w_in"][l]
    a0 = 0
    qa = w_in[:, 0 + 64 * h:0 + 64 * h + 64]
    ka = w_in[:, 256 + 64 * h:256 + 64 * h + 64]
    va = w_in[:, 512 + 64 * h:512 + 64 * h + 64]
    qb = w_in[:, 768 + 64 * h:768 + 64 * h + 64]
    kb = w_in[:, 1024 + 64 * h:1024 + 64 * h + 64]
    vb = w_in[:, 1280 + 64 * h:1280 + 64 * h + 64]
    fb = w_in[:, 1536 + h:1536 + h + 1]
    cu = w_in[:, 1540 + 64 * h:1540 + 64 * h + 64]
    cv = w_in[:, 1796 + 64 * h:1796 + 64 * h + 64]
    dp = w_in[:, 2052 + 64 * h:2052 + 64 * h + 64]
    perm = np.concatenate([np.arange(8, 16), np.arange(0, 8)])
    wfm = np.concatenate([qa, ka, qb, kb, dp, qa[:, perm], ka[:, perm]], axis=1)
    wtm = np.concatenate([va, vb, cu, cv, fb], axis=1)
    rep = lambda v: np.ascontiguousarray(np.broadcast_to(v[None, :], (128, v.shape[0]))).astype(np.float32)
    return {
        "xT": xT_b,
        "wfm": np.ascontiguousarray(wfm), "wtm": np.ascontiguousarray(wtm),
        "cs": _CACHE["cs"], "onehot": _CACHE["onehot"],
        "sguT": np.ascontiguousarray(P["sgu_w"][l, h].T), "tri": _CACHE["tri"], "ident": _CACHE["ident"],
        "sgub": np.ascontiguousarray(P["sgu_b"][l, h].reshape(128, 1)),
        "lng": rep(P["sgu_ln_g"][l, 64 * h:64 * h + 64]), "lnb": rep(P["sgu_ln_b"][l, 64 * h:64 * h + 64]),
        "poolw": np.ascontiguousarray(P["pool_w"][l, h]), "pscale": rep(P["pool_scale"][l, 64 * h:64 * h + 64]),
        "band": _CACHE["band"][h],
        "bfor": np.full((128, 1), P["b_forget"][l, h], np.float32),
    }


def _consts():
    if "cs" in _CACHE:
        return
    _CACHE["cs"] = _rot_tables()
    oh = np.zeros((32, S), np.float32)
    for n in range(32):
        oh[n, n * 256:(n + 1) * 256] = BIGM
    _CACHE["onehot"] = oh.astype(ml_dtypes.bfloat16)
    sidx = np.arange(128)
    _CACHE["tri"] = (sidx[:, None] <= sidx[None, :]).astype(np.float32)
    _CACHE["ident"] = np.eye(128, dtype=np.float32)
    _CACHE["band"] = [_band_mats(w) for w in (2, 4, 8, 16)]


def run_mixer(xT_all, l, P):
    _consts()
    if "nc_m" not in _CACHE:
        _CACHE["nc_m"] = build_mixer()
    in_maps = [_mixer_inputs(xT_all[c // 4], l, c % 4, P) for c in range(8)]
    res = run_bass_kernel_spmd(_CACHE["nc_m"], in_maps, core_ids=list(range(8)))
    y = np.zeros((NB, S, 1024), ml_dtypes.bfloat16)
    for c in range(8):
        b, h = c // 4, c % 4
        yy = np.asarray(res.results[c]["y"]).view(ml_dtypes.bfloat16).reshape(S, 256) if res.results[c]["y"].dtype != ml_dtypes.bfloat16 else res.results[c]["y"]
        for m in range(4):
            y[b, :, 256 * m + 64 * h:256 * m + 64 * h + 64] = yy[:, 64 * m:64 * m + 64]
    return y


NTOK = 2048
NCH = 44


def build_post(NT4=4):
    nc = bass.Bass("TRN2", target_bir_lowering=False)
    C = Ctx(nc)
    dt = nc.dram_tensor
    NTK = NT4 * 512
    yin = dt("yin", [128 + NTK, D], BF16, kind="ExternalInput").ap()
    xin = dt("xin", [128 + NTK, D], F32, kind="ExternalInput").ap()
    flag = dt("flag", [128, 1], F32, kind="ExternalInput").ap()
    wo = dt("wo", [D, D], F32, kind="ExternalInput").ap()
    wup = dt("wup", [NCH, 128, 8, 128], F32, kind="ExternalInput").ap()
    wdn = dt("wdn", [DFF, D], F32, kind="ExternalInput").ap()
    convw = dt("convw", [128, NCH, 3], F32, kind="ExternalInput").ap()
    convb = dt("convb", [128, NCH], F32, kind="ExternalInput").ap()
    lnp = dt("lnp", [4, 128, D], F32, kind="ExternalInput").ap()
    ident = dt("ident", [128, 128], F32, kind="ExternalInput").ap()
    xo = dt("xo", [NTK, D], F32, kind="ExternalOutput").ap()

    sb, ps, res = C.sb, C.ps, C.res
    wd_b = sb("wd_b", [128, 22, D], BF16)
    wo_b = sb("wo_b", [128, 8, D], BF16)
    A_T = sb("A_T", [128, 22, 512], BF16)
    X1T = [sb("X1T%d" % i, [128, 8, 512], BF16) for i in range(2)]
    X1Th = sb("X1Th", [128, 8, 2], BF16)
    X1 = sb("X1", [128, 4, D], F32)
    wu = [[sb("wu%d_%d" % (i, j), [128, 8, 128], BF16) for j in range(2)] for i in range(3)]
    H = [sb("H%d" % i, [128, 514], F32) for i in range(4)]
    tg = [sb("tg%d" % i, [128, 512], F32) for i in range(2)]
    tv = [sb("tv%d" % i, [128, 512], F32) for i in range(2)]
    sg = [sb("sg%d" % i, [128, 512], BF16) for i in range(2)]
    HALO = sb("HALO", [128, NCH, 2], F32)
    lnp_s = sb("lnp_s", [128, 4, D], F32)
    yt = [sb("yt%d" % i, [128, D], BF16) for i in range(2)]
    xt = [sb("xt%d" % i, [128, D], F32) for i in range(2)]
    rr = sb("rr", [128, D], F32)
    x1b = sb("x1b", [128, D], BF16)
    yT = sb("yT", [128, 8, 128], BF16)
    x2 = [sb("x2_%d" % i, [128, D], F32) for i in range(2)]
    stats = sb("stats", [128, 12], F32)
    mv = sb("mv", [128, 2], F32)
    sd = sb("sd", [128, 1], F32)
    rstd = sb("rstd", [128, 1], F32)
    cw_s = sb("cw_s", [128, NCH, 3], F32)
    cb_s = sb("cb_s", [128, NCH], F32)
    flag_s = sb("flag_s", [128, 1], F32)
    ident_b = sb("ident_b", [128, 128], BF16)
    PS = [ps("ps%d" % i, [128, 512], F32) for i in range(7)]
    PSB = ps("psb", [128, 1024], BF16)
    RPS = [res("ps", True) for _ in range(7)]
    RPSB = res("psb", True)
    R = lambda n: res(n)
    Rwd, Rwo, RAT, RX1, RX1Th, RHALO, Rlnp, Rrr, Rx1b, RyT = (R("wd"), R("wo"), R("at"), R("x1"), R("x1th"), R("halo"),
                                                             R("lnp"), R("rr"), R("x1b"), R("yT"))
    RX1T = [R("x1t") for _ in range(2)]
    Rwu = [[R("wu") for _ in range(2)] for _ in range(3)]
    RH = [R("h") for _ in range(4)]
    RHh = [R("hh") for _ in range(4)]
    RHALOc = [R("haloc") for _ in range(NCH)]
    Rtg = [R("tg") for _ in range(2)]
    Rtv = [R("tv") for _ in range(2)]
    Rsg = [R("sg") for _ in range(2)]
    Ryt = [R("yt") for _ in range(2)]
    Rxt = [R("xt") for _ in range(2)]
    Rx2 = [R("x2") for _ in range(2)]
    Rst, Rmv, Rsd, Rrstd, Rcw, Rcb, Rflag, Rid = (R("st"), R("mv"), R("sd"), R("rstd"), R("cw"), R("cb"), R("flag"), R("id"))

    C.dma("pool", ident_b[:], ident, writes=[Rid])
    wo_v = wo.rearrange("(kc f) n -> f kc n", f=128)
    for kc in range(0, 8, 4):
        C.dma("pool", wo_b[:, kc:kc + 4, :], wo_v[:, kc:kc + 4, :], writes=[Rwo], key="wo")
    C.dma("sp", lnp_s[:], lnp.rearrange("k p n -> p k n"), writes=[Rlnp])
    C.dma("sp", cw_s[:], convw, writes=[Rcw])
    C.dma("sp", cb_s[:], convb, writes=[Rcb])
    C.dma("sp", flag_s[:], flag, writes=[Rflag])

    def load_sub(i):
        sl = i % 2
        C.dma("sp", yt[sl][:], yin[i * 128:(i + 1) * 128, :], writes=[Ryt[sl]])
        C.dma("sp", xt[sl][:], xin[i * 128:(i + 1) * 128, :], writes=[Rxt[sl]])

    def layer_norm(src, Rsrc, dst, Rdst, gi):
        def st(e):
            e.bn_stats(out=stats[:, 0:6], in_=src[:, 0:512])
            return e.bn_stats(out=stats[:, 6:12], in_=src[:, 512:1024])
        C.dve(st, reads=[Rsrc], writes=[Rst])
        C.dve(lambda e: e.bn_aggr(out=mv[:], in_=stats[:]), reads=[Rst], writes=[Rmv])
        C.dve(lambda e: e.tensor_scalar(out=sd[:], in0=mv[:, 1:2], scalar1=EPS, scalar2=None, op0=ALU.add),
              reads=[Rmv], writes=[Rsd])
        C.act(lambda e: e.activation(out=sd[:], in_=sd[:], func=AF.Sqrt), reads=[Rsd], writes=[Rsd])
        C.dve(lambda e: e.reciprocal(out=rstd[:], in_=sd[:]), reads=[Rsd], writes=[Rrstd])
        C.dve(lambda e: e.tensor_scalar(out=src[:], in0=src[:], scalar1=mv[:, 0:1], scalar2=rstd[:, 0:1],
                                        op0=ALU.subtract, op1=ALU.mult), reads=[Rsrc, Rmv, Rrstd], writes=[Rsrc])
        C.dve(lambda e: e.tensor_tensor(out=src[:], in0=src[:], in1=lnp_s[:, gi, :], op=ALU.mult),
              reads=[Rsrc, Rlnp], writes=[Rsrc])
        C.dve(lambda e: e.tensor_tensor(out=dst, in0=src[:], in1=lnp_s[:, gi + 1, :], op=ALU.add),
              reads=[Rsrc, Rlnp], writes=[Rdst])

    def a_front(i):
        sl = i % 2

        def tr_y(e):
            ins = None
            for kc in range(8):
                ins = e.transpose(PSB[:, kc * 128:(kc + 1) * 128], yt[sl][:, kc * 128:(kc + 1) * 128], ident_b[:])
            return ins
        C.pe(tr_y, reads=[Ryt[sl], Rid], writes=[RPSB])
        C.act(lambda e: e.copy(out=yT[:].rearrange("p k n -> p (k n)"), in_=PSB[:]), reads=[RPSB], writes=[RyT])
        for half in range(2):
            C.pe(_mm_group(PS[half][:], [(yT[:, kc, :], wo_b[:, kc, half * 512:(half + 1) * 512]) for kc in range(8)]),
                 reads=[RyT, Rwo], writes=[RPS[half]])

    def a_x1src(i):
        if i == 0:
            return x2[0][:], Rx2[0]
        c = (i - 1) % 4
        return X1[:, c, :], RX1

    def a_cast(i):
        x1src, Rx1src = a_x1src(i)
        C.act(lambda e: e.copy(out=x1b[:], in_=x1src), reads=[Rx1src], writes=[Rx1b])

    def a_back(i):
        def tr_x(e):
            ins = None
            for kc in range(8):
                ins = e.transpose(PSB[:, kc * 128:(kc + 1) * 128], x1b[:, kc * 128:(kc + 1) * 128], ident_b[:])
            return ins
        C.pe(tr_x, reads=[Rx1b, Rid], writes=[RPSB])
        psv = PSB[:].rearrange("p (k n) -> p k n", n=128)
        if i == 0:
            C.act(lambda e: e.copy(out=X1Th[:], in_=psv[:, :, 126:128]), reads=[RPSB], writes=[RX1Th])
        else:
            T = (i - 1) // 4
            c = (i - 1) % 4
            C.act(lambda e: e.copy(out=X1T[T % 2][:, :, c * 128:(c + 1) * 128], in_=psv), reads=[RPSB], writes=[RX1T[T % 2]])

    def a_mid(i):
        sl = i % 2
        for half in range(2):
            C.dve(lambda e, half=half: e.scalar_tensor_tensor(out=rr[:, half * 512:(half + 1) * 512], in0=xt[sl][:, half * 512:(half + 1) * 512],
                                                              scalar=ALPHA, in1=PS[half][:], op0=ALU.mult, op1=ALU.add),
                  reads=[Rxt[sl], RPS[half]], writes=[Rrr])
        dst, Rdst = a_x1src(i)
        layer_norm(rr, Rrr, dst, Rdst, 0)

    def stage_a_tile(T):
        subs = ([0] if T == 0 else []) + [1 + 4 * T + c for c in range(4)]
        a_front(subs[0])
        for n, i in enumerate(subs):
            if i + 1 <= NT4 * 4:
                load_sub(i + 1)
            if n >= 1:
                a_cast(subs[n - 1])
            a_mid(i)
            if n >= 1:
                a_back(subs[n - 1])
            if n + 1 < len(subs):
                a_front(subs[n + 1])
        a_cast(subs[-1])
        a_back(subs[-1])

    wup_loaded = {}

    def load_wup(T, cc):
        sl = (T * 22 + cc) % 3
        if os.environ.get("P_NOLOAD") and T > 0:
            return
        C.dma("pool", wu[sl][0][:], wup[cc], writes=[Rwu[sl][0]])
        C.dma("pool", wu[sl][1][:], wup[22 + cc], writes=[Rwu[sl][1]])

    state = {"k": 0}

    def ffn_chunk(T, cc, which):
        ch = cc + 22 * which
        wsl = (T * 22 + cc) % 3
        k = 2 * (T * 22 + cc) + which
        hb = k % 4
        bank = (2, 3, 5, 6)[hb]
        xs = X1T[T % 2]
        if T == 0:
            C.pe(_mm_group(PS[4][:, 0:2], [(wu[wsl][which][:, kc, :], X1Th[:, kc, :]) for kc in range(8)]),
                 reads=[Rwu[wsl][which], RX1Th], writes=[RPS[4]])
            C.act(lambda e: e.activation(out=H[hb][:, 0:2], in_=PS[4][:, 0:2], func=AF.Copy, scale=flag_s[:, 0:1]),
                  reads=[RPS[4], Rflag], writes=[RHh[hb]])
        C.pe(_mm_group(PS[bank][:], [(wu[wsl][which][:, kc, :], xs[:, kc, :]) for kc in range(8)]),
             reads=[Rwu[wsl][which], RX1T[T % 2]], writes=[RPS[bank]])
        C.act(lambda e: e.copy(out=H[hb][:, 2:514], in_=PS[bank][:]), reads=[RPS[bank]], writes=[RH[hb]])
        C.pool(lambda e: e.tensor_copy(out=HALO[:, ch, :], in_=H[hb][:, 512:514]), reads=[RH[hb]], writes=[RHALOc[ch]])
        t, Rt = (tg[cc % 2], Rtg[cc % 2]) if which == 0 else (tv[cc % 2], Rtv[cc % 2])
        C.act(lambda e: e.activation(out=t[:], in_=H[hb][:, 0:512], func=AF.Identity, bias=cb_s[:, ch:ch + 1],
                                     scale=cw_s[:, ch, 0:1]), reads=[RH[hb], RHh[hb], Rcw, Rcb], writes=[Rt])
        C.dve(lambda e: e.scalar_tensor_tensor(out=t[:], in0=H[hb][:, 1:513], scalar=cw_s[:, ch, 1:2], in1=t[:],
                                               op0=ALU.mult, op1=ALU.add), reads=[RH[hb], RHh[hb], Rcw, Rt], writes=[Rt])
        C.dve(lambda e: e.scalar_tensor_tensor(out=t[:], in0=H[hb][:, 2:514], scalar=cw_s[:, ch, 2:3], in1=t[:],
                                               op0=ALU.mult, op1=ALU.add), reads=[RH[hb], Rcw, Rt], writes=[Rt])

    def halo_read(T, cc):
        if T == 0:
            return
        for which in range(2):
            ch = cc + 22 * which
            hb = (2 * (T * 22 + cc) + which) % 4
            C.pool(lambda e, ch=ch, hb=hb: e.tensor_copy(out=H[hb][:, 0:2], in_=HALO[:, ch, :]),
                   reads=[RHALOc[ch]], writes=[RHh[hb]])

    def ffn_pair(T, cc):
        if T * 22 + cc + 1 < NT4 * 22:
            nT1, ncc1 = divmod(T * 22 + cc + 1, 22)
            halo_read(nT1, ncc1)
        if T * 22 + cc + 2 < NT4 * 22:
            nT, ncc = divmod(T * 22 + cc + 2, 22)
            load_wup(nT, ncc)
        ffn_chunk(T, cc, 0)
        ffn_chunk(T, cc, 1)
        s2 = cc % 2
        C.act(lambda e: e.activation(out=sg[s2][:], in_=tg[s2][:], func=AF.Silu), reads=[Rtg[s2]], writes=[Rsg[s2]])
        C.pool(lambda e: e.tensor_tensor(out=A_T[:, cc, :], in0=sg[s2][:], in1=tv[s2][:], op=ALU.mult),
               reads=[Rsg[s2], Rtv[s2]], writes=[RAT])

    def stage_c(T, c):
        j = T * 4 + c
        banks = (5, 6) if j % 2 == 0 else (0, 1)
        for half in range(2):
            C.pe(_mm_group(PS[banks[half]][:], [(A_T[:, cc, c * 128:(c + 1) * 128], wd_b[:, cc, half * 512:(half + 1) * 512])
                                                 for cc in range(22)]), reads=[RAT, Rwd], writes=[RPS[banks[half]]])
        for half in range(2):
            C.dve(lambda e, half=half: e.scalar_tensor_tensor(out=rr[:, half * 512:(half + 1) * 512], in0=X1[:, c, half * 512:(half + 1) * 512],
                                                              scalar=ALPHA, in1=PS[banks[half]][:], op0=ALU.mult, op1=ALU.add),
                  reads=[RX1, RPS[banks[half]]], writes=[Rrr])
        o = j % 2
        layer_norm(rr, Rrr, x2[o][:], Rx2[o], 2)
        C.dma("sp", xo[j * 128:(j + 1) * 128, :], x2[o][:], reads=[Rx2[o]], key="xo%d" % o)

    load_sub(0)
    load_wup(0, 0)
    load_wup(0, 1)
    wd_v = wdn.rearrange("(cc p) n -> p cc n", p=128)
    for T in range(NT4):
        stage_a_tile(T)
        if T == 0:
            for c0 in range(0, 22, 2):
                C.dma("pool", wd_b[:, c0:c0 + 2, :], wd_v[:, c0:c0 + 2, :], writes=[Rwd], key="wd")
        for cc in range(22):
            ffn_pair(T, cc)
        for c in range(4):
            stage_c(T, c)
    return _finish_mixer(C, nc)


def _post_inputs(y_b, x_b, q, l, P):
    t0 = q * NTOK
    if q == 0:
        yh = np.zeros((128, D), ml_dtypes.bfloat16)
        xh = np.zeros((128, D), np.float32)
    else:
        yh = y_b[t0 - 128:t0]
        xh = x_b[t0 - 128:t0]
    rep = lambda v: np.broadcast_to(v[None, :], (128, v.shape[0]))
    key = ("post_w", l)
    if key not in _CACHE:
        w_up = P["w_up"][l]
        _CACHE[key] = {
            "wo": np.ascontiguousarray(P["w_o"][l]),
            "wup": np.ascontiguousarray(w_up.reshape(8, 128, NCH, 128).transpose(2, 1, 0, 3)),
            "wdn": np.ascontiguousarray(P["w_down"][l]),
            "convw": np.ascontiguousarray(P["conv_w"][l].reshape(3, NCH, 128).transpose(2, 1, 0)),
            "convb": np.ascontiguousarray(P["conv_b"][l].reshape(NCH, 128).T),
            "lnp": np.ascontiguousarray(np.stack([rep(P["ln1_g"][l]), rep(P["ln1_b"][l]), rep(P["ln2_g"][l]), rep(P["ln2_b"][l])]).astype(np.float32)),
            "ident": np.eye(128, dtype=np.float32),
        }
    m = dict(_CACHE[key])
    m["yin"] = np.ascontiguousarray(np.concatenate([yh, y_b[t0:t0 + NTOK]], axis=0))
    m["xin"] = np.ascontiguousarray(np.concatenate([xh, x_b[t0:t0 + NTOK]], axis=0))
    m["flag"] = np.full((128, 1), 0.0 if q == 0 else 1.0, np.float32)
    return m


def run_post(y, x, l, P):
    if "nc_p" not in _CACHE:
        _CACHE["nc_p"] = build_post()
    in_maps = [_post_inputs(y[c // 4], x[c // 4], c % 4, l, P) for c in range(8)]
    res = run_bass_kernel_spmd(_CACHE["nc_p"], in_maps, core_ids=list(range(8)))
    out = np.zeros((NB, S, D), np.float32)
    for c in range(8):
        out[c // 4, (c % 4) * NTOK:(c % 4 + 1) * NTOK] = res.results[c]["xo"]
    return out


def kernel(**inputs):
    P = {k: np.asarray(v) for k, v in inputs.items()}
    x = np.ascontiguousarray(P["x"], dtype=np.float32)
    for l in range(2):
        xT = [np.ascontiguousarray(x[b].T) for b in range(NB)]
        y = run_mixer(xT, l, P)
        x = run_post(y, x, l, P)
    return x
```

```python
import contextlib
import os
DBG = int(os.environ.get('MIX_DBG', '99'))
import numpy as np
import ml_dtypes
import concourse.bass as bass
import concourse.mybir as mybir
from concourse.bass_utils import run_bass_kernel_spmd

F32 = mybir.dt.float32
BF16 = mybir.dt.bfloat16
AF = mybir.ActivationFunctionType
ALU = mybir.AluOpType
AX = mybir.AxisListType

S = 8192
D = 1024
NB = 2
DFF = 2816
ALPHA = 4.0 ** 0.25
EPS = 1e-5
BIGM = 30000.0
NEGF = -1.0e30


class Res:
    __slots__ = ("name", "w", "r", "excl")

    def __init__(self, name, excl=False):
        self.name = name
        self.w = None
        self.r = []
        self.excl = excl


class Sched:
    STREAMS = ("pe", "act", "dve", "pool", "sp")

    def __init__(self, nc):
        self.nc = nc
        self.ops = {s: [] for s in self.STREAMS}
        self.ccount = {s: 0 for s in self.STREAMS}
        self.dcount = {}
        self.known = {s: {} for s in self.STREAMS}
        self.final_events = []

    def _need(self, stream, ev, waits):
        if ev is None:
            return
        sem, val, src = ev
        if src == stream and src == "pe":
            return
        if self.known[stream].get(sem, 0) >= val:
            return
        self.known[stream][sem] = val
        waits.append((sem, val))

    def op(self, stream, fn, reads=(), writes=(), dma_key=None):
        ex = [r for r in reads if r.excl]
        if ex:
            reads = [r for r in reads if not r.excl]
            writes = list(writes) + [r for r in ex if r not in writes]
        waits = []
        for r in reads:
            self._need(stream, r.w, waits)
        for w in writes:
            self._need(stream, w.w, waits)
            for e in w.r:
                self._need(stream, e, waits)
        if dma_key is not None:
            k = "d_" + dma_key
            self.dcount[k] = self.dcount.get(k, 0) + 1
            ev = (k, 16 * self.dcount[k], "dma")
            sig = (k, 16)
        else:
            self.ccount[stream] += 1
            ev = ("c_" + stream, self.ccount[stream], stream)
            sig = ("c_" + stream, 1)
        for r in reads:
            r.r.append(ev)
        for w in writes:
            w.w = ev
            w.r = []
        self.ops[stream].append((waits, fn, sig))
        return ev

    def emit(self, final_events):
        nc = self.nc
        names = set()
        for s in self.STREAMS:
            for waits, fn, sig in self.ops[s]:
                names.add(sig[0])
                for (sem, val) in waits:
                    names.add(sem)
        with contextlib.ExitStack() as st:
            sems = {n: st.enter_context(nc.semaphore(n)) for n in sorted(names)}
            block = st.enter_context(nc.Block())

            def make(stream):
                def body(eng):
                    for waits, fn, sig in self.ops[stream]:
                        for (sem, val) in waits:
                            eng.wait_ge(sems[sem], val)
                        ins = fn(eng)
                        ins.then_inc(sems[sig[0]], sig[1])
                    if stream == "sp":
                        for (sem, val, src) in final_events:
                            eng.wait_ge(sems[sem], val)
                return body

            block.tensor(make("pe"))
            block.scalar(make("act"))
            block.vector(make("dve"))
            block.gpsimd(make("pool"))
            block.sync(make("sp"))


class Ctx:
    def __init__(self, nc):
        self.nc = nc
        self.st = contextlib.ExitStack()
        self.S = Sched(nc)
        self.nres = 0

    def sb(self, name, shape, dt):
        return self.st.enter_context(self.nc.sbuf_tensor(name, shape, dt))

    def ps(self, name, shape, dt):
        return self.st.enter_context(self.nc.psum_tensor(name, shape, dt))

    def res(self, name="r", excl=False):
        self.nres += 1
        return Res("%s%d" % (name, self.nres), excl)

    def pe(self, fn, reads=(), writes=()):
        return self.S.op("pe", fn, reads, writes)

    def act(self, fn, reads=(), writes=()):
        return self.S.op("act", fn, reads, writes)

    def dve(self, fn, reads=(), writes=()):
        return self.S.op("dve", fn, reads, writes)

    def pool(self, fn, reads=(), writes=()):
        return self.S.op("pool", fn, reads, writes)

    def dma(self, queue, out, in_, reads=(), writes=(), key=None):
        if key is None:
            key = (writes[0].name if writes else reads[0].name)
        return self.S.op(queue, lambda e: e.dma_start(out=out, in_=in_), reads, writes, dma_key=key)


def _mm_group(out, pairs):
    def fn(e):
        n = len(pairs)
        ins = None
        for i, (l, r) in enumerate(pairs):
            ins = e.matmul(out, lhsT=l, rhs=r, start=(i == 0), stop=(i == n - 1))
        return ins
    return fn


NFM = 400
NTM = 257


def build_mixer(SEQ=S, PH=9):
    NT, NTT = SEQ // 512, SEQ // 128
    nc = bass.Bass("TRN2", target_bir_lowering=False)
    C = Ctx(nc)
    dt = nc.dram_tensor
    xT = dt("xT", [D, SEQ], F32, kind="ExternalInput").ap()
    wfm = dt("wfm", [D, NFM], F32, kind="ExternalInput").ap()
    wtm = dt("wtm", [D, NTM], F32, kind="ExternalInput").ap()
    cs = dt("cs", [2, 16, SEQ], F32, kind="ExternalInput").ap()
    onehot = dt("onehot", [32, SEQ], BF16, kind="ExternalInput").ap()
    sguT = dt("sguT", [128, 128], F32, kind="ExternalInput").ap()
    tri = dt("tri", [128, 128], F32, kind="ExternalInput").ap()
    ident = dt("ident", [128, 128], F32, kind="ExternalInput").ap()
    sgub = dt("sgub", [128, 1], F32, kind="ExternalInput").ap()
    lng = dt("lng", [128, 64], F32, kind="ExternalInput").ap()
    lnb = dt("lnb", [128, 64], F32, kind="ExternalInput").ap()
    poolw = dt("poolw", [64, 64], F32, kind="ExternalInput").ap()
    pscale = dt("pscale", [128, 64], F32, kind="ExternalInput").ap()
    band = dt("band", [3, 128, 128], F32, kind="ExternalInput").ap()
    bfor = dt("bfor", [128, 1], F32, kind="ExternalInput").ap()
    y = dt("y", [SEQ, 256], BF16, kind="ExternalOutput").ap()

    sb, ps, res = C.sb, C.ps, C.res
    QA = sb("QA", [128, SEQ], BF16)
    KA = sb("KA", [128, SEQ], BF16)
    QB = sb("QB", [128, SEQ], BF16)
    KB = sb("KB", [128, SEQ], BF16)
    VA = sb("VA", [128, NTT, 65], BF16)
    VB = sb("VB", [128, NTT, 65], BF16)
    xb = [sb("xb%d" % i, [128, 8, 512], BF16) for i in range(2)]
    wfm_b = sb("wfm_b", [128, 8, NFM], BF16)
    wtm_b = sb("wtm_b", [128, 8, NTM], BF16)
    cst = [sb("cst%d" % i, [80, 2, 512], F32) for i in range(2)]
    t1 = [sb("t1_%d" % i, [80, 512], F32) for i in range(2)]
    t2 = [sb("t2_%d" % i, [80, 512], F32) for i in range(2)]
    pTb = [sb("pTb%d" % i, [64, 512], BF16) for i in range(2)]
    ug = [sb("ug%d" % i, [128, 128], F32) for i in range(2)]
    stats = [sb("stats%d" % i, [128, 6], F32) for i in range(2)]
    mv = [sb("mv%d" % i, [128, 2], F32) for i in range(2)]
    rstd = [sb("rstd%d" % i, [128, 1], F32) for i in range(2)]
    vn = [sb("vn%d" % i, [128, 64], F32) for i in range(2)]
    vnb = [sb("vnb%d" % i, [128, 64], BF16) for i in range(2)]
    pwb = [sb("pwb%d" % i, [128, 64], BF16) for i in range(3)]
    YC = [sb("YC%d" % i, [128, 4, 64], BF16) for i in range(2)]
    YD = [sb("YD%d" % i, [128, 4, 64], BF16) for i in range(2)]
    UG = sb("UG", [128, NTT, 128], BF16)
    MV = sb("MV", [128, NTT, 2], F32)
    VE = sb("VE", [128, NTT], F32)
    RSTD = sb("RSTD", [128, NTT], F32)
    YAB = [sb("YAB%d" % i, [128, 4, 128], BF16) for i in range(2)]
    MBZ = sb("MBZ", [128, 64 + NTT * 32], F32)
    GM = sb("GM", [128, 32], F32)
    top8 = [sb("top8_%d" % i, [128, 8], F32) for i in range(2)]
    FRAW = sb("FRAW", [128, NTT], F32)
    LF = sb("LF", [128, NTT], F32)
    TOT = sb("TOT", [128, NTT], F32)
    PREF = sb("PREF", [128, NTT], F32)
    CP = sb("CP", [128, NTT], F32)
    Z = sb("Z", [128, 128], F32)
    kbar = sb("kbar", [64, 32], F32)
    kbar_b = sb("kbar_b", [64, 32], BF16)
    PT = [sb("PT%d" % i, [128, 512], BF16) for i in range(5)]
    rl = [sb("rl%d" % i, [128, 1], F32) for i in range(4)]
    ident_f = sb("ident_f", [128, 128], F32)
    tri_f = sb("tri_f", [128, 128], F32)
    tri_b = sb("tri_b", [128, 128], BF16)
    ones_f = sb("ones_f", [128, 128], F32)
    sgu_f = sb("sgu_f", [128, 128], F32)
    wmT_b = sb("wmT_b", [128, 128], BF16)
    band_b = sb("band_b", [128, 3, 128], BF16)
    sgub_s = sb("sgub_s", [128, 1], F32)
    lng_s = sb("lng_s", [128, 64], F32)
    lnb_s = sb("lnb_s", [128, 64], F32)
    poolw_b = sb("poolw_b", [64, 64], BF16)
    pscale_s = sb("pscale_s", [128, 64], F32)
    bfor_s = sb("bfor_s", [128, 1], F32)
    negb = sb("negb", [128, 1], F32)
    ef = sb("ef", [128, NTT], F32)
    PS = [ps("ps%d" % i, [128, 512], F32) for i in range(8)]
    RPS = [res("ps", True) for _ in range(8)]

    R = lambda n: res(n)
    RQA = [R("qa") for _ in range(NT)]
    RQAm = [R("qam") for _ in range(NT)]
    RKA = [R("ka") for _ in range(NT)]
    RQB = [R("qb") for _ in range(NT)]
    RQBc = [R("qbc") for _ in range(NT)]
    RKB = [R("kb") for _ in range(NT)]
    RVA = [R("va") for _ in range(NT)]
    RVB = [R("vb") for _ in range(NT)]
    Rxb = [R("xb") for _ in range(2)]
    Rcs = [R("cs") for _ in range(2)]
    Rt1 = [R("t1") for _ in range(2)]
    Rt2 = [R("t2") for _ in range(2)]
    Rt1k = [R("t1k") for _ in range(2)]
    Rt2k = [R("t2k") for _ in range(2)]
    RpT = [R("pT") for _ in range(2)]
    Rug = [R("ug") for _ in range(2)]
    Rst = [R("st") for _ in range(2)]
    Rmv = [R("mv") for _ in range(2)]
    Rrs = [R("rs") for _ in range(2)]
    Rvn = [R("vn") for _ in range(2)]
    Rvnb = [R("vnb") for _ in range(2)]
    Rpwb = [R("pwb") for _ in range(3)]
    RYC = [R("yc") for _ in range(2)]
    RYD = [R("yd") for _ in range(2)]
    RUG = [R("ugall") for _ in range(NT)]
    RMV, RVE, RRSTD = R("mvall"), R("ve"), R("rstdall")
    RYAB = [R("yab") for _ in range(2)]
    RMBZ = [R("mbz") for _ in range(NT)]
    RGM = R("gm")
    Rtop = [R("top") for _ in range(2)]
    RFRAW, RLF, RTOT, RPREF, RCP, RZ, Rkbar, Rkbarb, Ref = (R("fraw"), R("lf"), R("tot"), R("pref"),
                                                            R("cp"), R("z"), R("kbar"), R("kbarb"), R("ef"))
    RPT = [R("pt") for _ in range(5)]
    Rrl = [R("rl") for _ in range(4)]
    Rc = {k: R(k) for k in ["wfm", "wtm", "ident", "tri_f", "tri_b", "ones", "sgu_f", "wmT", "band", "sgub",
                            "lng", "lnb", "poolw", "pscale", "bfor", "negb", "kaoh", "misc"]}

    C.dma("pool", wfm_b[:], wfm.rearrange("(kc f) n -> f kc n", f=128), writes=[Rc["wfm"]])
    C.dma("pool", wtm_b[:], wtm.rearrange("(kc f) n -> f kc n", f=128), writes=[Rc["wtm"]])
    C.dma("sp", ident_f[:], ident, writes=[Rc["ident"]])
    C.dma("sp", tri_f[:], tri, writes=[Rc["tri_f"]])
    C.dma("pool", tri_b[:], tri, writes=[Rc["tri_b"]])
    C.dma("sp", sgu_f[:], sguT, writes=[Rc["sgu_f"]])
    C.dma("pool", band_b[:], band.rearrange("k s t -> s k t"), writes=[Rc["band"]])
    C.dma("sp", sgub_s[:], sgub, writes=[Rc["sgub"]])
    C.dma("sp", lng_s[:], lng, writes=[Rc["lng"]])
    C.dma("sp", lnb_s[:], lnb, writes=[Rc["lnb"]])
    C.dma("pool", poolw_b[:], poolw, writes=[Rc["poolw"]])
    C.dma("sp", pscale_s[:], pscale, writes=[Rc["pscale"]])
    C.dma("sp", bfor_s[:], bfor, writes=[Rc["bfor"]])
    C.dma("sp", KA[64:96, :], onehot, writes=[Rc["kaoh"]])
    C.dve(lambda e: e.tensor_tensor(out=wmT_b[:], in0=sgu_f[:], in1=tri_f[:], op=ALU.mult),
          reads=[Rc["sgu_f"], Rc["tri_f"]], writes=[Rc["wmT"]])
    C.dve(lambda e: e.tensor_scalar(out=negb[:], in0=bfor_s[:], scalar1=-1.0, scalar2=None, op0=ALU.mult),
          reads=[Rc["bfor"]], writes=[Rc["negb"]])
    C.dve(lambda e: e.memset(ones_f[:], 1.0), writes=[Rc["ones"]])
    C.dve(lambda e: e.memset(VA[:, :, 64:65], 1.0), writes=RVA)
    C.dve(lambda e: e.memset(VB[:, :, 64:65], 1.0), writes=RVB)
    C.dve(lambda e: e.memset(KB[64:65, :], 1.0), writes=RKB)
    C.dve(lambda e: e.memset(MBZ[:], 0.0), writes=RMBZ)
    C.dve(lambda e: e.memset(GM[:], NEGF), writes=[RGM])
    C.dve(lambda e: e.memset(Z[:], 0.0), writes=[RZ])
    C.dve(lambda e: e.memset(kbar[:], 0.0), writes=[Rkbar])
    C.dve(lambda e: e.memset(PREF[:, 0:1], 0.0), writes=[RPREF])

    if PH == 0:
        return _finish_mixer(C, nc)
    xT_v = xT.rearrange("(kc f) t -> f kc t", f=128)

    def load_tile(T):
        sl = T % 2
        C.dma("pool", xb[sl][:], xT_v[:, :, T * 512:(T + 1) * 512], writes=[Rxb[sl]])
        C.dma("sp", cst[sl][0:16], cs[:, :, T * 512:(T + 1) * 512].rearrange("k p t -> p k t"), writes=[Rcs[sl]], key="cs%d" % sl)
        C.dma("sp", cst[sl][64:80], cs[:, :, T * 512:(T + 1) * 512].rearrange("k p t -> p k t"), writes=[Rcs[sl]], key="cs%d" % sl)

    load_tile(0)
    fm_cols = {"qkA": (0, 128), "qkB": (128, 128), "pD": (256, 64), "qkAp": (320, 80)}
    pend = []

    def p1_tile(T):
        sl = T % 2
        if T + 1 < NT:
            load_tile(T + 1)
        cols = slice(T * 512, (T + 1) * 512)

        def fm(name, bank):
            c0, n = fm_cols[name]
            C.pe(_mm_group(PS[bank][0:n, :], [(wfm_b[:, kc, c0:c0 + n], xb[sl][:, kc, :]) for kc in range(8)]),
                 reads=[Rc["wfm"], Rxb[sl]], writes=[RPS[bank]])

        fm("qkA", 0)
        fm("qkAp", 1)
        C.act(lambda e: e.copy(out=QA[0:64, cols], in_=PS[0][0:64, :]), reads=[RPS[0]], writes=[RQA[T]])
        C.act(lambda e: e.copy(out=KA[0:64, cols], in_=PS[0][64:128, :]), reads=[RPS[0]], writes=[RKA[T]])
        C.dve(lambda e: e.tensor_tensor(out=t1[sl][0:16], in0=PS[0][0:16, :], in1=cst[sl][0:16, 0, :], op=ALU.mult),
              reads=[RPS[0], Rcs[sl]], writes=[Rt1[sl]])
        C.dve(lambda e: e.tensor_tensor(out=t1[sl][64:80], in0=PS[0][64:80, :], in1=cst[sl][64:80, 0, :], op=ALU.mult),
              reads=[RPS[0], Rcs[sl]], writes=[Rt1k[sl]])
        C.dve(lambda e: e.tensor_tensor(out=t2[sl][0:16], in0=PS[1][0:16, :], in1=cst[sl][0:16, 1, :], op=ALU.mult),
              reads=[RPS[1], Rcs[sl]], writes=[Rt2[sl]])
        C.dve(lambda e: e.tensor_tensor(out=t2[sl][64:80], in0=PS[1][64:80, :], in1=cst[sl][64:80, 1, :], op=ALU.mult),
              reads=[RPS[1], Rcs[sl]], writes=[Rt2k[sl]])
        C.dve(lambda e: e.tensor_tensor(out=QA[0:16, cols], in0=t1[sl][0:16], in1=t2[sl][0:16], op=ALU.add),
              reads=[Rt1[sl], Rt2[sl]], writes=[RQA[T]])
        C.dve(lambda e: e.tensor_tensor(out=KA[0:16, cols], in0=t1[sl][64:80], in1=t2[sl][64:80], op=ALU.add),
              reads=[Rt1k[sl], Rt2k[sl]], writes=[RKA[T]])
        C.dve(lambda e: e.tensor_reduce(out=kbar[:, 2 * T:2 * T + 2],
                                        in_=KA[0:64, cols].rearrange("p (b j) -> p b j", j=256),
                                        axis=AX.X, op=ALU.add), reads=[RKA[T]], writes=[Rkbar])
        fm("qkB", 4)
        C.act(lambda e: e.copy(out=QB[0:64, cols], in_=PS[4][0:64, :]), reads=[RPS[4]], writes=[RQB[T]])
        C.act(lambda e: e.copy(out=KB[0:64, cols], in_=PS[4][64:128, :]), reads=[RPS[4]], writes=[RKB[T]])
        fm("pD", 7)
        C.act(lambda e: e.copy(out=pTb[sl][:], in_=PS[7][0:64, :]), reads=[RPS[7]], writes=[RpT[sl]])
        if DBG < 4:
            return
        def sub(c):
            tt = 4 * T + c
            s2 = tt % 2
            bk = 2 + s2
            C.pe(_mm_group(PS[bk][:, 0:NTM], [(xb[sl][:, kc, c * 128:(c + 1) * 128], wtm_b[:, kc, :]) for kc in range(8)]),
                 reads=[Rc["wtm"], Rxb[sl]], writes=[RPS[bk]])
            C.act(lambda e, bk=bk, tt=tt: e.copy(out=VA[:, tt, 0:64], in_=PS[bk][:, 0:64]), reads=[RPS[bk]], writes=[RVA[T]])
            C.act(lambda e, bk=bk, tt=tt: e.copy(out=VB[:, tt, 0:64], in_=PS[bk][:, 64:128]), reads=[RPS[bk]], writes=[RVB[T]])
            C.dve(lambda e, bk=bk, tt=tt: e.tensor_copy(out=FRAW[:, tt:tt + 1], in_=PS[bk][:, 256:257]),
                  reads=[RPS[bk]], writes=[RFRAW])
            C.act(lambda e, bk=bk, tt=tt: e.activation(out=UG[:, tt, :], in_=PS[bk][:, 128:256], func=AF.Gelu),
                  reads=[RPS[bk]], writes=[RUG[T]])
            C.dve(lambda e, tt=tt, s2=s2: e.bn_stats(out=stats[s2][:], in_=UG[:, tt, 64:128]), reads=[RUG[T]], writes=[Rst[s2]])
            C.dve(lambda e, tt=tt, s2=s2: e.bn_aggr(out=MV[:, tt, :], in_=stats[s2][:]), reads=[Rst[s2]], writes=[RMV])
            s3 = tt % 3
            C.pe(lambda e, c=c: e.matmul(PS[5][:, 0:64], lhsT=pTb[sl][:, c * 128:(c + 1) * 128], rhs=poolw_b[:],
                                         start=True, stop=True), reads=[RpT[sl], Rc["poolw"]], writes=[RPS[5]])
            C.act(lambda e, s3=s3: e.copy(out=pwb[s3][:], in_=PS[5][:, 0:64]), reads=[RPS[5]], writes=[Rpwb[s3]])
            flush_band()
            pend.append((T, c))

        def flush_band():
            while pend:
                bT, bc = pend.pop(0)
                btt = 4 * bT + bc
                bs3, bp3, bsl = btt % 3, (btt - 1) % 3, bT % 2
                if btt == 0:
                    C.pe(lambda e, bs3=bs3: e.matmul(PS[6][:, 0:64], lhsT=band_b[:, 2, :], rhs=pwb[bs3][:], start=True, stop=True),
                         reads=[Rc["band"], Rpwb[bs3]], writes=[RPS[6]])
                else:
                    C.pe(_mm_group(PS[6][:, 0:64], [(band_b[:, 0, :], pwb[bs3][:]), (band_b[:, 1, :], pwb[bp3][:])]),
                         reads=[Rc["band"], Rpwb[bs3], Rpwb[bp3]], writes=[RPS[6]])
                C.dve(lambda e, bc=bc, bsl=bsl: e.tensor_tensor(out=YD[bsl][:, bc, :], in0=PS[6][:, 0:64], in1=pscale_s[:], op=ALU.mult),
                      reads=[RPS[6], Rc["pscale"]], writes=[RYD[bsl]])
        for c in range(4):
            sub(c)
        flush_band()
        C.dma("sp", y[T * 512:(T + 1) * 512, 192:256].rearrange("(c p) n -> p c n", p=128), YD[sl][:],
              reads=[RYD[sl]], key="yd%d" % sl)

    for T in range(NT):
        p1_tile(T)

    if PH == 1:
        return _finish_mixer(C, nc)
    C.dve(lambda e: e.tensor_scalar(out=VE[:], in0=MV[:, :, 1], scalar1=EPS, scalar2=None, op0=ALU.add),
          reads=[RMV], writes=[RVE])
    C.act(lambda e: e.activation(out=VE[:], in_=VE[:], func=AF.Sqrt), reads=[RVE], writes=[RVE])
    C.dve(lambda e: e.reciprocal(out=RSTD[:], in_=VE[:]), reads=[RVE], writes=[RRSTD])

    vn4 = [sb("vn4_%d" % i, [128, 64], F32) for i in range(4)]
    vnb4 = [sb("vnb4_%d" % i, [128, 64], BF16) for i in range(4)]
    Rvn4 = [R("vn4") for _ in range(4)]
    Rvnb4 = [R("vnb4") for _ in range(4)]

    def c_stage(T, stage):
        sl = T % 2
        if stage == 0:
            for c in range(4):
                tt = 4 * T + c
                C.dve(lambda e, c=c, tt=tt: e.tensor_scalar(out=vn4[c][:], in0=UG[:, tt, 64:128], scalar1=MV[:, tt, 0:1],
                                                            scalar2=RSTD[:, tt:tt + 1], op0=ALU.subtract, op1=ALU.mult),
                      reads=[RUG[T], RMV, RRSTD], writes=[Rvn4[c]])
                C.dve(lambda e, c=c: e.tensor_tensor(out=vn4[c][:], in0=vn4[c][:], in1=lng_s[:], op=ALU.mult),
                      reads=[Rvn4[c], Rc["lng"]], writes=[Rvn4[c]])
                C.dve(lambda e, c=c: e.tensor_tensor(out=vnb4[c][:], in0=vn4[c][:], in1=lnb_s[:], op=ALU.add),
                      reads=[Rvn4[c], Rc["lnb"]], writes=[Rvnb4[c]])
        elif stage == 1:
            def mix(e):
                ins = None
                for c in range(4):
                    ins = e.matmul(PS[5][:, c * 64:(c + 1) * 64], lhsT=wmT_b[:], rhs=vnb4[c][:], start=True, stop=True)
                return ins
            C.pe(mix, reads=[Rc["wmT"]] + Rvnb4, writes=[RPS[5]])
        else:
            for c in range(4):
                tt = 4 * T + c
                C.dve(lambda e, c=c, tt=tt: e.scalar_tensor_tensor(out=YC[sl][:, c, :], in0=PS[5][:, c * 64:(c + 1) * 64],
                                                                   scalar=sgub_s[:, 0:1], in1=UG[:, tt, 0:64],
                                                                   op0=ALU.add, op1=ALU.mult),
                      reads=[RPS[5], Rc["sgub"], RUG[T]], writes=[RYC[sl]])
            C.dma("sp", y[T * 512:(T + 1) * 512, 128:192].rearrange("(c p) n -> p c n", p=128), YC[sl][:],
                  reads=[RYC[sl]], key="yc%d" % sl)

    def c_tile(T):
        for s in range(3):
            c_stage(T, s)

    if PH < 4:
        for T in range(NT):
            c_tile(T)

    if PH == 2:
        return _finish_mixer(C, nc)
    C.act(lambda e: e.activation(out=ef[:], in_=FRAW[:], func=AF.Exp, bias=negb[:, 0:1], scale=-1.0),
          reads=[RFRAW, Rc["negb"]], writes=[Ref])
    C.act(lambda e: e.activation(out=LF[:], in_=ef[:], func=AF.Ln, bias=1.0, scale=1.0), reads=[Ref], writes=[RLF])
    C.pe(lambda e: e.matmul(PS[0][:, 0:NTT], lhsT=ones_f[:], rhs=LF[:], start=True, stop=True),
         reads=[Rc["ones"], RLF], writes=[RPS[0]])
    C.dve(lambda e: e.tensor_copy(out=TOT[:], in_=PS[0][:, 0:NTT]), reads=[RPS[0]], writes=[RTOT])
    for j in range(1, NTT):
        C.dve(lambda e, j=j: e.tensor_tensor(out=PREF[:, j:j + 1], in0=PREF[:, j - 1:j], in1=TOT[:, j - 1:j], op=ALU.add),
              reads=[RPREF, RTOT], writes=[RPREF])
    C.pe(lambda e: e.matmul(PS[1][:, 0:NTT], lhsT=tri_f[:], rhs=LF[:], start=True, stop=True),
         reads=[Rc["tri_f"], RLF], writes=[RPS[1]])
    C.dve(lambda e: e.tensor_tensor(out=CP[:], in0=PS[1][:, 0:NTT], in1=PREF[:], op=ALU.add),
          reads=[RPS[1], RPREF], writes=[RCP])
    C.dve(lambda e: e.tensor_scalar(out=Z[:, 64:64 + NTT], in0=CP[:], scalar1=-8.0, scalar2=None, op0=ALU.mult),
          reads=[RCP], writes=[RZ])
    C.dve(lambda e: e.tensor_scalar(out=kbar_b[:], in0=kbar[:], scalar1=1.0 / 256.0, scalar2=None, op0=ALU.mult),
          reads=[Rkbar], writes=[Rkbarb])
    GM4 = [sb("GM4_%d" % i, [128, 32], F32) for i in range(4)]
    top4 = [sb("top4_%d" % i, [128, 8], F32) for i in range(4)]
    RGM4 = [R("gm4") for _ in range(4)]
    Rtop4 = [R("top4") for _ in range(4)]
    for i in range(4):
        C.dve(lambda e, i=i: e.memset(GM4[i][:], NEGF), writes=[RGM4[i]])

    def p2_stage(T, stage):
        if stage == 0:
            def tr4(e):
                ins = None
                for c in range(4):
                    tt = 4 * T + c
                    ins = e.matmul(PS[5][0:65, c * 128:(c + 1) * 128], lhsT=Z[:, tt:tt + 65], rhs=ident_f[:],
                                   start=True, stop=True)
                return ins
            C.pe(tr4, reads=[RZ, Rc["ident"]], writes=[RPS[5]])

            def gates(e):
                ins = None
                for c in range(4):
                    tt = 4 * T + c
                    ins = e.matmul(PS[6][:, c * 32:(c + 1) * 32], lhsT=QA[0:64, tt * 128:(tt + 1) * 128], rhs=kbar_b[:],
                                   start=True, stop=True)
                return ins
            C.pe(gates, reads=[RQA[T], Rkbarb], writes=[RPS[6]])
        elif stage == 1:
            C.act(lambda e: e.copy(out=QB[64:65, T * 512:(T + 1) * 512], in_=PS[5][64:65, :]),
                  reads=[RPS[5]], writes=[RQBc[T]])
            for c in range(4):
                tt = 4 * T + c
                b = tt // 2
                if b == 0:
                    continue
                C.dve(lambda e, c=c, b=b: e.tensor_copy(out=GM4[c][:, 0:b], in_=PS[6][:, c * 32:c * 32 + b]),
                      reads=[RPS[6]], writes=[RGM4[c]])
                C.dve(lambda e, c=c: e.max(out=top4[c][:], in_=GM4[c][:]), reads=[RGM4[c]], writes=[Rtop4[c]])
                C.dve(lambda e, c=c, tt=tt, b=b: e.tensor_scalar(out=MBZ[:, 64 + 32 * tt:64 + 32 * tt + b], in0=GM4[c][:, 0:b],
                                                                 scalar1=top4[c][:, 2:3], scalar2=1.0,
                                                                 op0=ALU.is_ge, op1=ALU.subtract),
                      reads=[RGM4[c], Rtop4[c]], writes=[RMBZ[T]])
        elif stage == 2:
            def trm(e):
                ins = None
                for c in range(4):
                    tt = 4 * T + c
                    ins = e.matmul(PS[7][0:96, c * 128:(c + 1) * 128], lhsT=MBZ[:, 32 * tt:32 * tt + 96], rhs=ident_f[:],
                                   start=True, stop=True)
                return ins
            C.pe(trm, reads=[RMBZ[T], Rc["ident"]], writes=[RPS[7]])
        else:
            C.act(lambda e: e.copy(out=QA[64:96, T * 512:(T + 1) * 512], in_=PS[7][64:96, :]),
                  reads=[RPS[7]], writes=[RQAm[T]])

    def p2_tile(T):
        for s in range(4):
            p2_stage(T, s)

    if PH < 4:
        for T in range(NT):
            p2_tile(T)
    else:
        p2_tile(0)

    if PH == 3:
        return _finish_mixer(C, nc)
    ROacc = [[RPS[3]] * 4, [RPS[4]] * 4]
    LOOK = 3

    def att_score(p):
        qi, att, kj, oi, idx = p
        Q, K = (QA, KA) if att == 0 else (QB, KB)
        RQ, RQx, RK = (RQA, RQAm, RKA) if att == 0 else (RQB, RQBc, RKB)
        rows = 96 if att == 0 else 65
        d = kj - 4 * qi
        off = 128 * max(d, 0)
        n = 512 - off
        sbk = idx % 3
        pt = idx % 5
        C.pe(lambda e: e.matmul(PS[sbk][:, 0:n], lhsT=K[0:rows, kj * 128:(kj + 1) * 128],
                                rhs=Q[0:rows, qi * 512 + off:(qi + 1) * 512], start=True, stop=True),
             reads=[RQ[qi], RQx[qi], RK[kj // 4]] + ([Rc["kaoh"]] if att == 0 else []), writes=[RPS[sbk]])
        if att == 0:
            C.act(lambda e: e.activation(out=PT[pt][:, 0:n], in_=PS[sbk][:, 0:n], func=AF.Exp, scale=0.125),
                  reads=[RPS[sbk]], writes=[RPT[pt]])
        else:
            C.act(lambda e: e.activation(out=PT[pt][:, 0:n], in_=PS[sbk][:, 0:n], func=AF.Exp,
                                         bias=CP[:, kj:kj + 1], scale=0.125),
                  reads=[RPS[sbk], RCP], writes=[RPT[pt]])
        if d >= 0:
            C.dve(lambda e: e.tensor_tensor(out=PT[pt][:, 0:128], in0=PT[pt][:, 0:128], in1=tri_b[:], op=ALU.mult),
                  reads=[RPT[pt], Rc["tri_b"]], writes=[RPT[pt]])

    def att_pv(p):
        qi, att, kj, oi, idx = p
        V = VA if att == 0 else VB
        RV = RVA if att == 0 else RVB
        ysl = qi % 2
        ob = 3 + (oi % 2)
        Oacc = PS[ob][:, 0:260].rearrange("p (c n) -> p c n", n=65)
        RO = ROacc[ob - 3]
        d = kj - 4 * qi
        off = 128 * max(d, 0)
        pt = idx % 5
        c0 = max(d, 0)

        def pv(e):
            ins = None
            for c in range(c0, 4):
                ins = e.matmul(Oacc[:, c, :], lhsT=PT[pt][:, c * 128 - off:(c + 1) * 128 - off], rhs=V[:, kj, :],
                               start=(kj == 0 and c == 0), stop=(kj == 4 * qi + c), skip_group_check=True)
            return ins
        C.pe(pv, reads=[RPT[pt], RV[kj // 4]], writes=[RO[0]])
        if d >= 0:
            c = d
            r4 = (2 * (oi % 2) + (c % 2))
            C.dve(lambda e: e.reciprocal(out=rl[r4][:], in_=Oacc[:, c, 64:65]), reads=[RO[c]], writes=[Rrl[r4]])
            C.dve(lambda e: e.tensor_scalar(out=YAB[ysl][:, c, att * 64:(att + 1) * 64], in0=Oacc[:, c, 0:64],
                                            scalar1=rl[r4][:, 0:1], scalar2=None, op0=ALU.mult),
                  reads=[RO[c], Rrl[r4]], writes=[RYAB[ysl]])
        if att == 1 and d == 3:
            C.dma("sp", y[qi * 512:(qi + 1) * 512, 0:128].rearrange("(c p) n -> p c n", p=128), YAB[ysl][:],
                  reads=[RYAB[ysl]], key="yab%d" % ysl)

    plist = []
    oi = 0
    for qi in range(NT):
        for att in range(2):
            for kj in range(4 * qi + 4):
                plist.append((qi, att, kj, oi, len(plist)))
            oi += 1
    for i in range(len(plist) + LOOK):
        if i < len(plist):
            qi_, att_, kj_ = plist[i][0], plist[i][1], plist[i][2]
            n_ = 4 * qi_ + 4
            if att_ == 0 and qi_ + 1 < NT:
                offs = [0, n_ // 4, n_ // 2, (3 * n_) // 4]
                if kj_ in offs:
                    p2_stage(qi_ + 1, offs.index(kj_))
            if att_ == 1:
                offs = [0, n_ // 3, (2 * n_) // 3]
                if kj_ in offs:
                    c_stage(qi_, offs.index(kj_))
            att_score(plist[i])
        if i - LOOK >= 0:
            att_pv(plist[i - LOOK])

    if os.environ.get("MIX_DUMP"):
        allres = RQA + RQAm + RKA + RQB + RQBc + RKB + RVA + RVB + RUG + [RCP, RLF, RFRAW, RMV, RRSTD, Rkbar, Rkbarb, RZ, Rc["kaoh"]] + RMBZ + Rpwb + [Rc["band"], Rc["tri_b"], Rc["poolw"]]
        for nm, t, shp, dty in (("d_cp", CP, [128, NTT], F32), ("d_lf", LF, [128, NTT], F32), ("d_fraw", FRAW, [128, NTT], F32),
                                ("d_rstd", RSTD, [128, NTT], F32),
                                ("d_ug", UG, [128, NTT, 128], BF16), ("d_qa", QA, [128, SEQ], BF16), ("d_ka", KA, [128, SEQ], BF16),
                                ("d_qb", QB, [128, SEQ], BF16), ("d_kb", KB, [128, SEQ], BF16), ("d_va", VA, [128, NTT, 65], BF16),
                                ("d_vb", VB, [128, NTT, 65], BF16), ("d_kbar", kbar, [64, 32], F32), ("d_mbz", MBZ, [128, 64 + NTT * 32], F32),
                                ("d_pwb0", pwb[0], [128, 64], BF16), ("d_band", band_b, [128, 3, 128], BF16), ("d_trib", tri_b, [128, 128], BF16)):
            dd = dt(nm, shp, dty, kind="ExternalOutput").ap()
            C.dma("sp", dd, t[:], reads=allres, key=nm)

    return _finish_mixer(C, nc)


def _finish_mixer(C, nc):
    finals = []
    for k, cnt in C.S.dcount.items():
        finals.append((k, 16 * cnt, "dma"))
    C.S.emit(finals)
    C.st.close()
    return nc


def _rot_tables():
    pos = np.arange(S, dtype=np.float32)
    inv_freq = (np.float32(500000.0) ** (-np.arange(0, 16, 2, dtype=np.float32) / np.float32(16))).astype(np.float32)
    ang = (pos[:, None] * inv_freq[None, :]).astype(np.float32)
    cos = np.cos(ang).astype(np.float32).T
    sin = np.sin(ang).astype(np.float32).T
    cs = np.zeros((2, 16, S), np.float32)
    cs[0, 0:8] = cos
    cs[0, 8:16] = cos
    cs[1, 0:8] = -sin
    cs[1, 8:16] = sin
    return cs


def _band_mats(win):
    t = np.arange(128)
    s = np.arange(128)
    cur = ((t[None, :] - s[:, None] >= 0) & (t[None, :] - s[:, None] < win)).astype(np.float32) / win
    cur -= np.eye(128, dtype=np.float32)
    prev = ((t[None, :] + 128 - s[:, None]) < win).astype(np.float32) / win
    cnt = np.minimum(t + 1, win).astype(np.float32)
    first = ((t[None, :] - s[:, None] >= 0) & (t[None, :] - s[:, None] < win)).astype(np.float32) / cnt[None, :]
    first -= np.eye(128, dtype=np.float32)
    return np.stack([cur, prev, first]).astype(np.float32)


_CACHE = {}


def _mixer_inputs(xT_b, l, h, P):
    w_in = P["w_in"][l]
    a0 = 0
    qa = w_in[:, 0 + 64 * h:0 + 64 * h + 64]
    ka = w_in[:, 256 + 64 * h:256 + 64 * h + 64]
    va = w_in[:, 512 + 64 * h:512 + 64 * h + 64]
    qb = w_in[:, 768 + 64 * h:768 + 64 * h + 64]
    kb = w_in[:, 1024 + 64 * h:1024 + 64 * h + 64]
    vb = w_in[:, 1280 + 64 * h:1280 + 64 * h + 64]
    fb = w_in[:, 1536 + h:1536 + h + 1]
    cu = w_in[:, 1540 + 64 * h:1540 + 64 * h + 64]
    cv = w_in[:, 1796 + 64 * h:1796 + 64 * h + 64]
    dp = w_in[:, 2052 + 64 * h:2052 + 64 * h + 64]
    perm = np.concatenate([np.arange(8, 16), np.arange(0, 8)])
    wfm = np.concatenate([qa, ka, qb, kb, dp, qa[:, perm], np.zeros((D, 48), np.float32), ka[:, perm]], axis=1)
    wtm = np.concatenate([va, vb, cu, cv, fb], axis=1)
    rep = lambda v: np.ascontiguousarray(np.broadcast_to(v[None, :], (128, v.shape[0]))).astype(np.float32)
    return {
        "xT": xT_b,
        "wfm": np.ascontiguousarray(wfm), "wtm": np.ascontiguousarray(wtm),
        "cs": _CACHE["cs"], "onehot": _CACHE["onehot"],
        "sguT": np.ascontiguousarray(P["sgu_w"][l, h].T), "tri": _CACHE["tri"], "ident": _CACHE["ident"],
        "sgub": np.ascontiguousarray(P["sgu_b"][l, h].reshape(128, 1)),
        "lng": rep(P["sgu_ln_g"][l, 64 * h:64 * h + 64]), "lnb": rep(P["sgu_ln_b"][l, 64 * h:64 * h + 64]),
        "poolw": np.ascontiguousarray(P["pool_w"][l, h]), "pscale": rep(P["pool_scale"][l, 64 * h:64 * h + 64]),
        "band": _CACHE["band"][h],
        "bfor": np.full((128, 1), P["b_forget"][l, h], np.float32),
    }


def _consts():
    if "cs" in _CACHE:
        return
    _CACHE["cs"] = _rot_tables()
    oh = np.zeros((32, S), np.float32)
    for n in range(32):
        oh[n, n * 256:(n + 1) * 256] = BIGM
    _CACHE["onehot"] = oh.astype(ml_dtypes.bfloat16)
    sidx = np.arange(128)
    _CACHE["tri"] = (sidx[:, None] <= sidx[None, :]).astype(np.float32)
    _CACHE["ident"] = np.eye(128, dtype=np.float32)
    _CACHE["band"] = [_band_mats(w) for w in (2, 4, 8, 16)]


def run_mixer(xT_all, l, P):
    _consts()
    if "nc_m" not in _CACHE:
        _CACHE["nc_m"] = build_mixer()
    in_maps = [_mixer_inputs(xT_all[c // 4], l, c % 4, P) for c in range(8)]
    res = run_bass_kernel_spmd(_CACHE["nc_m"], in_maps, core_ids=list(range(8)))
    y = np.zeros((NB, S, 1024), ml_dtypes.bfloat16)
    for c in range(8):
        b, h = c // 4, c % 4
        yy = np.asarray(res.results[c]["y"]).view(ml_dtypes.bfloat16).reshape(S, 256) if res.results[c]["y"].dtype != ml_dtypes.bfloat16 else res.results[c]["y"]
        for m in range(4):
            y[b, :, 256 * m + 64 * h:256 * m + 64 * h + 64] = yy[:, 64 * m:64 * m + 64]
    return y


NTOK = 2048
NCH = 44


def build_post(NT4=4):
    nc = bass.Bass("TRN2", target_bir_lowering=False)
    C = Ctx(nc)
    dt = nc.dram_tensor
    NTK = NT4 * 512
    yin = dt("yin", [128 + NTK, D], BF16, kind="ExternalInput").ap()
    xin = dt("xin", [128 + NTK, D], F32, kind="ExternalInput").ap()
    flag = dt("flag", [128, 1], F32, kind="ExternalInput").ap()
    wo = dt("wo", [D, D], F32, kind="ExternalInput").ap()
    wup = dt("wup", [NCH, 128, 8, 128], F32, kind="ExternalInput").ap()
    wdn = dt("wdn", [DFF, D], F32, kind="ExternalInput").ap()
    convw = dt("convw", [128, NCH, 3], F32, kind="ExternalInput").ap()
    convb = dt("convb", [128, NCH], F32, kind="ExternalInput").ap()
    lnp = dt("lnp", [4, 128, D], F32, kind="ExternalInput").ap()
    ident = dt("ident", [128, 128], F32, kind="ExternalInput").ap()
    xo = dt("xo", [NTK, D], F32, kind="ExternalOutput").ap()

    sb, ps, res = C.sb, C.ps, C.res
    wd_b = sb("wd_b", [128, 22, D], BF16)
    wo_b = sb("wo_b", [128, 8, D], BF16)
    A_T = sb("A_T", [128, 22, 512], BF16)
    X1T = [sb("X1T%d" % i, [128, 8, 512], BF16) for i in range(2)]
    X1Th = sb("X1Th", [128, 8, 2], BF16)
    X1 = sb("X1", [128, 4, D], F32)
    wu = [[sb("wu%d_%d" % (i, j), [128, 8, 128], BF16) for j in range(2)] for i in range(3)]
    H = [sb("H%d" % i, [128, 514], F32) for i in range(4)]
    tg = [sb("tg%d" % i, [128, 512], F32) for i in range(2)]
    tv = [sb("tv%d" % i, [128, 512], F32) for i in range(2)]
    sg = [sb("sg%d" % i, [128, 512], BF16) for i in range(2)]
    HALO = sb("HALO", [128, NCH, 2], F32)
    lnp_s = sb("lnp_s", [128, 4, D], F32)
    yt = [sb("yt%d" % i, [128, D], BF16) for i in range(2)]
    xt = [sb("xt%d" % i, [128, D], F32) for i in range(2)]
    rr = sb("rr", [128, D], F32)
    x1b = sb("x1b", [128, D], BF16)
    yT = sb("yT", [128, 8, 128], BF16)
    x2 = [sb("x2_%d" % i, [128, D], F32) for i in range(2)]
    stats = sb("stats", [128, 12], F32)
    mv = sb("mv", [128, 2], F32)
    sd = sb("sd", [128, 1], F32)
    rstd = sb("rstd", [128, 1], F32)
    cw_s = sb("cw_s", [128, NCH, 3], F32)
    cb_s = sb("cb_s", [128, NCH], F32)
    flag_s = sb("flag_s", [128, 1], F32)
    ident_b = sb("ident_b", [128, 128], BF16)
    PS = [ps("ps%d" % i, [128, 512], F32) for i in range(7)]
    PSB = ps("psb", [128, 1024], BF16)
    RPS = [res("ps", True) for _ in range(7)]
    RPSB = res("psb", True)
    R = lambda n: res(n)
    Rwd, Rwo, RAT, RX1, RX1Th, RHALO, Rlnp, Rrr, Rx1b, RyT = (R("wd"), R("wo"), R("at"), R("x1"), R("x1th"), R("halo"),
                                                             R("lnp"), R("rr"), R("x1b"), R("yT"))
    RX1T = [R("x1t") for _ in range(2)]
    Rwu = [[R("wu") for _ in range(2)] for _ in range(3)]
    RH = [R("h") for _ in range(4)]
    RHh = [R("hh") for _ in range(4)]
    RHALOc = [R("haloc") for _ in range(NCH)]
    Rtg = [R("tg") for _ in range(2)]
    Rtv = [R("tv") for _ in range(2)]
    Rsg = [R("sg") for _ in range(2)]
    Ryt = [R("yt") for _ in range(2)]
    Rxt = [R("xt") for _ in range(2)]
    Rx2 = [R("x2") for _ in range(2)]
    Rst, Rmv, Rsd, Rrstd, Rcw, Rcb, Rflag, Rid = (R("st"), R("mv"), R("sd"), R("rstd"), R("cw"), R("cb"), R("flag"), R("id"))

    C.dma("pool", ident_b[:], ident, writes=[Rid])
    wo_v = wo.rearrange("(kc f) n -> f kc n", f=128)
    for kc in range(0, 8, 4):
        C.dma("pool", wo_b[:, kc:kc + 4, :], wo_v[:, kc:kc + 4, :], writes=[Rwo], key="wo")
    C.dma("sp", lnp_s[:], lnp.rearrange("k p n -> p k n"), writes=[Rlnp])
    C.dma("sp", cw_s[:], convw, writes=[Rcw])
    C.dma("sp", cb_s[:], convb, writes=[Rcb])
    C.dma("sp", flag_s[:], flag, writes=[Rflag])

    def load_sub(i):
        sl = i % 2
        C.dma("sp", yt[sl][:], yin[i * 128:(i + 1) * 128, :], writes=[Ryt[sl]])
        C.dma("sp", xt[sl][:], xin[i * 128:(i + 1) * 128, :], writes=[Rxt[sl]])

    def layer_norm(src, Rsrc, dst, Rdst, gi):
        def st(e):
            e.bn_stats(out=stats[:, 0:6], in_=src[:, 0:512])
            return e.bn_stats(out=stats[:, 6:12], in_=src[:, 512:1024])
        C.dve(st, reads=[Rsrc], writes=[Rst])
        C.dve(lambda e: e.bn_aggr(out=mv[:], in_=stats[:]), reads=[Rst], writes=[Rmv])
        C.dve(lambda e: e.tensor_scalar(out=sd[:], in0=mv[:, 1:2], scalar1=EPS, scalar2=None, op0=ALU.add),
              reads=[Rmv], writes=[Rsd])
        C.act(lambda e: e.activation(out=sd[:], in_=sd[:], func=AF.Sqrt), reads=[Rsd], writes=[Rsd])
        C.dve(lambda e: e.reciprocal(out=rstd[:], in_=sd[:]), reads=[Rsd], writes=[Rrstd])
        C.dve(lambda e: e.tensor_scalar(out=src[:], in0=src[:], scalar1=mv[:, 0:1], scalar2=rstd[:, 0:1],
                                        op0=ALU.subtract, op1=ALU.mult), reads=[Rsrc, Rmv, Rrstd], writes=[Rsrc])
        C.dve(lambda e: e.tensor_tensor(out=src[:], in0=src[:], in1=lnp_s[:, gi, :], op=ALU.mult),
              reads=[Rsrc, Rlnp], writes=[Rsrc])
        C.dve(lambda e: e.tensor_tensor(out=dst, in0=src[:], in1=lnp_s[:, gi + 1, :], op=ALU.add),
              reads=[Rsrc, Rlnp], writes=[Rdst])

    def a_front(i):
        sl = i % 2

        def tr_y(e):
            ins = None
            for kc in range(8):
                ins = e.transpose(PSB[:, kc * 128:(kc + 1) * 128], yt[sl][:, kc * 128:(kc + 1) * 128], ident_b[:])
            return ins
        C.pe(tr_y, reads=[Ryt[sl], Rid], writes=[RPSB])
        C.act(lambda e: e.copy(out=yT[:].rearrange("p k n -> p (k n)"), in_=PSB[:]), reads=[RPSB], writes=[RyT])
        for half in range(2):
            C.pe(_mm_group(PS[half][:], [(yT[:, kc, :], wo_b[:, kc, half * 512:(half + 1) * 512]) for kc in range(8)]),
                 reads=[RyT, Rwo], writes=[RPS[half]])

    def a_x1src(i):
        if i == 0:
            return x2[0][:], Rx2[0]
        c = (i - 1) % 4
        return X1[:, c, :], RX1

    def a_cast(i):
        x1src, Rx1src = a_x1src(i)
        C.act(lambda e: e.copy(out=x1b[:], in_=x1src), reads=[Rx1src], writes=[Rx1b])

    def a_back(i):
        def tr_x(e):
            ins = None
            for kc in range(8):
                ins = e.transpose(PSB[:, kc * 128:(kc + 1) * 128], x1b[:, kc * 128:(kc + 1) * 128], ident_b[:])
            return ins
        C.pe(tr_x, reads=[Rx1b, Rid], writes=[RPSB])
        psv = PSB[:].rearrange("p (k n) -> p k n", n=128)
        if i == 0:
            C.act(lambda e: e.copy(out=X1Th[:], in_=psv[:, :, 126:128]), reads=[RPSB], writes=[RX1Th])
        else:
            T = (i - 1) // 4
            c = (i - 1) % 4
            C.act(lambda e: e.copy(out=X1T[T % 2][:, :, c * 128:(c + 1) * 128], in_=psv), reads=[RPSB], writes=[RX1T[T % 2]])

    def a_mid(i):
        sl = i % 2
        for half in range(2):
            C.dve(lambda e, half=half: e.scalar_tensor_tensor(out=rr[:, half * 512:(half + 1) * 512], in0=xt[sl][:, half * 512:(half + 1) * 512],
                                                              scalar=ALPHA, in1=PS[half][:], op0=ALU.mult, op1=ALU.add),
                  reads=[Rxt[sl], RPS[half]], writes=[Rrr])
        dst, Rdst = a_x1src(i)
        layer_norm(rr, Rrr, dst, Rdst, 0)

    def stage_a_tile(T):
        subs = ([0] if T == 0 else []) + [1 + 4 * T + c for c in range(4)]
        a_front(subs[0])
        for n, i in enumerate(subs):
            if i + 1 <= NT4 * 4:
                load_sub(i + 1)
            if n >= 1:
                a_cast(subs[n - 1])
            a_mid(i)
            if n >= 1:
                a_back(subs[n - 1])
            if n + 1 < len(subs):
                a_front(subs[n + 1])
        a_cast(subs[-1])
        a_back(subs[-1])

    wup_loaded = {}

    def load_wup(T, cc):
        sl = (T * 22 + cc) % 3
        if os.environ.get("P_NOLOAD") and T > 0:
            return
        C.dma("pool", wu[sl][0][:], wup[cc], writes=[Rwu[sl][0]])
        C.dma("pool", wu[sl][1][:], wup[22 + cc], writes=[Rwu[sl][1]])

    state = {"k": 0}

    def ffn_chunk(T, cc, which):
        ch = cc + 22 * which
        wsl = (T * 22 + cc) % 3
        k = 2 * (T * 22 + cc) + which
        hb = k % 4
        bank = (2, 3, 5, 6)[hb]
        xs = X1T[T % 2]
        if T == 0:
            C.pe(_mm_group(PS[4][:, 0:2], [(wu[wsl][which][:, kc, :], X1Th[:, kc, :]) for kc in range(8)]),
                 reads=[Rwu[wsl][which], RX1Th], writes=[RPS[4]])
            C.act(lambda e: e.activation(out=H[hb][:, 0:2], in_=PS[4][:, 0:2], func=AF.Copy, scale=flag_s[:, 0:1]),
                  reads=[RPS[4], Rflag], writes=[RHh[hb]])
        C.pe(_mm_group(PS[bank][:], [(wu[wsl][which][:, kc, :], xs[:, kc, :]) for kc in range(8)]),
             reads=[Rwu[wsl][which], RX1T[T % 2]], writes=[RPS[bank]])
        C.act(lambda e: e.copy(out=H[hb][:, 2:514], in_=PS[bank][:]), reads=[RPS[bank]], writes=[RH[hb]])
        C.pool(lambda e: e.tensor_copy(out=HALO[:, ch, :], in_=H[hb][:, 512:514]), reads=[RH[hb]], writes=[RHALOc[ch]])
        t, Rt = (tg[cc % 2], Rtg[cc % 2]) if which == 0 else (tv[cc % 2], Rtv[cc % 2])
        C.act(lambda e: e.activation(out=t[:], in_=H[hb][:, 0:512], func=AF.Identity, bias=cb_s[:, ch:ch + 1],
                                     scale=cw_s[:, ch, 0:1]), reads=[RH[hb], RHh[hb], Rcw, Rcb], writes=[Rt])
        C.dve(lambda e: e.scalar_tensor_tensor(out=t[:], in0=H[hb][:, 1:513], scalar=cw_s[:, ch, 1:2], in1=t[:],
                                               op0=ALU.mult, op1=ALU.add), reads=[RH[hb], RHh[hb], Rcw, Rt], writes=[Rt])
        C.dve(lambda e: e.scalar_tensor_tensor(out=t[:], in0=H[hb][:, 2:514], scalar=cw_s[:, ch, 2:3], in1=t[:],
                                               op0=ALU.mult, op1=ALU.add), reads=[RH[hb], Rcw, Rt], writes=[Rt])

    def halo_read(T, cc):
        if T == 0:
            return
        for which in range(2):
            ch = cc + 22 * which
            hb = (2 * (T * 22 + cc) + which) % 4
            C.pool(lambda e, ch=ch, hb=hb: e.tensor_copy(out=H[hb][:, 0:2], in_=HALO[:, ch, :]),
                   reads=[RHALOc[ch]], writes=[RHh[hb]])

    def ffn_pair(T, cc):
        if T * 22 + cc + 1 < NT4 * 22:
            nT1, ncc1 = divmod(T * 22 + cc + 1, 22)
            halo_read(nT1, ncc1)
        if T * 22 + cc + 2 < NT4 * 22:
            nT, ncc = divmod(T * 22 + cc + 2, 22)
            load_wup(nT, ncc)
        ffn_chunk(T, cc, 0)
        ffn_chunk(T, cc, 1)
        s2 = cc % 2
        C.act(lambda e: e.activation(out=sg[s2][:], in_=tg[s2][:], func=AF.Silu), reads=[Rtg[s2]], writes=[Rsg[s2]])
        C.pool(lambda e: e.tensor_tensor(out=A_T[:, cc, :], in0=sg[s2][:], in1=tv[s2][:], op=ALU.mult),
               reads=[Rsg[s2], Rtv[s2]], writes=[RAT])

    def stage_c(T, c):
        j = T * 4 + c
        banks = (5, 6) if j % 2 == 0 else (0, 1)
        for half in range(2):
            C.pe(_mm_group(PS[banks[half]][:], [(A_T[:, cc, c * 128:(c + 1) * 128], wd_b[:, cc, half * 512:(half + 1) * 512])
                                                 for cc in range(22)]), reads=[RAT, Rwd], writes=[RPS[banks[half]]])
        for half in range(2):
            C.dve(lambda e, half=half: e.scalar_tensor_tensor(out=rr[:, half * 512:(half + 1) * 512], in0=X1[:, c, half * 512:(half + 1) * 512],
                                                              scalar=ALPHA, in1=PS[banks[half]][:], op0=ALU.mult, op1=ALU.add),
                  reads=[RX1, RPS[banks[half]]], writes=[Rrr])
        o = j % 2
        layer_norm(rr, Rrr, x2[o][:], Rx2[o], 2)
        C.dma("sp", xo[j * 128:(j + 1) * 128, :], x2[o][:], reads=[Rx2[o]], key="xo%d" % o)

    load_sub(0)
    load_wup(0, 0)
    load_wup(0, 1)
    wd_v = wdn.rearrange("(cc p) n -> p cc n", p=128)
    for T in range(NT4):
        stage_a_tile(T)
        if T == 0:
            for c0 in range(0, 22, 2):
                C.dma("pool", wd_b[:, c0:c0 + 2, :], wd_v[:, c0:c0 + 2, :], writes=[Rwd], key="wd")
        for cc in range(22):
            ffn_pair(T, cc)
        for c in range(4):
            stage_c(T, c)
    return _finish_mixer(C, nc)


def _post_inputs(y_b, x_b, q, l, P):
    t0 = q * NTOK
    if q == 0:
        yh = np.zeros((128, D), ml_dtypes.bfloat16)
        xh = np.zeros((128, D), np.float32)
    else:
        yh = y_b[t0 - 128:t0]
        xh = x_b[t0 - 128:t0]
    rep = lambda v: np.broadcast_to(v[None, :], (128, v.shape[0]))
    key = ("post_w", l)
    if key not in _CACHE:
        w_up = P["w_up"][l]
        _CACHE[key] = {
            "wo": np.ascontiguousarray(P["w_o"][l]),
            "wup": np.ascontiguousarray(w_up.reshape(8, 128, NCH, 128).transpose(2, 1, 0, 3)),
            "wdn": np.ascontiguousarray(P["w_down"][l]),
            "convw": np.ascontiguousarray(P["conv_w"][l].reshape(3, NCH, 128).transpose(2, 1, 0)),
            "convb": np.ascontiguousarray(P["conv_b"][l].reshape(NCH, 128).T),
            "lnp": np.ascontiguousarray(np.stack([rep(P["ln1_g"][l]), rep(P["ln1_b"][l]), rep(P["ln2_g"][l]), rep(P["ln2_b"][l])]).astype(np.float32)),
            "ident": np.eye(128, dtype=np.float32),
        }
    m = dict(_CACHE[key])
    m["yin"] = np.ascontiguousarray(np.concatenate([yh, y_b[t0:t0 + NTOK]], axis=0))
    m["xin"] = np.ascontiguousarray(np.concatenate([xh, x_b[t0:t0 + NTOK]], axis=0))
    m["flag"] = np.full((128, 1), 0.0 if q == 0 else 1.0, np.float32)
    return m


def run_post(y, x, l, P):
    if "nc_p" not in _CACHE:
        _CACHE["nc_p"] = build_post()
    in_maps = [_post_inputs(y[c // 4], x[c // 4], c % 4, l, P) for c in range(8)]
    res = run_bass_kernel_spmd(_CACHE["nc_p"], in_maps, core_ids=list(range(8)))
    out = np.zeros((NB, S, D), np.float32)
    for c in range(8):
        out[c // 4, (c % 4) * NTOK:(c % 4 + 1) * NTOK] = res.results[c]["xo"]
    return out


def kernel(**inputs):
    P = {k: np.asarray(v) for k, v in inputs.items()}
    x = np.ascontiguousarray(P["x"], dtype=np.float32)
    for l in range(2):
        xT = [np.ascontiguousarray(x[b].T) for b in range(NB)]
        y = run_mixer(xT, l, P)
        x = run_post(y, x, l, P)
    return x
```
